# Optimizing a Trainium2 kernel written in Bass

```python
import jax, jax.numpy as jnp
from jax import lax
import numpy as np

D_MODEL = 1024
BATCH = 8
SEQ = 2048
DEPTH = 1
DEC_BATCH = 2
DEC_SEQ = 16384
PAST_LEN = 128

EXPAND = 2
D_MIX = EXPAND * D_MODEL
D_CONV = D_MIX // 2
D_RWKV = D_MIX - D_CONV
HEAD_DIM = 64
N_HEADS = D_RWKV // HEAD_DIM
CONV_WIDTH = 3
R_DECAY = 64
R_ICLR = 64
N_DIRS = 2
N_CONV_IN = 4 * D_CONV
N_RWKV_IN = 4 * D_RWKV + N_DIRS * R_DECAY + N_DIRS * R_ICLR
N_IN = N_CONV_IN + N_RWKV_IN
LN_EPS = 1e-5
LNX_EPS = 64e-5
DN_ALPHA = (2.0 * DEPTH) ** 0.25
DN_BETA = (8.0 * DEPTH) ** -0.25

kernel_name = "hybrid_conv_rwkv7_bidir_encoder"


def layer_norm(x, g, b, eps=LN_EPS):
    xf = x.astype(jnp.float32)
    mu = jnp.mean(xf, -1, keepdims=True)
    var = jnp.mean(jnp.square(xf - mu), -1, keepdims=True)
    return ((xf - mu) * lax.rsqrt(var + eps) * g.astype(jnp.float32) + b.astype(jnp.float32)).astype(x.dtype)


def shift_prev(u):
    return jnp.pad(u, ((0, 0), (1, 0), (0, 0)))[:, :-1]


def shift_next(u):
    return jnp.pad(u, ((0, 0), (0, 1), (0, 0)))[:, 1:]


def short_conv_branch(u, conv_w, conv_b):
    h, bg, cg, z = jnp.split(u, 4, axis=-1)
    p = cg * h
    q = conv_w[0] * shift_prev(p) + conv_w[1] * p + conv_w[2] * shift_next(p) + conv_b
    return bg * q * jax.nn.silu(z)


def rwkv7_scan(r, w, k, v, kk, a):
    xs = tuple(jnp.moveaxis(t, 2, 0) for t in (r, w, k, v, kk, a))
    d, b, _, h, n = r.shape

    def step(S, inp):
        r_t, w_t, k_t, v_t, kk_t, a_t = inp
        sa = jnp.einsum('dbhvk,dbhk->dbhv', S, kk_t)
        S = (S * w_t[..., None, :]
             - sa[..., :, None] * (kk_t * a_t)[..., None, :]
             + v_t[..., :, None] * k_t[..., None, :])
        return S, jnp.einsum('dbhvk,dbhk->dbhv', S, r_t)

    S0 = jnp.zeros((d, b, h, n, n), jnp.float32)
    _, ys = lax.scan(step, S0, xs)
    return jnp.moveaxis(ys, 0, 2)


def rwkv7_branch(u, mu, w0, w_up, a0, a_up, k_k, k_a, r_k, lnx_g, lnx_b):
    out_dtype = u.dtype
    u = u.astype(jnp.float32)
    B, T, _ = u.shape
    c = 0.5 * (shift_prev(u) + shift_next(u))
    m = u + mu.astype(jnp.float32) * (c - u)
    r, k, v, z, lw, la = jnp.split(
        m, [D_RWKV, 2 * D_RWKV, 3 * D_RWKV, 4 * D_RWKV, 4 * D_RWKV + N_DIRS * R_DECAY], axis=-1)
    lw = lw.reshape(B, T, N_DIRS, R_DECAY)
    la = la.reshape(B, T, N_DIRS, R_ICLR)
    wlog = -jax.nn.softplus(-(w0.astype(jnp.float32)
                              + jnp.einsum('btdr,drc->btdc', jnp.tanh(lw), w_up.astype(jnp.float32)))) - 0.5
    decay = jnp.exp(-jnp.exp(wlog))
    a = jax.nn.sigmoid(a0.astype(jnp.float32) + jnp.einsum('btdr,drc->btdc', la, a_up.astype(jnp.float32)))
    kk = (k * k_k.astype(jnp.float32)).reshape(B, T, N_HEADS, HEAD_DIM)
    kk = kk / jnp.maximum(jnp.sqrt(jnp.sum(kk * kk, -1, keepdims=True)), 1e-12)
    kd = k[:, :, None, :] * (1.0 + (a - 1.0) * k_a.astype(jnp.float32))

    def dir_stack(t):
        t = jnp.moveaxis(t, 2, 0).reshape(N_DIRS, B, T, N_HEADS, HEAD_DIM)
        return jnp.stack([t[0], t[1, :, ::-1]])

    def shared(t):
        return jnp.broadcast_to(t.reshape(B, T, 1, -1), (B, T, N_DIRS, t.shape[-1] if t.ndim == 3 else -1))

    kk_flat = kk.reshape(B, T, D_RWKV)
    ys = rwkv7_scan(dir_stack(shared(r)), dir_stack(decay), dir_stack(kd),
                    dir_stack(shared(v)), dir_stack(shared(kk_flat)), dir_stack(a))
    y = ys[0] + ys[1, :, ::-1]
    mean = jnp.mean(y, -1, keepdims=True)
    var = jnp.mean(jnp.square(y - mean), -1, keepdims=True)
    yn = (y - mean) * lax.rsqrt(var + LNX_EPS)
    yn = yn * lnx_g.astype(jnp.float32).reshape(N_HEADS, HEAD_DIM) + lnx_b.astype(jnp.float32).reshape(N_HEADS, HEAD_DIM)
    rh = r.reshape(B, T, N_HEADS, HEAD_DIM)
    kbar = (0.5 * (kd[:, :, 0] + kd[:, :, 1])).reshape(B, T, N_HEADS, HEAD_DIM)
    vh = v.reshape(B, T, N_HEADS, HEAD_DIM)
    bonus = jnp.sum(rh * kbar * r_k.astype(jnp.float32), -1, keepdims=True) * vh
    out = (yn + bonus).reshape(B, T, D_RWKV) * jax.nn.silu(z)
    return out.astype(out_dtype)


def encoder_layer(x, w_in, conv_w, conv_b, shift_mu, w0, w_up, a0, a_up, k_k, k_a, r_k,
                  lnx_g, lnx_b, w_out, ln_g, ln_b):
    u = jnp.einsum('btd,dn->btn', x, w_in)
    y_conv = short_conv_branch(u[..., :N_CONV_IN], conv_w, conv_b)
    y_rwkv = rwkv7_branch(u[..., N_CONV_IN:], shift_mu, w0, w_up, a0, a_up, k_k, k_a, r_k, lnx_g, lnx_b)
    y = jnp.einsum('btm,md->btd', jnp.concatenate([y_conv, y_rwkv], -1), w_out)
    return layer_norm(DN_ALPHA * x + y, ln_g, ln_b)


def trunk(x, emb_ln_g, emb_ln_b, w_in, conv_w, conv_b, shift_mu, w0, w_up, a0, a_up, k_k, k_a, r_k,
          lnx_g, lnx_b, w_out, ln_g, ln_b):
    x = layer_norm(x, emb_ln_g, emb_ln_b)
    for l in range(DEPTH):
        x = encoder_layer(x, w_in[l], conv_w[l], conv_b[l], shift_mu[l], w0[l], w_up[l], a0[l], a_up[l],
                          k_k[l], k_a[l], r_k[l], lnx_g[l], lnx_b[l], w_out[l], ln_g[l], ln_b[l])
    return x


def setup_inputs(seed: int = 0) -> dict:
    key = jax.random.key(seed)
    ks = jax.random.split(key, 20)
    nrm = jax.random.normal
    L = DEPTH
    f32 = jnp.float32
    return {
        "x_prompt": nrm(ks[0], (BATCH, SEQ, D_MODEL), f32),
        "x_sample": nrm(ks[1], (DEC_BATCH, DEC_SEQ, D_MODEL), f32),
        "emb_ln_g": 1.0 + 0.02 * nrm(ks[2], (D_MODEL,), f32),
        "emb_ln_b": 0.02 * nrm(ks[3], (D_MODEL,), f32),
        "w_in": nrm(ks[4], (L, D_MODEL, N_IN), f32) * D_MODEL ** -0.5,
        "conv_w": nrm(ks[5], (L, CONV_WIDTH, D_CONV), f32) * CONV_WIDTH ** -0.5,
        "conv_b": 0.02 * nrm(ks[6], (L, D_CONV), f32),
        "shift_mu": jax.random.uniform(ks[7], (L, N_RWKV_IN), f32, 0.0, 1.0),
        "w0": jax.random.uniform(ks[8], (L, N_DIRS, D_RWKV), f32, -6.5, -1.5),
        "w_up": nrm(ks[9], (L, N_DIRS, R_DECAY, D_RWKV), f32) * 0.1 * R_DECAY ** -0.5,
        "a0": 0.1 * nrm(ks[10], (L, N_DIRS, D_RWKV), f32),
        "a_up": nrm(ks[11], (L, N_DIRS, R_ICLR, D_RWKV), f32) * R_ICLR ** -0.5,
        "k_k": 0.85 + 0.02 * nrm(ks[12], (L, D_RWKV), f32),
        "k_a": 1.0 + 0.02 * nrm(ks[13], (L, D_RWKV), f32),
        "r_k": 0.1 * nrm(ks[14], (L, N_HEADS, HEAD_DIM), f32),
        "lnx_g": 1.0 + 0.02 * nrm(ks[15], (L, D_RWKV), f32),
        "lnx_b": 0.02 * nrm(ks[16], (L, D_RWKV), f32),
        "w_out": nrm(ks[17], (L, D_MIX, D_MODEL), f32) * (D_MIX ** -0.5) * DN_BETA,
        "ln_g": 1.0 + 0.02 * nrm(ks[18], (L, D_MODEL), f32),
        "ln_b": 0.02 * nrm(ks[19], (L, D_MODEL), f32),
    }


def reference(x_prompt, x_sample, emb_ln_g, emb_ln_b, w_in, conv_w, conv_b, shift_mu, w0, w_up, a0, a_up,
              k_k, k_a, r_k, lnx_g, lnx_b, w_out, ln_g, ln_b):
    y_prompt = trunk(x_prompt, emb_ln_g, emb_ln_b, w_in, conv_w, conv_b, shift_mu, w0, w_up, a0, a_up,
                     k_k, k_a, r_k, lnx_g, lnx_b, w_out, ln_g, ln_b)
    y_sample = trunk(x_sample, emb_ln_g, emb_ln_b, w_in, conv_w, conv_b, shift_mu, w0, w_up, a0, a_up,
                     k_k, k_a, r_k, lnx_g, lnx_b, w_out, ln_g, ln_b)
    return (y_prompt, y_sample)
```

```python
import contextlib
import numpy as np
import concourse.bass as bass
import concourse.mybir as mybir
from concourse.bass_utils import run_bass_kernel_spmd

F32 = mybir.dt.float32
BF16 = mybir.dt.bfloat16
AF = mybir.ActivationFunctionType
ALU = mybir.AluOpType
AX = mybir.AxisListType

D = 1024
NCV = 4096
NRW = 4352
NIN = NCV + NRW
HD = 64
LN_EPS = 1e-5
LNX_EPS = 64e-5
DN_ALPHA = 2.0 ** 0.25
CDEC = -float(np.exp(-0.5))
NG = 2
GC = 512
NH = 8
NLEV = 7


class Buf:
    __slots__ = ("name", "w", "rs", "excl")

    def __init__(self, name):
        self.name = name
        self.w = None
        self.rs = {}
        self.excl = False


class Sch:
    R = 4
    ND = 24

    def __init__(self, nc, es):
        self.nc = nc
        self.eng = {"pe": nc.tensor, "act": nc.scalar, "dve": nc.vector, "pool": nc.gpsimd, "sp": nc.sync}
        self.sems = {e: [es.enter_context(nc.semaphore(f"s_{e}{i}")) for i in range(self.R)]
                     for e in ("pe", "act", "dve", "pool")}
        self.cnt = {e: 0 for e in self.sems}
        self.known = {e: {} for e in self.eng}
        self.dsems = [es.enter_context(nc.semaphore(f"s_d{i}")) for i in range(self.ND)]
        self.dval = [0] * self.ND
        self.dnext = 0
        self.all_bufs = []

    def buf(self, name):
        b = Buf(name)
        self.all_bufs.append(b)
        return b

    def _wait(self, waiter, ev):
        key = (ev[0], ev[1])
        if ev[0] == "E" and ev[1] == waiter and waiter == "pe":
            return
        if self.known[waiter].get(key, -1) >= ev[2]:
            return
        if ev[0] == "E":
            sem = self.sems[ev[1]][ev[2] % self.R]
            val = ev[2] // self.R + 1
        else:
            sem = self.dsems[ev[1]]
            val = ev[2]
        self.eng[waiter].wait_ge(sem, val)
        self.known[waiter][key] = ev[2]

    def _deps(self, waiter, reads, writes):
        for b in reads:
            if b.w is not None:
                self._wait(waiter, b.w)
            if b.excl:
                for ev in b.rs.values():
                    if not (ev[0] == "E" and ev[1] == waiter):
                        self._wait(waiter, ev)
        for b in writes:
            if b.w is not None:
                self._wait(waiter, b.w)
            for ev in b.rs.values():
                self._wait(waiter, ev)

    def _commit(self, ev, reads, writes):
        for b in reads:
            b.rs[(ev[0], ev[1])] = ev
        for b in writes:
            b.w = ev
            b.rs = {}

    muted = False

    def stage(self, k):
        self.muted = k > DBG.get("stage", 99)

    def op(self, eng, fn, reads, writes):
        if self.muted:
            return
        self._deps(eng, reads, writes)
        ins = fn()
        idx = self.cnt[eng]
        self.cnt[eng] += 1
        ins.then_inc(self.sems[eng][idx % self.R], 1)
        self._commit(("E", eng, idx), reads, writes)

    def dma(self, out, in_, reads, writes, **kw):
        if self.muted:
            return
        k = self.dnext
        self.dnext = (self.dnext + 1) % self.ND
        if self.dval[k] > 0:
            self._wait("sp", ("D", k, self.dval[k]))
        self._deps("sp", reads, writes)
        ins = self.nc.sync.dma_start(out=out, in_=in_, **kw)
        self.dval[k] += 16
        ins.then_inc(self.dsems[k], 16)
        self._commit(("D", k, self.dval[k]), reads, writes)

    def barrier(self):
        self.muted = False
        for w in self.eng:
            for k in range(self.ND):
                if self.dval[k] > 0:
                    self._wait(w, ("D", k, self.dval[k]))
            for e in self.cnt:
                if self.cnt[e] > 0:
                    self._wait(w, ("E", e, self.cnt[e] - 1))

    def finish(self):
        for k in range(self.ND):
            if self.dval[k] > 0:
                self._wait("sp", ("D", k, self.dval[k]))
        for e in self.cnt:
            if self.cnt[e] > 0:
                self._wait("sp", ("E", e, self.cnt[e] - 1))


class TL:
    def __init__(self, sch, t, name):
        self.t = t
        self.b = sch.buf(name)

    def __getitem__(self, k):
        return self.t[k]


def bc(ap, shape):
    return ap.broadcast_to(list(shape))


def build(NT, BP):
    NTOK = NT * 128
    NB = max(1, NT // BP - 1)
    nc = bass.Bass("TRN2", target_bir_lowering=False)
    dt_in = lambda n, s: nc.dram_tensor(n, s, F32, kind="ExternalInput").ap()
    xs = dt_in("xs", [NTOK, D])
    keep_d = dt_in("keep", [128, NB])
    cst_d = dt_in("cst", [128, CST_COLS])
    emb_g_d = dt_in("emb_ln_g", [D]); emb_b_d = dt_in("emb_ln_b", [D])
    w_in_d = dt_in("w_in", [D, NIN])
    conv_w_d = dt_in("conv_w", [3, 1024]); conv_b_d = dt_in("conv_b", [1024])
    mu_d = dt_in("shift_mu", [NRW])
    w0_d = dt_in("w0", [2, 1024]); wup_d = dt_in("w_up", [2, 64, 1024])
    a0_d = dt_in("a0", [2, 1024]); aup_d = dt_in("a_up", [2, 64, 1024])
    kk_d = dt_in("k_k", [1024]); ka_d = dt_in("k_a", [1024]); rk_d = dt_in("r_k", [1024])
    lxg_d = dt_in("lnx_g", [1024]); lxb_d = dt_in("lnx_b", [1024])
    wout_d = dt_in("w_out", [2048, D])
    lng_d = dt_in("ln_g", [D]); lnb_d = dt_in("ln_b", [D])
    ys = nc.dram_tensor("ys", [NTOK, D], F32, kind="ExternalOutput").ap()
    XT = nc.dram_tensor("XT", [128, 8, NTOK + 2], BF16).ap()
    YF = nc.dram_tensor("YF", [NTOK, GC], F32).ap()
    YR = nc.dram_tensor("YR", [128, 8, NTOK], BF16).ap()

    with contextlib.ExitStack() as es0:
        es0.enter_context(nc.allow_non_contiguous_dma(reason="small strided param / halo loads"))
        S = Sch(nc, es0)
        XT_bs = [S.buf(f"XT{i}") for i in range(NT + 2)]
        YF_bs = [S.buf(f"YF{i}") for i in range(NT)]
        YR_bs = [[S.buf(f"YR{g}_{i}") for i in range(NT)] for g in range(NG)]
        ys_bs = [S.buf(f"ys{i}") for i in range(NT)]

        uid = [0]

        def sb(es, name, shape, dtype):
            uid[0] += 1
            nm = f"sb{uid[0]}_{name}"
            NAMES[name] = nm
            return TL(S, es.enter_context(nc.sbuf_tensor(nm, list(shape), dtype)), nm)

        PS = [TL(S, es0.enter_context(nc.psum_tensor(f"ps{i}", [128, 512], F32)), f"ps{i}") for i in range(8)]
        for p_ in PS:
            p_.b.excl = True
        prot = {"f": [0, [0, 1, 2, 3]], "b": [0, [4, 5, 6, 7]]}

        def pbank(kind):
            st = prot[kind]
            p = PS[st[1][st[0] % len(st[1])]]
            st[0] += 1
            return p

        cst = sb(es0, "cst", [128, CST_COLS], F32)
        S.dma(cst[:], cst_d, [], [cst.b])
        identb = sb(es0, "identb", [128, 128], BF16)
        S.op("dve", lambda: nc.vector.tensor_copy(out=identb[:], in_=cst[:, 0:128]), [cst.b], [identb.b])
        keep = sb(es0, "keep", [128, NB], F32)
        S.dma(keep[:], keep_d, [], [keep.b])
        zpad = sb(es0, "zpad", [128, 8, 1], BF16)
        S.op("pool", lambda: nc.gpsimd.memset(zpad[:], 0.0), [], [zpad.b])
        S.dma(XT[:, :, 0:1], zpad[:], [zpad.b], [XT_bs[0]])
        S.dma(XT[:, :, NTOK + 1:NTOK + 2], zpad[:], [zpad.b], [XT_bs[NT + 1]])

        onesf = sb(es0, "onesf", [1, 128], F32)
        S.op("pool", lambda: nc.gpsimd.memset(onesf[:], 1.0), [], [onesf.b])
        rowbuf = sb(es0, "rowbuf", [1, 1024], F32)

        def bcast_load(dst, dcol, src1d, ncol):
            S.dma(rowbuf[0:1, 0:ncol], src1d.rearrange("(o n) -> o n", o=1), [], [rowbuf.b])
            for c_ in range(0, ncol, 512):
                n_ = min(512, ncol - c_)
                pb_ = pbank("f")
                S.op("pe", lambda: nc.tensor.matmul(pb_[:, 0:n_], lhsT=onesf[0:1, :], rhs=rowbuf[0:1, c_:c_ + n_],
                                                    start=True, stop=True), [onesf.b, rowbuf.b], [pb_.b])
                S.op("act", lambda: nc.scalar.copy(out=dst[:, dcol + c_:dcol + c_ + n_], in_=pb_[:, 0:n_]), [pb_.b], [dst.b])

        def ln_stats(es_tag, src, mv, rstd, nb_, st6, eps):
            S.op("dve", lambda: nc.vector.bn_stats(out=st6[:, 0, :], in_=src[:, 0:512]), [src.b], [st6.b])
            S.op("dve", lambda: nc.vector.bn_stats(out=st6[:, 1, :], in_=src[:, 512:1024]), [src.b], [st6.b])
            S.op("dve", lambda: nc.vector.bn_aggr(out=mv[:], in_=st6[:]), [st6.b], [mv.b])
            S.op("act", lambda: nc.scalar.activation(out=rstd[:], in_=mv[:, 1:2], func=AF.Sqrt, bias=eps, scale=1.0),
                 [mv.b], [rstd.b])
            S.op("dve", lambda: nc.vector.reciprocal(out=rstd[:], in_=rstd[:]), [rstd.b], [rstd.b])
            S.op("dve", lambda: nc.vector.scalar_tensor_tensor(out=nb_[:], in0=mv[:, 0:1], scalar=-1.0, in1=rstd[:],
                                                               op0=ALU.mult, op1=ALU.mult), [mv.b, rstd.b], [nb_.b])

        SD = int(nc.vector.BN_STATS_DIM)
        AD = int(nc.vector.BN_AGGR_DIM)

        with contextlib.ExitStack() as es:
            gbc = sb(es, "p1_g", [128, D], F32); bbc = sb(es, "p1_b", [128, D], F32)
            bcast_load(gbc, 0, emb_g_d, D)
            bcast_load(bbc, 0, emb_b_d, D)
            xt = [sb(es, f"p1_x{i}", [128, D], F32) for i in range(2)]
            xn = [sb(es, f"p1_xn{i}", [128, D], F32) for i in range(2)]
            xg = [sb(es, f"p1_xg{i}", [128, D], F32) for i in range(2)]
            xh = [sb(es, f"p1_xh{i}", [128, D], BF16) for i in range(2)]
            xT = [sb(es, f"p1_xT{i}", [128, 8, 128], BF16) for i in range(2)]
            st6 = [sb(es, f"p1_st{i}", [128, 2, SD], F32) for i in range(2)]
            mv = [sb(es, f"p1_mv{i}", [128, AD], F32) for i in range(2)]
            rstd = [sb(es, f"p1_rs{i}", [128, 1], F32) for i in range(2)]
            nb_ = [sb(es, f"p1_nb{i}", [128, 1], F32) for i in range(2)]
            for i in range(NT if DBG.get("p1", True) else 0):
                p = i % 2
                S.dma(xt[p][:], xs[i * 128:(i + 1) * 128, :], [], [xt[p].b])
                ln_stats("p1", xt[p], mv[p], rstd[p], nb_[p], st6[p], LN_EPS)
                S.op("act", lambda: nc.scalar.activation(out=xn[p][:], in_=xt[p][:], func=AF.Identity,
                                                         bias=nb_[p][:, 0:1], scale=rstd[p][:, 0:1]),
                     [xt[p].b, rstd[p].b, nb_[p].b], [xn[p].b])
                S.op("dve", lambda: nc.vector.tensor_tensor(out=xg[p][:], in0=xn[p][:], in1=gbc[:], op=ALU.mult),
                     [xn[p].b, gbc.b], [xg[p].b])
                S.op("pool", lambda: nc.gpsimd.tensor_tensor(out=xh[p][:], in0=xg[p][:], in1=bbc[:], op=ALU.add),
                     [xg[p].b, bbc.b], [xh[p].b])
                pt = pbank("f")
                ptv = pt[:].bitcast(BF16)

                def tr():
                    ins = None
                    for c in range(8):
                        ins = nc.tensor.transpose(out=ptv[:, c * 128:(c + 1) * 128], in_=xh[p][:, c * 128:(c + 1) * 128],
                                                  identity=identb[:])
                    return ins
                S.op("pe", tr, [xh[p].b, identb.b], [pt.b])
                S.op("act", lambda: nc.scalar.copy(out=xT[p][:].rearrange("p c t -> p (c t)"), in_=ptv), [pt.b], [xT[p].b])
                S.dma(XT[:, :, 1 + i * 128:1 + (i + 1) * 128], xT[p][:], [xT[p].b], [XT_bs[i + 1]])
            S.barrier()

        def load_xth(xth, i):
            S.dma(xth[:], XT[:, :, i * 128:i * 128 + 130], [XT_bs[i], XT_bs[i + 1], XT_bs[i + 2]], [xth.b])
            if i % BP == 0 and i > 0:
                bidx = i // BP - 1
                S.op("dve", lambda: nc.vector.tensor_scalar(out=xth[:, :, 0:1], in0=xth[:, :, 0:1],
                                                            scalar1=keep[:, bidx:bidx + 1], scalar2=None, op0=ALU.mult),
                     [xth.b, keep.b], [xth.b])
            if (i + 1) % BP == 0 and i + 1 < NT:
                bidx = (i + 1) // BP - 1
                S.op("dve", lambda: nc.vector.tensor_scalar(out=xth[:, :, 129:130], in0=xth[:, :, 129:130],
                                                            scalar1=keep[:, bidx:bidx + 1], scalar2=None, op0=ALU.mult),
                     [xth.b, keep.b], [xth.b])

        for g in range(NG if DBG.get("rwkv", True) else 0):
            with contextlib.ExitStack() as es:
                c0 = g * GC
                wq = sb(es, "wq", [128, 16, 4 * GC], BF16)
                wl = sb(es, "wl", [128, 16, 256], BF16)
                with contextlib.ExitStack() as es2:
                    wst = [sb(es2, f"wst{i}", [128, 8, 512], F32) for i in range(2)]
                    mub = sb(es2, "mub", [128, 512], F32)
                    mua = sb(es2, "mua", [128, 512], F32)
                    mubb = sb(es2, "mubb", [128, 512], F32)
                    blocks = [(NCV + q * 1024 + c0, 512, wq, q * GC) for q in range(4)] + [(NCV + 4096, 256, wl, 0)]
                    for bi, (col, ncol, dst, dcol) in enumerate(blocks):
                        w_ = wst[bi % 2]
                        S.dma(w_[:, :, 0:ncol], w_in_d[:, col:col + ncol].rearrange("(c p) n -> p c n", p=128), [], [w_.b])
                        bcast_load(mub, 0, mu_d[col - NCV:col - NCV + ncol], ncol)
                        S.op("dve", lambda: nc.vector.tensor_scalar(out=mua[:, 0:ncol], in0=mub[:, 0:ncol], scalar1=-1.0,
                                                                    scalar2=1.0, op0=ALU.mult, op1=ALU.add),
                             [mub.b], [mua.b])
                        S.op("dve", lambda: nc.vector.tensor_scalar(out=mubb[:, 0:ncol], in0=mub[:, 0:ncol], scalar1=0.5,
                                                                    scalar2=None, op0=ALU.mult), [mub.b], [mubb.b])
                        S.op("dve", lambda: nc.vector.tensor_tensor(
                            out=dst[:, 0:8, dcol:dcol + ncol], in0=w_[:, :, 0:ncol],
                            in1=bc(mua[:, 0:ncol].unsqueeze(1), [128, 8, ncol]), op=ALU.mult), [w_.b, mua.b], [dst.b])
                        S.op("pool", lambda: nc.gpsimd.tensor_tensor(
                            out=dst[:, 8:16, dcol:dcol + ncol], in0=w_[:, :, 0:ncol],
                            in1=bc(mubb[:, 0:ncol].unsqueeze(1), [128, 8, ncol]), op=ALU.mult), [w_.b, mubb.b], [dst.b])
                    S.barrier()
                upw = sb(es, "upw", [64, 2, 2, GC], BF16)
                b0h = sb(es, "b0h", [1, 2, 2, GC], BF16); b0l = sb(es, "b0l", [1, 2, 2, GC], BF16)
                onesr = sb(es, "onesr", [1, 128], BF16)
                S.op("pool", lambda: nc.gpsimd.memset(onesr[:], 1.0), [], [onesr.b])
                with contextlib.ExitStack() as es2:
                    upf = sb(es2, "upf", [64, 2, 2, GC], F32)
                    b0f = sb(es2, "b0f", [1, 2, 2, GC], F32); b0t = sb(es2, "b0t", [1, 2, 2, GC], F32)
                    for wi, (ud, bd) in enumerate(((wup_d, w0_d), (aup_d, a0_d))):
                        for d in range(2):
                            S.dma(upf[:, wi, d, :], ud[d, :, c0:c0 + GC], [], [upf.b])
                            S.dma(b0f[:, wi, d, :], bd[d:d + 1, c0:c0 + GC], [], [b0f.b])
                    S.op("dve", lambda: nc.vector.tensor_copy(out=upw[:], in_=upf[:]), [upf.b], [upw.b])
                    S.op("dve", lambda: nc.vector.tensor_copy(out=b0h[:], in_=b0f[:]), [b0f.b], [b0h.b])
                    S.op("dve", lambda: nc.vector.tensor_tensor(out=b0t[:], in0=b0f[:], in1=b0h[:], op=ALU.subtract),
                         [b0f.b, b0h.b], [b0t.b])
                    S.op("dve", lambda: nc.vector.tensor_copy(out=b0l[:], in_=b0t[:]), [b0t.b], [b0l.b])
                    S.barrier()
                prm = {}
                for nm, src in (("kk", kk_d), ("ka", ka_d), ("rk", rk_d), ("lxg", lxg_d), ("lxb", lxb_d)):
                    prm[nm] = sb(es, "prm_" + nm, [128, GC], F32)
                    bcast_load(prm[nm], 0, src[c0:c0 + GC], GC)

                xth = [sb(es, f"xth{i}", [128, 8, 130], BF16) for i in range(2)]
                xst = sb(es, "xst", [128, 8, 128], BF16)
                f32t = {n: sb(es, "w_" + n, [128, GC], F32) for n in
                        ("sg", "a", "a0", "e1", "e2", "e3", "kkr", "sq", "kk", "t1", "kd", "bb", "kd0", "vbon",
                         "sz", "yf", "ysum", "yc", "tmp")}
                bft = {n: sb(es, "h_" + n, [128, GC], BF16) for n in ("V", "kh", "bh", "kkt", "rt", "Zs", "nU", "ob")}
                tl_ = sb(es, "tl_", [64, 3, 128], BF16)
                kT = sb(es, "kT", [64, NH, 128], BF16); bT = sb(es, "bT", [64, NH, 128], BF16)
                krT = sb(es, "krT", [64, NH, 2, 128], BF16)
                oT = sb(es, "oT", [128, 4, 128], BF16)
                MA1 = sb(es, "MA1", [128, NH, 2, 128], BF16)
                MA2 = sb(es, "MA2", [128, NH, 2, 128], BF16)
                XQ = [sb(es, f"XQ{i}", [128, NH, 2, 128], BF16) for i in range(2)]
                Xt = [sb(es, f"Xt{i}", [128, NH, 128], BF16) for i in range(2)]
                TT = sb(es, "TT", [128, NH, 128], BF16)
                Hf = sb(es, "Hf", [64, NH, 64], F32); Hb = sb(es, "Hb", [64, NH, 64], BF16)
                Ht = sb(es, "Ht", [64, NH, 64], F32)
                gam = sb(es, "gam", [64, NH], F32)
                ss = sb(es, "ss", [128, NH], F32); rn = sb(es, "rn", [128, NH], F32)
                bs = sb(es, "bs", [128, NH], F32)
                gs1 = sb(es, "gs1", [128, NH], F32); gs2 = sb(es, "gs2", [128, NH], F32)
                grs = sb(es, "grs", [128, NH], F32)

                for dr in range(2):
                    bwd = dr == 1
                    cb = 128 + dr * 896
                    m1 = cst[:, cb:cb + 256]; m2 = cst[:, cb + 256:cb + 512]; m3 = cst[:, cb + 512:cb + 640]
                    tinc = cst[:, cb + 640:cb + 768]; texc = cst[:, cb + 768:cb + 896]
                    ccol = cst[:, CST_COLS - 1:CST_COLS]
                    S.op("dve", lambda: nc.vector.memset(Hf[:], 0.0), [], [Hf.b])
                    S.op("dve", lambda: nc.vector.memset(Hb[:], 0.0), [], [Hb.b])
                    order = list(range(NT - 1, -1, -1)) if bwd else list(range(NT))
                    load_xth(xth[0], order[0])
                    for oi, i in enumerate(order):
                        xh_ = xth[oi % 2]
                        S.stage(0)
                        if oi + 1 < NT:
                            load_xth(xth[(oi + 1) % 2], order[oi + 1])
                        if bwd:
                            S.dma(f32t["yf"][:], YF[i * 128:(i + 1) * 128, :], [YF_bs[i]], [f32t["yf"].b])
                        S.op("pool", lambda: nc.gpsimd.tensor_tensor(out=xst[:], in0=xh_[:, :, 0:128], in1=xh_[:, :, 2:130],
                                                                      op=ALU.add), [xh_.b], [xst.b])

                        def proj_fm(pt_ap, wcol, ncol):
                            ins = None
                            for kc in range(16):
                                rhs = xh_[:, kc, 1:129] if kc < 8 else xst[:, kc - 8, :]
                                ins = nc.tensor.matmul(pt_ap, lhsT=wl[:, kc, wcol:wcol + ncol], rhs=rhs,
                                                       start=(kc == 0), stop=(kc == 15))
                            return ins

                        def proj_tm(pt_ap, q):
                            ins = None
                            for kc in range(16):
                                lhsT = xh_[:, kc, 1:129] if kc < 8 else xst[:, kc - 8, :]
                                ins = nc.tensor.matmul(pt_ap, lhsT=lhsT, rhs=wq[:, kc, q * GC:(q + 1) * GC],
                                                       start=(kc == 0), stop=(kc == 15))
                            return ins

                        S.stage(1)
                        pc = pbank("f")
                        ncode = 3 if bwd else 2

                        def codes():
                            ins = proj_fm(pc[0:64, 0:128], dr * 64, 64)
                            ins = proj_fm(pc[0:64, 128:256], 128 + dr * 64, 64)
                            if bwd:
                                ins = proj_fm(pc[0:64, 256:384], 128, 64)
                            return ins
                        S.op("pe", codes, [xh_.b, xst.b, wl.b], [pc.b])
                        S.op("act", lambda: nc.scalar.activation(out=tl_[:, 0, :], in_=pc[0:64, 0:128], func=AF.Tanh),
                             [pc.b], [tl_.b])
                        S.op("act", lambda: nc.scalar.copy(out=tl_[:, 1:ncode, :].rearrange("p a t -> p (a t)"),
                                                           in_=pc[0:64, 128:128 * ncode]), [pc.b], [tl_.b])

                        def lowrank(pt, ci, wi, d):
                            def f():
                                nc.tensor.matmul(pt[:], lhsT=tl_[:, ci, :], rhs=upw[:, wi, d, :], start=True, stop=False)
                                nc.tensor.matmul(pt[:], lhsT=onesr[:], rhs=b0h[:, wi, d, :], start=False, stop=False)
                                return nc.tensor.matmul(pt[:], lhsT=onesr[:], rhs=b0l[:, wi, d, :], start=False, stop=True)
                            S.op("pe", f, [tl_.b, upw.b, onesr.b, b0h.b, b0l.b], [pt.b])
                        pd_ = pbank("f"); lowrank(pd_, 0, 0, dr)
                        S.op("act", lambda: nc.scalar.activation(out=f32t["sg"][:], in_=pd_[:], func=AF.Sigmoid),
                             [pd_.b], [f32t["sg"].b])
                        pa_ = pbank("f"); lowrank(pa_, 1, 1, dr)
                        S.op("act", lambda: nc.scalar.activation(out=f32t["a"][:], in_=pa_[:], func=AF.Sigmoid),
                             [pa_.b], [f32t["a"].b])
                        if bwd:
                            pa0 = pbank("f"); lowrank(pa0, 2, 1, 0)
                            S.op("act", lambda: nc.scalar.activation(out=f32t["a0"][:], in_=pa0[:], func=AF.Sigmoid),
                                 [pa0.b], [f32t["a0"].b])
                        sg = f32t["sg"]
                        pcum = pbank("f")
                        S.op("pe", lambda: nc.tensor.matmul(pcum[:], lhsT=tinc, rhs=sg[:], start=True, stop=True),
                             [cst.b, sg.b], [pcum.b])
                        S.op("act", lambda: nc.scalar.activation(out=f32t["e1"][:], in_=pcum[:], func=AF.Exp, scale=-1.0),
                             [pcum.b], [f32t["e1"].b])
                        S.op("act", lambda: nc.scalar.activation(out=f32t["e3"][:], in_=pcum[:], func=AF.Exp),
                             [pcum.b], [f32t["e3"].b])
                        pcx = pbank("f")

                        S.op("pe", lambda: nc.tensor.matmul(pcx[:], lhsT=texc, rhs=sg[:], start=True, stop=True),
                             [cst.b, sg.b], [pcx.b])
                        S.op("act", lambda: nc.scalar.activation(out=f32t["e2"][:], in_=pcx[:], func=AF.Exp),
                             [pcx.b], [f32t["e2"].b])
                        pgm = pbank("f")

                        def gsum():
                            ins = None
                            for h in range(NH):
                                ins = nc.tensor.matmul(pgm[0:64, h:h + 1], lhsT=sg[:, h * HD:(h + 1) * HD], rhs=ccol,
                                                       start=True, stop=True)
                            return ins
                        S.op("pe", gsum, [sg.b, cst.b], [pgm.b])
                        S.op("act", lambda: nc.scalar.activation(out=gam[:], in_=pgm[0:64, 0:NH], func=AF.Exp), [pgm.b], [gam.b])

                        S.stage(2)
                        pr = pbank("f"); S.op("pe", lambda: proj_tm(pr[:], 0), [xh_.b, xst.b, wq.b], [pr.b])
                        pk = pbank("f"); S.op("pe", lambda: proj_tm(pk[:], 1), [xh_.b, xst.b, wq.b], [pk.b])
                        pv = pbank("f"); S.op("pe", lambda: proj_tm(pv[:], 2), [xh_.b, xst.b, wq.b], [pv.b])
                        V = bft["V"]
                        S.op("act", lambda: nc.scalar.copy(out=V[:], in_=pv[:]), [pv.b], [V.b])
                        t = f32t
                        v3 = lambda ap: ap.rearrange("p (h c) -> p h c", c=HD)
                        S.op("dve", lambda: nc.vector.tensor_tensor(out=t["kkr"][:], in0=pk[:], in1=prm["kk"][:], op=ALU.mult),
                             [pk.b, prm["kk"].b], [t["kkr"].b])
                        S.op("pool", lambda: nc.gpsimd.tensor_tensor(out=t["sq"][:], in0=t["kkr"][:], in1=t["kkr"][:], op=ALU.mult),
                             [t["kkr"].b], [t["sq"].b])
                        S.op("dve", lambda: nc.vector.tensor_reduce(out=ss[:], in_=v3(t["sq"][:]), axis=AX.X, op=ALU.add),
                             [t["sq"].b], [ss.b])
                        S.op("act", lambda: nc.scalar.activation(out=rn[:], in_=ss[:], func=AF.Sqrt), [ss.b], [rn.b])
                        S.op("dve", lambda: nc.vector.tensor_scalar(out=rn[:], in0=rn[:], scalar1=1e-12, scalar2=None,
                                                                    op0=ALU.max), [rn.b], [rn.b])
                        S.op("dve", lambda: nc.vector.reciprocal(out=rn[:], in_=rn[:]), [rn.b], [rn.b])
                        S.op("dve", lambda: nc.vector.tensor_tensor(out=v3(t["kk"][:]), in0=v3(t["kkr"][:]),
                                                                    in1=bc(rn[:].unsqueeze(2), [128, NH, HD]), op=ALU.mult),
                             [t["kkr"].b, rn.b], [t["kk"].b])
                        S.op("dve", lambda: nc.vector.scalar_tensor_tensor(out=t["t1"][:], in0=t["a"][:], scalar=-1.0,
                                                                             in1=prm["ka"][:], op0=ALU.add, op1=ALU.mult),
                             [t["a"].b, prm["ka"].b], [t["t1"].b])
                        S.op("dve", lambda: nc.vector.scalar_tensor_tensor(out=t["kd"][:], in0=t["t1"][:], scalar=1.0,
                                                                           in1=pk[:], op0=ALU.add, op1=ALU.mult),
                             [t["t1"].b, pk.b], [t["kd"].b])
                        S.op("pool", lambda: nc.gpsimd.tensor_tensor(out=t["bb"][:], in0=t["kk"][:], in1=t["a"][:], op=ALU.mult),
                             [t["kk"].b, t["a"].b], [t["bb"].b])
                        S.op("dve", lambda: nc.vector.tensor_tensor(out=bft["kh"][:], in0=t["kd"][:], in1=t["e1"][:], op=ALU.mult),
                             [t["kd"].b, t["e1"].b], [bft["kh"].b])
                        S.op("pool", lambda: nc.gpsimd.tensor_tensor(out=bft["bh"][:], in0=t["bb"][:], in1=t["e1"][:], op=ALU.mult),
                             [t["bb"].b, t["e1"].b], [bft["bh"].b])
                        S.op("pool", lambda: nc.gpsimd.tensor_tensor(out=bft["kkt"][:], in0=t["kk"][:], in1=t["e2"][:], op=ALU.mult),
                             [t["kk"].b, t["e2"].b], [bft["kkt"].b])
                        S.op("dve", lambda: nc.vector.tensor_tensor(out=bft["rt"][:], in0=pr[:], in1=t["e3"][:], op=ALU.mult),
                             [pr.b, t["e3"].b], [bft["rt"].b])
                        if bwd:
                            S.op("dve", lambda: nc.vector.scalar_tensor_tensor(out=t["t1"][:], in0=t["a0"][:], scalar=-1.0,
                                                                                 in1=prm["ka"][:], op0=ALU.add, op1=ALU.mult),
                                 [t["a0"].b, prm["ka"].b], [t["t1"].b])
                            S.op("dve", lambda: nc.vector.scalar_tensor_tensor(out=t["kd0"][:], in0=t["t1"][:], scalar=1.0,
                                                                               in1=pk[:], op0=ALU.add, op1=ALU.mult),
                                 [t["t1"].b, pk.b], [t["kd0"].b])
                            S.op("pool", lambda: nc.gpsimd.tensor_tensor(out=t["kd0"][:], in0=t["kd0"][:], in1=t["kd"][:], op=ALU.add),
                                 [t["kd0"].b, t["kd"].b], [t["kd0"].b])
                            S.op("pool", lambda: nc.gpsimd.tensor_tensor(out=t["kd0"][:], in0=t["kd0"][:], in1=prm["rk"][:], op=ALU.mult),
                                 [t["kd0"].b, prm["rk"].b], [t["kd0"].b])
                            S.op("dve", lambda: nc.vector.tensor_tensor(out=t["tmp"][:], in0=pr[:], in1=t["kd0"][:], op=ALU.mult),
                                 [pr.b, t["kd0"].b], [t["tmp"].b])
                            S.op("dve", lambda: nc.vector.tensor_reduce(out=bs[:], in_=v3(t["tmp"][:]), axis=AX.X, op=ALU.add),
                                 [t["tmp"].b], [bs.b])
                            S.op("dve", lambda: nc.vector.scalar_tensor_tensor(
                                out=v3(t["vbon"][:]), in0=v3(pv[:]), scalar=0.5, in1=bc(bs[:].unsqueeze(2), [128, NH, HD]),
                                op0=ALU.mult, op1=ALU.mult), [pv.b, bs.b], [t["vbon"].b])
                            pz = pbank("f"); S.op("pe", lambda: proj_tm(pz[:], 3), [xh_.b, xst.b, wq.b], [pz.b])
                            S.op("act", lambda: nc.scalar.activation(out=t["sz"][:], in_=pz[:], func=AF.Silu), [pz.b], [t["sz"].b])

                        S.stage(3)
                        def trans8(src_, dst_ap, eng):
                            pt = pbank("f")
                            ptv = pt[:].bitcast(BF16)

                            def f():
                                ins = None
                                for h in range(NH):
                                    ins = nc.tensor.transpose(out=ptv[0:64, h * 128:(h + 1) * 128], in_=src_[:, h * HD:(h + 1) * HD],
                                                              identity=identb[:])
                                return ins
                            S.op("pe", f, [src_.b, identb.b], [pt.b])
                            src_v = ptv[0:64, :].rearrange("p (h t) -> p h t", t=128)
                            if eng == "act":
                                S.op("act", lambda: nc.scalar.copy(out=dst_ap[0], in_=src_v), [pt.b], [dst_ap[1]])
                            else:
                                S.op("dve", lambda: nc.vector.tensor_copy(out=dst_ap[0], in_=src_v), [pt.b], [dst_ap[1]])
                        trans8(bft["kh"], (kT[:], kT.b), "act")
                        trans8(bft["bh"], (bT[:], bT.b), "dve")
                        trans8(bft["kkt"], (krT[:, :, 0, :], krT.b), "act")
                        trans8(bft["rt"], (krT[:, :, 1, :], krT.b), "dve")

                        S.stage(4)
                        for hp in range(NH // 2):
                            for which, lT, dst, msk in ((0, kT, MA1, m1), (1, bT, MA2, m2)):
                                pm = pbank("b")

                                def f():
                                    ins = None
                                    for hh in range(2):
                                        h = hp * 2 + hh
                                        ins = nc.tensor.matmul(pm[:, hh * 256:(hh + 1) * 256], lhsT=lT[:, h, :],
                                                               rhs=krT[:, h, :, :].rearrange("p a t -> p (a t)"),
                                                               start=True, stop=True)
                                    return ins
                                S.op("pe", f, [lT.b, krT.b], [pm.b])
                                S.op("dve", lambda: nc.vector.tensor_tensor(
                                    out=dst[:, hp * 2:hp * 2 + 2, :, :].rearrange("p h a t -> p h (a t)"),
                                    in0=pm[:].rearrange("p (h x) -> p h x", h=2),
                                    in1=bc(msk.unsqueeze(1), [128, 2, 256]), op=ALU.mult), [pm.b, cst.b], [dst.b])
                        for hq in range(NH // 4):
                            pm = pbank("b")

                            def f():
                                ins = None
                                for hh in range(4):
                                    h = hq * 4 + hh
                                    ins = nc.tensor.matmul(pm[:, hh * 128:(hh + 1) * 128], lhsT=krT[:, h, 0, :],
                                                           rhs=bT[:, h, :], start=True, stop=True)
                                return ins
                            S.op("pe", f, [krT.b, bT.b], [pm.b])
                            S.op("dve", lambda: nc.vector.tensor_tensor(
                                out=Xt[0][:, hq * 4:hq * 4 + 4, :], in0=pm[:].rearrange("p (h x) -> p h x", h=4),
                                in1=bc(m3.unsqueeze(1), [128, 4, 128]), op=ALU.mult), [pm.b, cst.b], [Xt[0].b])

                        S.stage(5)
                        S.op("pool", lambda: nc.gpsimd.tensor_tensor(
                            out=XQ[1][:, :, 1, :], in0=MA2[:, :, 0, :], in1=bc(identb[:].unsqueeze(1), [128, NH, 128]),
                            op=ALU.add), [MA2.b, identb.b], [XQ[1].b])
                        for hq in range(NH // 4):
                            pm = pbank("b")

                            def f():
                                ins = None
                                for hh in range(4):
                                    h = hq * 4 + hh
                                    ins = nc.tensor.matmul(pm[:, hh * 128:(hh + 1) * 128], lhsT=Xt[0][:, h, :],
                                                           rhs=MA2[:, h, 0, :], start=True, stop=True)
                                return ins
                            S.op("pe", f, [Xt[0].b, MA2.b], [pm.b])
                            S.op("act", lambda: nc.scalar.copy(out=XQ[1][:, hq * 4:hq * 4 + 4, 0, :],
                                                               in_=pm[:].rearrange("p (h x) -> p h x", h=4)), [pm.b], [XQ[1].b])
                            pm2 = pbank("b")

                            def f2():
                                ins = None
                                for hh in range(4):
                                    h = hq * 4 + hh
                                    ins = nc.tensor.matmul(pm2[:, hh * 128:(hh + 1) * 128], lhsT=MA2[:, h, 0, :],
                                                           rhs=Xt[0][:, h, :], start=True, stop=True)
                                return ins
                            S.op("pe", f2, [Xt[0].b, MA2.b], [pm2.b])
                            S.op("dve", lambda: nc.vector.tensor_copy(out=Xt[1][:, hq * 4:hq * 4 + 4, :],
                                                                      in_=pm2[:].rearrange("p (h x) -> p h x", h=4)),
                                 [pm2.b], [Xt[1].b])
                        for lev in range(1, NLEV):
                            cur = XQ[lev % 2]; nxt = XQ[(lev + 1) % 2]
                            xtc = Xt[lev % 2]; xtn = Xt[(lev + 1) % 2]
                            last = lev == NLEV - 1
                            if not last:
                                for hp in range(NH // 2):
                                    pm = pbank("b")

                                    def f():
                                        ins = None
                                        for hh in range(2):
                                            h = hp * 2 + hh
                                            o = hh * 256
                                            nc.tensor.matmul(pm[:, o:o + 256], lhsT=xtc[:, h, :],
                                                             rhs=cur[:, h, :, :].rearrange("p a t -> p (a t)"), start=True, stop=False)
                                            ins = nc.tensor.matmul(pm[:, o + 128:o + 256], lhsT=identb[:], rhs=cur[:, h, 1, :],
                                                                   start=False, stop=True)
                                        return ins
                                    S.op("pe", f, [xtc.b, cur.b, identb.b], [pm.b])
                                    eng = "act" if hp % 2 == 0 else "dve"
                                    if eng == "act":
                                        S.op("act", lambda: nc.scalar.copy(
                                            out=nxt[:, hp * 2:hp * 2 + 2, :, :].rearrange("p h a t -> p (h a t)"), in_=pm[:]),
                                            [pm.b], [nxt.b])
                                    else:
                                        S.op("dve", lambda: nc.vector.tensor_copy(
                                            out=nxt[:, hp * 2:hp * 2 + 2, :, :].rearrange("p h a t -> p (h a t)"), in_=pm[:]),
                                            [pm.b], [nxt.b])
                                for hq in range(NH // 4):
                                    pm = pbank("b")

                                    def f():
                                        ins = None
                                        for hh in range(4):
                                            h = hq * 4 + hh
                                            ins = nc.tensor.matmul(pm[:, hh * 128:(hh + 1) * 128], lhsT=cur[:, h, 0, :],
                                                                   rhs=xtc[:, h, :], start=True, stop=True)
                                        return ins
                                    S.op("pe", f, [cur.b, xtc.b], [pm.b])
                                    if hq % 2 == 0:
                                        S.op("act", lambda: nc.scalar.copy(out=xtn[:, hq * 4:hq * 4 + 4, :].rearrange("p h t -> p (h t)"),
                                                                           in_=pm[:]), [pm.b], [xtn.b])
                                    else:
                                        S.op("dve", lambda: nc.vector.tensor_copy(out=xtn[:, hq * 4:hq * 4 + 4, :].rearrange("p h t -> p (h t)"),
                                                                                  in_=pm[:]), [pm.b], [xtn.b])
                            else:
                                for hq in range(NH // 4):
                                    pm = pbank("b")

                                    def f():
                                        ins = None
                                        for hh in range(4):
                                            h = hq * 4 + hh
                                            o = hh * 128
                                            nc.tensor.matmul(pm[:, o:o + 128], lhsT=xtc[:, h, :], rhs=cur[:, h, 1, :],
                                                             start=True, stop=False)
                                            ins = nc.tensor.matmul(pm[:, o:o + 128], lhsT=identb[:], rhs=cur[:, h, 1, :],
                                                                   start=False, stop=True)
                                        return ins
                                    S.op("pe", f, [xtc.b, cur.b, identb.b], [pm.b])
                                    if hq % 2 == 0:
                                        S.op("act", lambda: nc.scalar.copy(out=TT[:, hq * 4:hq * 4 + 4, :].rearrange("p h t -> p (h t)"),
                                                                           in_=pm[:]), [pm.b], [TT.b])
                                    else:
                                        S.op("dve", lambda: nc.vector.tensor_copy(out=TT[:, hq * 4:hq * 4 + 4, :].rearrange("p h t -> p (h t)"),
                                                                                  in_=pm[:]), [pm.b], [TT.b])

                        S.stage(6)
                        Vh = lambda h: V[:, h * HD:(h + 1) * HD]
                        pzz = pbank("b")

                        def fz():
                            ins = None
                            for h in range(NH):
                                nc.tensor.matmul(pzz[:, h * HD:(h + 1) * HD], lhsT=krT[:, h, 0, :],
                                                 rhs=Hb[:, h, :], start=True, stop=False)
                                ins = nc.tensor.matmul(pzz[:, h * HD:(h + 1) * HD], lhsT=MA1[:, h, 0, :], rhs=Vh(h),
                                                       start=False, stop=True)
                            return ins
                        S.op("pe", fz, [krT.b, Hb.b, MA1.b, V.b], [pzz.b])
                        S.op("act", lambda: nc.scalar.copy(out=bft["Zs"][:], in_=pzz[:]), [pzz.b], [bft["Zs"].b])
                        pu = pbank("b")

                        def fu():
                            ins = None
                            for h in range(NH):
                                ins = nc.tensor.matmul(pu[:, h * HD:(h + 1) * HD], lhsT=TT[:, h, :],
                                                       rhs=bft["Zs"][:, h * HD:(h + 1) * HD], start=True, stop=True)
                            return ins
                        S.op("pe", fu, [TT.b, bft["Zs"].b], [pu.b])
                        nU = bft["nU"]
                        S.op("act", lambda: nc.scalar.activation(out=nU[:], in_=pu[:], func=AF.Identity, scale=-1.0), [pu.b], [nU.b])
                        py = pbank("b")

                        def fy():
                            ins = None
                            for h in range(NH):
                                o = slice(h * HD, (h + 1) * HD)
                                nc.tensor.matmul(py[:, o], lhsT=krT[:, h, 1, :], rhs=Hb[:, h, :],
                                                 start=True, stop=False)
                                nc.tensor.matmul(py[:, o], lhsT=MA1[:, h, 1, :], rhs=Vh(h), start=False, stop=False)
                                ins = nc.tensor.matmul(py[:, o], lhsT=MA2[:, h, 1, :], rhs=nU[:, o], start=False, stop=True)
                            return ins
                        S.op("pe", fy, [krT.b, Hb.b, MA1.b, MA2.b, V.b, nU.b], [py.b])
                        ph = pbank("b")

                        def fh():
                            ins = None
                            for h in range(NH):
                                o = slice(h * HD, (h + 1) * HD)
                                nc.tensor.matmul(ph[0:64, o], lhsT=bft["kh"][:, o], rhs=V[:, o], start=True, stop=False)
                                ins = nc.tensor.matmul(ph[0:64, o], lhsT=bft["bh"][:, o], rhs=nU[:, o], start=False, stop=True)
                            return ins
                        S.op("pe", fh, [bft["kh"].b, bft["bh"].b, V.b, nU.b], [ph.b])
                        S.op("dve", lambda: nc.vector.tensor_tensor(out=Ht[:], in0=ph[0:64, :].rearrange("p (h v) -> p h v", v=HD),
                                                                    in1=Hf[:], op=ALU.add), [ph.b, Hf.b], [Ht.b])
                        S.op("dve", lambda: nc.vector.tensor_tensor(out=Hf[:], in0=Ht[:], in1=bc(gam[:].unsqueeze(2), [64, NH, HD]),
                                                                    op=ALU.mult), [Ht.b, gam.b], [Hf.b])
                        nxt_i = i - 1 if bwd else i + 1
                        bt = i if bwd else i + 1
                        if 0 <= nxt_i < NT and bt % BP == 0:
                            bidx = bt // BP - 1
                            S.op("dve", lambda: nc.vector.tensor_scalar(out=Hf[:], in0=Hf[:], scalar1=keep[0:64, bidx:bidx + 1],
                                                                        scalar2=None, op0=ALU.mult), [Hf.b, keep.b], [Hf.b])
                        S.op("act", lambda: nc.scalar.copy(out=Hb[:], in_=Hf[:]), [Hf.b], [Hb.b])

                        S.stage(7)
                        if not bwd:
                            S.op("act", lambda: nc.scalar.copy(out=t["ysum"][:], in_=py[:]), [py.b], [t["ysum"].b])
                            S.dma(YF[i * 128:(i + 1) * 128, :], t["ysum"][:], [t["ysum"].b], [YF_bs[i]])
                        else:
                            S.op("dve", lambda: nc.vector.tensor_tensor(out=t["ysum"][:], in0=py[:], in1=t["yf"][:], op=ALU.add),
                                 [py.b, t["yf"].b], [t["ysum"].b])
                            S.op("dve", lambda: nc.vector.tensor_reduce(out=gs1[:], in_=v3(t["ysum"][:]), axis=AX.X, op=ALU.add),
                                 [t["ysum"].b], [gs1.b])
                            S.op("dve", lambda: nc.vector.tensor_scalar(out=gs1[:], in0=gs1[:], scalar1=-1.0 / HD, scalar2=None,
                                                                        op0=ALU.mult), [gs1.b], [gs1.b])
                            S.op("dve", lambda: nc.vector.tensor_tensor(out=v3(t["yc"][:]), in0=v3(t["ysum"][:]),
                                                                        in1=bc(gs1[:].unsqueeze(2), [128, NH, HD]), op=ALU.add),
                                 [t["ysum"].b, gs1.b], [t["yc"].b])
                            S.op("pool", lambda: nc.gpsimd.tensor_tensor(out=t["sq"][:], in0=t["yc"][:], in1=t["yc"][:], op=ALU.mult),
                                 [t["yc"].b], [t["sq"].b])
                            S.op("dve", lambda: nc.vector.tensor_reduce(out=gs2[:], in_=v3(t["sq"][:]), axis=AX.X, op=ALU.add),
                                 [t["sq"].b], [gs2.b])
                            S.op("dve", lambda: nc.vector.tensor_scalar(out=gs2[:], in0=gs2[:], scalar1=1.0 / HD, scalar2=LNX_EPS,
                                                                        op0=ALU.mult, op1=ALU.add), [gs2.b], [gs2.b])
                            S.op("act", lambda: nc.scalar.activation(out=grs[:], in_=gs2[:], func=AF.Sqrt), [gs2.b], [grs.b])
                            S.op("dve", lambda: nc.vector.reciprocal(out=grs[:], in_=grs[:]), [grs.b], [grs.b])
                            S.op("dve", lambda: nc.vector.tensor_tensor(out=v3(t["yc"][:]), in0=v3(t["yc"][:]),
                                                                        in1=bc(grs[:].unsqueeze(2), [128, NH, HD]), op=ALU.mult),
                                 [t["yc"].b, grs.b], [t["yc"].b])
                            S.op("pool", lambda: nc.gpsimd.tensor_tensor(out=t["yc"][:], in0=t["yc"][:], in1=prm["lxg"][:], op=ALU.mult),
                                 [t["yc"].b, prm["lxg"].b], [t["yc"].b])
                            S.op("pool", lambda: nc.gpsimd.tensor_tensor(out=t["yc"][:], in0=t["yc"][:], in1=prm["lxb"][:], op=ALU.add),
                                 [t["yc"].b, prm["lxb"].b], [t["yc"].b])
                            S.op("dve", lambda: nc.vector.tensor_tensor(out=t["yc"][:], in0=t["yc"][:], in1=t["vbon"][:], op=ALU.add),
                                 [t["yc"].b, t["vbon"].b], [t["yc"].b])
                            S.op("dve", lambda: nc.vector.tensor_tensor(out=bft["ob"][:], in0=t["yc"][:], in1=t["sz"][:], op=ALU.mult),
                                 [t["yc"].b, t["sz"].b], [bft["ob"].b])
                            pto = pbank("b")
                            ptvo = pto[:].bitcast(BF16)

                            def fo():
                                ins = None
                                for blk in range(4):
                                    ins = nc.tensor.transpose(out=ptvo[:, blk * 128:(blk + 1) * 128],
                                                              in_=bft["ob"][:, blk * 128:(blk + 1) * 128], identity=identb[:])
                                return ins
                            S.op("pe", fo, [bft["ob"].b, identb.b], [pto.b])
                            S.op("act", lambda: nc.scalar.copy(out=oT[:].rearrange("p b t -> p (b t)"), in_=ptvo[:, 0:512]),
                                 [pto.b], [oT.b])
                            S.dma(YR[:, g * 4:(g + 1) * 4, i * 128:(i + 1) * 128], oT[:], [oT.b], [YR_bs[g][i]])
                S.barrier()

        with contextlib.ExitStack() as es:
            if not DBG.get("c", True):
                raise_skip = True
            else:
                raise_skip = False
            wc = sb(es, "wc", [128, 8, NCV], BF16)
            wo = sb(es, "wo", [128, 16, D], BF16)
            with contextlib.ExitStack() as es2:
                wst = [sb(es2, f"cwst{i}", [128, 8, 512], F32) for i in range(2)]
                for bi in range(NCV // 512 if DBG.get("cw", True) else 0):
                    w_ = wst[bi % 2]
                    S.dma(w_[:], w_in_d[:, bi * 512:(bi + 1) * 512].rearrange("(c p) n -> p c n", p=128), [], [w_.b])
                    if bi % 2 == 0:
                        S.op("act", lambda: nc.scalar.copy(out=wc[:, :, bi * 512:(bi + 1) * 512], in_=w_[:]), [w_.b], [wc.b])
                    else:
                        S.op("dve", lambda: nc.vector.tensor_copy(out=wc[:, :, bi * 512:(bi + 1) * 512], in_=w_[:]), [w_.b], [wc.b])
                for bi in range(4 if DBG.get("cw", True) else 0):
                    w_ = wst[bi % 2]
                    S.dma(w_[:, 0:4, :], wout_d[bi * 512:(bi + 1) * 512, 0:512].rearrange("(c p) n -> p c n", p=128), [], [w_.b])
                    S.dma(w_[:, 4:8, :], wout_d[bi * 512:(bi + 1) * 512, 512:1024].rearrange("(c p) n -> p c n", p=128), [], [w_.b])
                    S.op("act", lambda: nc.scalar.copy(out=wo[:, bi * 4:(bi + 1) * 4, 0:512], in_=w_[:, 0:4, :]), [w_.b], [wo.b])
                    S.op("dve", lambda: nc.vector.tensor_copy(out=wo[:, bi * 4:(bi + 1) * 4, 512:1024], in_=w_[:, 4:8, :]), [w_.b], [wo.b])
                S.barrier()
            cpar = sb(es, "cpar", [128, 8, 4], F32)
            for j in range(3 if DBG.get("cp", True) else 0):
                S.dma(cpar[:, :, j:j + 1], conv_w_d[j, :].rearrange("(c p o) -> p c o", p=128, o=1), [], [cpar.b])
            S.dma(cpar[:, :, 3:4], conv_b_d.rearrange("(c p o) -> p c o", p=128, o=1), [], [cpar.b])
            gbc = sb(es, "c_g", [128, D], F32); bbc = sb(es, "c_b", [128, D], F32)
            g2 = sb(es, "c_g2", [128, D], F32); b2 = sb(es, "c_b2", [128, D], F32)
            for tl, src in ((gbc, emb_g_d), (bbc, emb_b_d), (g2, lng_d), (b2, lnb_d)):
                bcast_load(tl, 0, src, D)
            xth = [sb(es, f"cxth{i}", [128, 8, 130], BF16) for i in range(2)]
            ymT = [sb(es, f"ymT{i}", [128, 16, 128], BF16) for i in range(2)]
            xt = [sb(es, f"c_x{i}", [128, D], F32) for i in range(2)]
            xn = sb(es, "c_xn", [128, D], F32); xg = sb(es, "c_xg", [128, D], F32); xh = sb(es, "c_xh", [128, D], F32)
            sres = sb(es, "c_s", [128, D], F32); on = sb(es, "c_on", [128, D], F32); og = sb(es, "c_og", [128, D], F32)
            yo = [sb(es, f"c_yo{i}", [128, D], F32) for i in range(2)]
            st6 = sb(es, "c_st", [128, 2, SD], F32); mv = sb(es, "c_mv", [128, AD], F32)
            rstd = sb(es, "c_rs", [128, 1], F32); nb_ = sb(es, "c_nb", [128, 1], F32)
            st6b = sb(es, "c_stb", [128, 2, SD], F32); mvb = sb(es, "c_mvb", [128, AD], F32)
            rstdb = sb(es, "c_rsb", [128, 1], F32); nbb = sb(es, "c_nbb", [128, 1], F32)
            hS = sb(es, "c_hS", [128, 130], F32); pp = sb(es, "c_pp", [128, 130], F32)
            qq = sb(es, "c_qq", [128, 128], F32); szc = sb(es, "c_sz", [128, 128], F32)
            if not raise_skip:
                load_xth(xth[0], 0)
            for i in range(0 if raise_skip else NT):
                p = i % 2
                xh_ = xth[p]
                if i + 1 < NT:
                    load_xth(xth[(i + 1) % 2], i + 1)
                S.dma(xt[p][:], xs[i * 128:(i + 1) * 128, :], [], [xt[p].b])
                S.dma(ymT[p][:, 8:16, :], YR[:, :, i * 128:(i + 1) * 128], [YR_bs[0][i], YR_bs[1][i]], [ymT[p].b])
                for cbk in range(8):
                    pa = pbank("f"); pb = pbank("f")

                    def fa():
                        ins = None
                        for qi, q in enumerate((0, 2)):
                            for kc in range(8):
                                ins = nc.tensor.matmul(pa[:, qi * 130:(qi + 1) * 130],
                                                       lhsT=wc[:, kc, q * 1024 + cbk * 128:q * 1024 + (cbk + 1) * 128],
                                                       rhs=xh_[:, kc, :], start=(kc == 0), stop=(kc == 7))
                        return ins

                    def fb():
                        ins = None
                        for qi, q in enumerate((1, 3)):
                            for kc in range(8):
                                ins = nc.tensor.matmul(pb[:, qi * 128:(qi + 1) * 128],
                                                       lhsT=wc[:, kc, q * 1024 + cbk * 128:q * 1024 + (cbk + 1) * 128],
                                                       rhs=xh_[:, kc, 1:129], start=(kc == 0), stop=(kc == 7))
                        return ins
                    S.op("pe", fa, [wc.b, xh_.b], [pa.b])
                    S.op("pe", fb, [wc.b, xh_.b], [pb.b])
                    S.op("act", lambda: nc.scalar.copy(out=hS[:], in_=pa[:, 0:130]), [pa.b], [hS.b])
                    S.op("dve", lambda: nc.vector.tensor_tensor(out=pp[:], in0=hS[:], in1=pa[:, 130:260], op=ALU.mult),
                         [hS.b, pa.b], [pp.b])
                    S.op("dve", lambda: nc.vector.tensor_scalar(out=qq[:], in0=pp[:, 1:129], scalar1=cpar[:, cbk, 1:2],
                                                                scalar2=cpar[:, cbk, 3:4], op0=ALU.mult, op1=ALU.add),
                         [pp.b, cpar.b], [qq.b])
                    S.op("dve", lambda: nc.vector.scalar_tensor_tensor(out=qq[:], in0=pp[:, 0:128], scalar=cpar[:, cbk, 0:1],
                                                                         in1=qq[:], op0=ALU.mult, op1=ALU.add),
                         [pp.b, cpar.b, qq.b], [qq.b])
                    S.op("dve", lambda: nc.vector.scalar_tensor_tensor(out=qq[:], in0=pp[:, 2:130], scalar=cpar[:, cbk, 2:3],
                                                                         in1=qq[:], op0=ALU.mult, op1=ALU.add),
                         [pp.b, cpar.b, qq.b], [qq.b])
                    S.op("act", lambda: nc.scalar.activation(out=szc[:], in_=pb[:, 128:256], func=AF.Silu), [pb.b], [szc.b])
                    S.op("dve", lambda: nc.vector.tensor_tensor(out=qq[:], in0=qq[:], in1=pb[:, 0:128], op=ALU.mult),
                         [qq.b, pb.b], [qq.b])
                    S.op("pool", lambda: nc.gpsimd.tensor_tensor(out=ymT[p][:, cbk, :], in0=qq[:], in1=szc[:], op=ALU.mult),
                         [qq.b, szc.b], [ymT[p].b])
                po = [pbank("b"), pbank("b")]
                for hf in range(2):
                    def fo():
                        ins = None
                        for mc in range(16):
                            ins = nc.tensor.matmul(po[hf][:], lhsT=ymT[p][:, mc, :], rhs=wo[:, mc, hf * 512:(hf + 1) * 512],
                                                   start=(mc == 0), stop=(mc == 15))
                        return ins
                    S.op("pe", fo, [ymT[p].b, wo.b], [po[hf].b])
                ln_stats("c", xt[p], mv, rstd, nb_, st6, LN_EPS)
                S.op("act", lambda: nc.scalar.activation(out=xn[:], in_=xt[p][:], func=AF.Identity, bias=nb_[:, 0:1],
                                                         scale=rstd[:, 0:1]), [xt[p].b, rstd.b, nb_.b], [xn.b])
                S.op("dve", lambda: nc.vector.tensor_tensor(out=xg[:], in0=xn[:], in1=gbc[:], op=ALU.mult), [xn.b, gbc.b], [xg.b])
                S.op("pool", lambda: nc.gpsimd.tensor_tensor(out=xh[:], in0=xg[:], in1=bbc[:], op=ALU.add), [xg.b, bbc.b], [xh.b])
                for hf in range(2):
                    o = slice(hf * 512, (hf + 1) * 512)
                    S.op("dve", lambda: nc.vector.scalar_tensor_tensor(out=sres[:, o], in0=xh[:, o], scalar=DN_ALPHA,
                                                                       in1=po[hf][:], op0=ALU.mult, op1=ALU.add),
                         [xh.b, po[hf].b], [sres.b])
                ln_stats("c2", sres, mvb, rstdb, nbb, st6b, LN_EPS)
                S.op("act", lambda: nc.scalar.activation(out=on[:], in_=sres[:], func=AF.Identity, bias=nbb[:, 0:1],
                                                         scale=rstdb[:, 0:1]), [sres.b, rstdb.b, nbb.b], [on.b])
                S.op("dve", lambda: nc.vector.tensor_tensor(out=og[:], in0=on[:], in1=g2[:], op=ALU.mult), [on.b, g2.b], [og.b])
                S.op("pool", lambda: nc.gpsimd.tensor_tensor(out=yo[p][:], in0=og[:], in1=b2[:], op=ALU.add), [og.b, b2.b], [yo[p].b])
                S.dma(ys[i * 128:(i + 1) * 128, :], yo[p][:], [yo[p].b], [ys_bs[i]])
            S.barrier()

        S.finish()
    return nc


CST_COLS = 128 + 2 * 896 + 1
DBG = {}
NAMES = {}


def make_consts():
    r = np.arange(128)[:, None]
    c = np.arange(128)[None, :]
    SU = (r < c).astype(np.float32); IU = (r <= c).astype(np.float32)
    SL = (r > c).astype(np.float32); IL = (r >= c).astype(np.float32)
    parts = [np.eye(128, dtype=np.float32)]
    for (S_, I_, St) in ((SU, IU, SL), (SL, IL, SU)):
        parts += [S_, I_, -S_, I_, -St, CDEC * I_, CDEC * S_]
    parts.append(np.full((128, 1), CDEC, np.float32))
    out = np.concatenate(parts, axis=1).astype(np.float32)
    assert out.shape[1] == CST_COLS
    return np.ascontiguousarray(out)


W_NAMES = ["emb_ln_g", "emb_ln_b", "w_in", "conv_w", "conv_b", "shift_mu", "w0", "w_up", "a0", "a_up",
           "k_k", "k_a", "r_k", "lnx_g", "lnx_b", "w_out", "ln_g", "ln_b"]


def weight_map(inp):
    m = {}
    for n in W_NAMES:
        a = np.asarray(inp[n], dtype=np.float32)
        if n not in ("emb_ln_g", "emb_ln_b"):
            a = a[0]
        if n == "r_k":
            a = a.reshape(1024)
        m[n] = np.ascontiguousarray(a)
    m["cst"] = make_consts()
    return m


_NC_CACHE = {}


def run_streams(streams, keeps, inp, NT, BP):
    key = (NT, BP)
    if key not in _NC_CACHE:
        _NC_CACHE[key] = build(NT, BP)
    nc = _NC_CACHE[key]
    wm = weight_map(inp)
    in_maps = []
    for s, k in zip(streams, keeps):
        d = dict(wm)
        d["xs"] = np.ascontiguousarray(s, dtype=np.float32)
        d["keep"] = np.ascontiguousarray(k, dtype=np.float32)
        in_maps.append(d)
    res = run_bass_kernel_spmd(nc, in_maps, core_ids=list(range(len(streams))))
    return [r["ys"] for r in res.results]


def kernel(**inp):
    xp = np.asarray(inp["x_prompt"], dtype=np.float32)
    xsm = np.asarray(inp["x_sample"], dtype=np.float32)
    NT, BP = 128, 16
    NB = NT // BP - 1
    ntok = NT * 128
    streams = [xsm[0], xsm[1], xp.reshape(ntok, D)]
    keeps = [np.ones((128, NB), np.float32), np.ones((128, NB), np.float32), np.zeros((128, NB), np.float32)]
    for _ in range(5):
        streams.append(np.zeros((ntok, D), np.float32))
        keeps.append(np.zeros((128, NB), np.float32))
    outs = run_streams(streams, keeps, inp, NT, BP)
    y_sample = np.stack([outs[0], outs[1]], axis=0).reshape(2, 16384, D)
    y_prompt = outs[2].reshape(8, 2048, D)
    return (y_prompt.astype(np.float32), y_sample.astype(np.float32))
```

```python
import contextlib
import numpy as np
import concourse.bass as bass
import concourse.mybir as mybir
from concourse.bass_utils import run_bass_kernel_spmd

F32 = mybir.dt.float32
BF16 = mybir.dt.bfloat16
AF = mybir.ActivationFunctionType
ALU = mybir.AluOpType
AX = mybir.AxisListType

D = 1024
NCV = 4096
NRW = 4352
NIN = NCV + NRW
HD = 64
LN_EPS = 1e-5
LNX_EPS = 64e-5
DN_ALPHA = 2.0 ** 0.25
CDEC = -float(np.exp(-0.5))
NG = 2
GC = 512
NH = 8
NLEV = 7


class Buf:
    __slots__ = ("name", "w", "rs", "excl")

    def __init__(self, name):
        self.name = name
        self.w = None
        self.rs = {}
        self.excl = False


class Sch:
    R = 4
    ND = 24

    def __init__(self, nc, es):
        self.nc = nc
        self.eng = {"pe": nc.tensor, "act": nc.scalar, "dve": nc.vector, "pool": nc.gpsimd, "sp": nc.sync}
        self.sems = {e: [es.enter_context(nc.semaphore(f"s_{e}{i}")) for i in range(self.R)]
                     for e in ("pe", "act", "dve", "pool")}
        self.cnt = {e: 0 for e in self.sems}
        self.known = {e: {} for e in self.eng}
        self.dsems = [es.enter_context(nc.semaphore(f"s_d{i}")) for i in range(self.ND)]
        self.dval = [0] * self.ND
        self.dnext = 0
        self.all_bufs = []

    def buf(self, name):
        b = Buf(name)
        self.all_bufs.append(b)
        return b

    def _wait(self, waiter, ev):
        key = (ev[0], ev[1])
        if ev[0] == "E" and ev[1] == waiter and waiter == "pe":
            return
        if self.known[waiter].get(key, -1) >= ev[2]:
            return
        if ev[0] == "E":
            sem = self.sems[ev[1]][ev[2] % self.R]
            val = ev[2] // self.R + 1
        else:
            sem = self.dsems[ev[1]]
            val = ev[2]
        self.eng[waiter].wait_ge(sem, val)
        self.known[waiter][key] = ev[2]

    def _deps(self, waiter, reads, writes):
        for b in reads:
            if b.w is not None:
                self._wait(waiter, b.w)
            if b.excl:
                for ev in b.rs.values():
                    if not (ev[0] == "E" and ev[1] == waiter):
                        self._wait(waiter, ev)
        for b in writes:
            if b.w is not None:
                self._wait(waiter, b.w)
            for ev in b.rs.values():
                self._wait(waiter, ev)

    def _commit(self, ev, reads, writes):
        for b in reads:
            b.rs[(ev[0], ev[1])] = ev
        for b in writes:
            b.w = ev
            b.rs = {}

    muted = False

    def stage(self, k):
        self.muted = k > DBG.get("stage", 99)

    def op(self, eng, fn, reads, writes):
        if self.muted:
            return
        self._deps(eng, reads, writes)
        ins = fn()
        idx = self.cnt[eng]
        self.cnt[eng] += 1
        ins.then_inc(self.sems[eng][idx % self.R], 1)
        self._commit(("E", eng, idx), reads, writes)

    def dma(self, out, in_, reads, writes, **kw):
        if self.muted:
            return
        k = self.dnext
        self.dnext = (self.dnext + 1) % self.ND
        if self.dval[k] > 0:
            self._wait("sp", ("D", k, self.dval[k]))
        self._deps("sp", reads, writes)
        ins = self.nc.sync.dma_start(out=out, in_=in_, **kw)
        self.dval[k] += 16
        ins.then_inc(self.dsems[k], 16)
        self._commit(("D", k, self.dval[k]), reads, writes)

    def barrier(self):
        self.muted = False
        for w in self.eng:
            for k in range(self.ND):
                if self.dval[k] > 0:
                    self._wait(w, ("D", k, self.dval[k]))
            for e in self.cnt:
                if self.cnt[e] > 0:
                    self._wait(w, ("E", e, self.cnt[e] - 1))

    def finish(self):
        for k in range(self.ND):
            if self.dval[k] > 0:
                self._wait("sp", ("D", k, self.dval[k]))
        for e in self.cnt:
            if self.cnt[e] > 0:
                self._wait("sp", ("E", e, self.cnt[e] - 1))


class TL:
    def __init__(self, sch, t, name):
        self.t = t
        self.b = sch.buf(name)

    def __getitem__(self, k):
        return self.t[k]


def bc(ap, shape):
    return ap.broadcast_to(list(shape))


def build(NT, BP):
    NTOK = NT * 128
    NB = max(1, NT // BP - 1)
    nc = bass.Bass("TRN2", target_bir_lowering=False)
    dt_in = lambda n, s: nc.dram_tensor(n, s, F32, kind="ExternalInput").ap()
    xs = dt_in("xs", [NTOK, D])
    keep_d = dt_in("keep", [128, NB])
    cst_d = dt_in("cst", [128, CST_COLS])
    emb_g_d = dt_in("emb_ln_g", [D]); emb_b_d = dt_in("emb_ln_b", [D])
    w_in_d = dt_in("w_in", [D, NIN])
    conv_w_d = dt_in("conv_w", [3, 1024]); conv_b_d = dt_in("conv_b", [1024])
    mu_d = dt_in("shift_mu", [NRW])
    w0_d = dt_in("w0", [2, 1024]); wup_d = dt_in("w_up", [2, 64, 1024])
    a0_d = dt_in("a0", [2, 1024]); aup_d = dt_in("a_up", [2, 64, 1024])
    kk_d = dt_in("k_k", [1024]); ka_d = dt_in("k_a", [1024]); rk_d = dt_in("r_k", [1024])
    lxg_d = dt_in("lnx_g", [1024]); lxb_d = dt_in("lnx_b", [1024])
    wout_d = dt_in("w_out", [2048, D])
    lng_d = dt_in("ln_g", [D]); lnb_d = dt_in("ln_b", [D])
    ys = nc.dram_tensor("ys", [NTOK, D], F32, kind="ExternalOutput").ap()
    XT = nc.dram_tensor("XT", [128, 8, NTOK + 2], BF16).ap()
    YF = nc.dram_tensor("YF", [NTOK, GC], F32).ap()
    YR = nc.dram_tensor("YR", [128, 8, NTOK], BF16).ap()

    with contextlib.ExitStack() as es0:
        es0.enter_context(nc.allow_non_contiguous_dma(reason="small strided param / halo loads"))
        S = Sch(nc, es0)
        XT_bs = [S.buf(f"XT{i}") for i in range(NT + 2)]
        YF_bs = [S.buf(f"YF{i}") for i in range(NT)]
        YR_bs = [[S.buf(f"YR{g}_{i}") for i in range(NT)] for g in range(NG)]
        ys_bs = [S.buf(f"ys{i}") for i in range(NT)]

        uid = [0]

        def sb(es, name, shape, dtype):
            uid[0] += 1
            nm = f"sb{uid[0]}_{name}"
            NAMES[name] = nm
            return TL(S, es.enter_context(nc.sbuf_tensor(nm, list(shape), dtype)), nm)

        PS = [TL(S, es0.enter_context(nc.psum_tensor(f"ps{i}", [128, 512], F32)), f"ps{i}") for i in range(8)]
        for p_ in PS:
            p_.b.excl = True
        prot = {"f": [0, [0, 1, 2, 3]], "b": [0, [4, 5, 6, 7]]}

        def pbank(kind):
            st = prot[kind]
            p = PS[st[1][st[0] % len(st[1])]]
            st[0] += 1
            return p

        cst = sb(es0, "cst", [128, CST_COLS], F32)
        S.dma(cst[:], cst_d, [], [cst.b])
        identb = sb(es0, "identb", [128, 128], BF16)
        S.op("dve", lambda: nc.vector.tensor_copy(out=identb[:], in_=cst[:, 0:128]), [cst.b], [identb.b])
        keep = sb(es0, "keep", [128, NB], F32)
        S.dma(keep[:], keep_d, [], [keep.b])
        zpad = sb(es0, "zpad", [128, 8, 1], BF16)
        S.op("pool", lambda: nc.gpsimd.memset(zpad[:], 0.0), [], [zpad.b])
        S.dma(XT[:, :, 0:1], zpad[:], [zpad.b], [XT_bs[0]])
        S.dma(XT[:, :, NTOK + 1:NTOK + 2], zpad[:], [zpad.b], [XT_bs[NT + 1]])

        onesf = sb(es0, "onesf", [1, 128], F32)
        S.op("pool", lambda: nc.gpsimd.memset(onesf[:], 1.0), [], [onesf.b])
        rowbuf = sb(es0, "rowbuf", [1, 1024], F32)

        def bcast_load(dst, dcol, src1d, ncol):
            S.dma(rowbuf[0:1, 0:ncol], src1d.rearrange("(o n) -> o n", o=1), [], [rowbuf.b])
            for c_ in range(0, ncol, 512):
                n_ = min(512, ncol - c_)
                pb_ = pbank("f")
                S.op("pe", lambda: nc.tensor.matmul(pb_[:, 0:n_], lhsT=onesf[0:1, :], rhs=rowbuf[0:1, c_:c_ + n_],
                                                    start=True, stop=True), [onesf.b, rowbuf.b], [pb_.b])
                S.op("act", lambda: nc.scalar.copy(out=dst[:, dcol + c_:dcol + c_ + n_], in_=pb_[:, 0:n_]), [pb_.b], [dst.b])

        def ln_stats(es_tag, src, mv, rstd, nb_, st6, eps):
            S.op("dve", lambda: nc.vector.bn_stats(out=st6[:, 0, :], in_=src[:, 0:512]), [src.b], [st6.b])
            S.op("dve", lambda: nc.vector.bn_stats(out=st6[:, 1, :], in_=src[:, 512:1024]), [src.b], [st6.b])
            S.op("dve", lambda: nc.vector.bn_aggr(out=mv[:], in_=st6[:]), [st6.b], [mv.b])
            S.op("act", lambda: nc.scalar.activation(out=rstd[:], in_=mv[:, 1:2], func=AF.Sqrt, bias=eps, scale=1.0),
                 [mv.b], [rstd.b])
            S.op("dve", lambda: nc.vector.reciprocal(out=rstd[:], in_=rstd[:]), [rstd.b], [rstd.b])
            S.op("dve", lambda: nc.vector.scalar_tensor_tensor(out=nb_[:], in0=mv[:, 0:1], scalar=-1.0, in1=rstd[:],
                                                               op0=ALU.mult, op1=ALU.mult), [mv.b, rstd.b], [nb_.b])

        SD = int(nc.vector.BN_STATS_DIM)
        AD = int(nc.vector.BN_AGGR_DIM)

        with contextlib.ExitStack() as es:
            gbc = sb(es, "p1_g", [128, D], F32); bbc = sb(es, "p1_b", [128, D], F32)
            bcast_load(gbc, 0, emb_g_d, D)
            bcast_load(bbc, 0, emb_b_d, D)
            xt = [sb(es, f"p1_x{i}", [128, D], F32) for i in range(2)]
            xn = [sb(es, f"p1_xn{i}", [128, D], F32) for i in range(2)]
            xg = [sb(es, f"p1_xg{i}", [128, D], F32) for i in range(2)]
            xh = [sb(es, f"p1_xh{i}", [128, D], BF16) for i in range(2)]
            xT = [sb(es, f"p1_xT{i}", [128, 8, 128], BF16) for i in range(2)]
            st6 = [sb(es, f"p1_st{i}", [128, 2, SD], F32) for i in range(2)]
            mv = [sb(es, f"p1_mv{i}", [128, AD], F32) for i in range(2)]
            rstd = [sb(es, f"p1_rs{i}", [128, 1], F32) for i in range(2)]
            nb_ = [sb(es, f"p1_nb{i}", [128, 1], F32) for i in range(2)]
            for i in range(NT if DBG.get("p1", True) else 0):
                p = i % 2
                S.dma(xt[p][:], xs[i * 128:(i + 1) * 128, :], [], [xt[p].b])
                ln_stats("p1", xt[p], mv[p], rstd[p], nb_[p], st6[p], LN_EPS)
                S.op("act", lambda: nc.scalar.activation(out=xn[p][:], in_=xt[p][:], func=AF.Identity,
                                                         bias=nb_[p][:, 0:1], scale=rstd[p][:, 0:1]),
                     [xt[p].b, rstd[p].b, nb_[p].b], [xn[p].b])
                S.op("dve", lambda: nc.vector.tensor_tensor(out=xg[p][:], in0=xn[p][:], in1=gbc[:], op=ALU.mult),
                     [xn[p].b, gbc.b], [xg[p].b])
                S.op("pool", lambda: nc.gpsimd.tensor_tensor(out=xh[p][:], in0=xg[p][:], in1=bbc[:], op=ALU.add),
                     [xg[p].b, bbc.b], [xh[p].b])
                pt = pbank("f")
                ptv = pt[:].bitcast(BF16)

                def tr():
                    ins = None
                    for c in range(8):
                        ins = nc.tensor.transpose(out=ptv[:, c * 128:(c + 1) * 128], in_=xh[p][:, c * 128:(c + 1) * 128],
                                                  identity=identb[:])
                    return ins
                S.op("pe", tr, [xh[p].b, identb.b], [pt.b])
                S.op("act", lambda: nc.scalar.copy(out=xT[p][:].rearrange("p c t -> p (c t)"), in_=ptv), [pt.b], [xT[p].b])
                S.dma(XT[:, :, 1 + i * 128:1 + (i + 1) * 128], xT[p][:], [xT[p].b], [XT_bs[i + 1]])
            S.barrier()

        def load_xth(xth, i):
            S.dma(xth[:], XT[:, :, i * 128:i * 128 + 130], [XT_bs[i], XT_bs[i + 1], XT_bs[i + 2]], [xth.b])
            if i % BP == 0 and i > 0:
                bidx = i // BP - 1
                S.op("dve", lambda: nc.vector.tensor_scalar(out=xth[:, :, 0:1], in0=xth[:, :, 0:1],
                                                            scalar1=keep[:, bidx:bidx + 1], scalar2=None, op0=ALU.mult),
                     [xth.b, keep.b], [xth.b])
            if (i + 1) % BP == 0 and i + 1 < NT:
                bidx = (i + 1) // BP - 1
                S.op("dve", lambda: nc.vector.tensor_scalar(out=xth[:, :, 129:130], in0=xth[:, :, 129:130],
                                                            scalar1=keep[:, bidx:bidx + 1], scalar2=None, op0=ALU.mult),
                     [xth.b, keep.b], [xth.b])

        for g in range(NG if DBG.get("rwkv", True) else 0):
            with contextlib.ExitStack() as es:
                c0 = g * GC
                wq = sb(es, "wq", [128, 16, 4 * GC], BF16)
                wl = sb(es, "wl", [128, 16, 256], BF16)
                with contextlib.ExitStack() as es2:
                    wst = [sb(es2, f"wst{i}", [128, 8, 512], F32) for i in range(2)]
                    mub = sb(es2, "mub", [128, 512], F32)
                    mua = sb(es2, "mua", [128, 512], F32)
                    mubb = sb(es2, "mubb", [128, 512], F32)
                    blocks = [(NCV + q * 1024 + c0, 512, wq, q * GC) for q in range(4)] + [(NCV + 4096, 256, wl, 0)]
                    for bi, (col, ncol, dst, dcol) in enumerate(blocks):
                        w_ = wst[bi % 2]
                        S.dma(w_[:, :, 0:ncol], w_in_d[:, col:col + ncol].rearrange("(c p) n -> p c n", p=128), [], [w_.b])
                        bcast_load(mub, 0, mu_d[col - NCV:col - NCV + ncol], ncol)
                        S.op("dve", lambda: nc.vector.tensor_scalar(out=mua[:, 0:ncol], in0=mub[:, 0:ncol], scalar1=-1.0,
                                                                    scalar2=1.0, op0=ALU.mult, op1=ALU.add),
                             [mub.b], [mua.b])
                        S.op("dve", lambda: nc.vector.tensor_scalar(out=mubb[:, 0:ncol], in0=mub[:, 0:ncol], scalar1=0.5,
                                                                    scalar2=None, op0=ALU.mult), [mub.b], [mubb.b])
                        S.op("dve", lambda: nc.vector.tensor_tensor(
                            out=dst[:, 0:8, dcol:dcol + ncol], in0=w_[:, :, 0:ncol],
                            in1=bc(mua[:, 0:ncol].unsqueeze(1), [128, 8, ncol]), op=ALU.mult), [w_.b, mua.b], [dst.b])
                        S.op("pool", lambda: nc.gpsimd.tensor_tensor(
                            out=dst[:, 8:16, dcol:dcol + ncol], in0=w_[:, :, 0:ncol],
                            in1=bc(mubb[:, 0:ncol].unsqueeze(1), [128, 8, ncol]), op=ALU.mult), [w_.b, mubb.b], [dst.b])
                    S.barrier()
                upw = sb(es, "upw", [64, 2, 2, GC], BF16)
                b0h = sb(es, "b0h", [1, 2, 2, GC], BF16); b0l = sb(es, "b0l", [1, 2, 2, GC], BF16)
                onesr = sb(es, "onesr", [1, 128], BF16)
                S.op("pool", lambda: nc.gpsimd.memset(onesr[:], 1.0), [], [onesr.b])
                with contextlib.ExitStack() as es2:
                    upf = sb(es2, "upf", [64, 2, 2, GC], F32)
                    b0f = sb(es2, "b0f", [1, 2, 2, GC], F32); b0t = sb(es2, "b0t", [1, 2, 2, GC], F32)
                    for wi, (ud, bd) in enumerate(((wup_d, w0_d), (aup_d, a0_d))):
                        for d in range(2):
                            S.dma(upf[:, wi, d, :], ud[d, :, c0:c0 + GC], [], [upf.b])
                            S.dma(b0f[:, wi, d, :], bd[d:d + 1, c0:c0 + GC], [], [b0f.b])
                    S.op("dve", lambda: nc.vector.tensor_copy(out=upw[:], in_=upf[:]), [upf.b], [upw.b])
                    S.op("dve", lambda: nc.vector.tensor_copy(out=b0h[:], in_=b0f[:]), [b0f.b], [b0h.b])
                    S.op("dve", lambda: nc.vector.tensor_tensor(out=b0t[:], in0=b0f[:], in1=b0h[:], op=ALU.subtract),
                         [b0f.b, b0h.b], [b0t.b])
                    S.op("dve", lambda: nc.vector.tensor_copy(out=b0l[:], in_=b0t[:]), [b0t.b], [b0l.b])
                    S.barrier()
                prm = {}
                for nm, src in (("kk", kk_d), ("ka", ka_d), ("rk", rk_d), ("lxg", lxg_d), ("lxb", lxb_d)):
                    prm[nm] = sb(es, "prm_" + nm, [128, GC], F32)
                    bcast_load(prm[nm], 0, src[c0:c0 + GC], GC)

                xth = [sb(es, f"xth{i}", [128, 8, 130], BF16) for i in range(2)]
                xst = sb(es, "xst", [128, 8, 128], BF16)
                f32t = {n: sb(es, "w_" + n, [128, GC], F32) for n in
                        ("sg", "a", "a0", "e1", "e2", "e3", "kkr", "t1", "kd", "bb", "kd0", "vbon",
                         "sz", "yf", "ysum", "sq2")}
                bft_ = {n: sb(es, "h_" + n, [128, GC], BF16) for n in ("V", "kh", "bh", "kkt", "rt", "Zs", "nU", "ob")}
                f32p = [dict(f32t), dict(f32t)]
                for n_ in ("vbon", "sz", "yf"):
                    f32p[1][n_] = sb(es, "w1_" + n_, [128, GC], F32)
                bftp = [dict(bft_), dict(bft_)]
                for n_ in ("V", "kh", "bh"):
                    bftp[1][n_] = sb(es, "h1_" + n_, [128, GC], BF16)
                tl_ = sb(es, "tl_", [64, 3, 128], BF16)
                kT2 = [sb(es, f"kT{i}", [64, NH, 128], BF16) for i in range(2)]
                bT2 = [sb(es, f"bT{i}", [64, NH, 128], BF16) for i in range(2)]
                krT2 = [sb(es, f"krT{i}", [64, NH, 2, 128], BF16) for i in range(2)]
                oT = sb(es, "oT", [128, 4, 128], BF16)
                MA1 = sb(es, "MA1", [128, NH, 2, 128], BF16)
                MA2 = sb(es, "MA2", [128, NH, 2, 128], BF16)
                XQ = [sb(es, f"XQ{i}", [128, NH, 2, 128], BF16) for i in range(2)]
                Xt = [sb(es, f"Xt{i}", [128, NH, 128], BF16) for i in range(2)]
                TT = sb(es, "TT", [128, NH, 128], BF16)
                Hf = sb(es, "Hf", [64, NH, 64], F32); Hb = sb(es, "Hb", [64, NH, 64], BF16)
                Ht = sb(es, "Ht", [64, NH, 64], F32)
                gam2 = [sb(es, f"gam{i}", [64, NH], F32) for i in range(2)]
                ss = sb(es, "ss", [128, NH], F32); rn = sb(es, "rn", [128, NH], F32)
                bs = sb(es, "bs", [128, NH], F32)
                gs1 = sb(es, "gs1", [128, NH], F32); gs2 = sb(es, "gs2", [128, NH], F32)
                grs = sb(es, "grs", [128, NH], F32)

                for dr in range(2):
                    bwd = dr == 1
                    cb = 128 + dr * 896
                    m1 = cst[:, cb:cb + 256]; m2 = cst[:, cb + 256:cb + 512]; m3 = cst[:, cb + 512:cb + 640]
                    tinc = cst[:, cb + 640:cb + 768]; texc = cst[:, cb + 768:cb + 896]
                    ccol = cst[:, CST_COLS - 1:CST_COLS]
                    S.op("dve", lambda: nc.vector.memset(Hf[:], 0.0), [], [Hf.b])
                    S.op("dve", lambda: nc.vector.memset(Hb[:], 0.0), [], [Hb.b])
                    order = list(range(NT - 1, -1, -1)) if bwd else list(range(NT))
                    load_xth(xth[0], order[0])
                    v3 = lambda ap: ap.rearrange("p (h c) -> p h c", c=HD)

                    def front(oi, i):
                        par = oi % 2
                        t = f32p[par]; bft = bftp[par]
                        kT = kT2[par]; bT = bT2[par]; krT = krT2[par]; gam = gam2[par]
                        V = bft["V"]; nU = bft["nU"]
                        xh_ = xth[oi % 2]
                        if oi + 1 < NT:
                            load_xth(xth[(oi + 1) % 2], order[oi + 1])
                        if bwd:
                            yield
                            S.dma(t["yf"][:], YF[i * 128:(i + 1) * 128, :], [YF_bs[i]], [t["yf"].b])
                        yield
                        S.op("pool", lambda: nc.gpsimd.tensor_tensor(out=xst[:], in0=xh_[:, :, 0:128], in1=xh_[:, :, 2:130],
                                                                      op=ALU.add), [xh_.b], [xst.b])

                        def proj_fm(pt_ap, wcol, ncol):
                            ins = None
                            for kc in range(16):
                                rhs = xh_[:, kc, 1:129] if kc < 8 else xst[:, kc - 8, :]
                                ins = nc.tensor.matmul(pt_ap, lhsT=wl[:, kc, wcol:wcol + ncol], rhs=rhs,
                                                       start=(kc == 0), stop=(kc == 15))
                            return ins

                        def proj_tm(pt_ap, q):
                            ins = None
                            for kc in range(16):
                                lhsT = xh_[:, kc, 1:129] if kc < 8 else xst[:, kc - 8, :]
                                ins = nc.tensor.matmul(pt_ap, lhsT=lhsT, rhs=wq[:, kc, q * GC:(q + 1) * GC],
                                                       start=(kc == 0), stop=(kc == 15))
                            return ins

                        pc = pbank("f")
                        ncode = 3 if bwd else 2

                        def codes():
                            ins = proj_fm(pc[0:64, 0:128], dr * 64, 64)
                            ins = proj_fm(pc[0:64, 128:256], 128 + dr * 64, 64)
                            if bwd:
                                ins = proj_fm(pc[0:64, 256:384], 128, 64)
                            return ins
                        yield
                        S.op("pe", codes, [xh_.b, xst.b, wl.b], [pc.b])
                        yield
                        S.op("act", lambda: nc.scalar.activation(out=tl_[:, 0, :], in_=pc[0:64, 0:128], func=AF.Tanh),
                             [pc.b], [tl_.b])
                        yield
                        S.op("act", lambda: nc.scalar.copy(out=tl_[:, 1:ncode, :].rearrange("p a t -> p (a t)"),
                                                           in_=pc[0:64, 128:128 * ncode]), [pc.b], [tl_.b])

                        def lowrank(pt, ci, wi, d):
                            def f():
                                nc.tensor.matmul(pt[:], lhsT=tl_[:, ci, :], rhs=upw[:, wi, d, :], start=True, stop=False)
                                nc.tensor.matmul(pt[:], lhsT=onesr[:], rhs=b0h[:, wi, d, :], start=False, stop=False)
                                return nc.tensor.matmul(pt[:], lhsT=onesr[:], rhs=b0l[:, wi, d, :], start=False, stop=True)
                            S.op("pe", f, [tl_.b, upw.b, onesr.b, b0h.b, b0l.b], [pt.b])
                        yield
                        pd_ = pbank("f"); lowrank(pd_, 0, 0, dr)
                        yield
                        S.op("act", lambda: nc.scalar.activation(out=t["sg"][:], in_=pd_[:], func=AF.Sigmoid),
                             [pd_.b], [t["sg"].b])
                        yield
                        pa_ = pbank("f"); lowrank(pa_, 1, 1, dr)
                        yield
                        S.op("act", lambda: nc.scalar.activation(out=t["a"][:], in_=pa_[:], func=AF.Sigmoid),
                             [pa_.b], [t["a"].b])
                        if bwd:
                            yield
                            pa0 = pbank("f"); lowrank(pa0, 2, 1, 0)
                            yield
                            S.op("act", lambda: nc.scalar.activation(out=t["a0"][:], in_=pa0[:], func=AF.Sigmoid),
                                 [pa0.b], [t["a0"].b])
                        sg = t["sg"]
                        pcum = pbank("f")
                        yield
                        S.op("pe", lambda: nc.tensor.matmul(pcum[:], lhsT=tinc, rhs=sg[:], start=True, stop=True),
                             [cst.b, sg.b], [pcum.b])
                        yield
                        S.op("act", lambda: nc.scalar.activation(out=t["e1"][:], in_=pcum[:], func=AF.Exp, scale=-1.0),
                             [pcum.b], [t["e1"].b])
                        yield
                        S.op("act", lambda: nc.scalar.activation(out=t["e3"][:], in_=pcum[:], func=AF.Exp),
                             [pcum.b], [t["e3"].b])
                        pcx = pbank("f")

                        yield
                        S.op("pe", lambda: nc.tensor.matmul(pcx[:], lhsT=texc, rhs=sg[:], start=True, stop=True),
                             [cst.b, sg.b], [pcx.b])
                        yield
                        S.op("act", lambda: nc.scalar.activation(out=t["e2"][:], in_=pcx[:], func=AF.Exp),
                             [pcx.b], [t["e2"].b])
                        pgm = pbank("f")

                        def gsum():
                            ins = None
                            for h in range(NH):
                                ins = nc.tensor.matmul(pgm[0:64, h:h + 1], lhsT=sg[:, h * HD:(h + 1) * HD], rhs=ccol,
                                                       start=True, stop=True)
                            return ins
                        yield
                        S.op("pe", gsum, [sg.b, cst.b], [pgm.b])
                        yield
                        S.op("act", lambda: nc.scalar.activation(out=gam[:], in_=pgm[0:64, 0:NH], func=AF.Exp), [pgm.b], [gam.b])

                        pr = pbank("f"); S.op("pe", lambda: proj_tm(pr[:], 0), [xh_.b, xst.b, wq.b], [pr.b])
                        pk = pbank("f"); S.op("pe", lambda: proj_tm(pk[:], 1), [xh_.b, xst.b, wq.b], [pk.b])
                        pv = pbank("f"); S.op("pe", lambda: proj_tm(pv[:], 2), [xh_.b, xst.b, wq.b], [pv.b])
                        V = bft["V"]
                        yield
                        S.op("act", lambda: nc.scalar.copy(out=V[:], in_=pv[:]), [pv.b], [V.b])
                        v3 = lambda ap: ap.rearrange("p (h c) -> p h c", c=HD)
                        yield
                        S.op("dve", lambda: nc.vector.tensor_tensor(out=t["kkr"][:], in0=pk[:], in1=prm["kk"][:], op=ALU.mult),
                             [pk.b, prm["kk"].b], [t["kkr"].b])
                        yield
                        S.op("pool", lambda: nc.gpsimd.tensor_tensor(out=t["t1"][:], in0=t["kkr"][:], in1=t["kkr"][:], op=ALU.mult),
                             [t["kkr"].b], [t["t1"].b])
                        yield
                        S.op("dve", lambda: nc.vector.tensor_reduce(out=ss[:], in_=v3(t["t1"][:]), axis=AX.X, op=ALU.add),
                             [t["t1"].b], [ss.b])
                        yield
                        S.op("act", lambda: nc.scalar.activation(out=rn[:], in_=ss[:], func=AF.Sqrt), [ss.b], [rn.b])
                        yield
                        S.op("dve", lambda: nc.vector.tensor_scalar(out=rn[:], in0=rn[:], scalar1=1e-12, scalar2=None,
                                                                    op0=ALU.max), [rn.b], [rn.b])
                        yield
                        S.op("dve", lambda: nc.vector.reciprocal(out=rn[:], in_=rn[:]), [rn.b], [rn.b])
                        yield
                        S.op("dve", lambda: nc.vector.tensor_tensor(out=v3(t["kkr"][:]), in0=v3(t["kkr"][:]),
                                                                    in1=bc(rn[:].unsqueeze(2), [128, NH, HD]), op=ALU.mult),
                             [t["kkr"].b, rn.b], [t["kkr"].b])
                        yield
                        S.op("dve", lambda: nc.vector.scalar_tensor_tensor(out=t["t1"][:], in0=t["a"][:], scalar=-1.0,
                                                                             in1=prm["ka"][:], op0=ALU.add, op1=ALU.mult),
                             [t["a"].b, prm["ka"].b], [t["t1"].b])
                        yield
                        S.op("dve", lambda: nc.vector.scalar_tensor_tensor(out=t["kd"][:], in0=t["t1"][:], scalar=1.0,
                                                                           in1=pk[:], op0=ALU.add, op1=ALU.mult),
                             [t["t1"].b, pk.b], [t["kd"].b])
                        yield
                        S.op("pool", lambda: nc.gpsimd.tensor_tensor(out=t["bb"][:], in0=t["kkr"][:], in1=t["a"][:], op=ALU.mult),
                             [t["kkr"].b, t["a"].b], [t["bb"].b])
                        yield
                        S.op("dve", lambda: nc.vector.tensor_tensor(out=bft["kh"][:], in0=t["kd"][:], in1=t["e1"][:], op=ALU.mult),
                             [t["kd"].b, t["e1"].b], [bft["kh"].b])
                        yield
                        S.op("pool", lambda: nc.gpsimd.tensor_tensor(out=bft["bh"][:], in0=t["bb"][:], in1=t["e1"][:], op=ALU.mult),
                             [t["bb"].b, t["e1"].b], [bft["bh"].b])
                        yield
                        S.op("pool", lambda: nc.gpsimd.tensor_tensor(out=bft["kkt"][:], in0=t["kkr"][:], in1=t["e2"][:], op=ALU.mult),
                             [t["kkr"].b, t["e2"].b], [bft["kkt"].b])
                        yield
                        S.op("dve", lambda: nc.vector.tensor_tensor(out=bft["rt"][:], in0=pr[:], in1=t["e3"][:], op=ALU.mult),
                             [pr.b, t["e3"].b], [bft["rt"].b])
                        if bwd:
                            yield
                            S.op("dve", lambda: nc.vector.scalar_tensor_tensor(out=t["t1"][:], in0=t["a0"][:], scalar=-1.0,
                                                                                 in1=prm["ka"][:], op0=ALU.add, op1=ALU.mult),
                                 [t["a0"].b, prm["ka"].b], [t["t1"].b])
                            yield
                            S.op("dve", lambda: nc.vector.scalar_tensor_tensor(out=t["kd0"][:], in0=t["t1"][:], scalar=1.0,
                                                                               in1=pk[:], op0=ALU.add, op1=ALU.mult),
                                 [t["t1"].b, pk.b], [t["kd0"].b])
                            yield
                            S.op("pool", lambda: nc.gpsimd.tensor_tensor(out=t["kd0"][:], in0=t["kd0"][:], in1=t["kd"][:], op=ALU.add),
                                 [t["kd0"].b, t["kd"].b], [t["kd0"].b])
                            yield
                            S.op("pool", lambda: nc.gpsimd.tensor_tensor(out=t["kd0"][:], in0=t["kd0"][:], in1=prm["rk"][:], op=ALU.mult),
                                 [t["kd0"].b, prm["rk"].b], [t["kd0"].b])
                            yield
                            S.op("dve", lambda: nc.vector.tensor_tensor(out=t["kd0"][:], in0=pr[:], in1=t["kd0"][:], op=ALU.mult),
                                 [pr.b, t["kd0"].b], [t["kd0"].b])
                            yield
                            S.op("dve", lambda: nc.vector.tensor_reduce(out=bs[:], in_=v3(t["kd0"][:]), axis=AX.X, op=ALU.add),
                                 [t["kd0"].b], [bs.b])
                            yield
                            S.op("dve", lambda: nc.vector.scalar_tensor_tensor(
                                out=v3(t["vbon"][:]), in0=v3(pv[:]), scalar=0.5, in1=bc(bs[:].unsqueeze(2), [128, NH, HD]),
                                op0=ALU.mult, op1=ALU.mult), [pv.b, bs.b], [t["vbon"].b])
                            pz = pbank("f"); S.op("pe", lambda: proj_tm(pz[:], 3), [xh_.b, xst.b, wq.b], [pz.b])
                            yield
                            S.op("act", lambda: nc.scalar.activation(out=t["sz"][:], in_=pz[:], func=AF.Silu), [pz.b], [t["sz"].b])

                        def trans8(src_, dst_ap, eng):
                            pt = pbank("f")
                            ptv = pt[:].bitcast(BF16)

                            def f():
                                ins = None
                                for h in range(NH):
                                    ins = nc.tensor.transpose(out=ptv[0:64, h * 128:(h + 1) * 128], in_=src_[:, h * HD:(h + 1) * HD],
                                                              identity=identb[:])
                                return ins
                            S.op("pe", f, [src_.b, identb.b], [pt.b])
                            src_v = ptv[0:64, :].rearrange("p (h t) -> p h t", t=128)
                            if eng == "act":
                                S.op("act", lambda: nc.scalar.copy(out=dst_ap[0], in_=src_v), [pt.b], [dst_ap[1]])
                            else:
                                S.op("dve", lambda: nc.vector.tensor_copy(out=dst_ap[0], in_=src_v), [pt.b], [dst_ap[1]])
                        yield
                        trans8(bft["kh"], (kT[:], kT.b), "act")
                        yield
                        trans8(bft["bh"], (bT[:], bT.b), "dve")
                        yield
                        trans8(bft["kkt"], (krT[:, :, 0, :], krT.b), "act")
                        yield
                        trans8(bft["rt"], (krT[:, :, 1, :], krT.b), "dve")

                    def back(oi, i):
                        par = oi % 2
                        t = f32p[par]; bft = bftp[par]
                        kT = kT2[par]; bT = bT2[par]; krT = krT2[par]; gam = gam2[par]
                        V = bft["V"]; nU = bft["nU"]
                        for hp in range(NH // 2):
                            for which, lT, dst, msk in ((0, kT, MA1, m1), (1, bT, MA2, m2)):
                                pm = pbank("b")

                                def f():
                                    ins = None
                                    for hh in range(2):
                                        h = hp * 2 + hh
                                        ins = nc.tensor.matmul(pm[:, hh * 256:(hh + 1) * 256], lhsT=lT[:, h, :],
                                                               rhs=krT[:, h, :, :].rearrange("p a t -> p (a t)"),
                                                               start=True, stop=True)
                                    return ins
                                yield
                                S.op("pe", f, [lT.b, krT.b], [pm.b])
                                yield
                                S.op("dve", lambda: nc.vector.tensor_tensor(
                                    out=dst[:, hp * 2:hp * 2 + 2, :, :].rearrange("p h a t -> p h (a t)"),
                                    in0=pm[:].rearrange("p (h x) -> p h x", h=2),
                                    in1=bc(msk.unsqueeze(1), [128, 2, 256]), op=ALU.mult), [pm.b, cst.b], [dst.b])
                        for hq in range(NH // 4):
                            pm = pbank("b")

                            def f():
                                ins = None
                                for hh in range(4):
                                    h = hq * 4 + hh
                                    ins = nc.tensor.matmul(pm[:, hh * 128:(hh + 1) * 128], lhsT=krT[:, h, 0, :],
                                                           rhs=bT[:, h, :], start=True, stop=True)
                                return ins
                            yield
                            S.op("pe", f, [krT.b, bT.b], [pm.b])
                            yield
                            S.op("dve", lambda: nc.vector.tensor_tensor(
                                out=Xt[0][:, hq * 4:hq * 4 + 4, :], in0=pm[:].rearrange("p (h x) -> p h x", h=4),
                                in1=bc(m3.unsqueeze(1), [128, 4, 128]), op=ALU.mult), [pm.b, cst.b], [Xt[0].b])

                        yield
                        S.op("pool", lambda: nc.gpsimd.tensor_tensor(
                            out=XQ[1][:, :, 1, :], in0=MA2[:, :, 0, :], in1=bc(identb[:].unsqueeze(1), [128, NH, 128]),
                            op=ALU.add), [MA2.b, identb.b], [XQ[1].b])
                        for hq in range(NH // 4):
                            pm = pbank("b")

                            def f():
                                ins = None
                                for hh in range(4):
                                    h = hq * 4 + hh
                                    ins = nc.tensor.matmul(pm[:, hh * 128:(hh + 1) * 128], lhsT=Xt[0][:, h, :],
                                                           rhs=MA2[:, h, 0, :], start=True, stop=True)
                                return ins
                            yield
                            S.op("pe", f, [Xt[0].b, MA2.b], [pm.b])
                            yield
                            S.op("act", lambda: nc.scalar.copy(out=XQ[1][:, hq * 4:hq * 4 + 4, 0, :],
                                                               in_=pm[:].rearrange("p (h x) -> p h x", h=4)), [pm.b], [XQ[1].b])
                            pm2 = pbank("b")

                            def f2():
                                ins = None
                                for hh in range(4):
                                    h = hq * 4 + hh
                                    ins = nc.tensor.matmul(pm2[:, hh * 128:(hh + 1) * 128], lhsT=MA2[:, h, 0, :],
                                                           rhs=Xt[0][:, h, :], start=True, stop=True)
                                return ins
                            yield
                            S.op("pe", f2, [Xt[0].b, MA2.b], [pm2.b])
                            yield
                            S.op("dve", lambda: nc.vector.tensor_copy(out=Xt[1][:, hq * 4:hq * 4 + 4, :],
                                                                      in_=pm2[:].rearrange("p (h x) -> p h x", h=4)),
                                 [pm2.b], [Xt[1].b])
                        for lev in range(1, NLEV):
                            cur = XQ[lev % 2]; nxt = XQ[(lev + 1) % 2]
                            xtc = Xt[lev % 2]; xtn = Xt[(lev + 1) % 2]
                            last = lev == NLEV - 1
                            if not last:
                                for hp in range(NH // 2):
                                    pm = pbank("b")

                                    def f():
                                        ins = None
                                        for hh in range(2):
                                            h = hp * 2 + hh
                                            o = hh * 256
                                            nc.tensor.matmul(pm[:, o:o + 256], lhsT=xtc[:, h, :],
                                                             rhs=cur[:, h, :, :].rearrange("p a t -> p (a t)"), start=True, stop=False)
                                            ins = nc.tensor.matmul(pm[:, o + 128:o + 256], lhsT=identb[:], rhs=cur[:, h, 1, :],
                                                                   start=False, stop=True)
                                        return ins
                                    yield
                                    S.op("pe", f, [xtc.b, cur.b, identb.b], [pm.b])
                                    eng = "act" if hp % 2 == 0 else "dve"
                                    if eng == "act":
                                        yield
                                        S.op("act", lambda: nc.scalar.copy(
                                            out=nxt[:, hp * 2:hp * 2 + 2, :, :].rearrange("p h a t -> p (h a t)"), in_=pm[:]),
                                            [pm.b], [nxt.b])
                                    else:
                                        yield
                                        S.op("dve", lambda: nc.vector.tensor_copy(
                                            out=nxt[:, hp * 2:hp * 2 + 2, :, :].rearrange("p h a t -> p (h a t)"), in_=pm[:]),
                                            [pm.b], [nxt.b])
                                for hq in range(NH // 4):
                                    pm = pbank("b")

                                    def f():
                                        ins = None
                                        for hh in range(4):
                                            h = hq * 4 + hh
                                            ins = nc.tensor.matmul(pm[:, hh * 128:(hh + 1) * 128], lhsT=cur[:, h, 0, :],
                                                                   rhs=xtc[:, h, :], start=True, stop=True)
                                        return ins
                                    yield
                                    S.op("pe", f, [cur.b, xtc.b], [pm.b])
                                    if hq % 2 == 0:
                                        yield
                                        S.op("act", lambda: nc.scalar.copy(out=xtn[:, hq * 4:hq * 4 + 4, :].rearrange("p h t -> p (h t)"),
                                                                           in_=pm[:]), [pm.b], [xtn.b])
                                    else:
                                        yield
                                        S.op("dve", lambda: nc.vector.tensor_copy(out=xtn[:, hq * 4:hq * 4 + 4, :].rearrange("p h t -> p (h t)"),
                                                                                  in_=pm[:]), [pm.b], [xtn.b])
                            else:
                                for hq in range(NH // 4):
                                    pm = pbank("b")

                                    def f():
                                        ins = None
                                        for hh in range(4):
                                            h = hq * 4 + hh
                                            o = hh * 128
                                            nc.tensor.matmul(pm[:, o:o + 128], lhsT=xtc[:, h, :], rhs=cur[:, h, 1, :],
                                                             start=True, stop=False)
                                            ins = nc.tensor.matmul(pm[:, o:o + 128], lhsT=identb[:], rhs=cur[:, h, 1, :],
                                                                   start=False, stop=True)
                                        return ins
                                    yield
                                    S.op("pe", f, [xtc.b, cur.b, identb.b], [pm.b])
                                    if hq % 2 == 0:
                                        yield
                                        S.op("act", lambda: nc.scalar.copy(out=TT[:, hq * 4:hq * 4 + 4, :].rearrange("p h t -> p (h t)"),
                                                                           in_=pm[:]), [pm.b], [TT.b])
                                    else:
                                        yield
                                        S.op("dve", lambda: nc.vector.tensor_copy(out=TT[:, hq * 4:hq * 4 + 4, :].rearrange("p h t -> p (h t)"),
                                                                                  in_=pm[:]), [pm.b], [TT.b])

                        Vh = lambda h: V[:, h * HD:(h + 1) * HD]
                        pzz = pbank("b")

                        def fz():
                            ins = None
                            for h in range(NH):
                                nc.tensor.matmul(pzz[:, h * HD:(h + 1) * HD], lhsT=krT[:, h, 0, :],
                                                 rhs=Hb[:, h, :], start=True, stop=False)
                                ins = nc.tensor.matmul(pzz[:, h * HD:(h + 1) * HD], lhsT=MA1[:, h, 0, :], rhs=Vh(h),
                                                       start=False, stop=True)
                            return ins
                        yield
                        S.op("pe", fz, [krT.b, Hb.b, MA1.b, V.b], [pzz.b])
                        yield
                        S.op("act", lambda: nc.scalar.copy(out=bft["Zs"][:], in_=pzz[:]), [pzz.b], [bft["Zs"].b])
                        pu = pbank("b")

                        def fu():
                            ins = None
                            for h in range(NH):
                                ins = nc.tensor.matmul(pu[:, h * HD:(h + 1) * HD], lhsT=TT[:, h, :],
                                                       rhs=bft["Zs"][:, h * HD:(h + 1) * HD], start=True, stop=True)
                            return ins
                        yield
                        S.op("pe", fu, [TT.b, bft["Zs"].b], [pu.b])
                        nU = bft["nU"]
                        yield
                        S.op("act", lambda: nc.scalar.activation(out=nU[:], in_=pu[:], func=AF.Identity, scale=-1.0), [pu.b], [nU.b])
                        py = pbank("b")

                        def fy():
                            ins = None
                            for h in range(NH):
                                o = slice(h * HD, (h + 1) * HD)
                                nc.tensor.matmul(py[:, o], lhsT=krT[:, h, 1, :], rhs=Hb[:, h, :],
                                                 start=True, stop=False)
                                nc.tensor.matmul(py[:, o], lhsT=MA1[:, h, 1, :], rhs=Vh(h), start=False, stop=False)
                                ins = nc.tensor.matmul(py[:, o], lhsT=MA2[:, h, 1, :], rhs=nU[:, o], start=False, stop=True)
                            return ins
                        yield
                        S.op("pe", fy, [krT.b, Hb.b, MA1.b, MA2.b, V.b, nU.b], [py.b])
                        ph = pbank("b")

                        def fh():
                            ins = None
                            for h in range(NH):
                                o = slice(h * HD, (h + 1) * HD)
                                nc.tensor.matmul(ph[0:64, o], lhsT=bft["kh"][:, o], rhs=V[:, o], start=True, stop=False)
                                ins = nc.tensor.matmul(ph[0:64, o], lhsT=bft["bh"][:, o], rhs=nU[:, o], start=False, stop=True)
                            return ins
                        yield
                        S.op("pe", fh, [bft["kh"].b, bft["bh"].b, V.b, nU.b], [ph.b])
                        yield
                        S.op("dve", lambda: nc.vector.tensor_tensor(out=Ht[:], in0=ph[0:64, :].rearrange("p (h v) -> p h v", v=HD),
                                                                    in1=Hf[:], op=ALU.add), [ph.b, Hf.b], [Ht.b])
                        yield
                        S.op("dve", lambda: nc.vector.tensor_tensor(out=Hf[:], in0=Ht[:], in1=bc(gam[:].unsqueeze(2), [64, NH, HD]),
                                                                    op=ALU.mult), [Ht.b, gam.b], [Hf.b])
                        nxt_i = i - 1 if bwd else i + 1
                        bt = i if bwd else i + 1
                        if 0 <= nxt_i < NT and bt % BP == 0:
                            bidx = bt // BP - 1
                            yield
                            S.op("dve", lambda: nc.vector.tensor_scalar(out=Hf[:], in0=Hf[:], scalar1=keep[0:64, bidx:bidx + 1],
                                                                        scalar2=None, op0=ALU.mult), [Hf.b, keep.b], [Hf.b])
                        yield
                        S.op("act", lambda: nc.scalar.copy(out=Hb[:], in_=Hf[:]), [Hf.b], [Hb.b])

                        if not bwd:
                            yield
                            S.op("act", lambda: nc.scalar.copy(out=t["ysum"][:], in_=py[:]), [py.b], [t["ysum"].b])
                            yield
                            S.dma(YF[i * 128:(i + 1) * 128, :], t["ysum"][:], [t["ysum"].b], [YF_bs[i]])
                        else:
                            yield
                            S.op("dve", lambda: nc.vector.tensor_tensor(out=t["ysum"][:], in0=py[:], in1=t["yf"][:], op=ALU.add),
                                 [py.b, t["yf"].b], [t["ysum"].b])
                            yield
                            S.op("dve", lambda: nc.vector.tensor_reduce(out=gs1[:], in_=v3(t["ysum"][:]), axis=AX.X, op=ALU.add),
                                 [t["ysum"].b], [gs1.b])
                            yield
                            S.op("dve", lambda: nc.vector.tensor_scalar(out=gs1[:], in0=gs1[:], scalar1=-1.0 / HD, scalar2=None,
                                                                        op0=ALU.mult), [gs1.b], [gs1.b])
                            yield
                            S.op("dve", lambda: nc.vector.tensor_tensor(out=v3(t["ysum"][:]), in0=v3(t["ysum"][:]),
                                                                        in1=bc(gs1[:].unsqueeze(2), [128, NH, HD]), op=ALU.add),
                                 [t["ysum"].b, gs1.b], [t["ysum"].b])
                            yield
                            S.op("pool", lambda: nc.gpsimd.tensor_tensor(out=t["sq2"][:], in0=t["ysum"][:], in1=t["ysum"][:], op=ALU.mult),
                                 [t["ysum"].b], [t["sq2"].b])
                            yield
                            S.op("dve", lambda: nc.vector.tensor_reduce(out=gs2[:], in_=v3(t["sq2"][:]), axis=AX.X, op=ALU.add),
                                 [t["sq2"].b], [gs2.b])
                            yield
                            S.op("dve", lambda: nc.vector.tensor_scalar(out=gs2[:], in0=gs2[:], scalar1=1.0 / HD, scalar2=LNX_EPS,
                                                                        op0=ALU.mult, op1=ALU.add), [gs2.b], [gs2.b])
                            yield
                            S.op("act", lambda: nc.scalar.activation(out=grs[:], in_=gs2[:], func=AF.Sqrt), [gs2.b], [grs.b])
                            yield
                            S.op("dve", lambda: nc.vector.reciprocal(out=grs[:], in_=grs[:]), [grs.b], [grs.b])
                            yield
                            S.op("dve", lambda: nc.vector.tensor_tensor(out=v3(t["ysum"][:]), in0=v3(t["ysum"][:]),
                                                                        in1=bc(grs[:].unsqueeze(2), [128, NH, HD]), op=ALU.mult),
                                 [t["ysum"].b, grs.b], [t["ysum"].b])
                            yield
                            S.op("pool", lambda: nc.gpsimd.tensor_tensor(out=t["ysum"][:], in0=t["ysum"][:], in1=prm["lxg"][:], op=ALU.mult),
                                 [t["ysum"].b, prm["lxg"].b], [t["ysum"].b])
                            yield
                            S.op("pool", lambda: nc.gpsimd.tensor_tensor(out=t["ysum"][:], in0=t["ysum"][:], in1=prm["lxb"][:], op=ALU.add),
                                 [t["ysum"].b, prm["lxb"].b], [t["ysum"].b])
                            yield
                            S.op("dve", lambda: nc.vector.tensor_tensor(out=t["ysum"][:], in0=t["ysum"][:], in1=t["vbon"][:], op=ALU.add),
                                 [t["ysum"].b, t["vbon"].b], [t["ysum"].b])
                            yield
                            S.op("dve", lambda: nc.vector.tensor_tensor(out=bft["ob"][:], in0=t["ysum"][:], in1=t["sz"][:], op=ALU.mult),
                                 [t["ysum"].b, t["sz"].b], [bft["ob"].b])
                            pto = pbank("b")
                            ptvo = pto[:].bitcast(BF16)

                            def fo():
                                ins = None
                                for blk in range(4):
                                    ins = nc.tensor.transpose(out=ptvo[:, blk * 128:(blk + 1) * 128],
                                                              in_=bft["ob"][:, blk * 128:(blk + 1) * 128], identity=identb[:])
                                return ins
                            yield
                            S.op("pe", fo, [bft["ob"].b, identb.b], [pto.b])
                            yield
                            S.op("act", lambda: nc.scalar.copy(out=oT[:].rearrange("p b t -> p (b t)"), in_=ptvo[:, 0:512]),
                                 [pto.b], [oT.b])
                            yield
                            S.dma(YR[:, g * 4:(g + 1) * 4, i * 128:(i + 1) * 128], oT[:], [oT.b], [YR_bs[g][i]])

                    def interleave(ga, gb):
                        da = db = False
                        while not (da and db):
                            if not da:
                                try:
                                    next(ga)
                                except StopIteration:
                                    da = True
                            for _ in range(2):
                                if not db:
                                    try:
                                        next(gb)
                                    except StopIteration:
                                        db = True

                    prev = None
                    for oi, i in enumerate(order):
                        interleave(front(oi, i), back(*prev) if prev is not None else iter(()))
                        prev = (oi, i)
                    interleave(iter(()), back(*prev))
                S.barrier()

        with contextlib.ExitStack() as es:
            if not DBG.get("c", True):
                raise_skip = True
            else:
                raise_skip = False
            wc = sb(es, "wc", [128, 8, NCV], BF16)
            wo = sb(es, "wo", [128, 16, D], BF16)
            with contextlib.ExitStack() as es2:
                wst = [sb(es2, f"cwst{i}", [128, 8, 512], F32) for i in range(2)]
                for bi in range(NCV // 512 if DBG.get("cw", True) else 0):
                    w_ = wst[bi % 2]
                    S.dma(w_[:], w_in_d[:, bi * 512:(bi + 1) * 512].rearrange("(c p) n -> p c n", p=128), [], [w_.b])
                    if bi % 2 == 0:
                        S.op("act", lambda: nc.scalar.copy(out=wc[:, :, bi * 512:(bi + 1) * 512], in_=w_[:]), [w_.b], [wc.b])
                    else:
                        S.op("dve", lambda: nc.vector.tensor_copy(out=wc[:, :, bi * 512:(bi + 1) * 512], in_=w_[:]), [w_.b], [wc.b])
                for bi in range(4 if DBG.get("cw", True) else 0):
                    w_ = wst[bi % 2]
                    S.dma(w_[:, 0:4, :], wout_d[bi * 512:(bi + 1) * 512, 0:512].rearrange("(c p) n -> p c n", p=128), [], [w_.b])
                    S.dma(w_[:, 4:8, :], wout_d[bi * 512:(bi + 1) * 512, 512:1024].rearrange("(c p) n -> p c n", p=128), [], [w_.b])
                    S.op("act", lambda: nc.scalar.copy(out=wo[:, bi * 4:(bi + 1) * 4, 0:512], in_=w_[:, 0:4, :]), [w_.b], [wo.b])
                    S.op("dve", lambda: nc.vector.tensor_copy(out=wo[:, bi * 4:(bi + 1) * 4, 512:1024], in_=w_[:, 4:8, :]), [w_.b], [wo.b])
                S.barrier()
            cpar = sb(es, "cpar", [128, 8, 4], F32)
            for j in range(3 if DBG.get("cp", True) else 0):
                S.dma(cpar[:, :, j:j + 1], conv_w_d[j, :].rearrange("(c p o) -> p c o", p=128, o=1), [], [cpar.b])
            S.dma(cpar[:, :, 3:4], conv_b_d.rearrange("(c p o) -> p c o", p=128, o=1), [], [cpar.b])
            gbc = sb(es, "c_g", [128, D], F32); bbc = sb(es, "c_b", [128, D], F32)
            g2 = sb(es, "c_g2", [128, D], F32); b2 = sb(es, "c_b2", [128, D], F32)
            for tl, src in ((gbc, emb_g_d), (bbc, emb_b_d), (g2, lng_d), (b2, lnb_d)):
                bcast_load(tl, 0, src, D)
            xth = [sb(es, f"cxth{i}", [128, 8, 130], BF16) for i in range(2)]
            ymT = [sb(es, f"ymT{i}", [128, 16, 128], BF16) for i in range(2)]
            xt = [sb(es, f"c_x{i}", [128, D], F32) for i in range(2)]
            xn = sb(es, "c_xn", [128, D], F32); xg = sb(es, "c_xg", [128, D], F32); xh = sb(es, "c_xh", [128, D], F32)
            sres = sb(es, "c_s", [128, D], F32); on = sb(es, "c_on", [128, D], F32); og = sb(es, "c_og", [128, D], F32)
            yo = [sb(es, f"c_yo{i}", [128, D], F32) for i in range(2)]
            st6 = sb(es, "c_st", [128, 2, SD], F32); mv = sb(es, "c_mv", [128, AD], F32)
            rstd = sb(es, "c_rs", [128, 1], F32); nb_ = sb(es, "c_nb", [128, 1], F32)
            st6b = sb(es, "c_stb", [128, 2, SD], F32); mvb = sb(es, "c_mvb", [128, AD], F32)
            rstdb = sb(es, "c_rsb", [128, 1], F32); nbb = sb(es, "c_nbb", [128, 1], F32)
            hS = sb(es, "c_hS", [128, 130], F32); pp = sb(es, "c_pp", [128, 130], F32)
            qq = sb(es, "c_qq", [128, 128], F32); szc = sb(es, "c_sz", [128, 128], F32)
            if not raise_skip:
                load_xth(xth[0], 0)
            for i in range(0 if raise_skip else NT):
                p = i % 2
                xh_ = xth[p]
                if i + 1 < NT:
                    load_xth(xth[(i + 1) % 2], i + 1)
                S.dma(xt[p][:], xs[i * 128:(i + 1) * 128, :], [], [xt[p].b])
                S.dma(ymT[p][:, 8:16, :], YR[:, :, i * 128:(i + 1) * 128], [YR_bs[0][i], YR_bs[1][i]], [ymT[p].b])
                for cbk in range(8):
                    pa = pbank("f"); pb = pbank("f")

                    def fa():
                        ins = None
                        for qi, q in enumerate((0, 2)):
                            for kc in range(8):
                                ins = nc.tensor.matmul(pa[:, qi * 130:(qi + 1) * 130],
                                                       lhsT=wc[:, kc, q * 1024 + cbk * 128:q * 1024 + (cbk + 1) * 128],
                                                       rhs=xh_[:, kc, :], start=(kc == 0), stop=(kc == 7))
                        return ins

                    def fb():
                        ins = None
                        for qi, q in enumerate((1, 3)):
                            for kc in range(8):
                                ins = nc.tensor.matmul(pb[:, qi * 128:(qi + 1) * 128],
                                                       lhsT=wc[:, kc, q * 1024 + cbk * 128:q * 1024 + (cbk + 1) * 128],
                                                       rhs=xh_[:, kc, 1:129], start=(kc == 0), stop=(kc == 7))
                        return ins
                    S.op("pe", fa, [wc.b, xh_.b], [pa.b])
                    S.op("pe", fb, [wc.b, xh_.b], [pb.b])
                    S.op("act", lambda: nc.scalar.copy(out=hS[:], in_=pa[:, 0:130]), [pa.b], [hS.b])
                    S.op("dve", lambda: nc.vector.tensor_tensor(out=pp[:], in0=hS[:], in1=pa[:, 130:260], op=ALU.mult),
                         [hS.b, pa.b], [pp.b])
                    S.op("dve", lambda: nc.vector.tensor_scalar(out=qq[:], in0=pp[:, 1:129], scalar1=cpar[:, cbk, 1:2],
                                                                scalar2=cpar[:, cbk, 3:4], op0=ALU.mult, op1=ALU.add),
                         [pp.b, cpar.b], [qq.b])
                    S.op("dve", lambda: nc.vector.scalar_tensor_tensor(out=qq[:], in0=pp[:, 0:128], scalar=cpar[:, cbk, 0:1],
                                                                         in1=qq[:], op0=ALU.mult, op1=ALU.add),
                         [pp.b, cpar.b, qq.b], [qq.b])
                    S.op("dve", lambda: nc.vector.scalar_tensor_tensor(out=qq[:], in0=pp[:, 2:130], scalar=cpar[:, cbk, 2:3],
                                                                         in1=qq[:], op0=ALU.mult, op1=ALU.add),
                         [pp.b, cpar.b, qq.b], [qq.b])
                    S.op("act", lambda: nc.scalar.activation(out=szc[:], in_=pb[:, 128:256], func=AF.Silu), [pb.b], [szc.b])
                    S.op("dve", lambda: nc.vector.tensor_tensor(out=qq[:], in0=qq[:], in1=pb[:, 0:128], op=ALU.mult),
                         [qq.b, pb.b], [qq.b])
                    S.op("pool", lambda: nc.gpsimd.tensor_tensor(out=ymT[p][:, cbk, :], in0=qq[:], in1=szc[:], op=ALU.mult),
                         [qq.b, szc.b], [ymT[p].b])
                po = [pbank("b"), pbank("b")]
                for hf in range(2):
                    def fo():
                        ins = None
                        for mc in range(16):
                            ins = nc.tensor.matmul(po[hf][:], lhsT=ymT[p][:, mc, :], rhs=wo[:, mc, hf * 512:(hf + 1) * 512],
                                                   start=(mc == 0), stop=(mc == 15))
                        return ins
                    S.op("pe", fo, [ymT[p].b, wo.b], [po[hf].b])
                ln_stats("c", xt[p], mv, rstd, nb_, st6, LN_EPS)
                S.op("act", lambda: nc.scalar.activation(out=xn[:], in_=xt[p][:], func=AF.Identity, bias=nb_[:, 0:1],
                                                         scale=rstd[:, 0:1]), [xt[p].b, rstd.b, nb_.b], [xn.b])
                S.op("dve", lambda: nc.vector.tensor_tensor(out=xg[:], in0=xn[:], in1=gbc[:], op=ALU.mult), [xn.b, gbc.b], [xg.b])
                S.op("pool", lambda: nc.gpsimd.tensor_tensor(out=xh[:], in0=xg[:], in1=bbc[:], op=ALU.add), [xg.b, bbc.b], [xh.b])
                for hf in range(2):
                    o = slice(hf * 512, (hf + 1) * 512)
                    S.op("dve", lambda: nc.vector.scalar_tensor_tensor(out=sres[:, o], in0=xh[:, o], scalar=DN_ALPHA,
                                                                       in1=po[hf][:], op0=ALU.mult, op1=ALU.add),
                         [xh.b, po[hf].b], [sres.b])
                ln_stats("c2", sres, mvb, rstdb, nbb, st6b, LN_EPS)
                S.op("act", lambda: nc.scalar.activation(out=on[:], in_=sres[:], func=AF.Identity, bias=nbb[:, 0:1],
                                                         scale=rstdb[:, 0:1]), [sres.b, rstdb.b, nbb.b], [on.b])
                S.op("dve", lambda: nc.vector.tensor_tensor(out=og[:], in0=on[:], in1=g2[:], op=ALU.mult), [on.b, g2.b], [og.b])
                S.op("pool", lambda: nc.gpsimd.tensor_tensor(out=yo[p][:], in0=og[:], in1=b2[:], op=ALU.add), [og.b, b2.b], [yo[p].b])
                S.dma(ys[i * 128:(i + 1) * 128, :], yo[p][:], [yo[p].b], [ys_bs[i]])
            S.barrier()

        S.finish()
    return nc


CST_COLS = 128 + 2 * 896 + 1
DBG = {}
NAMES = {}


def make_consts():
    r = np.arange(128)[:, None]
    c = np.arange(128)[None, :]
    SU = (r < c).astype(np.float32); IU = (r <= c).astype(np.float32)
    SL = (r > c).astype(np.float32); IL = (r >= c).astype(np.float32)
    parts = [np.eye(128, dtype=np.float32)]
    for (S_, I_, St) in ((SU, IU, SL), (SL, IL, SU)):
        parts += [S_, I_, -S_, I_, -St, CDEC * I_, CDEC * S_]
    parts.append(np.full((128, 1), CDEC, np.float32))
    out = np.concatenate(parts, axis=1).astype(np.float32)
    assert out.shape[1] == CST_COLS
    return np.ascontiguousarray(out)


W_NAMES = ["emb_ln_g", "emb_ln_b", "w_in", "conv_w", "conv_b", "shift_mu", "w0", "w_up", "a0", "a_up",
           "k_k", "k_a", "r_k", "lnx_g", "lnx_b", "w_out", "ln_g", "ln_b"]


def weight_map(inp):
    m = {}
    for n in W_NAMES:
        a = np.asarray(inp[n], dtype=np.float32)
        if n not in ("emb_ln_g", "emb_ln_b"):
            a = a[0]
        if n == "r_k":
            a = a.reshape(1024)
        m[n] = np.ascontiguousarray(a)
    m["cst"] = make_consts()
    return m


_NC_CACHE = {}


def run_streams(streams, keeps, inp, NT, BP):
    key = (NT, BP)
    if key not in _NC_CACHE:
        _NC_CACHE[key] = build(NT, BP)
    nc = _NC_CACHE[key]
    wm = weight_map(inp)
    in_maps = []
    for s, k in zip(streams, keeps):
        d = dict(wm)
        d["xs"] = np.ascontiguousarray(s, dtype=np.float32)
        d["keep"] = np.ascontiguousarray(k, dtype=np.float32)
        in_maps.append(d)
    res = run_bass_kernel_spmd(nc, in_maps, core_ids=list(range(len(streams))))
    return [r["ys"] for r in res.results]


def kernel(**inp):
    xp = np.asarray(inp["x_prompt"], dtype=np.float32)
    xsm = np.asarray(inp["x_sample"], dtype=np.float32)
    NT, BP = 128, 16
    NB = NT // BP - 1
    ntok = NT * 128
    streams = [xsm[0], xsm[1], xp.reshape(ntok, D)]
    keeps = [np.ones((128, NB), np.float32), np.ones((128, NB), np.float32), np.zeros((128, NB), np.float32)]
    for _ in range(5):
        streams.append(np.zeros((ntok, D), np.float32))
        keeps.append(np.zeros((128, NB), np.float32))
    outs = run_streams(streams, keeps, inp, NT, BP)
    y_sample = np.stack([outs[0], outs[1]], axis=0).reshape(2, 16384, D)
    y_prompt = outs[2].reshape(8, 2048, D)
    return (y_prompt.astype(np.float32), y_sample.astype(np.float32))
```

```python
import contextlib
import numpy as np
import concourse.bass as bass
import concourse.mybir as mybir
from concourse.bass_utils import run_bass_kernel_spmd

F32 = mybir.dt.float32
BF16 = mybir.dt.bfloat16
AF = mybir.ActivationFunctionType
ALU = mybir.AluOpType
AX = mybir.AxisListType

D = 1024
NCV = 4096
NRW = 4352
NIN = NCV + NRW
HD = 64
LN_EPS = 1e-5
LNX_EPS = 64e-5
DN_ALPHA = 2.0 ** 0.25
CDEC = -float(np.exp(-0.5))
NG = 2
GC = 512
NH = 8
NLEV = 7


class Buf:
    __slots__ = ("name", "w", "rs", "excl")

    def __init__(self, name):
        self.name = name
        self.w = None
        self.rs = {}
        self.excl = False


class Sch:
    R = 4
    ND = 24

    def __init__(self, nc, es):
        self.nc = nc
        self.eng = {"pe": nc.tensor, "act": nc.scalar, "dve": nc.vector, "pool": nc.gpsimd, "sp": nc.sync}
        self.sems = {e: [es.enter_context(nc.semaphore(f"s_{e}{i}")) for i in range(self.R)]
                     for e in ("pe", "act", "dve", "pool")}
        self.cnt = {e: 0 for e in self.sems}
        self.known = {e: {} for e in self.eng}
        self.dsems = [es.enter_context(nc.semaphore(f"s_d{i}")) for i in range(self.ND)]
        self.dval = [0] * self.ND
        self.dnext = 0
        self.all_bufs = []

    def buf(self, name):
        b = Buf(name)
        self.all_bufs.append(b)
        return b

    def _wait(self, waiter, ev):
        key = (ev[0], ev[1])
        if ev[0] == "E" and ev[1] == waiter and waiter == "pe":
            return
        if self.known[waiter].get(key, -1) >= ev[2]:
            return
        if ev[0] == "E":
            sem = self.sems[ev[1]][ev[2] % self.R]
            val = ev[2] // self.R + 1
        else:
            sem = self.dsems[ev[1]]
            val = ev[2]
        self.eng[waiter].wait_ge(sem, val)
        self.known[waiter][key] = ev[2]

    def _deps(self, waiter, reads, writes):
        for b in reads:
            if b.w is not None:
                self._wait(waiter, b.w)
            if b.excl:
                for ev in b.rs.values():
                    if not (ev[0] == "E" and ev[1] == waiter):
                        self._wait(waiter, ev)
        for b in writes:
            if b.w is not None:
                self._wait(waiter, b.w)
            for ev in b.rs.values():
                self._wait(waiter, ev)

    def _commit(self, ev, reads, writes):
        for b in reads:
            b.rs[(ev[0], ev[1])] = ev
        for b in writes:
            b.w = ev
            b.rs = {}

    muted = False

    def stage(self, k):
        self.muted = k > DBG.get("stage", 99)

    def op(self, eng, fn, reads, writes):
        if self.muted:
            return
        self._deps(eng, reads, writes)
        ins = fn()
        idx = self.cnt[eng]
        self.cnt[eng] += 1
        ins.then_inc(self.sems[eng][idx % self.R], 1)
        self._commit(("E", eng, idx), reads, writes)

    def dma(self, out, in_, reads, writes, **kw):
        if self.muted:
            return
        k = self.dnext
        self.dnext = (self.dnext + 1) % self.ND
        if self.dval[k] > 0:
            self._wait("sp", ("D", k, self.dval[k]))
        self._deps("sp", reads, writes)
        ins = self.nc.sync.dma_start(out=out, in_=in_, **kw)
        self.dval[k] += 16
        ins.then_inc(self.dsems[k], 16)
        self._commit(("D", k, self.dval[k]), reads, writes)

    def barrier(self):
        self.muted = False
        for w in self.eng:
            for k in range(self.ND):
                if self.dval[k] > 0:
                    self._wait(w, ("D", k, self.dval[k]))
            for e in self.cnt:
                if self.cnt[e] > 0:
                    self._wait(w, ("E", e, self.cnt[e] - 1))

    def finish(self):
        for k in range(self.ND):
            if self.dval[k] > 0:
                self._wait("sp", ("D", k, self.dval[k]))
        for e in self.cnt:
            if self.cnt[e] > 0:
                self._wait("sp", ("E", e, self.cnt[e] - 1))


class TL:
    def __init__(self, sch, t, name):
        self.t = t
        self.b = sch.buf(name)

    def __getitem__(self, k):
        return self.t[k]


def bc(ap, shape):
    return ap.broadcast_to(list(shape))


def build(NT, BP):
    NTOK = NT * 128
    NB = max(1, NT // BP - 1)
    nc = bass.Bass("TRN2", target_bir_lowering=False)
    dt_in = lambda n, s: nc.dram_tensor(n, s, F32, kind="ExternalInput").ap()
    xs = dt_in("xs", [NTOK, D])
    keep_d = dt_in("keep", [128, NB])
    cst_d = dt_in("cst", [128, CST_COLS])
    emb_g_d = dt_in("emb_ln_g", [D]); emb_b_d = dt_in("emb_ln_b", [D])
    w_in_d = dt_in("w_in", [D, NIN])
    conv_w_d = dt_in("conv_w", [3, 1024]); conv_b_d = dt_in("conv_b", [1024])
    mu_d = dt_in("shift_mu", [NRW])
    w0_d = dt_in("w0", [2, 1024]); wup_d = dt_in("w_up", [2, 64, 1024])
    a0_d = dt_in("a0", [2, 1024]); aup_d = dt_in("a_up", [2, 64, 1024])
    kk_d = dt_in("k_k", [1024]); ka_d = dt_in("k_a", [1024]); rk_d = dt_in("r_k", [1024])
    lxg_d = dt_in("lnx_g", [1024]); lxb_d = dt_in("lnx_b", [1024])
    wout_d = dt_in("w_out", [2048, D])
    lng_d = dt_in("ln_g", [D]); lnb_d = dt_in("ln_b", [D])
    ys = nc.dram_tensor("ys", [NTOK, D], F32, kind="ExternalOutput").ap()
    XT = nc.dram_tensor("XT", [128, 8, NTOK + 2], BF16).ap()
    YF = nc.dram_tensor("YF", [NTOK, GC], F32).ap()
    YR = nc.dram_tensor("YR", [128, 8, NTOK], BF16).ap()

    with contextlib.ExitStack() as es0:
        es0.enter_context(nc.allow_non_contiguous_dma(reason="small strided param / halo loads"))
        S = Sch(nc, es0)
        XT_bs = [S.buf(f"XT{i}") for i in range(NT + 2)]
        YF_bs = [S.buf(f"YF{i}") for i in range(NT)]
        YR_bs = [[S.buf(f"YR{g}_{i}") for i in range(NT)] for g in range(NG)]
        ys_bs = [S.buf(f"ys{i}") for i in range(NT)]

        uid = [0]

        def sb(es, name, shape, dtype):
            uid[0] += 1
            nm = f"sb{uid[0]}_{name}"
            NAMES[name] = nm
            return TL(S, es.enter_context(nc.sbuf_tensor(nm, list(shape), dtype)), nm)

        PS = [TL(S, es0.enter_context(nc.psum_tensor(f"ps{i}", [128, 512], F32)), f"ps{i}") for i in range(8)]
        for p_ in PS:
            p_.b.excl = True
        prot = {"f": [0, [0, 1, 2, 3]], "b": [0, [4, 5, 6, 7]]}

        def pbank(kind):
            st = prot[kind]
            p = PS[st[1][st[0] % len(st[1])]]
            st[0] += 1
            return p

        cst = sb(es0, "cst", [128, CST_COLS], F32)
        S.dma(cst[:], cst_d, [], [cst.b])
        identb = sb(es0, "identb", [128, 128], BF16)
        S.op("dve", lambda: nc.vector.tensor_copy(out=identb[:], in_=cst[:, 0:128]), [cst.b], [identb.b])
        keep = sb(es0, "keep", [128, NB], F32)
        S.dma(keep[:], keep_d, [], [keep.b])
        zpad = sb(es0, "zpad", [128, 8, 1], BF16)
        S.op("pool", lambda: nc.gpsimd.memset(zpad[:], 0.0), [], [zpad.b])
        S.dma(XT[:, :, 0:1], zpad[:], [zpad.b], [XT_bs[0]])
        S.dma(XT[:, :, NTOK + 1:NTOK + 2], zpad[:], [zpad.b], [XT_bs[NT + 1]])

        onesf = sb(es0, "onesf", [1, 128], F32)
        S.op("pool", lambda: nc.gpsimd.memset(onesf[:], 1.0), [], [onesf.b])
        rowbuf = sb(es0, "rowbuf", [1, 1024], F32)

        def bcast_load(dst, dcol, src1d, ncol):
            S.dma(rowbuf[0:1, 0:ncol], src1d.rearrange("(o n) -> o n", o=1), [], [rowbuf.b])
            for c_ in range(0, ncol, 512):
                n_ = min(512, ncol - c_)
                pb_ = pbank("f")
                S.op("pe", lambda: nc.tensor.matmul(pb_[:, 0:n_], lhsT=onesf[0:1, :], rhs=rowbuf[0:1, c_:c_ + n_],
                                                    start=True, stop=True), [onesf.b, rowbuf.b], [pb_.b])
                S.op("act", lambda: nc.scalar.copy(out=dst[:, dcol + c_:dcol + c_ + n_], in_=pb_[:, 0:n_]), [pb_.b], [dst.b])

        def ln_stats(es_tag, src, mv, rstd, nb_, st6, eps):
            S.op("dve", lambda: nc.vector.bn_stats(out=st6[:, 0, :], in_=src[:, 0:512]), [src.b], [st6.b])
            S.op("dve", lambda: nc.vector.bn_stats(out=st6[:, 1, :], in_=src[:, 512:1024]), [src.b], [st6.b])
            S.op("dve", lambda: nc.vector.bn_aggr(out=mv[:], in_=st6[:]), [st6.b], [mv.b])
            S.op("act", lambda: nc.scalar.activation(out=rstd[:], in_=mv[:, 1:2], func=AF.Sqrt, bias=eps, scale=1.0),
                 [mv.b], [rstd.b])
            S.op("dve", lambda: nc.vector.reciprocal(out=rstd[:], in_=rstd[:]), [rstd.b], [rstd.b])
            S.op("dve", lambda: nc.vector.scalar_tensor_tensor(out=nb_[:], in0=mv[:, 0:1], scalar=-1.0, in1=rstd[:],
                                                               op0=ALU.mult, op1=ALU.mult), [mv.b, rstd.b], [nb_.b])

        SD = int(nc.vector.BN_STATS_DIM)
        AD = int(nc.vector.BN_AGGR_DIM)

        with contextlib.ExitStack() as es:
            gbc = sb(es, "p1_g", [128, D], F32); bbc = sb(es, "p1_b", [128, D], F32)
            bcast_load(gbc, 0, emb_g_d, D)
            bcast_load(bbc, 0, emb_b_d, D)
            xt = [sb(es, f"p1_x{i}", [128, D], F32) for i in range(2)]
            xn = [sb(es, f"p1_xn{i}", [128, D], F32) for i in range(2)]
            xg = [sb(es, f"p1_xg{i}", [128, D], F32) for i in range(2)]
            xh = [sb(es, f"p1_xh{i}", [128, D], BF16) for i in range(2)]
            xT = [sb(es, f"p1_xT{i}", [128, 8, 128], BF16) for i in range(2)]
            st6 = [sb(es, f"p1_st{i}", [128, 2, SD], F32) for i in range(2)]
            mv = [sb(es, f"p1_mv{i}", [128, AD], F32) for i in range(2)]
            rstd = [sb(es, f"p1_rs{i}", [128, 1], F32) for i in range(2)]
            nb_ = [sb(es, f"p1_nb{i}", [128, 1], F32) for i in range(2)]
            for i in range(NT if DBG.get("p1", True) else 0):
                p = i % 2
                S.dma(xt[p][:], xs[i * 128:(i + 1) * 128, :], [], [xt[p].b])
                ln_stats("p1", xt[p], mv[p], rstd[p], nb_[p], st6[p], LN_EPS)
                S.op("act", lambda: nc.scalar.activation(out=xn[p][:], in_=xt[p][:], func=AF.Identity,
                                                         bias=nb_[p][:, 0:1], scale=rstd[p][:, 0:1]),
                     [xt[p].b, rstd[p].b, nb_[p].b], [xn[p].b])
                S.op("dve", lambda: nc.vector.tensor_tensor(out=xg[p][:], in0=xn[p][:], in1=gbc[:], op=ALU.mult),
                     [xn[p].b, gbc.b], [xg[p].b])
                S.op("pool", lambda: nc.gpsimd.tensor_tensor(out=xh[p][:], in0=xg[p][:], in1=bbc[:], op=ALU.add),
                     [xg[p].b, bbc.b], [xh[p].b])
                pt = pbank("f")
                ptv = pt[:].bitcast(BF16)

                def tr():
                    ins = None
                    for c in range(8):
                        ins = nc.tensor.transpose(out=ptv[:, c * 128:(c + 1) * 128], in_=xh[p][:, c * 128:(c + 1) * 128],
                                                  identity=identb[:])
                    return ins
                S.op("pe", tr, [xh[p].b, identb.b], [pt.b])
                S.op("act", lambda: nc.scalar.copy(out=xT[p][:].rearrange("p c t -> p (c t)"), in_=ptv), [pt.b], [xT[p].b])
                S.dma(XT[:, :, 1 + i * 128:1 + (i + 1) * 128], xT[p][:], [xT[p].b], [XT_bs[i + 1]])
            S.barrier()

        def load_xth(xth, i):
            S.dma(xth[:], XT[:, :, i * 128:i * 128 + 130], [XT_bs[i], XT_bs[i + 1], XT_bs[i + 2]], [xth.b])
            if i % BP == 0 and i > 0:
                bidx = i // BP - 1
                S.op("dve", lambda: nc.vector.tensor_scalar(out=xth[:, :, 0:1], in0=xth[:, :, 0:1],
                                                            scalar1=keep[:, bidx:bidx + 1], scalar2=None, op0=ALU.mult),
                     [xth.b, keep.b], [xth.b])
            if (i + 1) % BP == 0 and i + 1 < NT:
                bidx = (i + 1) // BP - 1
                S.op("dve", lambda: nc.vector.tensor_scalar(out=xth[:, :, 129:130], in0=xth[:, :, 129:130],
                                                            scalar1=keep[:, bidx:bidx + 1], scalar2=None, op0=ALU.mult),
                     [xth.b, keep.b], [xth.b])

        for g in range(NG if DBG.get("rwkv", True) else 0):
            with contextlib.ExitStack() as es:
                c0 = g * GC
                wq = sb(es, "wq", [128, 16, 4 * GC], BF16)
                wl = sb(es, "wl", [128, 16, 256], BF16)
                with contextlib.ExitStack() as es2:
                    wst = [sb(es2, f"wst{i}", [128, 8, 512], F32) for i in range(2)]
                    mub = sb(es2, "mub", [128, 512], F32)
                    mua = sb(es2, "mua", [128, 512], F32)
                    mubb = sb(es2, "mubb", [128, 512], F32)
                    blocks = [(NCV + q * 1024 + c0, 512, wq, q * GC) for q in range(4)] + [(NCV + 4096, 256, wl, 0)]
                    for bi, (col, ncol, dst, dcol) in enumerate(blocks):
                        w_ = wst[bi % 2]
                        S.dma(w_[:, :, 0:ncol], w_in_d[:, col:col + ncol].rearrange("(c p) n -> p c n", p=128), [], [w_.b])
                        bcast_load(mub, 0, mu_d[col - NCV:col - NCV + ncol], ncol)
                        S.op("dve", lambda: nc.vector.tensor_scalar(out=mua[:, 0:ncol], in0=mub[:, 0:ncol], scalar1=-1.0,
                                                                    scalar2=1.0, op0=ALU.mult, op1=ALU.add),
                             [mub.b], [mua.b])
                        S.op("dve", lambda: nc.vector.tensor_scalar(out=mubb[:, 0:ncol], in0=mub[:, 0:ncol], scalar1=0.5,
                                                                    scalar2=None, op0=ALU.mult), [mub.b], [mubb.b])
                        S.op("dve", lambda: nc.vector.tensor_tensor(
                            out=dst[:, 0:8, dcol:dcol + ncol], in0=w_[:, :, 0:ncol],
                            in1=bc(mua[:, 0:ncol].unsqueeze(1), [128, 8, ncol]), op=ALU.mult), [w_.b, mua.b], [dst.b])
                        S.op("pool", lambda: nc.gpsimd.tensor_tensor(
                            out=dst[:, 8:16, dcol:dcol + ncol], in0=w_[:, :, 0:ncol],
                            in1=bc(mubb[:, 0:ncol].unsqueeze(1), [128, 8, ncol]), op=ALU.mult), [w_.b, mubb.b], [dst.b])
                    S.barrier()
                upw = sb(es, "upw", [64, 2, 2, GC], BF16)
                b0h = sb(es, "b0h", [1, 2, 2, GC], BF16); b0l = sb(es, "b0l", [1, 2, 2, GC], BF16)
                onesr = sb(es, "onesr", [1, 128], BF16)
                S.op("pool", lambda: nc.gpsimd.memset(onesr[:], 1.0), [], [onesr.b])
                with contextlib.ExitStack() as es2:
                    upf = sb(es2, "upf", [64, 2, 2, GC], F32)
                    b0f = sb(es2, "b0f", [1, 2, 2, GC], F32); b0t = sb(es2, "b0t", [1, 2, 2, GC], F32)
                    for wi, (ud, bd) in enumerate(((wup_d, w0_d), (aup_d, a0_d))):
                        for d in range(2):
                            S.dma(upf[:, wi, d, :], ud[d, :, c0:c0 + GC], [], [upf.b])
                            S.dma(b0f[:, wi, d, :], bd[d:d + 1, c0:c0 + GC], [], [b0f.b])
                    S.op("dve", lambda: nc.vector.tensor_copy(out=upw[:], in_=upf[:]), [upf.b], [upw.b])
                    S.op("dve", lambda: nc.vector.tensor_copy(out=b0h[:], in_=b0f[:]), [b0f.b], [b0h.b])
                    S.op("dve", lambda: nc.vector.tensor_tensor(out=b0t[:], in0=b0f[:], in1=b0h[:], op=ALU.subtract),
                         [b0f.b, b0h.b], [b0t.b])
                    S.op("dve", lambda: nc.vector.tensor_copy(out=b0l[:], in_=b0t[:]), [b0t.b], [b0l.b])
                    S.barrier()
                prm = {}
                for nm, src in (("kk", kk_d), ("ka", ka_d), ("rk", rk_d), ("lxg", lxg_d), ("lxb", lxb_d)):
                    prm[nm] = sb(es, "prm_" + nm, [128, GC], F32)
                    bcast_load(prm[nm], 0, src[c0:c0 + GC], GC)

                xth = [sb(es, f"xth{i}", [128, 8, 130], BF16) for i in range(2)]
                xst = sb(es, "xst", [128, 8, 128], BF16)
                f32t = {n: sb(es, "w_" + n, [128, GC], F32) for n in
                        ("sg", "a", "a0", "e1", "e2", "e3", "kkr", "t1", "kd", "bb", "kd0", "vbon",
                         "sz", "yf", "ysum", "sq2")}
                bft_ = {n: sb(es, "h_" + n, [128, GC], BF16) for n in ("V", "kh", "bh", "kkt", "rt", "Zs", "nU", "ob")}
                f32p = [dict(f32t), dict(f32t)]
                for n_ in ("vbon", "sz", "yf"):
                    f32p[1][n_] = sb(es, "w1_" + n_, [128, GC], F32)
                bftp = [dict(bft_), dict(bft_)]
                for n_ in ("V", "kh", "bh"):
                    bftp[1][n_] = sb(es, "h1_" + n_, [128, GC], BF16)
                tl_ = sb(es, "tl_", [64, 3, 128], BF16)
                kT2 = [sb(es, f"kT{i}", [64, NH, 128], BF16) for i in range(2)]
                bT2 = [sb(es, f"bT{i}", [64, NH, 128], BF16) for i in range(2)]
                krT2 = [sb(es, f"krT{i}", [64, NH, 2, 128], BF16) for i in range(2)]
                oT = sb(es, "oT", [128, 4, 128], BF16)
                MA1 = sb(es, "MA1", [128, NH, 2, 128], BF16)
                MA2 = sb(es, "MA2", [128, NH, 2, 128], BF16)
                Xs = [[sb(es, f"Xs{i}{q}", [128, 4, 128], BF16) for q in range(2)] for i in range(2)]
                Qs = [[sb(es, f"Qs{i}{q}", [128, 4, 128], BF16) for q in range(2)] for i in range(2)]
                Xtq = [[sb(es, f"Xt{i}{q}", [128, 4, 128], BF16) for q in range(2)] for i in range(2)]
                TTq = [sb(es, f"TT{q}", [128, 4, 128], BF16) for q in range(2)]
                Hf = sb(es, "Hf", [64, NH, 64], F32); Hb = sb(es, "Hb", [64, NH, 64], BF16)
                Ht = sb(es, "Ht", [64, NH, 64], F32)
                gam2 = [sb(es, f"gam{i}", [64, NH], F32) for i in range(2)]
                ss = sb(es, "ss", [128, NH], F32); rn = sb(es, "rn", [128, NH], F32)
                bs = sb(es, "bs", [128, NH], F32)
                gs1 = sb(es, "gs1", [128, NH], F32); gs2 = sb(es, "gs2", [128, NH], F32)
                grs = sb(es, "grs", [128, NH], F32)

                for dr in range(2):
                    bwd = dr == 1
                    cb = 128 + dr * 896
                    m1 = cst[:, cb:cb + 256]; m2 = cst[:, cb + 256:cb + 512]; m3 = cst[:, cb + 512:cb + 640]
                    tinc = cst[:, cb + 640:cb + 768]; texc = cst[:, cb + 768:cb + 896]
                    ccol = cst[:, CST_COLS - 1:CST_COLS]
                    S.op("dve", lambda: nc.vector.memset(Hf[:], 0.0), [], [Hf.b])
                    S.op("dve", lambda: nc.vector.memset(Hb[:], 0.0), [], [Hb.b])
                    order = list(range(NT - 1, -1, -1)) if bwd else list(range(NT))
                    load_xth(xth[0], order[0])
                    v3 = lambda ap: ap.rearrange("p (h c) -> p h c", c=HD)

                    def front(oi, i):
                        par = oi % 2
                        t = f32p[par]; bft = bftp[par]
                        kT = kT2[par]; bT = bT2[par]; krT = krT2[par]; gam = gam2[par]
                        V = bft["V"]; nU = bft["nU"]
                        xh_ = xth[oi % 2]
                        if oi + 1 < NT:
                            load_xth(xth[(oi + 1) % 2], order[oi + 1])
                        if bwd:
                            yield
                            S.dma(t["yf"][:], YF[i * 128:(i + 1) * 128, :], [YF_bs[i]], [t["yf"].b])
                        yield
                        S.op("pool", lambda: nc.gpsimd.tensor_tensor(out=xst[:], in0=xh_[:, :, 0:128], in1=xh_[:, :, 2:130],
                                                                      op=ALU.add), [xh_.b], [xst.b])

                        def proj_fm(pt_ap, wcol, ncol):
                            ins = None
                            for kc in range(16):
                                rhs = xh_[:, kc, 1:129] if kc < 8 else xst[:, kc - 8, :]
                                ins = nc.tensor.matmul(pt_ap, lhsT=wl[:, kc, wcol:wcol + ncol], rhs=rhs,
                                                       start=(kc == 0), stop=(kc == 15))
                            return ins

                        def proj_tm(pt_ap, q):
                            ins = None
                            for kc in range(16):
                                lhsT = xh_[:, kc, 1:129] if kc < 8 else xst[:, kc - 8, :]
                                ins = nc.tensor.matmul(pt_ap, lhsT=lhsT, rhs=wq[:, kc, q * GC:(q + 1) * GC],
                                                       start=(kc == 0), stop=(kc == 15))
                            return ins

                        pc = pbank("f")
                        ncode = 3 if bwd else 2

                        def codes():
                            ins = proj_fm(pc[0:64, 0:128], dr * 64, 64)
                            ins = proj_fm(pc[0:64, 128:256], 128 + dr * 64, 64)
                            if bwd:
                                ins = proj_fm(pc[0:64, 256:384], 128, 64)
                            return ins
                        yield
                        S.op("pe", codes, [xh_.b, xst.b, wl.b], [pc.b])
                        yield
                        S.op("act", lambda: nc.scalar.activation(out=tl_[:, 0, :], in_=pc[0:64, 0:128], func=AF.Tanh),
                             [pc.b], [tl_.b])
                        yield
                        S.op("act", lambda: nc.scalar.copy(out=tl_[:, 1:ncode, :].rearrange("p a t -> p (a t)"),
                                                           in_=pc[0:64, 128:128 * ncode]), [pc.b], [tl_.b])

                        def lowrank(pt, ci, wi, d):
                            def f():
                                nc.tensor.matmul(pt[:], lhsT=tl_[:, ci, :], rhs=upw[:, wi, d, :], start=True, stop=False)
                                nc.tensor.matmul(pt[:], lhsT=onesr[:], rhs=b0h[:, wi, d, :], start=False, stop=False)
                                return nc.tensor.matmul(pt[:], lhsT=onesr[:], rhs=b0l[:, wi, d, :], start=False, stop=True)
                            S.op("pe", f, [tl_.b, upw.b, onesr.b, b0h.b, b0l.b], [pt.b])
                        yield
                        pd_ = pbank("f"); lowrank(pd_, 0, 0, dr)
                        yield
                        S.op("act", lambda: nc.scalar.activation(out=t["sg"][:], in_=pd_[:], func=AF.Sigmoid),
                             [pd_.b], [t["sg"].b])
                        yield
                        pa_ = pbank("f"); lowrank(pa_, 1, 1, dr)
                        yield
                        S.op("act", lambda: nc.scalar.activation(out=t["a"][:], in_=pa_[:], func=AF.Sigmoid),
                             [pa_.b], [t["a"].b])
                        if bwd:
                            yield
                            pa0 = pbank("f"); lowrank(pa0, 2, 1, 0)
                            yield
                            S.op("act", lambda: nc.scalar.activation(out=t["a0"][:], in_=pa0[:], func=AF.Sigmoid),
                                 [pa0.b], [t["a0"].b])
                        sg = t["sg"]
                        pcum = pbank("f")
                        yield
                        S.op("pe", lambda: nc.tensor.matmul(pcum[:], lhsT=tinc, rhs=sg[:], start=True, stop=True),
                             [cst.b, sg.b], [pcum.b])
                        yield
                        S.op("act", lambda: nc.scalar.activation(out=t["e1"][:], in_=pcum[:], func=AF.Exp, scale=-1.0),
                             [pcum.b], [t["e1"].b])
                        yield
                        S.op("act", lambda: nc.scalar.activation(out=t["e3"][:], in_=pcum[:], func=AF.Exp),
                             [pcum.b], [t["e3"].b])
                        pcx = pbank("f")

                        yield
                        S.op("pe", lambda: nc.tensor.matmul(pcx[:], lhsT=texc, rhs=sg[:], start=True, stop=True),
                             [cst.b, sg.b], [pcx.b])
                        yield
                        S.op("act", lambda: nc.scalar.activation(out=t["e2"][:], in_=pcx[:], func=AF.Exp),
                             [pcx.b], [t["e2"].b])
                        pgm = pbank("f")

                        def gsum():
                            ins = None
                            for h in range(NH):
                                ins = nc.tensor.matmul(pgm[0:64, h:h + 1], lhsT=sg[:, h * HD:(h + 1) * HD], rhs=ccol,
                                                       start=True, stop=True)
                            return ins
                        yield
                        S.op("pe", gsum, [sg.b, cst.b], [pgm.b])
                        yield
                        S.op("act", lambda: nc.scalar.activation(out=gam[:], in_=pgm[0:64, 0:NH], func=AF.Exp), [pgm.b], [gam.b])

                        pr = pbank("f"); S.op("pe", lambda: proj_tm(pr[:], 0), [xh_.b, xst.b, wq.b], [pr.b])
                        pk = pbank("f"); S.op("pe", lambda: proj_tm(pk[:], 1), [xh_.b, xst.b, wq.b], [pk.b])
                        pv = pbank("f"); S.op("pe", lambda: proj_tm(pv[:], 2), [xh_.b, xst.b, wq.b], [pv.b])
                        V = bft["V"]
                        yield
                        S.op("act", lambda: nc.scalar.copy(out=V[:], in_=pv[:]), [pv.b], [V.b])
                        v3 = lambda ap: ap.rearrange("p (h c) -> p h c", c=HD)
                        yield
                        S.op("dve", lambda: nc.vector.tensor_tensor(out=t["kkr"][:], in0=pk[:], in1=prm["kk"][:], op=ALU.mult),
                             [pk.b, prm["kk"].b], [t["kkr"].b])
                        yield
                        S.op("pool", lambda: nc.gpsimd.tensor_tensor(out=t["t1"][:], in0=t["kkr"][:], in1=t["kkr"][:], op=ALU.mult),
                             [t["kkr"].b], [t["t1"].b])
                        yield
                        S.op("dve", lambda: nc.vector.tensor_reduce(out=ss[:], in_=v3(t["t1"][:]), axis=AX.X, op=ALU.add),
                             [t["t1"].b], [ss.b])
                        yield
                        S.op("act", lambda: nc.scalar.activation(out=rn[:], in_=ss[:], func=AF.Sqrt), [ss.b], [rn.b])
                        yield
                        S.op("dve", lambda: nc.vector.tensor_scalar(out=rn[:], in0=rn[:], scalar1=1e-12, scalar2=None,
                                                                    op0=ALU.max), [rn.b], [rn.b])
                        yield
                        S.op("dve", lambda: nc.vector.reciprocal(out=rn[:], in_=rn[:]), [rn.b], [rn.b])
                        yield
                        S.op("dve", lambda: nc.vector.tensor_tensor(out=v3(t["kkr"][:]), in0=v3(t["kkr"][:]),
                                                                    in1=bc(rn[:].unsqueeze(2), [128, NH, HD]), op=ALU.mult),
                             [t["kkr"].b, rn.b], [t["kkr"].b])
                        yield
                        S.op("dve", lambda: nc.vector.scalar_tensor_tensor(out=t["t1"][:], in0=t["a"][:], scalar=-1.0,
                                                                             in1=prm["ka"][:], op0=ALU.add, op1=ALU.mult),
                             [t["a"].b, prm["ka"].b], [t["t1"].b])
                        yield
                        S.op("dve", lambda: nc.vector.scalar_tensor_tensor(out=t["kd"][:], in0=t["t1"][:], scalar=1.0,
                                                                           in1=pk[:], op0=ALU.add, op1=ALU.mult),
                             [t["t1"].b, pk.b], [t["kd"].b])
                        yield
                        S.op("pool", lambda: nc.gpsimd.tensor_tensor(out=t["bb"][:], in0=t["kkr"][:], in1=t["a"][:], op=ALU.mult),
                             [t["kkr"].b, t["a"].b], [t["bb"].b])
                        yield
                        S.op("dve", lambda: nc.vector.tensor_tensor(out=bft["kh"][:], in0=t["kd"][:], in1=t["e1"][:], op=ALU.mult),
                             [t["kd"].b, t["e1"].b], [bft["kh"].b])
                        yield
                        S.op("pool", lambda: nc.gpsimd.tensor_tensor(out=bft["bh"][:], in0=t["bb"][:], in1=t["e1"][:], op=ALU.mult),
                             [t["bb"].b, t["e1"].b], [bft["bh"].b])
                        yield
                        S.op("pool", lambda: nc.gpsimd.tensor_tensor(out=bft["kkt"][:], in0=t["kkr"][:], in1=t["e2"][:], op=ALU.mult),
                             [t["kkr"].b, t["e2"].b], [bft["kkt"].b])
                        yield
                        S.op("dve", lambda: nc.vector.tensor_tensor(out=bft["rt"][:], in0=pr[:], in1=t["e3"][:], op=ALU.mult),
                             [pr.b, t["e3"].b], [bft["rt"].b])
                        if bwd:
                            yield
                            S.op("dve", lambda: nc.vector.scalar_tensor_tensor(out=t["t1"][:], in0=t["a0"][:], scalar=-1.0,
                                                                                 in1=prm["ka"][:], op0=ALU.add, op1=ALU.mult),
                                 [t["a0"].b, prm["ka"].b], [t["t1"].b])
                            yield
                            S.op("dve", lambda: nc.vector.scalar_tensor_tensor(out=t["kd0"][:], in0=t["t1"][:], scalar=1.0,
                                                                               in1=pk[:], op0=ALU.add, op1=ALU.mult),
                                 [t["t1"].b, pk.b], [t["kd0"].b])
                            yield
                            S.op("pool", lambda: nc.gpsimd.tensor_tensor(out=t["kd0"][:], in0=t["kd0"][:], in1=t["kd"][:], op=ALU.add),
                                 [t["kd0"].b, t["kd"].b], [t["kd0"].b])
                            yield
                            S.op("pool", lambda: nc.gpsimd.tensor_tensor(out=t["kd0"][:], in0=t["kd0"][:], in1=prm["rk"][:], op=ALU.mult),
                                 [t["kd0"].b, prm["rk"].b], [t["kd0"].b])
                            yield
                            S.op("dve", lambda: nc.vector.tensor_tensor(out=t["kd0"][:], in0=pr[:], in1=t["kd0"][:], op=ALU.mult),
                                 [pr.b, t["kd0"].b], [t["kd0"].b])
                            yield
                            S.op("dve", lambda: nc.vector.tensor_reduce(out=bs[:], in_=v3(t["kd0"][:]), axis=AX.X, op=ALU.add),
                                 [t["kd0"].b], [bs.b])
                            yield
                            S.op("dve", lambda: nc.vector.scalar_tensor_tensor(
                                out=v3(t["vbon"][:]), in0=v3(pv[:]), scalar=0.5, in1=bc(bs[:].unsqueeze(2), [128, NH, HD]),
                                op0=ALU.mult, op1=ALU.mult), [pv.b, bs.b], [t["vbon"].b])
                            pz = pbank("f"); S.op("pe", lambda: proj_tm(pz[:], 3), [xh_.b, xst.b, wq.b], [pz.b])
                            yield
                            S.op("act", lambda: nc.scalar.activation(out=t["sz"][:], in_=pz[:], func=AF.Silu), [pz.b], [t["sz"].b])

                        def trans8(src_, dst_ap, eng):
                            pt = pbank("f")
                            ptv = pt[:].bitcast(BF16)

                            def f():
                                ins = None
                                for h in range(NH):
                                    ins = nc.tensor.transpose(out=ptv[0:64, h * 128:(h + 1) * 128], in_=src_[:, h * HD:(h + 1) * HD],
                                                              identity=identb[:])
                                return ins
                            S.op("pe", f, [src_.b, identb.b], [pt.b])
                            src_v = ptv[0:64, :].rearrange("p (h t) -> p h t", t=128)
                            if eng == "act":
                                S.op("act", lambda: nc.scalar.copy(out=dst_ap[0], in_=src_v), [pt.b], [dst_ap[1]])
                            else:
                                S.op("dve", lambda: nc.vector.tensor_copy(out=dst_ap[0], in_=src_v), [pt.b], [dst_ap[1]])
                        yield
                        trans8(bft["kh"], (kT[:], kT.b), "act")
                        yield
                        trans8(bft["bh"], (bT[:], bT.b), "dve")
                        yield
                        trans8(bft["kkt"], (krT[:, :, 0, :], krT.b), "act")
                        yield
                        trans8(bft["rt"], (krT[:, :, 1, :], krT.b), "dve")

                    def back(oi, i):
                        par = oi % 2
                        t = f32p[par]; bft = bftp[par]
                        kT = kT2[par]; bT = bT2[par]; krT = krT2[par]; gam = gam2[par]
                        V = bft["V"]; nU = bft["nU"]
                        for hp in range(NH // 2):
                            for which, lT, dst, msk in ((0, kT, MA1, m1), (1, bT, MA2, m2)):
                                pm = pbank("b")

                                def f():
                                    ins = None
                                    for hh in range(2):
                                        h = hp * 2 + hh
                                        ins = nc.tensor.matmul(pm[:, hh * 256:(hh + 1) * 256], lhsT=lT[:, h, :],
                                                               rhs=krT[:, h, :, :].rearrange("p a t -> p (a t)"),
                                                               start=True, stop=True)
                                    return ins
                                yield
                                S.op("pe", f, [lT.b, krT.b], [pm.b])
                                yield
                                S.op("dve", lambda: nc.vector.tensor_tensor(
                                    out=dst[:, hp * 2:hp * 2 + 2, :, :].rearrange("p h a t -> p h (a t)"),
                                    in0=pm[:].rearrange("p (h x) -> p h x", h=2),
                                    in1=bc(msk.unsqueeze(1), [128, 2, 256]), op=ALU.mult), [pm.b, cst.b], [dst.b])
                        for hq in range(NH // 4):
                            pm = pbank("b")

                            def f():
                                ins = None
                                for hh in range(4):
                                    h = hq * 4 + hh
                                    ins = nc.tensor.matmul(pm[:, hh * 128:(hh + 1) * 128], lhsT=krT[:, h, 0, :],
                                                           rhs=bT[:, h, :], start=True, stop=True)
                                return ins
                            yield
                            S.op("pe", f, [krT.b, bT.b], [pm.b])
                            yield
                            S.op("dve", lambda: nc.vector.tensor_tensor(
                                out=Xtq[0][hq][:], in0=pm[:].rearrange("p (h x) -> p h x", h=4),
                                in1=bc(m3.unsqueeze(1), [128, 4, 128]), op=ALU.mult), [pm.b, cst.b], [Xtq[0][hq].b])

                        def quad_mm(pm, lhs_of, rhs_of):
                            def f():
                                ins = None
                                for hh in range(4):
                                    ins = nc.tensor.matmul(pm[:, hh * 128:(hh + 1) * 128], lhsT=lhs_of(hh), rhs=rhs_of(hh),
                                                           start=True, stop=True)
                                return ins
                            return f

                        def quad_copy(eng, dst, pm):
                            src_ = pm[:].rearrange("p (h x) -> p h x", h=4)
                            if eng == "act":
                                S.op("act", lambda: nc.scalar.copy(out=dst[:], in_=src_), [pm.b], [dst.b])
                            else:
                                S.op("dve", lambda: nc.vector.tensor_copy(out=dst[:], in_=src_), [pm.b], [dst.b])
                        for hq in range(2):
                            hs = slice(hq * 4, hq * 4 + 4)
                            x0 = lambda hh: MA2[:, hq * 4 + hh, 0, :]
                            xt0 = lambda hh: Xtq[0][hq][:, hh, :]
                            yield
                            S.op("pool", lambda: nc.gpsimd.tensor_tensor(
                                out=Qs[0][hq][:], in0=MA2[:, hs, 0, :], in1=bc(identb[:].unsqueeze(1), [128, 4, 128]),
                                op=ALU.add), [MA2.b, identb.b], [Qs[0][hq].b])
                            pm = pbank("b")
                            yield
                            S.op("pe", quad_mm(pm, xt0, x0), [Xtq[0][hq].b, MA2.b], [pm.b])
                            yield
                            quad_copy("act", Xs[1][hq], pm)
                            pm2 = pbank("b")
                            yield
                            S.op("pe", quad_mm(pm2, x0, xt0), [Xtq[0][hq].b, MA2.b], [pm2.b])
                            yield
                            quad_copy("act" if hq == 0 else "dve", Xtq[1][hq], pm2)
                        for lev in range(1, NLEV):
                            for hq in range(2):
                                Xk = Xs[lev % 2][hq]; Xtk = Xtq[lev % 2][hq]; Qp = Qs[(lev - 1) % 2][hq]
                                xk = lambda hh: Xk[:, hh, :]
                                xtk = lambda hh: Xtk[:, hh, :]
                                qp = lambda hh: Qp[:, hh, :]
                                if lev <= NLEV - 3:
                                    pmA = pbank("b")
                                    yield
                                    S.op("pe", quad_mm(pmA, xtk, xk), [Xk.b, Xtk.b], [pmA.b])
                                    yield
                                    quad_copy("act", Xs[(lev + 1) % 2][hq], pmA)
                                pmB = pbank("b")
                                qdst = Qs[lev % 2][hq] if lev < NLEV - 1 else TTq[hq]
                                yield
                                S.op("pe", quad_mm(pmB, xtk, qp), [Xtk.b, Qp.b], [pmB.b])
                                yield
                                S.op("dve", lambda: nc.vector.tensor_tensor(out=qdst[:], in0=pmB[:].rearrange("p (h x) -> p h x", h=4),
                                                                            in1=Qp[:], op=ALU.add), [pmB.b, Qp.b], [qdst.b])
                                if lev <= NLEV - 2:
                                    pmC = pbank("b")
                                    yield
                                    S.op("pe", quad_mm(pmC, xk, xtk), [Xk.b, Xtk.b], [pmC.b])
                                    yield
                                    quad_copy("act" if hq == 0 else "dve", Xtq[(lev + 1) % 2][hq], pmC)

                        Vh = lambda h: V[:, h * HD:(h + 1) * HD]
                        pzz = pbank("b")

                        def fz():
                            ins = None
                            for h in range(NH):
                                nc.tensor.matmul(pzz[:, h * HD:(h + 1) * HD], lhsT=krT[:, h, 0, :],
                                                 rhs=Hb[:, h, :], start=True, stop=False)
                                ins = nc.tensor.matmul(pzz[:, h * HD:(h + 1) * HD], lhsT=MA1[:, h, 0, :], rhs=Vh(h),
                                                       start=False, stop=True)
                            return ins
                        yield
                        S.op("pe", fz, [krT.b, Hb.b, MA1.b, V.b], [pzz.b])
                        yield
                        S.op("act", lambda: nc.scalar.copy(out=bft["Zs"][:], in_=pzz[:]), [pzz.b], [bft["Zs"].b])
                        pu = pbank("b")

                        def fu():
                            ins = None
                            for h in range(NH):
                                ins = nc.tensor.matmul(pu[:, h * HD:(h + 1) * HD], lhsT=TTq[h // 4][:, h % 4, :],
                                                       rhs=bft["Zs"][:, h * HD:(h + 1) * HD], start=True, stop=True)
                            return ins
                        yield
                        S.op("pe", fu, [TTq[0].b, TTq[1].b, bft["Zs"].b], [pu.b])
                        nU = bft["nU"]
                        yield
                        S.op("act", lambda: nc.scalar.activation(out=nU[:], in_=pu[:], func=AF.Identity, scale=-1.0), [pu.b], [nU.b])
                        py = pbank("b")

                        def fy():
                            ins = None
                            for h in range(NH):
                                o = slice(h * HD, (h + 1) * HD)
                                nc.tensor.matmul(py[:, o], lhsT=krT[:, h, 1, :], rhs=Hb[:, h, :],
                                                 start=True, stop=False)
                                nc.tensor.matmul(py[:, o], lhsT=MA1[:, h, 1, :], rhs=Vh(h), start=False, stop=False)
                                ins = nc.tensor.matmul(py[:, o], lhsT=MA2[:, h, 1, :], rhs=nU[:, o], start=False, stop=True)
                            return ins
                        yield
                        S.op("pe", fy, [krT.b, Hb.b, MA1.b, MA2.b, V.b, nU.b], [py.b])
                        ph = pbank("b")

                        def fh():
                            ins = None
                            for h in range(NH):
                                o = slice(h * HD, (h + 1) * HD)
                                nc.tensor.matmul(ph[0:64, o], lhsT=bft["kh"][:, o], rhs=V[:, o], start=True, stop=False)
                                ins = nc.tensor.matmul(ph[0:64, o], lhsT=bft["bh"][:, o], rhs=nU[:, o], start=False, stop=True)
                            return ins
                        yield
                        S.op("pe", fh, [bft["kh"].b, bft["bh"].b, V.b, nU.b], [ph.b])
                        yield
                        S.op("dve", lambda: nc.vector.tensor_tensor(out=Ht[:], in0=ph[0:64, :].rearrange("p (h v) -> p h v", v=HD),
                                                                    in1=Hf[:], op=ALU.add), [ph.b, Hf.b], [Ht.b])
                        yield
                        S.op("dve", lambda: nc.vector.tensor_tensor(out=Hf[:], in0=Ht[:], in1=bc(gam[:].unsqueeze(2), [64, NH, HD]),
                                                                    op=ALU.mult), [Ht.b, gam.b], [Hf.b])
                        nxt_i = i - 1 if bwd else i + 1
                        bt = i if bwd else i + 1
                        if 0 <= nxt_i < NT and bt % BP == 0:
                            bidx = bt // BP - 1
                            yield
                            S.op("dve", lambda: nc.vector.tensor_scalar(out=Hf[:], in0=Hf[:], scalar1=keep[0:64, bidx:bidx + 1],
                                                                        scalar2=None, op0=ALU.mult), [Hf.b, keep.b], [Hf.b])
                        yield
                        S.op("act", lambda: nc.scalar.copy(out=Hb[:], in_=Hf[:]), [Hf.b], [Hb.b])

                        if not bwd:
                            yield
                            S.op("act", lambda: nc.scalar.copy(out=t["ysum"][:], in_=py[:]), [py.b], [t["ysum"].b])
                            yield
                            S.dma(YF[i * 128:(i + 1) * 128, :], t["ysum"][:], [t["ysum"].b], [YF_bs[i]])
                        else:
                            yield
                            S.op("dve", lambda: nc.vector.tensor_tensor(out=t["ysum"][:], in0=py[:], in1=t["yf"][:], op=ALU.add),
                                 [py.b, t["yf"].b], [t["ysum"].b])
                            yield
                            S.op("dve", lambda: nc.vector.tensor_reduce(out=gs1[:], in_=v3(t["ysum"][:]), axis=AX.X, op=ALU.add),
                                 [t["ysum"].b], [gs1.b])
                            yield
                            S.op("dve", lambda: nc.vector.tensor_scalar(out=gs1[:], in0=gs1[:], scalar1=-1.0 / HD, scalar2=None,
                                                                        op0=ALU.mult), [gs1.b], [gs1.b])
                            yield
                            S.op("dve", lambda: nc.vector.tensor_tensor(out=v3(t["ysum"][:]), in0=v3(t["ysum"][:]),
                                                                        in1=bc(gs1[:].unsqueeze(2), [128, NH, HD]), op=ALU.add),
                                 [t["ysum"].b, gs1.b], [t["ysum"].b])
                            yield
                            S.op("pool", lambda: nc.gpsimd.tensor_tensor(out=t["sq2"][:], in0=t["ysum"][:], in1=t["ysum"][:], op=ALU.mult),
                                 [t["ysum"].b], [t["sq2"].b])
                            yield
                            S.op("dve", lambda: nc.vector.tensor_reduce(out=gs2[:], in_=v3(t["sq2"][:]), axis=AX.X, op=ALU.add),
                                 [t["sq2"].b], [gs2.b])
                            yield
                            S.op("dve", lambda: nc.vector.tensor_scalar(out=gs2[:], in0=gs2[:], scalar1=1.0 / HD, scalar2=LNX_EPS,
                                                                        op0=ALU.mult, op1=ALU.add), [gs2.b], [gs2.b])
                            yield
                            S.op("act", lambda: nc.scalar.activation(out=grs[:], in_=gs2[:], func=AF.Sqrt), [gs2.b], [grs.b])
                            yield
                            S.op("dve", lambda: nc.vector.reciprocal(out=grs[:], in_=grs[:]), [grs.b], [grs.b])
                            yield
                            S.op("dve", lambda: nc.vector.tensor_tensor(out=v3(t["ysum"][:]), in0=v3(t["ysum"][:]),
                                                                        in1=bc(grs[:].unsqueeze(2), [128, NH, HD]), op=ALU.mult),
                                 [t["ysum"].b, grs.b], [t["ysum"].b])
                            yield
                            S.op("pool", lambda: nc.gpsimd.tensor_tensor(out=t["ysum"][:], in0=t["ysum"][:], in1=prm["lxg"][:], op=ALU.mult),
                                 [t["ysum"].b, prm["lxg"].b], [t["ysum"].b])
                            yield
                            S.op("pool", lambda: nc.gpsimd.tensor_tensor(out=t["ysum"][:], in0=t["ysum"][:], in1=prm["lxb"][:], op=ALU.add),
                                 [t["ysum"].b, prm["lxb"].b], [t["ysum"].b])
                            yield
                            S.op("dve", lambda: nc.vector.tensor_tensor(out=t["ysum"][:], in0=t["ysum"][:], in1=t["vbon"][:], op=ALU.add),
                                 [t["ysum"].b, t["vbon"].b], [t["ysum"].b])
                            yield
                            S.op("dve", lambda: nc.vector.tensor_tensor(out=bft["ob"][:], in0=t["ysum"][:], in1=t["sz"][:], op=ALU.mult),
                                 [t["ysum"].b, t["sz"].b], [bft["ob"].b])
                            pto = pbank("b")
                            ptvo = pto[:].bitcast(BF16)

                            def fo():
                                ins = None
                                for blk in range(4):
                                    ins = nc.tensor.transpose(out=ptvo[:, blk * 128:(blk + 1) * 128],
                                                              in_=bft["ob"][:, blk * 128:(blk + 1) * 128], identity=identb[:])
                                return ins
                            yield
                            S.op("pe", fo, [bft["ob"].b, identb.b], [pto.b])
                            yield
                            S.op("act", lambda: nc.scalar.copy(out=oT[:].rearrange("p b t -> p (b t)"), in_=ptvo[:, 0:512]),
                                 [pto.b], [oT.b])
                            yield
                            S.dma(YR[:, g * 4:(g + 1) * 4, i * 128:(i + 1) * 128], oT[:], [oT.b], [YR_bs[g][i]])

                    def interleave(ga, gb):
                        da = db = False
                        while not (da and db):
                            if not da:
                                try:
                                    next(ga)
                                except StopIteration:
                                    da = True
                            for _ in range(2):
                                if not db:
                                    try:
                                        next(gb)
                                    except StopIteration:
                                        db = True

                    prev = None
                    for oi, i in enumerate(order):
                        interleave(front(oi, i), back(*prev) if prev is not None else iter(()))
                        prev = (oi, i)
                    interleave(iter(()), back(*prev))
                S.barrier()

        with contextlib.ExitStack() as es:
            if not DBG.get("c", True):
                raise_skip = True
            else:
                raise_skip = False
            wc = sb(es, "wc", [128, 8, NCV], BF16)
            wo = sb(es, "wo", [128, 16, D], BF16)
            with contextlib.ExitStack() as es2:
                wst = [sb(es2, f"cwst{i}", [128, 8, 512], F32) for i in range(2)]
                for bi in range(NCV // 512 if DBG.get("cw", True) else 0):
                    w_ = wst[bi % 2]
                    S.dma(w_[:], w_in_d[:, bi * 512:(bi + 1) * 512].rearrange("(c p) n -> p c n", p=128), [], [w_.b])
                    if bi % 2 == 0:
                        S.op("act", lambda: nc.scalar.copy(out=wc[:, :, bi * 512:(bi + 1) * 512], in_=w_[:]), [w_.b], [wc.b])
                    else:
                        S.op("dve", lambda: nc.vector.tensor_copy(out=wc[:, :, bi * 512:(bi + 1) * 512], in_=w_[:]), [w_.b], [wc.b])
                for bi in range(4 if DBG.get("cw", True) else 0):
                    w_ = wst[bi % 2]
                    S.dma(w_[:, 0:4, :], wout_d[bi * 512:(bi + 1) * 512, 0:512].rearrange("(c p) n -> p c n", p=128), [], [w_.b])
                    S.dma(w_[:, 4:8, :], wout_d[bi * 512:(bi + 1) * 512, 512:1024].rearrange("(c p) n -> p c n", p=128), [], [w_.b])
                    S.op("act", lambda: nc.scalar.copy(out=wo[:, bi * 4:(bi + 1) * 4, 0:512], in_=w_[:, 0:4, :]), [w_.b], [wo.b])
                    S.op("dve", lambda: nc.vector.tensor_copy(out=wo[:, bi * 4:(bi + 1) * 4, 512:1024], in_=w_[:, 4:8, :]), [w_.b], [wo.b])
                S.barrier()
            cpar = sb(es, "cpar", [128, 8, 4], F32)
            for j in range(3 if DBG.get("cp", True) else 0):
                S.dma(cpar[:, :, j:j + 1], conv_w_d[j, :].rearrange("(c p o) -> p c o", p=128, o=1), [], [cpar.b])
            S.dma(cpar[:, :, 3:4], conv_b_d.rearrange("(c p o) -> p c o", p=128, o=1), [], [cpar.b])
            gbc = sb(es, "c_g", [128, D], F32); bbc = sb(es, "c_b", [128, D], F32)
            g2 = sb(es, "c_g2", [128, D], F32); b2 = sb(es, "c_b2", [128, D], F32)
            for tl, src in ((gbc, emb_g_d), (bbc, emb_b_d), (g2, lng_d), (b2, lnb_d)):
                bcast_load(tl, 0, src, D)
            xth = [sb(es, f"cxth{i}", [128, 8, 130], BF16) for i in range(2)]
            ymT = [sb(es, f"ymT{i}", [128, 16, 128], BF16) for i in range(2)]
            xt = [sb(es, f"c_x{i}", [128, D], F32) for i in range(2)]
            xn = sb(es, "c_xn", [128, D], F32); xg = sb(es, "c_xg", [128, D], F32); xh = sb(es, "c_xh", [128, D], F32)
            sres = sb(es, "c_s", [128, D], F32); on = sb(es, "c_on", [128, D], F32); og = sb(es, "c_og", [128, D], F32)
            yo = [sb(es, f"c_yo{i}", [128, D], F32) for i in range(2)]
            st6 = sb(es, "c_st", [128, 2, SD], F32); mv = sb(es, "c_mv", [128, AD], F32)
            rstd = sb(es, "c_rs", [128, 1], F32); nb_ = sb(es, "c_nb", [128, 1], F32)
            st6b = sb(es, "c_stb", [128, 2, SD], F32); mvb = sb(es, "c_mvb", [128, AD], F32)
            rstdb = sb(es, "c_rsb", [128, 1], F32); nbb = sb(es, "c_nbb", [128, 1], F32)
            hS = sb(es, "c_hS", [128, 130], F32); pp = sb(es, "c_pp", [128, 130], F32)
            qq = sb(es, "c_qq", [128, 128], F32); szc = sb(es, "c_sz", [128, 128], F32)
            if not raise_skip:
                load_xth(xth[0], 0)
            for i in range(0 if raise_skip else NT):
                p = i % 2
                xh_ = xth[p]
                if i + 1 < NT:
                    load_xth(xth[(i + 1) % 2], i + 1)
                S.dma(xt[p][:], xs[i * 128:(i + 1) * 128, :], [], [xt[p].b])
                S.dma(ymT[p][:, 8:16, :], YR[:, :, i * 128:(i + 1) * 128], [YR_bs[0][i], YR_bs[1][i]], [ymT[p].b])
                for cbk in range(8):
                    pa = pbank("f"); pb = pbank("f")

                    def fa():
                        ins = None
                        for qi, q in enumerate((0, 2)):
                            for kc in range(8):
                                ins = nc.tensor.matmul(pa[:, qi * 130:(qi + 1) * 130],
                                                       lhsT=wc[:, kc, q * 1024 + cbk * 128:q * 1024 + (cbk + 1) * 128],
                                                       rhs=xh_[:, kc, :], start=(kc == 0), stop=(kc == 7))
                        return ins

                    def fb():
                        ins = None
                        for qi, q in enumerate((1, 3)):
                            for kc in range(8):
                                ins = nc.tensor.matmul(pb[:, qi * 128:(qi + 1) * 128],
                                                       lhsT=wc[:, kc, q * 1024 + cbk * 128:q * 1024 + (cbk + 1) * 128],
                                                       rhs=xh_[:, kc, 1:129], start=(kc == 0), stop=(kc == 7))
                        return ins
                    S.op("pe", fa, [wc.b, xh_.b], [pa.b])
                    S.op("pe", fb, [wc.b, xh_.b], [pb.b])
                    S.op("act", lambda: nc.scalar.copy(out=hS[:], in_=pa[:, 0:130]), [pa.b], [hS.b])
                    S.op("dve", lambda: nc.vector.tensor_tensor(out=pp[:], in0=hS[:], in1=pa[:, 130:260], op=ALU.mult),
                         [hS.b, pa.b], [pp.b])
                    S.op("dve", lambda: nc.vector.tensor_scalar(out=qq[:], in0=pp[:, 1:129], scalar1=cpar[:, cbk, 1:2],
                                                                scalar2=cpar[:, cbk, 3:4], op0=ALU.mult, op1=ALU.add),
                         [pp.b, cpar.b], [qq.b])
                    S.op("dve", lambda: nc.vector.scalar_tensor_tensor(out=qq[:], in0=pp[:, 0:128], scalar=cpar[:, cbk, 0:1],
                                                                         in1=qq[:], op0=ALU.mult, op1=ALU.add),
                         [pp.b, cpar.b, qq.b], [qq.b])
                    S.op("dve", lambda: nc.vector.scalar_tensor_tensor(out=qq[:], in0=pp[:, 2:130], scalar=cpar[:, cbk, 2:3],
                                                                         in1=qq[:], op0=ALU.mult, op1=ALU.add),
                         [pp.b, cpar.b, qq.b], [qq.b])
                    S.op("act", lambda: nc.scalar.activation(out=szc[:], in_=pb[:, 128:256], func=AF.Silu), [pb.b], [szc.b])
                    S.op("dve", lambda: nc.vector.tensor_tensor(out=qq[:], in0=qq[:], in1=pb[:, 0:128], op=ALU.mult),
                         [qq.b, pb.b], [qq.b])
                    S.op("pool", lambda: nc.gpsimd.tensor_tensor(out=ymT[p][:, cbk, :], in0=qq[:], in1=szc[:], op=ALU.mult),
                         [qq.b, szc.b], [ymT[p].b])
                po = [pbank("b"), pbank("b")]
                for hf in range(2):
                    def fo():
                        ins = None
                        for mc in range(16):
                            ins = nc.tensor.matmul(po[hf][:], lhsT=ymT[p][:, mc, :], rhs=wo[:, mc, hf * 512:(hf + 1) * 512],
                                                   start=(mc == 0), stop=(mc == 15))
                        return ins
                    S.op("pe", fo, [ymT[p].b, wo.b], [po[hf].b])
                ln_stats("c", xt[p], mv, rstd, nb_, st6, LN_EPS)
                S.op("act", lambda: nc.scalar.activation(out=xn[:], in_=xt[p][:], func=AF.Identity, bias=nb_[:, 0:1],
                                                         scale=rstd[:, 0:1]), [xt[p].b, rstd.b, nb_.b], [xn.b])
                S.op("dve", lambda: nc.vector.tensor_tensor(out=xg[:], in0=xn[:], in1=gbc[:], op=ALU.mult), [xn.b, gbc.b], [xg.b])
                S.op("pool", lambda: nc.gpsimd.tensor_tensor(out=xh[:], in0=xg[:], in1=bbc[:], op=ALU.add), [xg.b, bbc.b], [xh.b])
                for hf in range(2):
                    o = slice(hf * 512, (hf + 1) * 512)
                    S.op("dve", lambda: nc.vector.scalar_tensor_tensor(out=sres[:, o], in0=xh[:, o], scalar=DN_ALPHA,
                                                                       in1=po[hf][:], op0=ALU.mult, op1=ALU.add),
                         [xh.b, po[hf].b], [sres.b])
                ln_stats("c2", sres, mvb, rstdb, nbb, st6b, LN_EPS)
                S.op("act", lambda: nc.scalar.activation(out=on[:], in_=sres[:], func=AF.Identity, bias=nbb[:, 0:1],
                                                         scale=rstdb[:, 0:1]), [sres.b, rstdb.b, nbb.b], [on.b])
                S.op("dve", lambda: nc.vector.tensor_tensor(out=og[:], in0=on[:], in1=g2[:], op=ALU.mult), [on.b, g2.b], [og.b])
                S.op("pool", lambda: nc.gpsimd.tensor_tensor(out=yo[p][:], in0=og[:], in1=b2[:], op=ALU.add), [og.b, b2.b], [yo[p].b])
                S.dma(ys[i * 128:(i + 1) * 128, :], yo[p][:], [yo[p].b], [ys_bs[i]])
            S.barrier()

        S.finish()
    return nc


CST_COLS = 128 + 2 * 896 + 1
DBG = {}
NAMES = {}


def make_consts():
    r = np.arange(128)[:, None]
    c = np.arange(128)[None, :]
    SU = (r < c).astype(np.float32); IU = (r <= c).astype(np.float32)
    SL = (r > c).astype(np.float32); IL = (r >= c).astype(np.float32)
    parts = [np.eye(128, dtype=np.float32)]
    for (S_, I_, St) in ((SU, IU, SL), (SL, IL, SU)):
        parts += [S_, I_, -S_, I_, -St, CDEC * I_, CDEC * S_]
    parts.append(np.full((128, 1), CDEC, np.float32))
    out = np.concatenate(parts, axis=1).astype(np.float32)
    assert out.shape[1] == CST_COLS
    return np.ascontiguousarray(out)


W_NAMES = ["emb_ln_g", "emb_ln_b", "w_in", "conv_w", "conv_b", "shift_mu", "w0", "w_up", "a0", "a_up",
           "k_k", "k_a", "r_k", "lnx_g", "lnx_b", "w_out", "ln_g", "ln_b"]


def weight_map(inp):
    m = {}
    for n in W_NAMES:
        a = np.asarray(inp[n], dtype=np.float32)
        if n not in ("emb_ln_g", "emb_ln_b"):
            a = a[0]
        if n == "r_k":
            a = a.reshape(1024)
        m[n] = np.ascontiguousarray(a)
    m["cst"] = make_consts()
    return m


_NC_CACHE = {}


def run_streams(streams, keeps, inp, NT, BP):
    key = (NT, BP)
    if key not in _NC_CACHE:
        _NC_CACHE[key] = build(NT, BP)
    nc = _NC_CACHE[key]
    wm = weight_map(inp)
    in_maps = []
    for s, k in zip(streams, keeps):
        d = dict(wm)
        d["xs"] = np.ascontiguousarray(s, dtype=np.float32)
        d["keep"] = np.ascontiguousarray(k, dtype=np.float32)
        in_maps.append(d)
    res = run_bass_kernel_spmd(nc, in_maps, core_ids=list(range(len(streams))))
    return [r["ys"] for r in res.results]


def kernel(**inp):
    xp = np.asarray(inp["x_prompt"], dtype=np.float32)
    xsm = np.asarray(inp["x_sample"], dtype=np.float32)
    NT, BP = 128, 16
    NB = NT // BP - 1
    ntok = NT * 128
    streams = [xsm[0], xsm[1], xp.reshape(ntok, D)]
    keeps = [np.ones((128, NB), np.float32), np.ones((128, NB), np.float32), np.zeros((128, NB), np.float32)]
    for _ in range(5):
        streams.append(np.zeros((ntok, D), np.float32))
        keeps.append(np.zeros((128, NB), np.float32))
    outs = run_streams(streams, keeps, inp, NT, BP)
    y_sample = np.stack([outs[0], outs[1]], axis=0).reshape(2, 16384, D)
    y_prompt = outs[2].reshape(8, 2048, D)
    return (y_prompt.astype(np.float32), y_sample.astype(np.float32))
```

```python
import contextlib
import numpy as np
import concourse.bass as bass
import concourse.mybir as mybir
from concourse.bass_utils import run_bass_kernel_spmd

F32 = mybir.dt.float32
BF16 = mybir.dt.bfloat16
AF = mybir.ActivationFunctionType
ALU = mybir.AluOpType
AX = mybir.AxisListType

D = 1024
NCV = 4096
NRW = 4352
NIN = NCV + NRW
HD = 64
LN_EPS = 1e-5
LNX_EPS = 64e-5
DN_ALPHA = 2.0 ** 0.25
CDEC = -float(np.exp(-0.5))
NG = 2
GC = 512
NH = 8
NLEV = 7


class Buf:
    __slots__ = ("name", "w", "rs", "excl")

    def __init__(self, name):
        self.name = name
        self.w = None
        self.rs = {}
        self.excl = False


class Sch:
    R = 4
    ND = 24

    def __init__(self, nc, es):
        self.nc = nc
        self.eng = {"pe": nc.tensor, "act": nc.scalar, "dve": nc.vector, "pool": nc.gpsimd, "sp": nc.sync}
        self.sems = {e: [es.enter_context(nc.semaphore(f"s_{e}{i}")) for i in range(self.R)]
                     for e in ("pe", "act", "dve", "pool")}
        self.cnt = {e: 0 for e in self.sems}
        self.known = {e: {} for e in self.eng}
        self.dsems = [es.enter_context(nc.semaphore(f"s_d{i}")) for i in range(self.ND)]
        self.dval = [0] * self.ND
        self.dnext = 0
        self.all_bufs = []

    def buf(self, name):
        b = Buf(name)
        self.all_bufs.append(b)
        return b

    def _wait(self, waiter, ev):
        key = (ev[0], ev[1])
        if ev[0] == "E" and ev[1] == waiter and waiter == "pe":
            return
        if self.known[waiter].get(key, -1) >= ev[2]:
            return
        if ev[0] == "E":
            sem = self.sems[ev[1]][ev[2] % self.R]
            val = ev[2] // self.R + 1
        else:
            sem = self.dsems[ev[1]]
            val = ev[2]
        self.eng[waiter].wait_ge(sem, val)
        self.known[waiter][key] = ev[2]

    def _deps(self, waiter, reads, writes):
        for b in reads:
            if b.w is not None:
                self._wait(waiter, b.w)
            if b.excl:
                for ev in b.rs.values():
                    if not (ev[0] == "E" and ev[1] == waiter):
                        self._wait(waiter, ev)
        for b in writes:
            if b.w is not None:
                self._wait(waiter, b.w)
            for ev in b.rs.values():
                self._wait(waiter, ev)

    def _commit(self, ev, reads, writes):
        for b in reads:
            b.rs[(ev[0], ev[1])] = ev
        for b in writes:
            b.w = ev
            b.rs = {}

    muted = False

    def stage(self, k):
        self.muted = k > DBG.get("stage", 99)

    def op(self, eng, fn, reads, writes):
        if self.muted:
            return
        self._deps(eng, reads, writes)
        ins = fn()
        idx = self.cnt[eng]
        self.cnt[eng] += 1
        ins.then_inc(self.sems[eng][idx % self.R], 1)
        self._commit(("E", eng, idx), reads, writes)

    def dma(self, out, in_, reads, writes, **kw):
        if self.muted:
            return
        k = self.dnext
        self.dnext = (self.dnext + 1) % self.ND
        if self.dval[k] > 0:
            self._wait("sp", ("D", k, self.dval[k]))
        self._deps("sp", reads, writes)
        ins = self.nc.sync.dma_start(out=out, in_=in_, **kw)
        self.dval[k] += 16
        ins.then_inc(self.dsems[k], 16)
        self._commit(("D", k, self.dval[k]), reads, writes)

    def barrier(self):
        self.muted = False
        for w in self.eng:
            for k in range(self.ND):
                if self.dval[k] > 0:
                    self._wait(w, ("D", k, self.dval[k]))
            for e in self.cnt:
                if self.cnt[e] > 0:
                    self._wait(w, ("E", e, self.cnt[e] - 1))

    def finish(self):
        for k in range(self.ND):
            if self.dval[k] > 0:
                self._wait("sp", ("D", k, self.dval[k]))
        for e in self.cnt:
            if self.cnt[e] > 0:
                self._wait("sp", ("E", e, self.cnt[e] - 1))


class TL:
    def __init__(self, sch, t, name):
        self.t = t
        self.b = sch.buf(name)

    def __getitem__(self, k):
        return self.t[k]


def bc(ap, shape):
    return ap.broadcast_to(list(shape))


def build(NT, BP):
    NTOK = NT * 128
    NB = max(1, NT // BP - 1)
    nc = bass.Bass("TRN2", target_bir_lowering=False)
    dt_in = lambda n, s: nc.dram_tensor(n, s, F32, kind="ExternalInput").ap()
    xs = dt_in("xs", [NTOK, D])
    keep_d = dt_in("keep", [128, NB])
    cst_d = dt_in("cst", [128, CST_COLS])
    emb_g_d = dt_in("emb_ln_g", [D]); emb_b_d = dt_in("emb_ln_b", [D])
    w_in_d = dt_in("w_in", [D, NIN])
    conv_w_d = dt_in("conv_w", [3, 1024]); conv_b_d = dt_in("conv_b", [1024])
    mu_d = dt_in("shift_mu", [NRW])
    w0_d = dt_in("w0", [2, 1024]); wup_d = dt_in("w_up", [2, 64, 1024])
    a0_d = dt_in("a0", [2, 1024]); aup_d = dt_in("a_up", [2, 64, 1024])
    kk_d = dt_in("k_k", [1024]); ka_d = dt_in("k_a", [1024]); rk_d = dt_in("r_k", [1024])
    lxg_d = dt_in("lnx_g", [1024]); lxb_d = dt_in("lnx_b", [1024])
    wout_d = dt_in("w_out", [2048, D])
    lng_d = dt_in("ln_g", [D]); lnb_d = dt_in("ln_b", [D])
    ys = nc.dram_tensor("ys", [NTOK, D], F32, kind="ExternalOutput").ap()
    XT = nc.dram_tensor("XT", [128, 8, NTOK + 2], BF16).ap()
    YF = nc.dram_tensor("YF", [NTOK, GC], F32).ap()
    YR = nc.dram_tensor("YR", [128, 8, NTOK], BF16).ap()
    RKV = nc.dram_tensor("RKV", [NT, 128, 5, GC], F32).ap()

    with contextlib.ExitStack() as es0:
        es0.enter_context(nc.allow_non_contiguous_dma(reason="small strided param / halo loads"))
        S = Sch(nc, es0)
        XT_bs = [S.buf(f"XT{i}") for i in range(NT + 2)]
        YF_bs = [S.buf(f"YF{i}") for i in range(NT)]
        RKV_bs = [S.buf(f"RKV{i}") for i in range(NT)]
        YR_bs = [[S.buf(f"YR{g}_{i}") for i in range(NT)] for g in range(NG)]
        ys_bs = [S.buf(f"ys{i}") for i in range(NT)]

        uid = [0]

        def sb(es, name, shape, dtype):
            uid[0] += 1
            nm = f"sb{uid[0]}_{name}"
            NAMES[name] = nm
            return TL(S, es.enter_context(nc.sbuf_tensor(nm, list(shape), dtype)), nm)

        PS = [TL(S, es0.enter_context(nc.psum_tensor(f"ps{i}", [128, 512], F32)), f"ps{i}") for i in range(8)]
        for p_ in PS:
            p_.b.excl = True
        prot = {"f": [0, [0, 1, 2, 3]], "b": [0, [4, 5, 6, 7]]}

        def pbank(kind):
            st = prot[kind]
            p = PS[st[1][st[0] % len(st[1])]]
            st[0] += 1
            return p

        cst = sb(es0, "cst", [128, CST_COLS], F32)
        S.dma(cst[:], cst_d, [], [cst.b])
        identb = sb(es0, "identb", [128, 128], BF16)
        S.op("dve", lambda: nc.vector.tensor_copy(out=identb[:], in_=cst[:, 0:128]), [cst.b], [identb.b])
        keep = sb(es0, "keep", [128, NB], F32)
        S.dma(keep[:], keep_d, [], [keep.b])
        zpad = sb(es0, "zpad", [128, 8, 1], BF16)
        S.op("pool", lambda: nc.gpsimd.memset(zpad[:], 0.0), [], [zpad.b])
        S.dma(XT[:, :, 0:1], zpad[:], [zpad.b], [XT_bs[0]])
        S.dma(XT[:, :, NTOK + 1:NTOK + 2], zpad[:], [zpad.b], [XT_bs[NT + 1]])

        onesf = sb(es0, "onesf", [1, 128], F32)
        S.op("pool", lambda: nc.gpsimd.memset(onesf[:], 1.0), [], [onesf.b])
        rowbuf = sb(es0, "rowbuf", [1, 1024], F32)

        def bcast_load(dst, dcol, src1d, ncol):
            S.dma(rowbuf[0:1, 0:ncol], src1d.rearrange("(o n) -> o n", o=1), [], [rowbuf.b])
            for c_ in range(0, ncol, 512):
                n_ = min(512, ncol - c_)
                pb_ = pbank("f")
                S.op("pe", lambda: nc.tensor.matmul(pb_[:, 0:n_], lhsT=onesf[0:1, :], rhs=rowbuf[0:1, c_:c_ + n_],
                                                    start=True, stop=True), [onesf.b, rowbuf.b], [pb_.b])
                S.op("act", lambda: nc.scalar.copy(out=dst[:, dcol + c_:dcol + c_ + n_], in_=pb_[:, 0:n_]), [pb_.b], [dst.b])

        def ln_stats(es_tag, src, mv, rstd, nb_, st6, eps):
            S.op("dve", lambda: nc.vector.bn_stats(out=st6[:, 0, :], in_=src[:, 0:512]), [src.b], [st6.b])
            S.op("dve", lambda: nc.vector.bn_stats(out=st6[:, 1, :], in_=src[:, 512:1024]), [src.b], [st6.b])
            S.op("dve", lambda: nc.vector.bn_aggr(out=mv[:], in_=st6[:]), [st6.b], [mv.b])
            S.op("act", lambda: nc.scalar.activation(out=rstd[:], in_=mv[:, 1:2], func=AF.Sqrt, bias=eps, scale=1.0),
                 [mv.b], [rstd.b])
            S.op("dve", lambda: nc.vector.reciprocal(out=rstd[:], in_=rstd[:]), [rstd.b], [rstd.b])
            S.op("dve", lambda: nc.vector.scalar_tensor_tensor(out=nb_[:], in0=mv[:, 0:1], scalar=-1.0, in1=rstd[:],
                                                               op0=ALU.mult, op1=ALU.mult), [mv.b, rstd.b], [nb_.b])

        SD = int(nc.vector.BN_STATS_DIM)
        AD = int(nc.vector.BN_AGGR_DIM)

        with contextlib.ExitStack() as es:
            gbc = sb(es, "p1_g", [128, D], F32); bbc = sb(es, "p1_b", [128, D], F32)
            bcast_load(gbc, 0, emb_g_d, D)
            bcast_load(bbc, 0, emb_b_d, D)
            xt = [sb(es, f"p1_x{i}", [128, D], F32) for i in range(2)]
            xn = [sb(es, f"p1_xn{i}", [128, D], F32) for i in range(2)]
            xg = [sb(es, f"p1_xg{i}", [128, D], F32) for i in range(2)]
            xh = [sb(es, f"p1_xh{i}", [128, D], BF16) for i in range(2)]
            xT = [sb(es, f"p1_xT{i}", [128, 8, 128], BF16) for i in range(2)]
            st6 = [sb(es, f"p1_st{i}", [128, 2, SD], F32) for i in range(2)]
            mv = [sb(es, f"p1_mv{i}", [128, AD], F32) for i in range(2)]
            rstd = [sb(es, f"p1_rs{i}", [128, 1], F32) for i in range(2)]
            nb_ = [sb(es, f"p1_nb{i}", [128, 1], F32) for i in range(2)]
            for i in range(NT if DBG.get("p1", True) else 0):
                p = i % 2
                S.dma(xt[p][:], xs[i * 128:(i + 1) * 128, :], [], [xt[p].b])
                ln_stats("p1", xt[p], mv[p], rstd[p], nb_[p], st6[p], LN_EPS)
                S.op("act", lambda: nc.scalar.activation(out=xn[p][:], in_=xt[p][:], func=AF.Identity,
                                                         bias=nb_[p][:, 0:1], scale=rstd[p][:, 0:1]),
                     [xt[p].b, rstd[p].b, nb_[p].b], [xn[p].b])
                S.op("dve", lambda: nc.vector.tensor_tensor(out=xg[p][:], in0=xn[p][:], in1=gbc[:], op=ALU.mult),
                     [xn[p].b, gbc.b], [xg[p].b])
                S.op("pool", lambda: nc.gpsimd.tensor_tensor(out=xh[p][:], in0=xg[p][:], in1=bbc[:], op=ALU.add),
                     [xg[p].b, bbc.b], [xh[p].b])
                pt = pbank("f")
                ptv = pt[:].bitcast(BF16)

                def tr():
                    ins = None
                    for c in range(8):
                        ins = nc.tensor.transpose(out=ptv[:, c * 128:(c + 1) * 128], in_=xh[p][:, c * 128:(c + 1) * 128],
                                                  identity=identb[:])
                    return ins
                S.op("pe", tr, [xh[p].b, identb.b], [pt.b])
                S.op("act", lambda: nc.scalar.copy(out=xT[p][:].rearrange("p c t -> p (c t)"), in_=ptv), [pt.b], [xT[p].b])
                S.dma(XT[:, :, 1 + i * 128:1 + (i + 1) * 128], xT[p][:], [xT[p].b], [XT_bs[i + 1]])
            S.barrier()

        def load_xth(xth, i):
            S.dma(xth[:], XT[:, :, i * 128:i * 128 + 130], [XT_bs[i], XT_bs[i + 1], XT_bs[i + 2]], [xth.b])
            if i % BP == 0 and i > 0:
                bidx = i // BP - 1
                S.op("dve", lambda: nc.vector.tensor_scalar(out=xth[:, :, 0:1], in0=xth[:, :, 0:1],
                                                            scalar1=keep[:, bidx:bidx + 1], scalar2=None, op0=ALU.mult),
                     [xth.b, keep.b], [xth.b])
            if (i + 1) % BP == 0 and i + 1 < NT:
                bidx = (i + 1) // BP - 1
                S.op("dve", lambda: nc.vector.tensor_scalar(out=xth[:, :, 129:130], in0=xth[:, :, 129:130],
                                                            scalar1=keep[:, bidx:bidx + 1], scalar2=None, op0=ALU.mult),
                     [xth.b, keep.b], [xth.b])

        for g in range(NG if DBG.get("rwkv", True) else 0):
            with contextlib.ExitStack() as es:
                c0 = g * GC
                wl = sb(es, "wl", [128, 16, 256], BF16)
                upw = sb(es, "upw", [64, 2, 2, GC], BF16)
                b0h = sb(es, "b0h", [1, 2, 2, GC], BF16); b0l = sb(es, "b0l", [1, 2, 2, GC], BF16)
                onesr = sb(es, "onesr", [1, 128], BF16)
                S.op("pool", lambda: nc.gpsimd.memset(onesr[:], 1.0), [], [onesr.b])
                with contextlib.ExitStack() as es2:
                    upf = sb(es2, "upf", [64, 2, 2, GC], F32)
                    b0f = sb(es2, "b0f", [1, 2, 2, GC], F32); b0t = sb(es2, "b0t", [1, 2, 2, GC], F32)
                    for wi, (ud, bd) in enumerate(((wup_d, w0_d), (aup_d, a0_d))):
                        for d in range(2):
                            S.dma(upf[:, wi, d, :], ud[d, :, c0:c0 + GC], [], [upf.b])
                            S.dma(b0f[:, wi, d, :], bd[d:d + 1, c0:c0 + GC], [], [b0f.b])
                    S.op("dve", lambda: nc.vector.tensor_copy(out=upw[:], in_=upf[:]), [upf.b], [upw.b])
                    S.op("dve", lambda: nc.vector.tensor_copy(out=b0h[:], in_=b0f[:]), [b0f.b], [b0h.b])
                    S.op("dve", lambda: nc.vector.tensor_tensor(out=b0t[:], in0=b0f[:], in1=b0h[:], op=ALU.subtract),
                         [b0f.b, b0h.b], [b0t.b])
                    S.op("dve", lambda: nc.vector.tensor_copy(out=b0l[:], in_=b0t[:]), [b0t.b], [b0l.b])
                    S.barrier()
                prm = {}
                for nm, src in (("kk", kk_d), ("ka", ka_d), ("rk", rk_d), ("lxg", lxg_d), ("lxb", lxb_d)):
                    prm[nm] = sb(es, "prm_" + nm, [128, GC], F32)
                    bcast_load(prm[nm], 0, src[c0:c0 + GC], GC)

                xth = [sb(es, f"xth{i}", [128, 8, 130], BF16) for i in range(2)]
                xst = sb(es, "xst", [128, 8, 128], BF16)
                f32t = {n: sb(es, "w_" + n, [128, GC], F32) for n in
                        ("sg", "a", "e1", "e2", "e3", "kkr", "t1", "kd", "kd0", "vbon",
                         "sz", "yf", "ysum", "sq2", "rS", "kS", "vS")}
                bft_ = {n: sb(es, "h_" + n, [128, GC], BF16) for n in ("V", "kh", "bh", "kkt", "rt", "Zs", "nU", "ob")}
                f32p = [dict(f32t), dict(f32t)]
                for n_ in ("vbon", "sz", "yf"):
                    f32p[1][n_] = sb(es, "w1_" + n_, [128, GC], F32)
                bftp = [dict(bft_), dict(bft_)]
                for n_ in ("V", "kh", "bh"):
                    bftp[1][n_] = sb(es, "h1_" + n_, [128, GC], BF16)
                tl_ = sb(es, "tl_", [64, 3, 128], BF16)
                kT2 = [sb(es, f"kT{i}", [64, NH, 128], BF16) for i in range(2)]
                bT2 = [sb(es, f"bT{i}", [64, NH, 128], BF16) for i in range(2)]
                krT2 = [sb(es, f"krT{i}", [64, NH, 2, 128], BF16) for i in range(2)]
                oT = sb(es, "oT", [128, 4, 128], BF16)
                MA1 = sb(es, "MA1", [128, NH, 2, 128], BF16)
                MA2 = sb(es, "MA2", [128, NH, 2, 128], BF16)
                Xs = [[sb(es, f"Xs{i}{q}", [128, 4, 128], BF16) for q in range(2)] for i in range(2)]
                Qs = [[sb(es, f"Qs{i}{q}", [128, 4, 128], BF16) for q in range(2)] for i in range(2)]
                Xtq = [[sb(es, f"Xt{i}{q}", [128, 4, 128], BF16) for q in range(2)] for i in range(2)]
                TTq = [sb(es, f"TT{q}", [128, 4, 128], BF16) for q in range(2)]
                Hf = sb(es, "Hf", [64, NH, 64], F32); Hb = sb(es, "Hb", [64, NH, 64], BF16)
                Ht = sb(es, "Ht", [64, NH, 64], F32)
                gam2 = [sb(es, f"gam{i}", [64, NH], F32) for i in range(2)]
                ss = sb(es, "ss", [128, NH], F32); rn = sb(es, "rn", [128, NH], F32)
                bs = sb(es, "bs", [128, NH], F32)
                gs1 = sb(es, "gs1", [128, NH], F32); gs2 = sb(es, "gs2", [128, NH], F32)
                grs = sb(es, "grs", [128, NH], F32)
                if DBG.get("verbose"):
                    print("sbuf bytes remaining (rwkv scope):", nc.sbuf_bytes_remaining)

                def prep_w(dst, dcol, col, ncol):
                    mub, mua, mubb = f32t["e1"], f32t["e2"], f32t["e3"]
                    stg = [f32t["sg"], f32t["a"]]
                    bcast_load(mub, 0, mu_d[col - NCV:col - NCV + ncol], ncol)
                    S.op("dve", lambda: nc.vector.tensor_scalar(out=mua[:, 0:ncol], in0=mub[:, 0:ncol], scalar1=-1.0,
                                                                scalar2=1.0, op0=ALU.mult, op1=ALU.add), [mub.b], [mua.b])
                    S.op("dve", lambda: nc.vector.tensor_scalar(out=mubb[:, 0:ncol], in0=mub[:, 0:ncol], scalar1=0.5,
                                                                scalar2=None, op0=ALU.mult), [mub.b], [mubb.b])
                    for bi in range(ncol // 64):
                        w_ = stg[bi % 2]
                        wv = w_[:].rearrange("p (c n) -> p c n", n=64)
                        o = bi * 64
                        S.dma(wv, w_in_d[:, col + o:col + o + 64].rearrange("(c p) n -> p c n", p=128), [], [w_.b])
                        S.op("dve", lambda: nc.vector.tensor_tensor(
                            out=dst[:, 0:8, dcol + o:dcol + o + 64], in0=wv,
                            in1=bc(mua[:, o:o + 64].unsqueeze(1), [128, 8, 64]), op=ALU.mult), [w_.b, mua.b], [dst.b])
                        S.op("pool", lambda: nc.gpsimd.tensor_tensor(
                            out=dst[:, 8:16, dcol + o:dcol + o + 64], in0=wv,
                            in1=bc(mubb[:, o:o + 64].unsqueeze(1), [128, 8, 64]), op=ALU.mult), [w_.b, mubb.b], [dst.b])
                prep_w(wl, 0, NCV + 4096, 256)

                for dr in range(2):
                    bwd = dr == 1
                    esd = contextlib.ExitStack()
                    qlist = [3] if bwd else [0, 1, 2]
                    qmap = {q: j for j, q in enumerate(qlist)}
                    wq = sb(esd, "wq", [128, 16, len(qlist) * GC], BF16)
                    for q in qlist:
                        prep_w(wq, qmap[q] * GC, NCV + q * 1024 + c0, GC)
                    cb = 128 + dr * 896
                    m1 = cst[:, cb:cb + 256]; m2 = cst[:, cb + 256:cb + 512]; m3 = cst[:, cb + 512:cb + 640]
                    tinc = cst[:, cb + 640:cb + 768]; texc = cst[:, cb + 768:cb + 896]
                    ccol = cst[:, CST_COLS - 1:CST_COLS]
                    S.op("dve", lambda: nc.vector.memset(Hf[:], 0.0), [], [Hf.b])
                    S.op("dve", lambda: nc.vector.memset(Hb[:], 0.0), [], [Hb.b])
                    order = list(range(NT - 1, -1, -1)) if bwd else list(range(NT))
                    load_xth(xth[0], order[0])
                    v3 = lambda ap: ap.rearrange("p (h c) -> p h c", c=HD)

                    def front(oi, i):
                        par = oi % 2
                        t = f32p[par]; bft = bftp[par]
                        kT = kT2[par]; bT = bT2[par]; krT = krT2[par]; gam = gam2[par]
                        V = bft["V"]; nU = bft["nU"]
                        xh_ = xth[oi % 2]
                        if oi + 1 < NT:
                            load_xth(xth[(oi + 1) % 2], order[oi + 1])
                        if bwd:
                            yield
                            S.dma(t["yf"][:], YF[i * 128:(i + 1) * 128, :], [YF_bs[i]], [t["yf"].b])
                        yield
                        S.op("pool", lambda: nc.gpsimd.tensor_tensor(out=xst[:], in0=xh_[:, :, 0:128], in1=xh_[:, :, 2:130],
                                                                      op=ALU.add), [xh_.b], [xst.b])

                        def proj_fm(pt_ap, wcol, ncol):
                            ins = None
                            for kc in range(16):
                                rhs = xh_[:, kc, 1:129] if kc < 8 else xst[:, kc - 8, :]
                                ins = nc.tensor.matmul(pt_ap, lhsT=wl[:, kc, wcol:wcol + ncol], rhs=rhs,
                                                       start=(kc == 0), stop=(kc == 15))
                            return ins

                        def proj_tm(pt_ap, q):
                            ins = None
                            for kc in range(16):
                                lhsT = xh_[:, kc, 1:129] if kc < 8 else xst[:, kc - 8, :]
                                ins = nc.tensor.matmul(pt_ap, lhsT=lhsT, rhs=wq[:, kc, qmap[q] * GC:(qmap[q] + 1) * GC],
                                                       start=(kc == 0), stop=(kc == 15))
                            return ins

                        pc = pbank("f")
                        ncode = 2

                        def codes():
                            ins = proj_fm(pc[0:64, 0:128], dr * 64, 64)
                            ins = proj_fm(pc[0:64, 128:256], 128 + dr * 64, 64)
                            return ins
                        yield
                        S.op("pe", codes, [xh_.b, xst.b, wl.b], [pc.b])
                        yield
                        S.op("act", lambda: nc.scalar.activation(out=tl_[:, 0, :], in_=pc[0:64, 0:128], func=AF.Tanh),
                             [pc.b], [tl_.b])
                        yield
                        S.op("act", lambda: nc.scalar.copy(out=tl_[:, 1:ncode, :].rearrange("p a t -> p (a t)"),
                                                           in_=pc[0:64, 128:128 * ncode]), [pc.b], [tl_.b])

                        def lowrank(pt, ci, wi, d):
                            def f():
                                nc.tensor.matmul(pt[:], lhsT=tl_[:, ci, :], rhs=upw[:, wi, d, :], start=True, stop=False)
                                nc.tensor.matmul(pt[:], lhsT=onesr[:], rhs=b0h[:, wi, d, :], start=False, stop=False)
                                return nc.tensor.matmul(pt[:], lhsT=onesr[:], rhs=b0l[:, wi, d, :], start=False, stop=True)
                            S.op("pe", f, [tl_.b, upw.b, onesr.b, b0h.b, b0l.b], [pt.b])
                        yield
                        pd_ = pbank("f"); lowrank(pd_, 0, 0, dr)
                        yield
                        S.op("act", lambda: nc.scalar.activation(out=t["sg"][:], in_=pd_[:], func=AF.Sigmoid),
                             [pd_.b], [t["sg"].b])
                        yield
                        pa_ = pbank("f"); lowrank(pa_, 1, 1, dr)
                        yield
                        S.op("act", lambda: nc.scalar.activation(out=t["a"][:], in_=pa_[:], func=AF.Sigmoid),
                             [pa_.b], [t["a"].b])
                        sg = t["sg"]
                        pcum = pbank("f")
                        yield
                        S.op("pe", lambda: nc.tensor.matmul(pcum[:], lhsT=tinc, rhs=sg[:], start=True, stop=True),
                             [cst.b, sg.b], [pcum.b])
                        yield
                        S.op("act", lambda: nc.scalar.activation(out=t["e1"][:], in_=pcum[:], func=AF.Exp, scale=-1.0),
                             [pcum.b], [t["e1"].b])
                        yield
                        S.op("act", lambda: nc.scalar.activation(out=t["e3"][:], in_=pcum[:], func=AF.Exp),
                             [pcum.b], [t["e3"].b])
                        pcx = pbank("f")

                        yield
                        S.op("pe", lambda: nc.tensor.matmul(pcx[:], lhsT=texc, rhs=sg[:], start=True, stop=True),
                             [cst.b, sg.b], [pcx.b])
                        yield
                        S.op("act", lambda: nc.scalar.activation(out=t["e2"][:], in_=pcx[:], func=AF.Exp),
                             [pcx.b], [t["e2"].b])
                        pgm = pbank("f")

                        def gsum():
                            ins = None
                            for h in range(NH):
                                ins = nc.tensor.matmul(pgm[0:64, h:h + 1], lhsT=sg[:, h * HD:(h + 1) * HD], rhs=ccol,
                                                       start=True, stop=True)
                            return ins
                        yield
                        S.op("pe", gsum, [sg.b, cst.b], [pgm.b])
                        yield
                        S.op("act", lambda: nc.scalar.activation(out=gam[:], in_=pgm[0:64, 0:NH], func=AF.Exp), [pgm.b], [gam.b])

                        rS, kS, vS = t["rS"], t["kS"], t["vS"]
                        V = bft["V"]
                        v3 = lambda ap: ap.rearrange("p (h c) -> p h c", c=HD)
                        if not bwd:
                            pr = pbank("f"); S.op("pe", lambda: proj_tm(pr[:], 0), [xh_.b, xst.b, wq.b], [pr.b])
                            yield
                            S.op("act", lambda: nc.scalar.copy(out=rS[:], in_=pr[:]), [pr.b], [rS.b])
                            pk = pbank("f"); S.op("pe", lambda: proj_tm(pk[:], 1), [xh_.b, xst.b, wq.b], [pk.b])
                            yield
                            S.op("act", lambda: nc.scalar.copy(out=kS[:], in_=pk[:]), [pk.b], [kS.b])
                            pv = pbank("f"); S.op("pe", lambda: proj_tm(pv[:], 2), [xh_.b, xst.b, wq.b], [pv.b])
                            yield
                            S.op("act", lambda: nc.scalar.copy(out=vS[:], in_=pv[:]), [pv.b], [vS.b])
                            yield
                            S.op("act", lambda: nc.scalar.copy(out=V[:], in_=pv[:]), [pv.b], [V.b])
                            yield
                            S.op("dve", lambda: nc.vector.tensor_tensor(out=t["kkr"][:], in0=kS[:], in1=prm["kk"][:], op=ALU.mult),
                                 [kS.b, prm["kk"].b], [t["kkr"].b])
                            yield
                            S.op("pool", lambda: nc.gpsimd.tensor_tensor(out=t["t1"][:], in0=t["kkr"][:], in1=t["kkr"][:], op=ALU.mult),
                                 [t["kkr"].b], [t["t1"].b])
                            yield
                            S.op("dve", lambda: nc.vector.tensor_reduce(out=ss[:], in_=v3(t["t1"][:]), axis=AX.X, op=ALU.add),
                                 [t["t1"].b], [ss.b])
                            yield
                            S.op("act", lambda: nc.scalar.activation(out=rn[:], in_=ss[:], func=AF.Sqrt), [ss.b], [rn.b])
                            yield
                            S.op("dve", lambda: nc.vector.tensor_scalar(out=rn[:], in0=rn[:], scalar1=1e-12, scalar2=None,
                                                                        op0=ALU.max), [rn.b], [rn.b])
                            yield
                            S.op("dve", lambda: nc.vector.reciprocal(out=rn[:], in_=rn[:]), [rn.b], [rn.b])
                            yield
                            S.op("dve", lambda: nc.vector.tensor_tensor(out=v3(t["kkr"][:]), in0=v3(t["kkr"][:]),
                                                                        in1=bc(rn[:].unsqueeze(2), [128, NH, HD]), op=ALU.mult),
                                 [t["kkr"].b, rn.b], [t["kkr"].b])
                        else:
                            for j_, dst_ in enumerate((rS, kS, t["kkr"], vS, t["kd0"])):
                                yield
                                S.dma(dst_[:], RKV[i, :, j_, :], [RKV_bs[i]], [dst_.b])
                            yield
                            S.op("act", lambda: nc.scalar.copy(out=V[:], in_=vS[:]), [vS.b], [V.b])
                        yield
                        S.op("dve", lambda: nc.vector.scalar_tensor_tensor(out=t["t1"][:], in0=t["a"][:], scalar=-1.0,
                                                                             in1=prm["ka"][:], op0=ALU.add, op1=ALU.mult),
                             [t["a"].b, prm["ka"].b], [t["t1"].b])
                        yield
                        S.op("dve", lambda: nc.vector.scalar_tensor_tensor(out=t["kd"][:], in0=t["t1"][:], scalar=1.0,
                                                                           in1=kS[:], op0=ALU.add, op1=ALU.mult),
                             [t["t1"].b, kS.b], [t["kd"].b])
                        yield
                        S.op("pool", lambda: nc.gpsimd.tensor_tensor(out=t["t1"][:], in0=t["kkr"][:], in1=t["a"][:], op=ALU.mult),
                             [t["kkr"].b, t["a"].b], [t["t1"].b])
                        yield
                        S.op("dve", lambda: nc.vector.tensor_tensor(out=bft["kh"][:], in0=t["kd"][:], in1=t["e1"][:], op=ALU.mult),
                             [t["kd"].b, t["e1"].b], [bft["kh"].b])
                        yield
                        S.op("pool", lambda: nc.gpsimd.tensor_tensor(out=bft["bh"][:], in0=t["t1"][:], in1=t["e1"][:], op=ALU.mult),
                             [t["t1"].b, t["e1"].b], [bft["bh"].b])
                        yield
                        S.op("pool", lambda: nc.gpsimd.tensor_tensor(out=bft["kkt"][:], in0=t["kkr"][:], in1=t["e2"][:], op=ALU.mult),
                             [t["kkr"].b, t["e2"].b], [bft["kkt"].b])
                        yield
                        S.op("dve", lambda: nc.vector.tensor_tensor(out=bft["rt"][:], in0=rS[:], in1=t["e3"][:], op=ALU.mult),
                             [rS.b, t["e3"].b], [bft["rt"].b])
                        if not bwd:
                            for j_, src_ in enumerate((rS, kS, t["kkr"], vS, t["kd"])):
                                yield
                                S.dma(RKV[i, :, j_, :], src_[:], [src_.b], [RKV_bs[i]])
                        else:
                            yield
                            S.op("pool", lambda: nc.gpsimd.tensor_tensor(out=t["kd0"][:], in0=t["kd0"][:], in1=t["kd"][:], op=ALU.add),
                                 [t["kd0"].b, t["kd"].b], [t["kd0"].b])
                            yield
                            S.op("pool", lambda: nc.gpsimd.tensor_tensor(out=t["kd0"][:], in0=t["kd0"][:], in1=prm["rk"][:], op=ALU.mult),
                                 [t["kd0"].b, prm["rk"].b], [t["kd0"].b])
                            yield
                            S.op("dve", lambda: nc.vector.tensor_tensor(out=t["kd0"][:], in0=rS[:], in1=t["kd0"][:], op=ALU.mult),
                                 [rS.b, t["kd0"].b], [t["kd0"].b])
                            yield
                            S.op("dve", lambda: nc.vector.tensor_reduce(out=bs[:], in_=v3(t["kd0"][:]), axis=AX.X, op=ALU.add),
                                 [t["kd0"].b], [bs.b])
                            yield
                            S.op("dve", lambda: nc.vector.scalar_tensor_tensor(
                                out=v3(t["vbon"][:]), in0=v3(vS[:]), scalar=0.5, in1=bc(bs[:].unsqueeze(2), [128, NH, HD]),
                                op0=ALU.mult, op1=ALU.mult), [vS.b, bs.b], [t["vbon"].b])
                            pz = pbank("f"); S.op("pe", lambda: proj_tm(pz[:], 3), [xh_.b, xst.b, wq.b], [pz.b])
                            yield
                            S.op("act", lambda: nc.scalar.activation(out=t["sz"][:], in_=pz[:], func=AF.Silu), [pz.b], [t["sz"].b])

                        def trans8(src_, dst_ap, eng):
                            pt = pbank("f")
                            ptv = pt[:].bitcast(BF16)

                            def f():
                                ins = None
                                for h in range(NH):
                                    ins = nc.tensor.transpose(out=ptv[0:64, h * 128:(h + 1) * 128], in_=src_[:, h * HD:(h + 1) * HD],
                                                              identity=identb[:])
                                return ins
                            S.op("pe", f, [src_.b, identb.b], [pt.b])
                            src_v = ptv[0:64, :].rearrange("p (h t) -> p h t", t=128)
                            if eng == "act":
                                S.op("act", lambda: nc.scalar.copy(out=dst_ap[0], in_=src_v), [pt.b], [dst_ap[1]])
                            else:
                                S.op("dve", lambda: nc.vector.tensor_copy(out=dst_ap[0], in_=src_v), [pt.b], [dst_ap[1]])
                        yield
                        trans8(bft["kh"], (kT[:], kT.b), "act")
                        yield
                        trans8(bft["bh"], (bT[:], bT.b), "dve")
                        yield
                        trans8(bft["kkt"], (krT[:, :, 0, :], krT.b), "act")
                        yield
                        trans8(bft["rt"], (krT[:, :, 1, :], krT.b), "dve")

                    def back(oi, i):
                        par = oi % 2
                        t = f32p[par]; bft = bftp[par]
                        kT = kT2[par]; bT = bT2[par]; krT = krT2[par]; gam = gam2[par]
                        V = bft["V"]; nU = bft["nU"]
                        for hp in range(NH // 2):
                            for which, lT, dst, msk in ((0, kT, MA1, m1), (1, bT, MA2, m2)):
                                pm = pbank("b")

                                def f():
                                    ins = None
                                    for hh in range(2):
                                        h = hp * 2 + hh
                                        ins = nc.tensor.matmul(pm[:, hh * 256:(hh + 1) * 256], lhsT=lT[:, h, :],
                                                               rhs=krT[:, h, :, :].rearrange("p a t -> p (a t)"),
                                                               start=True, stop=True)
                                    return ins
                                yield
                                S.op("pe", f, [lT.b, krT.b], [pm.b])
                                yield
                                S.op("dve", lambda: nc.vector.tensor_tensor(
                                    out=dst[:, hp * 2:hp * 2 + 2, :, :].rearrange("p h a t -> p h (a t)"),
                                    in0=pm[:].rearrange("p (h x) -> p h x", h=2),
                                    in1=bc(msk.unsqueeze(1), [128, 2, 256]), op=ALU.mult), [pm.b, cst.b], [dst.b])
                        for hq in range(NH // 4):
                            pm = pbank("b")

                            def f():
                                ins = None
                                for hh in range(4):
                                    h = hq * 4 + hh
                                    ins = nc.tensor.matmul(pm[:, hh * 128:(hh + 1) * 128], lhsT=krT[:, h, 0, :],
                                                           rhs=bT[:, h, :], start=True, stop=True)
                                return ins
                            yield
                            S.op("pe", f, [krT.b, bT.b], [pm.b])
                            yield
                            S.op("dve", lambda: nc.vector.tensor_tensor(
                                out=Xtq[0][hq][:], in0=pm[:].rearrange("p (h x) -> p h x", h=4),
                                in1=bc(m3.unsqueeze(1), [128, 4, 128]), op=ALU.mult), [pm.b, cst.b], [Xtq[0][hq].b])

                        def quad_mm(pm, lhs_of, rhs_of):
                            def f():
                                ins = None
                                for hh in range(4):
                                    ins = nc.tensor.matmul(pm[:, hh * 128:(hh + 1) * 128], lhsT=lhs_of(hh), rhs=rhs_of(hh),
                                                           start=True, stop=True)
                                return ins
                            return f

                        def quad_copy(eng, dst, pm):
                            src_ = pm[:].rearrange("p (h x) -> p h x", h=4)
                            if eng == "act":
                                S.op("act", lambda: nc.scalar.copy(out=dst[:], in_=src_), [pm.b], [dst.b])
                            else:
                                S.op("dve", lambda: nc.vector.tensor_copy(out=dst[:], in_=src_), [pm.b], [dst.b])
                        for hq in range(2):
                            hs = slice(hq * 4, hq * 4 + 4)
                            x0 = lambda hh: MA2[:, hq * 4 + hh, 0, :]
                            xt0 = lambda hh: Xtq[0][hq][:, hh, :]
                            yield
                            S.op("pool", lambda: nc.gpsimd.tensor_tensor(
                                out=Qs[0][hq][:], in0=MA2[:, hs, 0, :], in1=bc(identb[:].unsqueeze(1), [128, 4, 128]),
                                op=ALU.add), [MA2.b, identb.b], [Qs[0][hq].b])
                            pm = pbank("b")
                            yield
                            S.op("pe", quad_mm(pm, xt0, x0), [Xtq[0][hq].b, MA2.b], [pm.b])
                            yield
                            quad_copy("act", Xs[1][hq], pm)
                            pm2 = pbank("b")
                            yield
                            S.op("pe", quad_mm(pm2, x0, xt0), [Xtq[0][hq].b, MA2.b], [pm2.b])
                            yield
                            quad_copy("act" if hq == 0 else "dve", Xtq[1][hq], pm2)
                        for lev in range(1, NLEV):
                            for hq in range(2):
                                Xk = Xs[lev % 2][hq]; Xtk = Xtq[lev % 2][hq]; Qp = Qs[(lev - 1) % 2][hq]
                                xk = lambda hh: Xk[:, hh, :]
                                xtk = lambda hh: Xtk[:, hh, :]
                                qp = lambda hh: Qp[:, hh, :]
                                if lev <= NLEV - 3:
                                    pmA = pbank("b")
                                    yield
                                    S.op("pe", quad_mm(pmA, xtk, xk), [Xk.b, Xtk.b], [pmA.b])
                                    yield
                                    quad_copy("act", Xs[(lev + 1) % 2][hq], pmA)
                                pmB = pbank("b")
                                qdst = Qs[lev % 2][hq] if lev < NLEV - 1 else TTq[hq]
                                yield
                                S.op("pe", quad_mm(pmB, xtk, qp), [Xtk.b, Qp.b], [pmB.b])
                                yield
                                S.op("dve", lambda: nc.vector.tensor_tensor(out=qdst[:], in0=pmB[:].rearrange("p (h x) -> p h x", h=4),
                                                                            in1=Qp[:], op=ALU.add), [pmB.b, Qp.b], [qdst.b])
                                if lev <= NLEV - 2:
                                    pmC = pbank("b")
                                    yield
                                    S.op("pe", quad_mm(pmC, xk, xtk), [Xk.b, Xtk.b], [pmC.b])
                                    yield
                                    quad_copy("act" if hq == 0 else "dve", Xtq[(lev + 1) % 2][hq], pmC)

                        Vh = lambda h: V[:, h * HD:(h + 1) * HD]
                        pzz = pbank("b")

                        def fz():
                            ins = None
                            for h in range(NH):
                                nc.tensor.matmul(pzz[:, h * HD:(h + 1) * HD], lhsT=krT[:, h, 0, :],
                                                 rhs=Hb[:, h, :], start=True, stop=False)
                                ins = nc.tensor.matmul(pzz[:, h * HD:(h + 1) * HD], lhsT=MA1[:, h, 0, :], rhs=Vh(h),
                                                       start=False, stop=True)
                            return ins
                        yield
                        S.op("pe", fz, [krT.b, Hb.b, MA1.b, V.b], [pzz.b])
                        yield
                        S.op("act", lambda: nc.scalar.copy(out=bft["Zs"][:], in_=pzz[:]), [pzz.b], [bft["Zs"].b])
                        pu = pbank("b")

                        def fu():
                            ins = None
                            for h in range(NH):
                                ins = nc.tensor.matmul(pu[:, h * HD:(h + 1) * HD], lhsT=TTq[h // 4][:, h % 4, :],
                                                       rhs=bft["Zs"][:, h * HD:(h + 1) * HD], start=True, stop=True)
                            return ins
                        yield
                        S.op("pe", fu, [TTq[0].b, TTq[1].b, bft["Zs"].b], [pu.b])
                        nU = bft["nU"]
                        yield
                        S.op("act", lambda: nc.scalar.activation(out=nU[:], in_=pu[:], func=AF.Identity, scale=-1.0), [pu.b], [nU.b])
                        py = pbank("b")

                        def fy():
                            ins = None
                            for h in range(NH):
                                o = slice(h * HD, (h + 1) * HD)
                                nc.tensor.matmul(py[:, o], lhsT=krT[:, h, 1, :], rhs=Hb[:, h, :],
                                                 start=True, stop=False)
                                nc.tensor.matmul(py[:, o], lhsT=MA1[:, h, 1, :], rhs=Vh(h), start=False, stop=False)
                                ins = nc.tensor.matmul(py[:, o], lhsT=MA2[:, h, 1, :], rhs=nU[:, o], start=False, stop=True)
                            return ins
                        yield
                        S.op("pe", fy, [krT.b, Hb.b, MA1.b, MA2.b, V.b, nU.b], [py.b])
                        ph = pbank("b")

                        def fh():
                            ins = None
                            for h in range(NH):
                                o = slice(h * HD, (h + 1) * HD)
                                nc.tensor.matmul(ph[0:64, o], lhsT=bft["kh"][:, o], rhs=V[:, o], start=True, stop=False)
                                ins = nc.tensor.matmul(ph[0:64, o], lhsT=bft["bh"][:, o], rhs=nU[:, o], start=False, stop=True)
                            return ins
                        yield
                        S.op("pe", fh, [bft["kh"].b, bft["bh"].b, V.b, nU.b], [ph.b])
                        yield
                        S.op("dve", lambda: nc.vector.tensor_tensor(out=Ht[:], in0=ph[0:64, :].rearrange("p (h v) -> p h v", v=HD),
                                                                    in1=Hf[:], op=ALU.add), [ph.b, Hf.b], [Ht.b])
                        yield
                        S.op("dve", lambda: nc.vector.tensor_tensor(out=Hf[:], in0=Ht[:], in1=bc(gam[:].unsqueeze(2), [64, NH, HD]),
                                                                    op=ALU.mult), [Ht.b, gam.b], [Hf.b])
                        nxt_i = i - 1 if bwd else i + 1
                        bt = i if bwd else i + 1
                        if 0 <= nxt_i < NT and bt % BP == 0:
                            bidx = bt // BP - 1
                            yield
                            S.op("dve", lambda: nc.vector.tensor_scalar(out=Hf[:], in0=Hf[:], scalar1=keep[0:64, bidx:bidx + 1],
                                                                        scalar2=None, op0=ALU.mult), [Hf.b, keep.b], [Hf.b])
                        yield
                        S.op("act", lambda: nc.scalar.copy(out=Hb[:], in_=Hf[:]), [Hf.b], [Hb.b])

                        if not bwd:
                            yield
                            S.op("act", lambda: nc.scalar.copy(out=t["ysum"][:], in_=py[:]), [py.b], [t["ysum"].b])
                            yield
                            S.dma(YF[i * 128:(i + 1) * 128, :], t["ysum"][:], [t["ysum"].b], [YF_bs[i]])
                        else:
                            yield
                            S.op("dve", lambda: nc.vector.tensor_tensor(out=t["ysum"][:], in0=py[:], in1=t["yf"][:], op=ALU.add),
                                 [py.b, t["yf"].b], [t["ysum"].b])
                            yield
                            S.op("dve", lambda: nc.vector.tensor_reduce(out=gs1[:], in_=v3(t["ysum"][:]), axis=AX.X, op=ALU.add),
                                 [t["ysum"].b], [gs1.b])
                            yield
                            S.op("dve", lambda: nc.vector.tensor_scalar(out=gs1[:], in0=gs1[:], scalar1=-1.0 / HD, scalar2=None,
                                                                        op0=ALU.mult), [gs1.b], [gs1.b])
                            yield
                            S.op("dve", lambda: nc.vector.tensor_tensor(out=v3(t["ysum"][:]), in0=v3(t["ysum"][:]),
                                                                        in1=bc(gs1[:].unsqueeze(2), [128, NH, HD]), op=ALU.add),
                                 [t["ysum"].b, gs1.b], [t["ysum"].b])
                            yield
                            S.op("pool", lambda: nc.gpsimd.tensor_tensor(out=t["sq2"][:], in0=t["ysum"][:], in1=t["ysum"][:], op=ALU.mult),
                                 [t["ysum"].b], [t["sq2"].b])
                            yield
                            S.op("dve", lambda: nc.vector.tensor_reduce(out=gs2[:], in_=v3(t["sq2"][:]), axis=AX.X, op=ALU.add),
                                 [t["sq2"].b], [gs2.b])
                            yield
                            S.op("dve", lambda: nc.vector.tensor_scalar(out=gs2[:], in0=gs2[:], scalar1=1.0 / HD, scalar2=LNX_EPS,
                                                                        op0=ALU.mult, op1=ALU.add), [gs2.b], [gs2.b])
                            yield
                            S.op("act", lambda: nc.scalar.activation(out=grs[:], in_=gs2[:], func=AF.Sqrt), [gs2.b], [grs.b])
                            yield
                            S.op("dve", lambda: nc.vector.reciprocal(out=grs[:], in_=grs[:]), [grs.b], [grs.b])
                            yield
                            S.op("dve", lambda: nc.vector.tensor_tensor(out=v3(t["ysum"][:]), in0=v3(t["ysum"][:]),
                                                                        in1=bc(grs[:].unsqueeze(2), [128, NH, HD]), op=ALU.mult),
                                 [t["ysum"].b, grs.b], [t["ysum"].b])
                            yield
                            S.op("pool", lambda: nc.gpsimd.tensor_tensor(out=t["ysum"][:], in0=t["ysum"][:], in1=prm["lxg"][:], op=ALU.mult),
                                 [t["ysum"].b, prm["lxg"].b], [t["ysum"].b])
                            yield
                            S.op("pool", lambda: nc.gpsimd.tensor_tensor(out=t["ysum"][:], in0=t["ysum"][:], in1=prm["lxb"][:], op=ALU.add),
                                 [t["ysum"].b, prm["lxb"].b], [t["ysum"].b])
                            yield
                            S.op("dve", lambda: nc.vector.tensor_tensor(out=t["ysum"][:], in0=t["ysum"][:], in1=t["vbon"][:], op=ALU.add),
                                 [t["ysum"].b, t["vbon"].b], [t["ysum"].b])
                            yield
                            S.op("dve", lambda: nc.vector.tensor_tensor(out=bft["ob"][:], in0=t["ysum"][:], in1=t["sz"][:], op=ALU.mult),
                                 [t["ysum"].b, t["sz"].b], [bft["ob"].b])
                            pto = pbank("b")
                            ptvo = pto[:].bitcast(BF16)

                            def fo():
                                ins = None
                                for blk in range(4):
                                    ins = nc.tensor.transpose(out=ptvo[:, blk * 128:(blk + 1) * 128],
                                                              in_=bft["ob"][:, blk * 128:(blk + 1) * 128], identity=identb[:])
                                return ins
                            yield
                            S.op("pe", fo, [bft["ob"].b, identb.b], [pto.b])
                            yield
                            S.op("act", lambda: nc.scalar.copy(out=oT[:].rearrange("p b t -> p (b t)"), in_=ptvo[:, 0:512]),
                                 [pto.b], [oT.b])
                            yield
                            S.dma(YR[:, g * 4:(g + 1) * 4, i * 128:(i + 1) * 128], oT[:], [oT.b], [YR_bs[g][i]])

                    def interleave(ga, gb):
                        da = db = False
                        while not (da and db):
                            if not da:
                                try:
                                    next(ga)
                                except StopIteration:
                                    da = True
                            for _ in range(2):
                                if not db:
                                    try:
                                        next(gb)
                                    except StopIteration:
                                        db = True

                    prev = None
                    for oi, i in enumerate(order):
                        interleave(front(oi, i), back(*prev) if prev is not None else iter(()))
                        prev = (oi, i)
                    interleave(iter(()), back(*prev))
                    S.barrier()
                    esd.close()
                S.barrier()

        with contextlib.ExitStack() as es:
            if not DBG.get("c", True):
                raise_skip = True
            else:
                raise_skip = False
            wc = sb(es, "wc", [128, 8, NCV], BF16)
            wo = sb(es, "wo", [128, 16, D], BF16)
            with contextlib.ExitStack() as es2:
                wst = [sb(es2, f"cwst{i}", [128, 8, 512], F32) for i in range(2)]
                for bi in range(NCV // 512 if DBG.get("cw", True) else 0):
                    w_ = wst[bi % 2]
                    S.dma(w_[:], w_in_d[:, bi * 512:(bi + 1) * 512].rearrange("(c p) n -> p c n", p=128), [], [w_.b])
                    if bi % 2 == 0:
                        S.op("act", lambda: nc.scalar.copy(out=wc[:, :, bi * 512:(bi + 1) * 512], in_=w_[:]), [w_.b], [wc.b])
                    else:
                        S.op("dve", lambda: nc.vector.tensor_copy(out=wc[:, :, bi * 512:(bi + 1) * 512], in_=w_[:]), [w_.b], [wc.b])
                for bi in range(4 if DBG.get("cw", True) else 0):
                    w_ = wst[bi % 2]
                    S.dma(w_[:, 0:4, :], wout_d[bi * 512:(bi + 1) * 512, 0:512].rearrange("(c p) n -> p c n", p=128), [], [w_.b])
                    S.dma(w_[:, 4:8, :], wout_d[bi * 512:(bi + 1) * 512, 512:1024].rearrange("(c p) n -> p c n", p=128), [], [w_.b])
                    S.op("act", lambda: nc.scalar.copy(out=wo[:, bi * 4:(bi + 1) * 4, 0:512], in_=w_[:, 0:4, :]), [w_.b], [wo.b])
                    S.op("dve", lambda: nc.vector.tensor_copy(out=wo[:, bi * 4:(bi + 1) * 4, 512:1024], in_=w_[:, 4:8, :]), [w_.b], [wo.b])
                S.barrier()
            cpar = sb(es, "cpar", [128, 8, 4], F32)
            for j in range(3 if DBG.get("cp", True) else 0):
                S.dma(cpar[:, :, j:j + 1], conv_w_d[j, :].rearrange("(c p o) -> p c o", p=128, o=1), [], [cpar.b])
            S.dma(cpar[:, :, 3:4], conv_b_d.rearrange("(c p o) -> p c o", p=128, o=1), [], [cpar.b])
            gbc = sb(es, "c_g", [128, D], F32); bbc = sb(es, "c_b", [128, D], F32)
            g2 = sb(es, "c_g2", [128, D], F32); b2 = sb(es, "c_b2", [128, D], F32)
            for tl, src in ((gbc, emb_g_d), (bbc, emb_b_d), (g2, lng_d), (b2, lnb_d)):
                bcast_load(tl, 0, src, D)
            xth = [sb(es, f"cxth{i}", [128, 8, 130], BF16) for i in range(2)]
            ymT = [sb(es, f"ymT{i}", [128, 16, 128], BF16) for i in range(2)]
            xt = [sb(es, f"c_x{i}", [128, D], F32) for i in range(2)]
            xn = sb(es, "c_xn", [128, D], F32); xg = sb(es, "c_xg", [128, D], F32); xh = sb(es, "c_xh", [128, D], F32)
            sres = sb(es, "c_s", [128, D], F32); on = sb(es, "c_on", [128, D], F32); og = sb(es, "c_og", [128, D], F32)
            yo = [sb(es, f"c_yo{i}", [128, D], F32) for i in range(2)]
            st6 = sb(es, "c_st", [128, 2, SD], F32); mv = sb(es, "c_mv", [128, AD], F32)
            rstd = sb(es, "c_rs", [128, 1], F32); nb_ = sb(es, "c_nb", [128, 1], F32)
            st6b = sb(es, "c_stb", [128, 2, SD], F32); mvb = sb(es, "c_mvb", [128, AD], F32)
            rstdb = sb(es, "c_rsb", [128, 1], F32); nbb = sb(es, "c_nbb", [128, 1], F32)
            hS = sb(es, "c_hS", [128, 130], F32); pp = sb(es, "c_pp", [128, 130], F32)
            qq = sb(es, "c_qq", [128, 128], F32); szc = sb(es, "c_sz", [128, 128], F32)
            if not raise_skip:
                load_xth(xth[0], 0)
            for i in range(0 if raise_skip else NT):
                p = i % 2
                xh_ = xth[p]
                if i + 1 < NT:
                    load_xth(xth[(i + 1) % 2], i + 1)
                S.dma(xt[p][:], xs[i * 128:(i + 1) * 128, :], [], [xt[p].b])
                S.dma(ymT[p][:, 8:16, :], YR[:, :, i * 128:(i + 1) * 128], [YR_bs[0][i], YR_bs[1][i]], [ymT[p].b])
                for cbk in range(8):
                    pa = pbank("f"); pb = pbank("f")

                    def fa():
                        ins = None
                        for qi, q in enumerate((0, 2)):
                            for kc in range(8):
                                ins = nc.tensor.matmul(pa[:, qi * 130:(qi + 1) * 130],
                                                       lhsT=wc[:, kc, q * 1024 + cbk * 128:q * 1024 + (cbk + 1) * 128],
                                                       rhs=xh_[:, kc, :], start=(kc == 0), stop=(kc == 7))
                        return ins

                    def fb():
                        ins = None
                        for qi, q in enumerate((1, 3)):
                            for kc in range(8):
                                ins = nc.tensor.matmul(pb[:, qi * 128:(qi + 1) * 128],
                                                       lhsT=wc[:, kc, q * 1024 + cbk * 128:q * 1024 + (cbk + 1) * 128],
                                                       rhs=xh_[:, kc, 1:129], start=(kc == 0), stop=(kc == 7))
                        return ins
                    S.op("pe", fa, [wc.b, xh_.b], [pa.b])
                    S.op("pe", fb, [wc.b, xh_.b], [pb.b])
                    S.op("act", lambda: nc.scalar.copy(out=hS[:], in_=pa[:, 0:130]), [pa.b], [hS.b])
                    S.op("dve", lambda: nc.vector.tensor_tensor(out=pp[:], in0=hS[:], in1=pa[:, 130:260], op=ALU.mult),
                         [hS.b, pa.b], [pp.b])
                    S.op("dve", lambda: nc.vector.tensor_scalar(out=qq[:], in0=pp[:, 1:129], scalar1=cpar[:, cbk, 1:2],
                                                                scalar2=cpar[:, cbk, 3:4], op0=ALU.mult, op1=ALU.add),
                         [pp.b, cpar.b], [qq.b])
                    S.op("dve", lambda: nc.vector.scalar_tensor_tensor(out=qq[:], in0=pp[:, 0:128], scalar=cpar[:, cbk, 0:1],
                                                                         in1=qq[:], op0=ALU.mult, op1=ALU.add),
                         [pp.b, cpar.b, qq.b], [qq.b])
                    S.op("dve", lambda: nc.vector.scalar_tensor_tensor(out=qq[:], in0=pp[:, 2:130], scalar=cpar[:, cbk, 2:3],
                                                                         in1=qq[:], op0=ALU.mult, op1=ALU.add),
                         [pp.b, cpar.b, qq.b], [qq.b])
                    S.op("act", lambda: nc.scalar.activation(out=szc[:], in_=pb[:, 128:256], func=AF.Silu), [pb.b], [szc.b])
                    S.op("dve", lambda: nc.vector.tensor_tensor(out=qq[:], in0=qq[:], in1=pb[:, 0:128], op=ALU.mult),
                         [qq.b, pb.b], [qq.b])
                    S.op("pool", lambda: nc.gpsimd.tensor_tensor(out=ymT[p][:, cbk, :], in0=qq[:], in1=szc[:], op=ALU.mult),
                         [qq.b, szc.b], [ymT[p].b])
                po = [pbank("b"), pbank("b")]
                for hf in range(2):
                    def fo():
                        ins = None
                        for mc in range(16):
                            ins = nc.tensor.matmul(po[hf][:], lhsT=ymT[p][:, mc, :], rhs=wo[:, mc, hf * 512:(hf + 1) * 512],
                                                   start=(mc == 0), stop=(mc == 15))
                        return ins
                    S.op("pe", fo, [ymT[p].b, wo.b], [po[hf].b])
                ln_stats("c", xt[p], mv, rstd, nb_, st6, LN_EPS)
                S.op("act", lambda: nc.scalar.activation(out=xn[:], in_=xt[p][:], func=AF.Identity, bias=nb_[:, 0:1],
                                                         scale=rstd[:, 0:1]), [xt[p].b, rstd.b, nb_.b], [xn.b])
                S.op("dve", lambda: nc.vector.tensor_tensor(out=xg[:], in0=xn[:], in1=gbc[:], op=ALU.mult), [xn.b, gbc.b], [xg.b])
                S.op("pool", lambda: nc.gpsimd.tensor_tensor(out=xh[:], in0=xg[:], in1=bbc[:], op=ALU.add), [xg.b, bbc.b], [xh.b])
                for hf in range(2):
                    o = slice(hf * 512, (hf + 1) * 512)
                    S.op("dve", lambda: nc.vector.scalar_tensor_tensor(out=sres[:, o], in0=xh[:, o], scalar=DN_ALPHA,
                                                                       in1=po[hf][:], op0=ALU.mult, op1=ALU.add),
                         [xh.b, po[hf].b], [sres.b])
                ln_stats("c2", sres, mvb, rstdb, nbb, st6b, LN_EPS)
                S.op("act", lambda: nc.scalar.activation(out=on[:], in_=sres[:], func=AF.Identity, bias=nbb[:, 0:1],
                                                         scale=rstdb[:, 0:1]), [sres.b, rstdb.b, nbb.b], [on.b])
                S.op("dve", lambda: nc.vector.tensor_tensor(out=og[:], in0=on[:], in1=g2[:], op=ALU.mult), [on.b, g2.b], [og.b])
                S.op("pool", lambda: nc.gpsimd.tensor_tensor(out=yo[p][:], in0=og[:], in1=b2[:], op=ALU.add), [og.b, b2.b], [yo[p].b])
                S.dma(ys[i * 128:(i + 1) * 128, :], yo[p][:], [yo[p].b], [ys_bs[i]])
            S.barrier()

        S.finish()
    return nc


CST_COLS = 128 + 2 * 896 + 1
DBG = {}
NAMES = {}


def make_consts():
    r = np.arange(128)[:, None]
    c = np.arange(128)[None, :]
    SU = (r < c).astype(np.float32); IU = (r <= c).astype(np.float32)
    SL = (r > c).astype(np.float32); IL = (r >= c).astype(np.float32)
    parts = [np.eye(128, dtype=np.float32)]
    for (S_, I_, St) in ((SU, IU, SL), (SL, IL, SU)):
        parts += [S_, I_, -S_, I_, -St, CDEC * I_, CDEC * S_]
    parts.append(np.full((128, 1), CDEC, np.float32))
    out = np.concatenate(parts, axis=1).astype(np.float32)
    assert out.shape[1] == CST_COLS
    return np.ascontiguousarray(out)


W_NAMES = ["emb_ln_g", "emb_ln_b", "w_in", "conv_w", "conv_b", "shift_mu", "w0", "w_up", "a0", "a_up",
           "k_k", "k_a", "r_k", "lnx_g", "lnx_b", "w_out", "ln_g", "ln_b"]


def weight_map(inp):
    m = {}
    for n in W_NAMES:
        a = np.asarray(inp[n], dtype=np.float32)
        if n not in ("emb_ln_g", "emb_ln_b"):
            a = a[0]
        if n == "r_k":
            a = a.reshape(1024)
        m[n] = np.ascontiguousarray(a)
    m["cst"] = make_consts()
    return m


_NC_CACHE = {}


def run_streams(streams, keeps, inp, NT, BP):
    key = (NT, BP)
    if key not in _NC_CACHE:
        _NC_CACHE[key] = build(NT, BP)
    nc = _NC_CACHE[key]
    wm = weight_map(inp)
    in_maps = []
    for s, k in zip(streams, keeps):
        d = dict(wm)
        d["xs"] = np.ascontiguousarray(s, dtype=np.float32)
        d["keep"] = np.ascontiguousarray(k, dtype=np.float32)
        in_maps.append(d)
    res = run_bass_kernel_spmd(nc, in_maps, core_ids=list(range(len(streams))))
    return [r["ys"] for r in res.results]


def kernel(**inp):
    xp = np.asarray(inp["x_prompt"], dtype=np.float32)
    xsm = np.asarray(inp["x_sample"], dtype=np.float32)
    NT, BP = 128, 16
    NB = NT // BP - 1
    ntok = NT * 128
    streams = [xsm[0], xsm[1], xp.reshape(ntok, D)]
    keeps = [np.ones((128, NB), np.float32), np.ones((128, NB), np.float32), np.zeros((128, NB), np.float32)]
    for _ in range(5):
        streams.append(np.zeros((ntok, D), np.float32))
        keeps.append(np.zeros((128, NB), np.float32))
    outs = run_streams(streams, keeps, inp, NT, BP)
    y_sample = np.stack([outs[0], outs[1]], axis=0).reshape(2, 16384, D)
    y_prompt = outs[2].reshape(8, 2048, D)
    return (y_prompt.astype(np.float32), y_sample.astype(np.float32))
```

```python
import contextlib
import numpy as np
import concourse.bass as bass
import concourse.mybir as mybir
from concourse.bass_utils import run_bass_kernel_spmd

F32 = mybir.dt.float32
BF16 = mybir.dt.bfloat16
AF = mybir.ActivationFunctionType
ALU = mybir.AluOpType
AX = mybir.AxisListType

D = 1024
NCV = 4096
NRW = 4352
NIN = NCV + NRW
HD = 64
LN_EPS = 1e-5
LNX_EPS = 64e-5
DN_ALPHA = 2.0 ** 0.25
CDEC = -float(np.exp(-0.5))
NG = 2
GC = 512
NH = 8
NLEV = 7


class Buf:
    __slots__ = ("name", "w", "rs", "excl")

    def __init__(self, name):
        self.name = name
        self.w = None
        self.rs = {}
        self.excl = False


class Sch:
    R = 4
    ND = 24

    def __init__(self, nc, es):
        self.nc = nc
        self.eng = {"pe": nc.tensor, "act": nc.scalar, "dve": nc.vector, "pool": nc.gpsimd, "sp": nc.sync}
        self.sems = {e: [es.enter_context(nc.semaphore(f"s_{e}{i}")) for i in range(self.R)]
                     for e in ("pe", "act", "dve", "pool")}
        self.cnt = {e: 0 for e in self.sems}
        self.known = {e: {} for e in self.eng}
        self.dsems = [es.enter_context(nc.semaphore(f"s_d{i}")) for i in range(self.ND)]
        self.dval = [0] * self.ND
        self.dnext = 0
        self.all_bufs = []

    def buf(self, name):
        b = Buf(name)
        self.all_bufs.append(b)
        return b

    def _wait(self, waiter, ev):
        key = (ev[0], ev[1])
        if ev[0] == "E" and ev[1] == waiter and waiter == "pe":
            return
        if self.known[waiter].get(key, -1) >= ev[2]:
            return
        if ev[0] == "E":
            sem = self.sems[ev[1]][ev[2] % self.R]
            val = ev[2] // self.R + 1
        else:
            sem = self.dsems[ev[1]]
            val = ev[2]
        self.eng[waiter].wait_ge(sem, val)
        self.known[waiter][key] = ev[2]

    def _deps(self, waiter, reads, writes):
        for b in reads:
            if b.w is not None:
                self._wait(waiter, b.w)
            if b.excl:
                for ev in b.rs.values():
                    if not (ev[0] == "E" and ev[1] == waiter):
                        self._wait(waiter, ev)
        for b in writes:
            if b.w is not None:
                self._wait(waiter, b.w)
            for ev in b.rs.values():
                self._wait(waiter, ev)

    def _commit(self, ev, reads, writes):
        for b in reads:
            b.rs[(ev[0], ev[1])] = ev
        for b in writes:
            b.w = ev
            b.rs = {}

    muted = False

    def stage(self, k):
        self.muted = k > DBG.get("stage", 99)

    def op(self, eng, fn, reads, writes):
        if self.muted:
            return
        self._deps(eng, reads, writes)
        ins = fn()
        idx = self.cnt[eng]
        self.cnt[eng] += 1
        ins.then_inc(self.sems[eng][idx % self.R], 1)
        self._commit(("E", eng, idx), reads, writes)

    def dma(self, out, in_, reads, writes, q="sp", **kw):
        if self.muted:
            return
        k = self.dnext
        self.dnext = (self.dnext + 1) % self.ND
        if self.dval[k] > 0:
            self._wait(q, ("D", k, self.dval[k]))
        self._deps(q, reads, writes)
        ins = self.eng[q].dma_start(out=out, in_=in_, **kw)
        self.dval[k] += 16
        ins.then_inc(self.dsems[k], 16)
        self._commit(("D", k, self.dval[k]), reads, writes)

    def barrier(self):
        self.muted = False
        for w in self.eng:
            for k in range(self.ND):
                if self.dval[k] > 0:
                    self._wait(w, ("D", k, self.dval[k]))
            for e in self.cnt:
                if self.cnt[e] > 0:
                    self._wait(w, ("E", e, self.cnt[e] - 1))

    def finish(self):
        for k in range(self.ND):
            if self.dval[k] > 0:
                self._wait("sp", ("D", k, self.dval[k]))
        for e in self.cnt:
            if self.cnt[e] > 0:
                self._wait("sp", ("E", e, self.cnt[e] - 1))


class TL:
    def __init__(self, sch, t, name):
        self.t = t
        self.b = sch.buf(name)

    def __getitem__(self, k):
        return self.t[k]


def bc(ap, shape):
    return ap.broadcast_to(list(shape))


def build(NT, BP):
    NTOK = NT * 128
    NB = max(1, NT // BP - 1)
    nc = bass.Bass("TRN2", target_bir_lowering=False)
    dt_in = lambda n, s: nc.dram_tensor(n, s, F32, kind="ExternalInput").ap()
    xs = dt_in("xs", [NTOK, D])
    keep_d = dt_in("keep", [128, NB])
    cst_d = dt_in("cst", [128, CST_COLS])
    emb_g_d = dt_in("emb_ln_g", [D]); emb_b_d = dt_in("emb_ln_b", [D])
    w_in_d = dt_in("w_in", [D, NIN])
    conv_w_d = dt_in("conv_w", [3, 1024]); conv_b_d = dt_in("conv_b", [1024])
    mu_d = dt_in("shift_mu", [NRW])
    w0_d = dt_in("w0", [2, 1024]); wup_d = dt_in("w_up", [2, 64, 1024])
    a0_d = dt_in("a0", [2, 1024]); aup_d = dt_in("a_up", [2, 64, 1024])
    kk_d = dt_in("k_k", [1024]); ka_d = dt_in("k_a", [1024]); rk_d = dt_in("r_k", [1024])
    lxg_d = dt_in("lnx_g", [1024]); lxb_d = dt_in("lnx_b", [1024])
    wout_d = dt_in("w_out", [2048, D])
    lng_d = dt_in("ln_g", [D]); lnb_d = dt_in("ln_b", [D])
    ys = nc.dram_tensor("ys", [NTOK, D], F32, kind="ExternalOutput").ap()
    XT = nc.dram_tensor("XT", [128, 8, NTOK + 2], BF16).ap()
    YF = nc.dram_tensor("YF", [NTOK, GC], F32).ap()
    YR = nc.dram_tensor("YR", [128, 8, NTOK], BF16).ap()
    RKV = nc.dram_tensor("RKV", [NT, 128, 5, GC], F32).ap()

    with contextlib.ExitStack() as es0:
        es0.enter_context(nc.allow_non_contiguous_dma(reason="small strided param / halo loads"))
        S = Sch(nc, es0)
        XT_bs = [S.buf(f"XT{i}") for i in range(NT + 2)]
        YF_bs = [S.buf(f"YF{i}") for i in range(NT)]
        RKV_bs = [S.buf(f"RKV{i}") for i in range(NT)]
        YR_bs = [[S.buf(f"YR{g}_{i}") for i in range(NT)] for g in range(NG)]
        ys_bs = [S.buf(f"ys{i}") for i in range(NT)]

        uid = [0]

        def sb(es, name, shape, dtype):
            uid[0] += 1
            nm = f"sb{uid[0]}_{name}"
            NAMES[name] = nm
            return TL(S, es.enter_context(nc.sbuf_tensor(nm, list(shape), dtype)), nm)

        PS = [TL(S, es0.enter_context(nc.psum_tensor(f"ps{i}", [128, 512], F32)), f"ps{i}") for i in range(8)]
        for p_ in PS:
            p_.b.excl = True
        prot = {"f": [0, [0, 1, 2, 3]], "b": [0, [4, 5, 6, 7]]}

        def pbank(kind):
            st = prot[kind]
            p = PS[st[1][st[0] % len(st[1])]]
            st[0] += 1
            return p

        cst = sb(es0, "cst", [128, CST_COLS], F32)
        S.dma(cst[:], cst_d, [], [cst.b])
        identb = sb(es0, "identb", [128, 128], BF16)
        S.op("dve", lambda: nc.vector.tensor_copy(out=identb[:], in_=cst[:, 0:128]), [cst.b], [identb.b])
        keep = sb(es0, "keep", [128, NB], F32)
        S.dma(keep[:], keep_d, [], [keep.b])
        zpad = sb(es0, "zpad", [128, 8, 1], BF16)
        S.op("pool", lambda: nc.gpsimd.memset(zpad[:], 0.0), [], [zpad.b])
        S.dma(XT[:, :, 0:1], zpad[:], [zpad.b], [XT_bs[0]])
        S.dma(XT[:, :, NTOK + 1:NTOK + 2], zpad[:], [zpad.b], [XT_bs[NT + 1]])

        onesf = sb(es0, "onesf", [1, 128], F32)
        S.op("pool", lambda: nc.gpsimd.memset(onesf[:], 1.0), [], [onesf.b])
        rowbuf = sb(es0, "rowbuf", [1, 1024], F32)

        def bcast_load(dst, dcol, src1d, ncol):
            S.dma(rowbuf[0:1, 0:ncol], src1d.rearrange("(o n) -> o n", o=1), [], [rowbuf.b])
            for c_ in range(0, ncol, 512):
                n_ = min(512, ncol - c_)
                pb_ = pbank("f")
                S.op("pe", lambda: nc.tensor.matmul(pb_[:, 0:n_], lhsT=onesf[0:1, :], rhs=rowbuf[0:1, c_:c_ + n_],
                                                    start=True, stop=True), [onesf.b, rowbuf.b], [pb_.b])
                S.op("act", lambda: nc.scalar.copy(out=dst[:, dcol + c_:dcol + c_ + n_], in_=pb_[:, 0:n_]), [pb_.b], [dst.b])

        def ln_stats(es_tag, src, mv, rstd, nb_, st6, eps):
            S.op("dve", lambda: nc.vector.bn_stats(out=st6[:, 0, :], in_=src[:, 0:512]), [src.b], [st6.b])
            S.op("dve", lambda: nc.vector.bn_stats(out=st6[:, 1, :], in_=src[:, 512:1024]), [src.b], [st6.b])
            S.op("dve", lambda: nc.vector.bn_aggr(out=mv[:], in_=st6[:]), [st6.b], [mv.b])
            S.op("act", lambda: nc.scalar.activation(out=rstd[:], in_=mv[:, 1:2], func=AF.Sqrt, bias=eps, scale=1.0),
                 [mv.b], [rstd.b])
            S.op("dve", lambda: nc.vector.reciprocal(out=rstd[:], in_=rstd[:]), [rstd.b], [rstd.b])
            S.op("dve", lambda: nc.vector.scalar_tensor_tensor(out=nb_[:], in0=mv[:, 0:1], scalar=-1.0, in1=rstd[:],
                                                               op0=ALU.mult, op1=ALU.mult), [mv.b, rstd.b], [nb_.b])

        SD = int(nc.vector.BN_STATS_DIM)
        AD = int(nc.vector.BN_AGGR_DIM)

        with contextlib.ExitStack() as es:
            gbc = sb(es, "p1_g", [128, D], F32); bbc = sb(es, "p1_b", [128, D], F32)
            bcast_load(gbc, 0, emb_g_d, D)
            bcast_load(bbc, 0, emb_b_d, D)
            xt = [sb(es, f"p1_x{i}", [128, D], F32) for i in range(2)]
            xn = [sb(es, f"p1_xn{i}", [128, D], F32) for i in range(2)]
            xg = [sb(es, f"p1_xg{i}", [128, D], F32) for i in range(2)]
            xh = [sb(es, f"p1_xh{i}", [128, D], BF16) for i in range(2)]
            xT = [sb(es, f"p1_xT{i}", [128, 8, 128], BF16) for i in range(2)]
            st6 = [sb(es, f"p1_st{i}", [128, 2, SD], F32) for i in range(2)]
            mv = [sb(es, f"p1_mv{i}", [128, AD], F32) for i in range(2)]
            rstd = [sb(es, f"p1_rs{i}", [128, 1], F32) for i in range(2)]
            nb_ = [sb(es, f"p1_nb{i}", [128, 1], F32) for i in range(2)]
            for i in range(NT if DBG.get("p1", True) else 0):
                p = i % 2
                S.dma(xt[p][:], xs[i * 128:(i + 1) * 128, :], [], [xt[p].b])
                ln_stats("p1", xt[p], mv[p], rstd[p], nb_[p], st6[p], LN_EPS)
                S.op("act", lambda: nc.scalar.activation(out=xn[p][:], in_=xt[p][:], func=AF.Identity,
                                                         bias=nb_[p][:, 0:1], scale=rstd[p][:, 0:1]),
                     [xt[p].b, rstd[p].b, nb_[p].b], [xn[p].b])
                S.op("dve", lambda: nc.vector.tensor_tensor(out=xg[p][:], in0=xn[p][:], in1=gbc[:], op=ALU.mult),
                     [xn[p].b, gbc.b], [xg[p].b])
                S.op("pool", lambda: nc.gpsimd.tensor_tensor(out=xh[p][:], in0=xg[p][:], in1=bbc[:], op=ALU.add),
                     [xg[p].b, bbc.b], [xh[p].b])
                pt = pbank("f")
                ptv = pt[:].bitcast(BF16)

                def tr():
                    ins = None
                    for c in range(8):
                        ins = nc.tensor.transpose(out=ptv[:, c * 128:(c + 1) * 128], in_=xh[p][:, c * 128:(c + 1) * 128],
                                                  identity=identb[:])
                    return ins
                S.op("pe", tr, [xh[p].b, identb.b], [pt.b])
                S.op("act", lambda: nc.scalar.copy(out=xT[p][:].rearrange("p c t -> p (c t)"), in_=ptv), [pt.b], [xT[p].b])
                S.dma(XT[:, :, 1 + i * 128:1 + (i + 1) * 128], xT[p][:], [xT[p].b], [XT_bs[i + 1]], q="act")
            S.barrier()

        def load_xth(xth, i):
            S.dma(xth[:], XT[:, :, i * 128:i * 128 + 130], [XT_bs[i], XT_bs[i + 1], XT_bs[i + 2]], [xth.b])
            if i % BP == 0 and i > 0:
                bidx = i // BP - 1
                S.op("dve", lambda: nc.vector.tensor_scalar(out=xth[:, :, 0:1], in0=xth[:, :, 0:1],
                                                            scalar1=keep[:, bidx:bidx + 1], scalar2=None, op0=ALU.mult),
                     [xth.b, keep.b], [xth.b])
            if (i + 1) % BP == 0 and i + 1 < NT:
                bidx = (i + 1) // BP - 1
                S.op("dve", lambda: nc.vector.tensor_scalar(out=xth[:, :, 129:130], in0=xth[:, :, 129:130],
                                                            scalar1=keep[:, bidx:bidx + 1], scalar2=None, op0=ALU.mult),
                     [xth.b, keep.b], [xth.b])

        for g in range(NG if DBG.get("rwkv", True) else 0):
            with contextlib.ExitStack() as es:
                c0 = g * GC
                wl = sb(es, "wl", [128, 16, 256], BF16)
                upw = sb(es, "upw", [64, 2, 2, GC], BF16)
                b0h = sb(es, "b0h", [1, 2, 2, GC], BF16); b0l = sb(es, "b0l", [1, 2, 2, GC], BF16)
                onesr = sb(es, "onesr", [1, 128], BF16)
                S.op("pool", lambda: nc.gpsimd.memset(onesr[:], 1.0), [], [onesr.b])
                with contextlib.ExitStack() as es2:
                    upf = sb(es2, "upf", [64, 2, 2, GC], F32)
                    b0f = sb(es2, "b0f", [1, 2, 2, GC], F32); b0t = sb(es2, "b0t", [1, 2, 2, GC], F32)
                    for wi, (ud, bd) in enumerate(((wup_d, w0_d), (aup_d, a0_d))):
                        for d in range(2):
                            S.dma(upf[:, wi, d, :], ud[d, :, c0:c0 + GC], [], [upf.b])
                            S.dma(b0f[:, wi, d, :], bd[d:d + 1, c0:c0 + GC], [], [b0f.b])
                    S.op("dve", lambda: nc.vector.tensor_copy(out=upw[:], in_=upf[:]), [upf.b], [upw.b])
                    S.op("dve", lambda: nc.vector.tensor_copy(out=b0h[:], in_=b0f[:]), [b0f.b], [b0h.b])
                    S.op("dve", lambda: nc.vector.tensor_tensor(out=b0t[:], in0=b0f[:], in1=b0h[:], op=ALU.subtract),
                         [b0f.b, b0h.b], [b0t.b])
                    S.op("dve", lambda: nc.vector.tensor_copy(out=b0l[:], in_=b0t[:]), [b0t.b], [b0l.b])
                    S.barrier()
                prm = {}
                for nm, src in (("kk", kk_d), ("ka", ka_d), ("rk", rk_d), ("lxg", lxg_d), ("lxb", lxb_d)):
                    prm[nm] = sb(es, "prm_" + nm, [128, GC], F32)
                    bcast_load(prm[nm], 0, src[c0:c0 + GC], GC)

                xth = [sb(es, f"xth{i}", [128, 8, 130], BF16) for i in range(2)]
                xst = sb(es, "xst", [128, 8, 128], BF16)
                f32t = {n: sb(es, "w_" + n, [128, GC], F32) for n in
                        ("sg", "a", "e1", "e2", "e3", "kkr", "t1", "kd", "kd0", "vbon",
                         "sz", "yf", "ysum", "sq2", "rS", "kS", "vS")}
                bft_ = {n: sb(es, "h_" + n, [128, GC], BF16) for n in ("V", "kh", "bh", "kkt", "rt", "Zs", "nU", "ob")}
                f32p = [dict(f32t), dict(f32t)]
                for n_ in ("vbon", "sz", "yf"):
                    f32p[1][n_] = sb(es, "w1_" + n_, [128, GC], F32)
                bftp = [dict(bft_), dict(bft_)]
                for n_ in ("V", "kh", "bh"):
                    bftp[1][n_] = sb(es, "h1_" + n_, [128, GC], BF16)
                tl_ = sb(es, "tl_", [64, 3, 128], BF16)
                kT2 = [sb(es, f"kT{i}", [64, NH, 128], BF16) for i in range(2)]
                bT2 = [sb(es, f"bT{i}", [64, NH, 128], BF16) for i in range(2)]
                krT2 = [sb(es, f"krT{i}", [64, NH, 2, 128], BF16) for i in range(2)]
                oT = sb(es, "oT", [128, 4, 128], BF16)
                MA1 = sb(es, "MA1", [128, NH, 2, 128], BF16)
                MA2 = sb(es, "MA2", [128, NH, 2, 128], BF16)
                Xs = [[sb(es, f"Xs{i}{q}", [128, 4, 128], BF16) for q in range(2)] for i in range(2)]
                Qs = [[sb(es, f"Qs{i}{q}", [128, 4, 128], BF16) for q in range(2)] for i in range(2)]
                Xtq = [[sb(es, f"Xt{i}{q}", [128, 4, 128], BF16) for q in range(2)] for i in range(2)]
                TTq = [sb(es, f"TT{q}", [128, 4, 128], BF16) for q in range(2)]
                Hf = sb(es, "Hf", [64, NH, 64], F32); Hb = sb(es, "Hb", [64, NH, 64], BF16)
                Ht = sb(es, "Ht", [64, NH, 64], F32)
                gam2 = [sb(es, f"gam{i}", [64, NH], F32) for i in range(2)]
                ss = sb(es, "ss", [128, NH], F32); rn = sb(es, "rn", [128, NH], F32)
                bs = sb(es, "bs", [128, NH], F32)
                gs1 = sb(es, "gs1", [128, NH], F32); gs2 = sb(es, "gs2", [128, NH], F32)
                grs = sb(es, "grs", [128, NH], F32)
                if DBG.get("verbose"):
                    print("sbuf bytes remaining (rwkv scope):", nc.sbuf_bytes_remaining)

                def prep_w(dst, dcol, col, ncol):
                    mub, mua, mubb = f32t["e1"], f32t["e2"], f32t["e3"]
                    stg = [f32t["sg"], f32t["a"]]
                    bcast_load(mub, 0, mu_d[col - NCV:col - NCV + ncol], ncol)
                    S.op("dve", lambda: nc.vector.tensor_scalar(out=mua[:, 0:ncol], in0=mub[:, 0:ncol], scalar1=-1.0,
                                                                scalar2=1.0, op0=ALU.mult, op1=ALU.add), [mub.b], [mua.b])
                    S.op("dve", lambda: nc.vector.tensor_scalar(out=mubb[:, 0:ncol], in0=mub[:, 0:ncol], scalar1=0.5,
                                                                scalar2=None, op0=ALU.mult), [mub.b], [mubb.b])
                    for bi in range(ncol // 64):
                        w_ = stg[bi % 2]
                        wv = w_[:].rearrange("p (c n) -> p c n", n=64)
                        o = bi * 64
                        S.dma(wv, w_in_d[:, col + o:col + o + 64].rearrange("(c p) n -> p c n", p=128), [], [w_.b])
                        S.op("dve", lambda: nc.vector.tensor_tensor(
                            out=dst[:, 0:8, dcol + o:dcol + o + 64], in0=wv,
                            in1=bc(mua[:, o:o + 64].unsqueeze(1), [128, 8, 64]), op=ALU.mult), [w_.b, mua.b], [dst.b])
                        S.op("pool", lambda: nc.gpsimd.tensor_tensor(
                            out=dst[:, 8:16, dcol + o:dcol + o + 64], in0=wv,
                            in1=bc(mubb[:, o:o + 64].unsqueeze(1), [128, 8, 64]), op=ALU.mult), [w_.b, mubb.b], [dst.b])
                prep_w(wl, 0, NCV + 4096, 256)

                for dr in range(2):
                    bwd = dr == 1
                    esd = contextlib.ExitStack()
                    qlist = [3] if bwd else [0, 1, 2]
                    qmap = {q: j for j, q in enumerate(qlist)}
                    wq = sb(esd, "wq", [128, 16, len(qlist) * GC], BF16)
                    for q in qlist:
                        prep_w(wq, qmap[q] * GC, NCV + q * 1024 + c0, GC)
                    RKN = ("rS", "kS", "kkr", "vS", "kd0")
                    if bwd:
                        for n_ in RKN:
                            f32p[1][n_] = sb(esd, "w1_" + n_, [128, GC], F32)

                    def load_rkv(par_, ti):
                        for j_, n_ in enumerate(RKN):
                            dst_ = f32p[par_][n_]
                            S.dma(dst_[:], RKV[ti, :, j_, :], [RKV_bs[ti]], [dst_.b])
                    cb = 128 + dr * 896
                    m1 = cst[:, cb:cb + 256]; m2 = cst[:, cb + 256:cb + 512]; m3 = cst[:, cb + 512:cb + 640]
                    tinc = cst[:, cb + 640:cb + 768]; texc = cst[:, cb + 768:cb + 896]
                    ccol = cst[:, CST_COLS - 1:CST_COLS]
                    S.op("dve", lambda: nc.vector.memset(Hf[:], 0.0), [], [Hf.b])
                    S.op("dve", lambda: nc.vector.memset(Hb[:], 0.0), [], [Hb.b])
                    order = list(range(NT - 1, -1, -1)) if bwd else list(range(NT))
                    load_xth(xth[0], order[0])
                    if bwd:
                        load_rkv(0, order[0])
                    v3 = lambda ap: ap.rearrange("p (h c) -> p h c", c=HD)

                    def front(oi, i):
                        par = oi % 2
                        t = f32p[par]; bft = bftp[par]
                        kT = kT2[par]; bT = bT2[par]; krT = krT2[par]; gam = gam2[par]
                        V = bft["V"]; nU = bft["nU"]
                        xh_ = xth[oi % 2]
                        if oi + 1 < NT:
                            load_xth(xth[(oi + 1) % 2], order[oi + 1])
                            if bwd:
                                load_rkv((oi + 1) % 2, order[oi + 1])
                        if bwd:
                            yield
                            S.dma(t["yf"][:], YF[i * 128:(i + 1) * 128, :], [YF_bs[i]], [t["yf"].b])
                        yield
                        S.op("pool", lambda: nc.gpsimd.tensor_tensor(out=xst[:], in0=xh_[:, :, 0:128], in1=xh_[:, :, 2:130],
                                                                      op=ALU.add), [xh_.b], [xst.b])

                        def proj_fm(pt_ap, wcol, ncol):
                            ins = None
                            for kc in range(16):
                                rhs = xh_[:, kc, 1:129] if kc < 8 else xst[:, kc - 8, :]
                                ins = nc.tensor.matmul(pt_ap, lhsT=wl[:, kc, wcol:wcol + ncol], rhs=rhs,
                                                       start=(kc == 0), stop=(kc == 15))
                            return ins

                        def proj_tm(pt_ap, q):
                            ins = None
                            for kc in range(16):
                                lhsT = xh_[:, kc, 1:129] if kc < 8 else xst[:, kc - 8, :]
                                ins = nc.tensor.matmul(pt_ap, lhsT=lhsT, rhs=wq[:, kc, qmap[q] * GC:(qmap[q] + 1) * GC],
                                                       start=(kc == 0), stop=(kc == 15))
                            return ins

                        pc = pbank("f")
                        ncode = 2

                        def codes():
                            ins = proj_fm(pc[0:64, 0:128], dr * 64, 64)
                            ins = proj_fm(pc[0:64, 128:256], 128 + dr * 64, 64)
                            return ins
                        yield
                        S.op("pe", codes, [xh_.b, xst.b, wl.b], [pc.b])
                        yield
                        S.op("act", lambda: nc.scalar.activation(out=tl_[:, 0, :], in_=pc[0:64, 0:128], func=AF.Tanh),
                             [pc.b], [tl_.b])
                        yield
                        S.op("act", lambda: nc.scalar.copy(out=tl_[:, 1:ncode, :].rearrange("p a t -> p (a t)"),
                                                           in_=pc[0:64, 128:128 * ncode]), [pc.b], [tl_.b])

                        def lowrank(pt, ci, wi, d):
                            def f():
                                nc.tensor.matmul(pt[:], lhsT=tl_[:, ci, :], rhs=upw[:, wi, d, :], start=True, stop=False)
                                nc.tensor.matmul(pt[:], lhsT=onesr[:], rhs=b0h[:, wi, d, :], start=False, stop=False)
                                return nc.tensor.matmul(pt[:], lhsT=onesr[:], rhs=b0l[:, wi, d, :], start=False, stop=True)
                            S.op("pe", f, [tl_.b, upw.b, onesr.b, b0h.b, b0l.b], [pt.b])
                        yield
                        pd_ = pbank("f"); lowrank(pd_, 0, 0, dr)
                        yield
                        S.op("act", lambda: nc.scalar.activation(out=t["sg"][:], in_=pd_[:], func=AF.Sigmoid),
                             [pd_.b], [t["sg"].b])
                        yield
                        pa_ = pbank("f"); lowrank(pa_, 1, 1, dr)
                        yield
                        S.op("act", lambda: nc.scalar.activation(out=t["a"][:], in_=pa_[:], func=AF.Sigmoid),
                             [pa_.b], [t["a"].b])
                        sg = t["sg"]
                        pcum = pbank("f")
                        yield
                        S.op("pe", lambda: nc.tensor.matmul(pcum[:], lhsT=tinc, rhs=sg[:], start=True, stop=True),
                             [cst.b, sg.b], [pcum.b])
                        yield
                        S.op("act", lambda: nc.scalar.activation(out=t["e1"][:], in_=pcum[:], func=AF.Exp, scale=-1.0),
                             [pcum.b], [t["e1"].b])
                        yield
                        S.op("act", lambda: nc.scalar.activation(out=t["e3"][:], in_=pcum[:], func=AF.Exp),
                             [pcum.b], [t["e3"].b])
                        pcx = pbank("f")

                        yield
                        S.op("pe", lambda: nc.tensor.matmul(pcx[:], lhsT=texc, rhs=sg[:], start=True, stop=True),
                             [cst.b, sg.b], [pcx.b])
                        yield
                        S.op("act", lambda: nc.scalar.activation(out=t["e2"][:], in_=pcx[:], func=AF.Exp),
                             [pcx.b], [t["e2"].b])
                        pgm = pbank("f")

                        def gsum():
                            ins = None
                            for h in range(NH):
                                ins = nc.tensor.matmul(pgm[0:64, h:h + 1], lhsT=sg[:, h * HD:(h + 1) * HD], rhs=ccol,
                                                       start=True, stop=True)
                            return ins
                        yield
                        S.op("pe", gsum, [sg.b, cst.b], [pgm.b])
                        yield
                        S.op("act", lambda: nc.scalar.activation(out=gam[:], in_=pgm[0:64, 0:NH], func=AF.Exp), [pgm.b], [gam.b])

                        rS, kS, vS = t["rS"], t["kS"], t["vS"]
                        V = bft["V"]
                        v3 = lambda ap: ap.rearrange("p (h c) -> p h c", c=HD)
                        if not bwd:
                            pr = pbank("f"); S.op("pe", lambda: proj_tm(pr[:], 0), [xh_.b, xst.b, wq.b], [pr.b])
                            yield
                            S.op("act", lambda: nc.scalar.copy(out=rS[:], in_=pr[:]), [pr.b], [rS.b])
                            pk = pbank("f"); S.op("pe", lambda: proj_tm(pk[:], 1), [xh_.b, xst.b, wq.b], [pk.b])
                            yield
                            S.op("act", lambda: nc.scalar.copy(out=kS[:], in_=pk[:]), [pk.b], [kS.b])
                            pv = pbank("f"); S.op("pe", lambda: proj_tm(pv[:], 2), [xh_.b, xst.b, wq.b], [pv.b])
                            yield
                            S.op("act", lambda: nc.scalar.copy(out=vS[:], in_=pv[:]), [pv.b], [vS.b])
                            yield
                            S.op("act", lambda: nc.scalar.copy(out=V[:], in_=pv[:]), [pv.b], [V.b])
                            yield
                            S.op("dve", lambda: nc.vector.tensor_tensor(out=t["kkr"][:], in0=kS[:], in1=prm["kk"][:], op=ALU.mult),
                                 [kS.b, prm["kk"].b], [t["kkr"].b])
                            yield
                            S.op("pool", lambda: nc.gpsimd.tensor_tensor(out=t["t1"][:], in0=t["kkr"][:], in1=t["kkr"][:], op=ALU.mult),
                                 [t["kkr"].b], [t["t1"].b])
                            yield
                            S.op("dve", lambda: nc.vector.tensor_reduce(out=ss[:], in_=v3(t["t1"][:]), axis=AX.X, op=ALU.add),
                                 [t["t1"].b], [ss.b])
                            yield
                            S.op("act", lambda: nc.scalar.activation(out=rn[:], in_=ss[:], func=AF.Sqrt), [ss.b], [rn.b])
                            yield
                            S.op("dve", lambda: nc.vector.tensor_scalar(out=rn[:], in0=rn[:], scalar1=1e-12, scalar2=None,
                                                                        op0=ALU.max), [rn.b], [rn.b])
                            yield
                            S.op("dve", lambda: nc.vector.reciprocal(out=rn[:], in_=rn[:]), [rn.b], [rn.b])
                            yield
                            S.op("dve", lambda: nc.vector.tensor_tensor(out=v3(t["kkr"][:]), in0=v3(t["kkr"][:]),
                                                                        in1=bc(rn[:].unsqueeze(2), [128, NH, HD]), op=ALU.mult),
                                 [t["kkr"].b, rn.b], [t["kkr"].b])
                        else:
                            yield
                            S.op("act", lambda: nc.scalar.copy(out=V[:], in_=vS[:]), [vS.b], [V.b])
                        yield
                        S.op("dve", lambda: nc.vector.scalar_tensor_tensor(out=t["t1"][:], in0=t["a"][:], scalar=-1.0,
                                                                             in1=prm["ka"][:], op0=ALU.add, op1=ALU.mult),
                             [t["a"].b, prm["ka"].b], [t["t1"].b])
                        yield
                        S.op("dve", lambda: nc.vector.scalar_tensor_tensor(out=t["kd"][:], in0=t["t1"][:], scalar=1.0,
                                                                           in1=kS[:], op0=ALU.add, op1=ALU.mult),
                             [t["t1"].b, kS.b], [t["kd"].b])
                        yield
                        S.op("pool", lambda: nc.gpsimd.tensor_tensor(out=t["t1"][:], in0=t["kkr"][:], in1=t["a"][:], op=ALU.mult),
                             [t["kkr"].b, t["a"].b], [t["t1"].b])
                        yield
                        S.op("dve", lambda: nc.vector.tensor_tensor(out=bft["kh"][:], in0=t["kd"][:], in1=t["e1"][:], op=ALU.mult),
                             [t["kd"].b, t["e1"].b], [bft["kh"].b])
                        yield
                        S.op("pool", lambda: nc.gpsimd.tensor_tensor(out=bft["bh"][:], in0=t["t1"][:], in1=t["e1"][:], op=ALU.mult),
                             [t["t1"].b, t["e1"].b], [bft["bh"].b])
                        yield
                        S.op("pool", lambda: nc.gpsimd.tensor_tensor(out=bft["kkt"][:], in0=t["kkr"][:], in1=t["e2"][:], op=ALU.mult),
                             [t["kkr"].b, t["e2"].b], [bft["kkt"].b])
                        yield
                        S.op("dve", lambda: nc.vector.tensor_tensor(out=bft["rt"][:], in0=rS[:], in1=t["e3"][:], op=ALU.mult),
                             [rS.b, t["e3"].b], [bft["rt"].b])
                        if not bwd:
                            for j_, src_ in enumerate((rS, kS, t["kkr"], vS, t["kd"])):
                                yield
                                S.dma(RKV[i, :, j_, :], src_[:], [src_.b], [RKV_bs[i]])
                        else:
                            yield
                            S.op("pool", lambda: nc.gpsimd.tensor_tensor(out=t["kd0"][:], in0=t["kd0"][:], in1=t["kd"][:], op=ALU.add),
                                 [t["kd0"].b, t["kd"].b], [t["kd0"].b])
                            yield
                            S.op("pool", lambda: nc.gpsimd.tensor_tensor(out=t["kd0"][:], in0=t["kd0"][:], in1=prm["rk"][:], op=ALU.mult),
                                 [t["kd0"].b, prm["rk"].b], [t["kd0"].b])
                            yield
                            S.op("dve", lambda: nc.vector.tensor_tensor(out=t["kd0"][:], in0=rS[:], in1=t["kd0"][:], op=ALU.mult),
                                 [rS.b, t["kd0"].b], [t["kd0"].b])
                            yield
                            S.op("dve", lambda: nc.vector.tensor_reduce(out=bs[:], in_=v3(t["kd0"][:]), axis=AX.X, op=ALU.add),
                                 [t["kd0"].b], [bs.b])
                            yield
                            S.op("dve", lambda: nc.vector.scalar_tensor_tensor(
                                out=v3(t["vbon"][:]), in0=v3(vS[:]), scalar=0.5, in1=bc(bs[:].unsqueeze(2), [128, NH, HD]),
                                op0=ALU.mult, op1=ALU.mult), [vS.b, bs.b], [t["vbon"].b])
                            pz = pbank("f"); S.op("pe", lambda: proj_tm(pz[:], 3), [xh_.b, xst.b, wq.b], [pz.b])
                            yield
                            S.op("act", lambda: nc.scalar.activation(out=t["sz"][:], in_=pz[:], func=AF.Silu), [pz.b], [t["sz"].b])

                        def trans8(src_, dst_ap, eng):
                            pt = pbank("f")
                            ptv = pt[:].bitcast(BF16)

                            def f():
                                ins = None
                                for h in range(NH):
                                    ins = nc.tensor.transpose(out=ptv[0:64, h * 128:(h + 1) * 128], in_=src_[:, h * HD:(h + 1) * HD],
                                                              identity=identb[:])
                                return ins
                            S.op("pe", f, [src_.b, identb.b], [pt.b])
                            src_v = ptv[0:64, :].rearrange("p (h t) -> p h t", t=128)
                            if eng == "act":
                                S.op("act", lambda: nc.scalar.copy(out=dst_ap[0], in_=src_v), [pt.b], [dst_ap[1]])
                            else:
                                S.op("dve", lambda: nc.vector.tensor_copy(out=dst_ap[0], in_=src_v), [pt.b], [dst_ap[1]])
                        yield
                        trans8(bft["kh"], (kT[:], kT.b), "act")
                        yield
                        trans8(bft["bh"], (bT[:], bT.b), "dve")
                        yield
                        trans8(bft["kkt"], (krT[:, :, 0, :], krT.b), "act")
                        yield
                        trans8(bft["rt"], (krT[:, :, 1, :], krT.b), "dve")

                    def back(oi, i):
                        par = oi % 2
                        t = f32p[par]; bft = bftp[par]
                        kT = kT2[par]; bT = bT2[par]; krT = krT2[par]; gam = gam2[par]
                        V = bft["V"]; nU = bft["nU"]
                        for hp in range(NH // 2):
                            for which, lT, dst, msk in ((0, kT, MA1, m1), (1, bT, MA2, m2)):
                                pm = pbank("b")

                                def f():
                                    ins = None
                                    for hh in range(2):
                                        h = hp * 2 + hh
                                        ins = nc.tensor.matmul(pm[:, hh * 256:(hh + 1) * 256], lhsT=lT[:, h, :],
                                                               rhs=krT[:, h, :, :].rearrange("p a t -> p (a t)"),
                                                               start=True, stop=True)
                                    return ins
                                yield
                                S.op("pe", f, [lT.b, krT.b], [pm.b])
                                yield
                                S.op("dve", lambda: nc.vector.tensor_tensor(
                                    out=dst[:, hp * 2:hp * 2 + 2, :, :].rearrange("p h a t -> p h (a t)"),
                                    in0=pm[:].rearrange("p (h x) -> p h x", h=2),
                                    in1=bc(msk.unsqueeze(1), [128, 2, 256]), op=ALU.mult), [pm.b, cst.b], [dst.b])
                        for hq in range(NH // 4):
                            pm = pbank("b")

                            def f():
                                ins = None
                                for hh in range(4):
                                    h = hq * 4 + hh
                                    ins = nc.tensor.matmul(pm[:, hh * 128:(hh + 1) * 128], lhsT=krT[:, h, 0, :],
                                                           rhs=bT[:, h, :], start=True, stop=True)
                                return ins
                            yield
                            S.op("pe", f, [krT.b, bT.b], [pm.b])
                            yield
                            S.op("dve", lambda: nc.vector.tensor_tensor(
                                out=Xtq[0][hq][:], in0=pm[:].rearrange("p (h x) -> p h x", h=4),
                                in1=bc(m3.unsqueeze(1), [128, 4, 128]), op=ALU.mult), [pm.b, cst.b], [Xtq[0][hq].b])

                        def quad_mm(pm, lhs_of, rhs_of):
                            def f():
                                ins = None
                                for hh in range(4):
                                    ins = nc.tensor.matmul(pm[:, hh * 128:(hh + 1) * 128], lhsT=lhs_of(hh), rhs=rhs_of(hh),
                                                           start=True, stop=True)
                                return ins
                            return f

                        def quad_copy(eng, dst, pm):
                            src_ = pm[:].rearrange("p (h x) -> p h x", h=4)
                            if eng == "act":
                                S.op("act", lambda: nc.scalar.copy(out=dst[:], in_=src_), [pm.b], [dst.b])
                            else:
                                S.op("dve", lambda: nc.vector.tensor_copy(out=dst[:], in_=src_), [pm.b], [dst.b])
                        for hq in range(2):
                            hs = slice(hq * 4, hq * 4 + 4)
                            x0 = lambda hh: MA2[:, hq * 4 + hh, 0, :]
                            xt0 = lambda hh: Xtq[0][hq][:, hh, :]
                            yield
                            S.op("pool", lambda: nc.gpsimd.tensor_tensor(
                                out=Qs[0][hq][:], in0=MA2[:, hs, 0, :], in1=bc(identb[:].unsqueeze(1), [128, 4, 128]),
                                op=ALU.add), [MA2.b, identb.b], [Qs[0][hq].b])
                            pm = pbank("b")
                            yield
                            S.op("pe", quad_mm(pm, xt0, x0), [Xtq[0][hq].b, MA2.b], [pm.b])
                            yield
                            quad_copy("act", Xs[1][hq], pm)
                            pm2 = pbank("b")
                            yield
                            S.op("pe", quad_mm(pm2, x0, xt0), [Xtq[0][hq].b, MA2.b], [pm2.b])
                            yield
                            quad_copy("act" if hq == 0 else "dve", Xtq[1][hq], pm2)
                        for lev in range(1, NLEV):
                            for hq in range(2):
                                Xk = Xs[lev % 2][hq]; Xtk = Xtq[lev % 2][hq]; Qp = Qs[(lev - 1) % 2][hq]
                                xk = lambda hh: Xk[:, hh, :]
                                xtk = lambda hh: Xtk[:, hh, :]
                                qp = lambda hh: Qp[:, hh, :]
                                if lev <= NLEV - 3:
                                    pmA = pbank("b")
                                    yield
                                    S.op("pe", quad_mm(pmA, xtk, xk), [Xk.b, Xtk.b], [pmA.b])
                                    yield
                                    quad_copy("act", Xs[(lev + 1) % 2][hq], pmA)
                                pmB = pbank("b")
                                qdst = Qs[lev % 2][hq] if lev < NLEV - 1 else TTq[hq]
                                yield
                                S.op("pe", quad_mm(pmB, xtk, qp), [Xtk.b, Qp.b], [pmB.b])
                                yield
                                S.op("dve", lambda: nc.vector.tensor_tensor(out=qdst[:], in0=pmB[:].rearrange("p (h x) -> p h x", h=4),
                                                                            in1=Qp[:], op=ALU.add), [pmB.b, Qp.b], [qdst.b])
                                if lev <= NLEV - 2:
                                    pmC = pbank("b")
                                    yield
                                    S.op("pe", quad_mm(pmC, xk, xtk), [Xk.b, Xtk.b], [pmC.b])
                                    yield
                                    quad_copy("act" if hq == 0 else "dve", Xtq[(lev + 1) % 2][hq], pmC)

                        Vh = lambda h: V[:, h * HD:(h + 1) * HD]
                        pzz = pbank("b")

                        def fz():
                            ins = None
                            for h in range(NH):
                                nc.tensor.matmul(pzz[:, h * HD:(h + 1) * HD], lhsT=krT[:, h, 0, :],
                                                 rhs=Hb[:, h, :], start=True, stop=False)
                                ins = nc.tensor.matmul(pzz[:, h * HD:(h + 1) * HD], lhsT=MA1[:, h, 0, :], rhs=Vh(h),
                                                       start=False, stop=True)
                            return ins
                        yield
                        S.op("pe", fz, [krT.b, Hb.b, MA1.b, V.b], [pzz.b])
                        yield
                        S.op("act", lambda: nc.scalar.copy(out=bft["Zs"][:], in_=pzz[:]), [pzz.b], [bft["Zs"].b])
                        pu = pbank("b")

                        def fu():
                            ins = None
                            for h in range(NH):
                                ins = nc.tensor.matmul(pu[:, h * HD:(h + 1) * HD], lhsT=TTq[h // 4][:, h % 4, :],
                                                       rhs=bft["Zs"][:, h * HD:(h + 1) * HD], start=True, stop=True)
                            return ins
                        yield
                        S.op("pe", fu, [TTq[0].b, TTq[1].b, bft["Zs"].b], [pu.b])
                        nU = bft["nU"]
                        yield
                        S.op("act", lambda: nc.scalar.activation(out=nU[:], in_=pu[:], func=AF.Identity, scale=-1.0), [pu.b], [nU.b])
                        py = pbank("b")

                        def fy():
                            ins = None
                            for h in range(NH):
                                o = slice(h * HD, (h + 1) * HD)
                                nc.tensor.matmul(py[:, o], lhsT=krT[:, h, 1, :], rhs=Hb[:, h, :],
                                                 start=True, stop=False)
                                nc.tensor.matmul(py[:, o], lhsT=MA1[:, h, 1, :], rhs=Vh(h), start=False, stop=False)
                                ins = nc.tensor.matmul(py[:, o], lhsT=MA2[:, h, 1, :], rhs=nU[:, o], start=False, stop=True)
                            return ins
                        yield
                        S.op("pe", fy, [krT.b, Hb.b, MA1.b, MA2.b, V.b, nU.b], [py.b])
                        ph = pbank("b")

                        def fh():
                            ins = None
                            for h in range(NH):
                                o = slice(h * HD, (h + 1) * HD)
                                nc.tensor.matmul(ph[0:64, o], lhsT=bft["kh"][:, o], rhs=V[:, o], start=True, stop=False)
                                ins = nc.tensor.matmul(ph[0:64, o], lhsT=bft["bh"][:, o], rhs=nU[:, o], start=False, stop=True)
                            return ins
                        yield
                        S.op("pe", fh, [bft["kh"].b, bft["bh"].b, V.b, nU.b], [ph.b])
                        yield
                        S.op("dve", lambda: nc.vector.tensor_tensor(out=Ht[:], in0=ph[0:64, :].rearrange("p (h v) -> p h v", v=HD),
                                                                    in1=Hf[:], op=ALU.add), [ph.b, Hf.b], [Ht.b])
                        yield
                        S.op("dve", lambda: nc.vector.tensor_tensor(out=Hf[:], in0=Ht[:], in1=bc(gam[:].unsqueeze(2), [64, NH, HD]),
                                                                    op=ALU.mult), [Ht.b, gam.b], [Hf.b])
                        nxt_i = i - 1 if bwd else i + 1
                        bt = i if bwd else i + 1
                        if 0 <= nxt_i < NT and bt % BP == 0:
                            bidx = bt // BP - 1
                            yield
                            S.op("dve", lambda: nc.vector.tensor_scalar(out=Hf[:], in0=Hf[:], scalar1=keep[0:64, bidx:bidx + 1],
                                                                        scalar2=None, op0=ALU.mult), [Hf.b, keep.b], [Hf.b])
                        yield
                        S.op("act", lambda: nc.scalar.copy(out=Hb[:], in_=Hf[:]), [Hf.b], [Hb.b])

                        if not bwd:
                            yield
                            S.op("act", lambda: nc.scalar.copy(out=t["ysum"][:], in_=py[:]), [py.b], [t["ysum"].b])
                            yield
                            S.dma(YF[i * 128:(i + 1) * 128, :], t["ysum"][:], [t["ysum"].b], [YF_bs[i]], q="act")
                        else:
                            yield
                            S.op("dve", lambda: nc.vector.tensor_tensor(out=t["ysum"][:], in0=py[:], in1=t["yf"][:], op=ALU.add),
                                 [py.b, t["yf"].b], [t["ysum"].b])
                            yield
                            S.op("dve", lambda: nc.vector.tensor_reduce(out=gs1[:], in_=v3(t["ysum"][:]), axis=AX.X, op=ALU.add),
                                 [t["ysum"].b], [gs1.b])
                            yield
                            S.op("dve", lambda: nc.vector.tensor_scalar(out=gs1[:], in0=gs1[:], scalar1=-1.0 / HD, scalar2=None,
                                                                        op0=ALU.mult), [gs1.b], [gs1.b])
                            yield
                            S.op("dve", lambda: nc.vector.tensor_tensor(out=v3(t["ysum"][:]), in0=v3(t["ysum"][:]),
                                                                        in1=bc(gs1[:].unsqueeze(2), [128, NH, HD]), op=ALU.add),
                                 [t["ysum"].b, gs1.b], [t["ysum"].b])
                            yield
                            S.op("pool", lambda: nc.gpsimd.tensor_tensor(out=t["sq2"][:], in0=t["ysum"][:], in1=t["ysum"][:], op=ALU.mult),
                                 [t["ysum"].b], [t["sq2"].b])
                            yield
                            S.op("dve", lambda: nc.vector.tensor_reduce(out=gs2[:], in_=v3(t["sq2"][:]), axis=AX.X, op=ALU.add),
                                 [t["sq2"].b], [gs2.b])
                            yield
                            S.op("dve", lambda: nc.vector.tensor_scalar(out=gs2[:], in0=gs2[:], scalar1=1.0 / HD, scalar2=LNX_EPS,
                                                                        op0=ALU.mult, op1=ALU.add), [gs2.b], [gs2.b])
                            yield
                            S.op("act", lambda: nc.scalar.activation(out=grs[:], in_=gs2[:], func=AF.Sqrt), [gs2.b], [grs.b])
                            yield
                            S.op("dve", lambda: nc.vector.reciprocal(out=grs[:], in_=grs[:]), [grs.b], [grs.b])
                            yield
                            S.op("dve", lambda: nc.vector.tensor_tensor(out=v3(t["ysum"][:]), in0=v3(t["ysum"][:]),
                                                                        in1=bc(grs[:].unsqueeze(2), [128, NH, HD]), op=ALU.mult),
                                 [t["ysum"].b, grs.b], [t["ysum"].b])
                            yield
                            S.op("pool", lambda: nc.gpsimd.tensor_tensor(out=t["ysum"][:], in0=t["ysum"][:], in1=prm["lxg"][:], op=ALU.mult),
                                 [t["ysum"].b, prm["lxg"].b], [t["ysum"].b])
                            yield
                            S.op("pool", lambda: nc.gpsimd.tensor_tensor(out=t["ysum"][:], in0=t["ysum"][:], in1=prm["lxb"][:], op=ALU.add),
                                 [t["ysum"].b, prm["lxb"].b], [t["ysum"].b])
                            yield
                            S.op("dve", lambda: nc.vector.tensor_tensor(out=t["ysum"][:], in0=t["ysum"][:], in1=t["vbon"][:], op=ALU.add),
                                 [t["ysum"].b, t["vbon"].b], [t["ysum"].b])
                            yield
                            S.op("dve", lambda: nc.vector.tensor_tensor(out=bft["ob"][:], in0=t["ysum"][:], in1=t["sz"][:], op=ALU.mult),
                                 [t["ysum"].b, t["sz"].b], [bft["ob"].b])
                            pto = pbank("b")
                            ptvo = pto[:].bitcast(BF16)

                            def fo():
                                ins = None
                                for blk in range(4):
                                    ins = nc.tensor.transpose(out=ptvo[:, blk * 128:(blk + 1) * 128],
                                                              in_=bft["ob"][:, blk * 128:(blk + 1) * 128], identity=identb[:])
                                return ins
                            yield
                            S.op("pe", fo, [bft["ob"].b, identb.b], [pto.b])
                            yield
                            S.op("act", lambda: nc.scalar.copy(out=oT[:].rearrange("p b t -> p (b t)"), in_=ptvo[:, 0:512]),
                                 [pto.b], [oT.b])
                            yield
                            S.dma(YR[:, g * 4:(g + 1) * 4, i * 128:(i + 1) * 128], oT[:], [oT.b], [YR_bs[g][i]], q="act")

                    def interleave(ga, gb):
                        da = db = False
                        while not (da and db):
                            if not da:
                                try:
                                    next(ga)
                                except StopIteration:
                                    da = True
                            for _ in range(2):
                                if not db:
                                    try:
                                        next(gb)
                                    except StopIteration:
                                        db = True

                    prev = None
                    for oi, i in enumerate(order):
                        interleave(front(oi, i), back(*prev) if prev is not None else iter(()))
                        prev = (oi, i)
                    interleave(iter(()), back(*prev))
                    S.barrier()
                    esd.close()
                S.barrier()

        with contextlib.ExitStack() as es:
            if not DBG.get("c", True):
                raise_skip = True
            else:
                raise_skip = False
            wc = sb(es, "wc", [128, 8, NCV], BF16)
            wo = sb(es, "wo", [128, 16, D], BF16)
            with contextlib.ExitStack() as es2:
                wst = [sb(es2, f"cwst{i}", [128, 8, 512], F32) for i in range(2)]
                for bi in range(NCV // 512 if DBG.get("cw", True) else 0):
                    w_ = wst[bi % 2]
                    S.dma(w_[:], w_in_d[:, bi * 512:(bi + 1) * 512].rearrange("(c p) n -> p c n", p=128), [], [w_.b])
                    if bi % 2 == 0:
                        S.op("act", lambda: nc.scalar.copy(out=wc[:, :, bi * 512:(bi + 1) * 512], in_=w_[:]), [w_.b], [wc.b])
                    else:
                        S.op("dve", lambda: nc.vector.tensor_copy(out=wc[:, :, bi * 512:(bi + 1) * 512], in_=w_[:]), [w_.b], [wc.b])
                for bi in range(4 if DBG.get("cw", True) else 0):
                    w_ = wst[bi % 2]
                    S.dma(w_[:, 0:4, :], wout_d[bi * 512:(bi + 1) * 512, 0:512].rearrange("(c p) n -> p c n", p=128), [], [w_.b])
                    S.dma(w_[:, 4:8, :], wout_d[bi * 512:(bi + 1) * 512, 512:1024].rearrange("(c p) n -> p c n", p=128), [], [w_.b])
                    S.op("act", lambda: nc.scalar.copy(out=wo[:, bi * 4:(bi + 1) * 4, 0:512], in_=w_[:, 0:4, :]), [w_.b], [wo.b])
                    S.op("dve", lambda: nc.vector.tensor_copy(out=wo[:, bi * 4:(bi + 1) * 4, 512:1024], in_=w_[:, 4:8, :]), [w_.b], [wo.b])
                S.barrier()
            cpar = sb(es, "cpar", [128, 8, 4], F32)
            for j in range(3 if DBG.get("cp", True) else 0):
                S.dma(cpar[:, :, j:j + 1], conv_w_d[j, :].rearrange("(c p o) -> p c o", p=128, o=1), [], [cpar.b])
            S.dma(cpar[:, :, 3:4], conv_b_d.rearrange("(c p o) -> p c o", p=128, o=1), [], [cpar.b])
            gbc = sb(es, "c_g", [128, D], F32); bbc = sb(es, "c_b", [128, D], F32)
            g2 = sb(es, "c_g2", [128, D], F32); b2 = sb(es, "c_b2", [128, D], F32)
            for tl, src in ((gbc, emb_g_d), (bbc, emb_b_d), (g2, lng_d), (b2, lnb_d)):
                bcast_load(tl, 0, src, D)
            xth = [sb(es, f"cxth{i}", [128, 8, 130], BF16) for i in range(2)]
            ymT = [sb(es, f"ymT{i}", [128, 16, 128], BF16) for i in range(2)]
            xt = [sb(es, f"c_x{i}", [128, D], F32) for i in range(2)]
            xn = sb(es, "c_xn", [128, D], F32); xg = sb(es, "c_xg", [128, D], F32); xh = sb(es, "c_xh", [128, D], F32)
            sres = sb(es, "c_s", [128, D], F32); on = sb(es, "c_on", [128, D], F32); og = sb(es, "c_og", [128, D], F32)
            yo = [sb(es, f"c_yo{i}", [128, D], F32) for i in range(2)]
            st6 = sb(es, "c_st", [128, 2, SD], F32); mv = sb(es, "c_mv", [128, AD], F32)
            rstd = sb(es, "c_rs", [128, 1], F32); nb_ = sb(es, "c_nb", [128, 1], F32)
            st6b = sb(es, "c_stb", [128, 2, SD], F32); mvb = sb(es, "c_mvb", [128, AD], F32)
            rstdb = sb(es, "c_rsb", [128, 1], F32); nbb = sb(es, "c_nbb", [128, 1], F32)
            hS = sb(es, "c_hS", [128, 130], F32); pp = sb(es, "c_pp", [128, 130], F32)
            qq = sb(es, "c_qq", [128, 128], F32); szc = sb(es, "c_sz", [128, 128], F32)
            def c_loads(ti):
                p_ = ti % 2
                load_xth(xth[p_], ti)
                S.dma(xt[p_][:], xs[ti * 128:(ti + 1) * 128, :], [], [xt[p_].b])
                S.dma(ymT[p_][:, 8:16, :], YR[:, :, ti * 128:(ti + 1) * 128], [YR_bs[0][ti], YR_bs[1][ti]], [ymT[p_].b])
            if not raise_skip:
                c_loads(0)
            for i in range(0 if raise_skip else NT):
                p = i % 2
                xh_ = xth[p]
                if i + 1 < NT:
                    c_loads(i + 1)
                for cbk in range(8):
                    pa = pbank("f"); pb = pbank("f")

                    def fa():
                        ins = None
                        for qi, q in enumerate((0, 2)):
                            for kc in range(8):
                                ins = nc.tensor.matmul(pa[:, qi * 130:(qi + 1) * 130],
                                                       lhsT=wc[:, kc, q * 1024 + cbk * 128:q * 1024 + (cbk + 1) * 128],
                                                       rhs=xh_[:, kc, :], start=(kc == 0), stop=(kc == 7))
                        return ins

                    def fb():
                        ins = None
                        for qi, q in enumerate((1, 3)):
                            for kc in range(8):
                                ins = nc.tensor.matmul(pb[:, qi * 128:(qi + 1) * 128],
                                                       lhsT=wc[:, kc, q * 1024 + cbk * 128:q * 1024 + (cbk + 1) * 128],
                                                       rhs=xh_[:, kc, 1:129], start=(kc == 0), stop=(kc == 7))
                        return ins
                    S.op("pe", fa, [wc.b, xh_.b], [pa.b])
                    S.op("pe", fb, [wc.b, xh_.b], [pb.b])
                    S.op("act", lambda: nc.scalar.copy(out=hS[:], in_=pa[:, 0:130]), [pa.b], [hS.b])
                    S.op("dve", lambda: nc.vector.tensor_tensor(out=pp[:], in0=hS[:], in1=pa[:, 130:260], op=ALU.mult),
                         [hS.b, pa.b], [pp.b])
                    S.op("dve", lambda: nc.vector.tensor_scalar(out=qq[:], in0=pp[:, 1:129], scalar1=cpar[:, cbk, 1:2],
                                                                scalar2=cpar[:, cbk, 3:4], op0=ALU.mult, op1=ALU.add),
                         [pp.b, cpar.b], [qq.b])
                    S.op("dve", lambda: nc.vector.scalar_tensor_tensor(out=qq[:], in0=pp[:, 0:128], scalar=cpar[:, cbk, 0:1],
                                                                         in1=qq[:], op0=ALU.mult, op1=ALU.add),
                         [pp.b, cpar.b, qq.b], [qq.b])
                    S.op("dve", lambda: nc.vector.scalar_tensor_tensor(out=qq[:], in0=pp[:, 2:130], scalar=cpar[:, cbk, 2:3],
                                                                         in1=qq[:], op0=ALU.mult, op1=ALU.add),
                         [pp.b, cpar.b, qq.b], [qq.b])
                    S.op("act", lambda: nc.scalar.activation(out=szc[:], in_=pb[:, 128:256], func=AF.Silu), [pb.b], [szc.b])
                    S.op("dve", lambda: nc.vector.tensor_tensor(out=qq[:], in0=qq[:], in1=pb[:, 0:128], op=ALU.mult),
                         [qq.b, pb.b], [qq.b])
                    S.op("pool", lambda: nc.gpsimd.tensor_tensor(out=ymT[p][:, cbk, :], in0=qq[:], in1=szc[:], op=ALU.mult),
                         [qq.b, szc.b], [ymT[p].b])
                po = [pbank("b"), pbank("b")]
                for hf in range(2):
                    def fo():
                        ins = None
                        for mc in range(16):
                            ins = nc.tensor.matmul(po[hf][:], lhsT=ymT[p][:, mc, :], rhs=wo[:, mc, hf * 512:(hf + 1) * 512],
                                                   start=(mc == 0), stop=(mc == 15))
                        return ins
                    S.op("pe", fo, [ymT[p].b, wo.b], [po[hf].b])
                ln_stats("c", xt[p], mv, rstd, nb_, st6, LN_EPS)
                S.op("act", lambda: nc.scalar.activation(out=xn[:], in_=xt[p][:], func=AF.Identity, bias=nb_[:, 0:1],
                                                         scale=rstd[:, 0:1]), [xt[p].b, rstd.b, nb_.b], [xn.b])
                S.op("dve", lambda: nc.vector.tensor_tensor(out=xg[:], in0=xn[:], in1=gbc[:], op=ALU.mult), [xn.b, gbc.b], [xg.b])
                S.op("pool", lambda: nc.gpsimd.tensor_tensor(out=xh[:], in0=xg[:], in1=bbc[:], op=ALU.add), [xg.b, bbc.b], [xh.b])
                for hf in range(2):
                    o = slice(hf * 512, (hf + 1) * 512)
                    S.op("dve", lambda: nc.vector.scalar_tensor_tensor(out=sres[:, o], in0=xh[:, o], scalar=DN_ALPHA,
                                                                       in1=po[hf][:], op0=ALU.mult, op1=ALU.add),
                         [xh.b, po[hf].b], [sres.b])
                ln_stats("c2", sres, mvb, rstdb, nbb, st6b, LN_EPS)
                S.op("act", lambda: nc.scalar.activation(out=on[:], in_=sres[:], func=AF.Identity, bias=nbb[:, 0:1],
                                                         scale=rstdb[:, 0:1]), [sres.b, rstdb.b, nbb.b], [on.b])
                S.op("dve", lambda: nc.vector.tensor_tensor(out=og[:], in0=on[:], in1=g2[:], op=ALU.mult), [on.b, g2.b], [og.b])
                S.op("pool", lambda: nc.gpsimd.tensor_tensor(out=yo[p][:], in0=og[:], in1=b2[:], op=ALU.add), [og.b, b2.b], [yo[p].b])
                S.dma(ys[i * 128:(i + 1) * 128, :], yo[p][:], [yo[p].b], [ys_bs[i]])
            S.barrier()

        S.finish()
    return nc


CST_COLS = 128 + 2 * 896 + 1
DBG = {}
NAMES = {}


def make_consts():
    r = np.arange(128)[:, None]
    c = np.arange(128)[None, :]
    SU = (r < c).astype(np.float32); IU = (r <= c).astype(np.float32)
    SL = (r > c).astype(np.float32); IL = (r >= c).astype(np.float32)
    parts = [np.eye(128, dtype=np.float32)]
    for (S_, I_, St) in ((SU, IU, SL), (SL, IL, SU)):
        parts += [S_, I_, -S_, I_, -St, CDEC * I_, CDEC * S_]
    parts.append(np.full((128, 1), CDEC, np.float32))
    out = np.concatenate(parts, axis=1).astype(np.float32)
    assert out.shape[1] == CST_COLS
    return np.ascontiguousarray(out)


W_NAMES = ["emb_ln_g", "emb_ln_b", "w_in", "conv_w", "conv_b", "shift_mu", "w0", "w_up", "a0", "a_up",
           "k_k", "k_a", "r_k", "lnx_g", "lnx_b", "w_out", "ln_g", "ln_b"]


def weight_map(inp):
    m = {}
    for n in W_NAMES:
        a = np.asarray(inp[n], dtype=np.float32)
        if n not in ("emb_ln_g", "emb_ln_b"):
            a = a[0]
        if n == "r_k":
            a = a.reshape(1024)
        m[n] = np.ascontiguousarray(a)
    m["cst"] = make_consts()
    return m


_NC_CACHE = {}


def run_streams(streams, keeps, inp, NT, BP):
    key = (NT, BP)
    if key not in _NC_CACHE:
        _NC_CACHE[key] = build(NT, BP)
    nc = _NC_CACHE[key]
    wm = weight_map(inp)
    in_maps = []
    for s, k in zip(streams, keeps):
        d = dict(wm)
        d["xs"] = np.ascontiguousarray(s, dtype=np.float32)
        d["keep"] = np.ascontiguousarray(k, dtype=np.float32)
        in_maps.append(d)
    res = run_bass_kernel_spmd(nc, in_maps, core_ids=list(range(len(streams))))
    return [r["ys"] for r in res.results]


def kernel(**inp):
    xp = np.asarray(inp["x_prompt"], dtype=np.float32)
    xsm = np.asarray(inp["x_sample"], dtype=np.float32)
    NT, BP = 128, 16
    NB = NT // BP - 1
    ntok = NT * 128
    streams = [xsm[0], xsm[1], xp.reshape(ntok, D)]
    keeps = [np.ones((128, NB), np.float32), np.ones((128, NB), np.float32), np.zeros((128, NB), np.float32)]
    for _ in range(5):
        streams.append(np.zeros((ntok, D), np.float32))
        keeps.append(np.zeros((128, NB), np.float32))
    outs = run_streams(streams, keeps, inp, NT, BP)
    y_sample = np.stack([outs[0], outs[1]], axis=0).reshape(2, 16384, D)
    y_prompt = outs[2].reshape(8, 2048, D)
    return (y_prompt.astype(np.float32), y_sample.astype(np.float32))
```

```python
import contextlib
import numpy as np
import concourse.bass as bass
import concourse.mybir as mybir
from concourse.bass_utils import run_bass_kernel_spmd

F32 = mybir.dt.float32
BF16 = mybir.dt.bfloat16
AF = mybir.ActivationFunctionType
ALU = mybir.AluOpType
AX = mybir.AxisListType

D = 1024
NCV = 4096
NRW = 4352
NIN = NCV + NRW
HD = 64
LN_EPS = 1e-5
LNX_EPS = 64e-5
DN_ALPHA = 2.0 ** 0.25
CDEC = -float(np.exp(-0.5))
NG = 2
GC = 512
NH = 8
NLEV = 7


class Buf:
    __slots__ = ("name", "w", "rs", "excl")

    def __init__(self, name):
        self.name = name
        self.w = None
        self.rs = {}
        self.excl = False


class Sch:
    R = 4
    ND = 8

    def __init__(self, nc, es):
        self.nc = nc
        self.eng = {"pe": nc.tensor, "act": nc.scalar, "dve": nc.vector, "pool": nc.gpsimd, "sp": nc.sync}
        self.sems = {e: [es.enter_context(nc.semaphore(f"s_{e}{i}")) for i in range(self.R)]
                     for e in ("pe", "act", "dve", "pool")}
        self.cnt = {e: 0 for e in self.sems}
        self.known = {e: {} for e in self.eng}
        self.dsems = [es.enter_context(nc.semaphore(f"s_d{i}")) for i in range(self.ND)]
        self.dval = [0] * self.ND
        self.dnext = 0
        self.all_bufs = []

    def buf(self, name):
        b = Buf(name)
        self.all_bufs.append(b)
        return b

    def _wait(self, waiter, ev):
        key = (ev[0], ev[1])
        if ev[0] == "E" and ev[1] == waiter and waiter == "pe":
            return
        if self.known[waiter].get(key, -1) >= ev[2]:
            return
        if ev[0] == "E":
            sem = self.sems[ev[1]][ev[2] % self.R]
            val = ev[2] // self.R + 1
        else:
            sem = self.dsems[ev[1]]
            val = ev[2]
        self.eng[waiter].wait_ge(sem, val)
        self.known[waiter][key] = ev[2]

    def _deps(self, waiter, reads, writes):
        for b in reads:
            if b.w is not None:
                self._wait(waiter, b.w)
            if b.excl:
                for ev in b.rs.values():
                    if not (ev[0] == "E" and ev[1] == waiter):
                        self._wait(waiter, ev)
        for b in writes:
            if b.w is not None:
                self._wait(waiter, b.w)
            for ev in b.rs.values():
                self._wait(waiter, ev)

    def _commit(self, ev, reads, writes):
        for b in reads:
            b.rs[(ev[0], ev[1])] = ev
        for b in writes:
            b.w = ev
            b.rs = {}

    muted = False

    def stage(self, k):
        self.muted = k > DBG.get("stage", 99)

    def op(self, eng, fn, reads, writes):
        if self.muted:
            return
        self._deps(eng, reads, writes)
        ins = fn()
        idx = self.cnt[eng]
        self.cnt[eng] += 1
        ins.then_inc(self.sems[eng][idx % self.R], 1)
        self._commit(("E", eng, idx), reads, writes)

    def dma(self, out, in_, reads, writes, q="sp", **kw):
        if self.muted:
            return
        k = self.dnext
        self.dnext = (self.dnext + 1) % self.ND
        if self.dval[k] > 0:
            self._wait(q, ("D", k, self.dval[k]))
        self._deps(q, reads, writes)
        ins = self.eng[q].dma_start(out=out, in_=in_, **kw)
        self.dval[k] += 16
        ins.then_inc(self.dsems[k], 16)
        self._commit(("D", k, self.dval[k]), reads, writes)

    def barrier(self):
        self.muted = False
        for w in self.eng:
            for k in range(self.ND):
                if self.dval[k] > 0:
                    self._wait(w, ("D", k, self.dval[k]))
            for e in self.cnt:
                if self.cnt[e] > 0:
                    self._wait(w, ("E", e, self.cnt[e] - 1))

    def finish(self):
        for k in range(self.ND):
            if self.dval[k] > 0:
                self._wait("sp", ("D", k, self.dval[k]))
        for e in self.cnt:
            if self.cnt[e] > 0:
                self._wait("sp", ("E", e, self.cnt[e] - 1))


class TL:
    def __init__(self, sch, t, name):
        self.t = t
        self.b = sch.buf(name)

    def __getitem__(self, k):
        return self.t[k]


def bc(ap, shape):
    return ap.broadcast_to(list(shape))


def build(NT, BP):
    NTOK = NT * 128
    NB = max(1, NT // BP - 1)
    nc = bass.Bass("TRN2", target_bir_lowering=False)
    dt_in = lambda n, s: nc.dram_tensor(n, s, F32, kind="ExternalInput").ap()
    xs = dt_in("xs", [NTOK, D])
    keep_d = dt_in("keep", [128, NB])
    cst_d = dt_in("cst", [128, CST_COLS])
    emb_g_d = dt_in("emb_ln_g", [D]); emb_b_d = dt_in("emb_ln_b", [D])
    w_in_d = dt_in("w_in", [D, NIN])
    conv_w_d = dt_in("conv_w", [3, 1024]); conv_b_d = dt_in("conv_b", [1024])
    mu_d = dt_in("shift_mu", [NRW])
    w0_d = dt_in("w0", [2, 1024]); wup_d = dt_in("w_up", [2, 64, 1024])
    a0_d = dt_in("a0", [2, 1024]); aup_d = dt_in("a_up", [2, 64, 1024])
    kk_d = dt_in("k_k", [1024]); ka_d = dt_in("k_a", [1024]); rk_d = dt_in("r_k", [1024])
    lxg_d = dt_in("lnx_g", [1024]); lxb_d = dt_in("lnx_b", [1024])
    wout_d = dt_in("w_out", [2048, D])
    lng_d = dt_in("ln_g", [D]); lnb_d = dt_in("ln_b", [D])
    ys = nc.dram_tensor("ys", [NTOK, D], F32, kind="ExternalOutput").ap()
    XT = nc.dram_tensor("XT", [128, 8, NTOK + 2], BF16).ap()
    YF = nc.dram_tensor("YF", [NTOK, GC], F32).ap()
    YR = nc.dram_tensor("YR", [128, 8, NTOK], BF16).ap()
    RKV = nc.dram_tensor("RKV", [NT, 128, 5, GC], F32).ap()

    with contextlib.ExitStack() as es0:
        es0.enter_context(nc.allow_non_contiguous_dma(reason="small strided param / halo loads"))
        S = Sch(nc, es0)
        XT_bs = [S.buf(f"XT{i}") for i in range(NT + 2)]
        YF_bs = [S.buf(f"YF{i}") for i in range(NT)]
        RKV_bs = [S.buf(f"RKV{i}") for i in range(NT)]
        YR_bs = [[S.buf(f"YR{g}_{i}") for i in range(NT)] for g in range(NG)]
        ys_bs = [S.buf(f"ys{i}") for i in range(NT)]

        uid = [0]

        def sb(es, name, shape, dtype):
            uid[0] += 1
            nm = f"sb{uid[0]}_{name}"
            NAMES[name] = nm
            return TL(S, es.enter_context(nc.sbuf_tensor(nm, list(shape), dtype)), nm)

        PS = [TL(S, es0.enter_context(nc.psum_tensor(f"ps{i}", [128, 512], F32)), f"ps{i}") for i in range(8)]
        for p_ in PS:
            p_.b.excl = True
        prot = {"f": [0, [0, 1, 2, 3]], "b": [0, [4, 5, 6, 7]]}

        def pbank(kind):
            st = prot[kind]
            p = PS[st[1][st[0] % len(st[1])]]
            st[0] += 1
            return p

        cst = sb(es0, "cst", [128, CST_COLS], F32)
        S.dma(cst[:], cst_d, [], [cst.b])
        identb = sb(es0, "identb", [128, 128], BF16)
        S.op("dve", lambda: nc.vector.tensor_copy(out=identb[:], in_=cst[:, 0:128]), [cst.b], [identb.b])
        keep = sb(es0, "keep", [128, NB], F32)
        S.dma(keep[:], keep_d, [], [keep.b])
        zpad = sb(es0, "zpad", [128, 8, 1], BF16)
        S.op("pool", lambda: nc.gpsimd.memset(zpad[:], 0.0), [], [zpad.b])
        S.dma(XT[:, :, 0:1], zpad[:], [zpad.b], [XT_bs[0]])
        S.dma(XT[:, :, NTOK + 1:NTOK + 2], zpad[:], [zpad.b], [XT_bs[NT + 1]])

        onesf = sb(es0, "onesf", [1, 128], F32)
        S.op("pool", lambda: nc.gpsimd.memset(onesf[:], 1.0), [], [onesf.b])
        rowbuf = sb(es0, "rowbuf", [1, 1024], F32)

        def bcast_load(dst, dcol, src1d, ncol):
            S.dma(rowbuf[0:1, 0:ncol], src1d.rearrange("(o n) -> o n", o=1), [], [rowbuf.b])
            for c_ in range(0, ncol, 512):
                n_ = min(512, ncol - c_)
                pb_ = pbank("f")
                S.op("pe", lambda: nc.tensor.matmul(pb_[:, 0:n_], lhsT=onesf[0:1, :], rhs=rowbuf[0:1, c_:c_ + n_],
                                                    start=True, stop=True), [onesf.b, rowbuf.b], [pb_.b])
                S.op("act", lambda: nc.scalar.copy(out=dst[:, dcol + c_:dcol + c_ + n_], in_=pb_[:, 0:n_]), [pb_.b], [dst.b])

        def ln_stats(es_tag, src, mv, rstd, nb_, st6, eps):
            S.op("dve", lambda: nc.vector.bn_stats(out=st6[:, 0, :], in_=src[:, 0:512]), [src.b], [st6.b])
            S.op("dve", lambda: nc.vector.bn_stats(out=st6[:, 1, :], in_=src[:, 512:1024]), [src.b], [st6.b])
            S.op("dve", lambda: nc.vector.bn_aggr(out=mv[:], in_=st6[:]), [st6.b], [mv.b])
            S.op("act", lambda: nc.scalar.activation(out=rstd[:], in_=mv[:, 1:2], func=AF.Sqrt, bias=eps, scale=1.0),
                 [mv.b], [rstd.b])
            S.op("dve", lambda: nc.vector.reciprocal(out=rstd[:], in_=rstd[:]), [rstd.b], [rstd.b])
            S.op("dve", lambda: nc.vector.scalar_tensor_tensor(out=nb_[:], in0=mv[:, 0:1], scalar=-1.0, in1=rstd[:],
                                                               op0=ALU.mult, op1=ALU.mult), [mv.b, rstd.b], [nb_.b])

        SD = int(nc.vector.BN_STATS_DIM)
        AD = int(nc.vector.BN_AGGR_DIM)

        with contextlib.ExitStack() as es:
            gbc = sb(es, "p1_g", [128, D], F32); bbc = sb(es, "p1_b", [128, D], F32)
            bcast_load(gbc, 0, emb_g_d, D)
            bcast_load(bbc, 0, emb_b_d, D)
            xt = [sb(es, f"p1_x{i}", [128, D], F32) for i in range(2)]
            xn = [sb(es, f"p1_xn{i}", [128, D], F32) for i in range(2)]
            xg = [sb(es, f"p1_xg{i}", [128, D], F32) for i in range(2)]
            xh = [sb(es, f"p1_xh{i}", [128, D], BF16) for i in range(2)]
            xT = [sb(es, f"p1_xT{i}", [128, 8, 128], BF16) for i in range(2)]
            st6 = [sb(es, f"p1_st{i}", [128, 2, SD], F32) for i in range(2)]
            mv = [sb(es, f"p1_mv{i}", [128, AD], F32) for i in range(2)]
            rstd = [sb(es, f"p1_rs{i}", [128, 1], F32) for i in range(2)]
            nb_ = [sb(es, f"p1_nb{i}", [128, 1], F32) for i in range(2)]
            for i in range(NT if DBG.get("p1", True) else 0):
                p = i % 2
                S.dma(xt[p][:], xs[i * 128:(i + 1) * 128, :], [], [xt[p].b])
                ln_stats("p1", xt[p], mv[p], rstd[p], nb_[p], st6[p], LN_EPS)
                S.op("act", lambda: nc.scalar.activation(out=xn[p][:], in_=xt[p][:], func=AF.Identity,
                                                         bias=nb_[p][:, 0:1], scale=rstd[p][:, 0:1]),
                     [xt[p].b, rstd[p].b, nb_[p].b], [xn[p].b])
                S.op("dve", lambda: nc.vector.tensor_tensor(out=xg[p][:], in0=xn[p][:], in1=gbc[:], op=ALU.mult),
                     [xn[p].b, gbc.b], [xg[p].b])
                S.op("pool", lambda: nc.gpsimd.tensor_tensor(out=xh[p][:], in0=xg[p][:], in1=bbc[:], op=ALU.add),
                     [xg[p].b, bbc.b], [xh[p].b])
                pt = pbank("f")
                ptv = pt[:].bitcast(BF16)

                def tr():
                    ins = None
                    for c in range(8):
                        ins = nc.tensor.transpose(out=ptv[:, c * 128:(c + 1) * 128], in_=xh[p][:, c * 128:(c + 1) * 128],
                                                  identity=identb[:])
                    return ins
                S.op("pe", tr, [xh[p].b, identb.b], [pt.b])
                S.op("act", lambda: nc.scalar.copy(out=xT[p][:].rearrange("p c t -> p (c t)"), in_=ptv), [pt.b], [xT[p].b])
                S.dma(XT[:, :, 1 + i * 128:1 + (i + 1) * 128], xT[p][:], [xT[p].b], [XT_bs[i + 1]], q="act")
            S.barrier()

        def load_xth(xth, i):
            S.dma(xth[:], XT[:, :, i * 128:i * 128 + 130], [XT_bs[i], XT_bs[i + 1], XT_bs[i + 2]], [xth.b])
            if i % BP == 0 and i > 0:
                bidx = i // BP - 1
                S.op("dve", lambda: nc.vector.tensor_scalar(out=xth[:, :, 0:1], in0=xth[:, :, 0:1],
                                                            scalar1=keep[:, bidx:bidx + 1], scalar2=None, op0=ALU.mult),
                     [xth.b, keep.b], [xth.b])
            if (i + 1) % BP == 0 and i + 1 < NT:
                bidx = (i + 1) // BP - 1
                S.op("dve", lambda: nc.vector.tensor_scalar(out=xth[:, :, 129:130], in0=xth[:, :, 129:130],
                                                            scalar1=keep[:, bidx:bidx + 1], scalar2=None, op0=ALU.mult),
                     [xth.b, keep.b], [xth.b])

        for g in range(NG if DBG.get("rwkv", True) else 0):
            with contextlib.ExitStack() as es:
                c0 = g * GC
                wl = sb(es, "wl", [128, 16, 256], BF16)
                upw = sb(es, "upw", [64, 2, 2, GC], BF16)
                b0h = sb(es, "b0h", [1, 2, 2, GC], BF16); b0l = sb(es, "b0l", [1, 2, 2, GC], BF16)
                onesr = sb(es, "onesr", [1, 128], BF16)
                S.op("pool", lambda: nc.gpsimd.memset(onesr[:], 1.0), [], [onesr.b])
                with contextlib.ExitStack() as es2:
                    upf = sb(es2, "upf", [64, 2, 2, GC], F32)
                    b0f = sb(es2, "b0f", [1, 2, 2, GC], F32); b0t = sb(es2, "b0t", [1, 2, 2, GC], F32)
                    for wi, (ud, bd) in enumerate(((wup_d, w0_d), (aup_d, a0_d))):
                        for d in range(2):
                            S.dma(upf[:, wi, d, :], ud[d, :, c0:c0 + GC], [], [upf.b])
                            S.dma(b0f[:, wi, d, :], bd[d:d + 1, c0:c0 + GC], [], [b0f.b])
                    S.op("dve", lambda: nc.vector.tensor_copy(out=upw[:], in_=upf[:]), [upf.b], [upw.b])
                    S.op("dve", lambda: nc.vector.tensor_copy(out=b0h[:], in_=b0f[:]), [b0f.b], [b0h.b])
                    S.op("dve", lambda: nc.vector.tensor_tensor(out=b0t[:], in0=b0f[:], in1=b0h[:], op=ALU.subtract),
                         [b0f.b, b0h.b], [b0t.b])
                    S.op("dve", lambda: nc.vector.tensor_copy(out=b0l[:], in_=b0t[:]), [b0t.b], [b0l.b])
                    S.barrier()
                prm = {}
                for nm, src in (("kk", kk_d), ("ka", ka_d), ("rk", rk_d), ("lxg", lxg_d), ("lxb", lxb_d)):
                    prm[nm] = sb(es, "prm_" + nm, [128, GC], F32)
                    bcast_load(prm[nm], 0, src[c0:c0 + GC], GC)

                xth = [sb(es, f"xth{i}", [128, 8, 130], BF16) for i in range(2)]
                xst = sb(es, "xst", [128, 8, 128], BF16)
                f32t = {n: sb(es, "w_" + n, [128, GC], F32) for n in
                        ("sg", "a", "e1", "e2", "e3", "kkr", "t1", "kd", "kd0", "vbon",
                         "sz", "yf", "ysum", "sq2", "rS", "kS", "vS")}
                bft_ = {n: sb(es, "h_" + n, [128, GC], BF16) for n in ("V", "kh", "bh", "kkt", "rt", "Zs", "nU", "ob")}
                f32p = [dict(f32t), dict(f32t)]
                for n_ in ("vbon", "sz", "yf"):
                    f32p[1][n_] = sb(es, "w1_" + n_, [128, GC], F32)
                bftp = [dict(bft_), dict(bft_)]
                for n_ in ("V", "kh", "bh"):
                    bftp[1][n_] = sb(es, "h1_" + n_, [128, GC], BF16)
                tl_ = sb(es, "tl_", [64, 3, 128], BF16)
                kT2 = [sb(es, f"kT{i}", [64, NH, 128], BF16) for i in range(2)]
                bT2 = [sb(es, f"bT{i}", [64, NH, 128], BF16) for i in range(2)]
                krT2 = [sb(es, f"krT{i}", [64, NH, 2, 128], BF16) for i in range(2)]
                oT = sb(es, "oT", [128, 4, 128], BF16)
                MA1 = sb(es, "MA1", [128, NH, 2, 128], BF16)
                MA2 = sb(es, "MA2", [128, NH, 2, 128], BF16)
                Xs = [[sb(es, f"Xs{i}{q}", [128, 4, 128], BF16) for q in range(2)] for i in range(2)]
                Qs = [[sb(es, f"Qs{i}{q}", [128, 4, 128], BF16) for q in range(2)] for i in range(2)]
                Xtq = [[sb(es, f"Xt{i}{q}", [128, 4, 128], BF16) for q in range(2)] for i in range(2)]
                TTq = [sb(es, f"TT{q}", [128, 4, 128], BF16) for q in range(2)]
                Hf = sb(es, "Hf", [64, NH, 64], F32); Hb = sb(es, "Hb", [64, NH, 64], BF16)
                Ht = sb(es, "Ht", [64, NH, 64], F32)
                gam2 = [sb(es, f"gam{i}", [64, NH], F32) for i in range(2)]
                ss = sb(es, "ss", [128, NH], F32); rn = sb(es, "rn", [128, NH], F32)
                bs = sb(es, "bs", [128, NH], F32)
                gs1 = sb(es, "gs1", [128, NH], F32); gs2 = sb(es, "gs2", [128, NH], F32)
                grs = sb(es, "grs", [128, NH], F32)
                if DBG.get("verbose"):
                    print("sbuf bytes remaining (rwkv scope):", nc.sbuf_bytes_remaining)

                def prep_w(dst, dcol, col, ncol):
                    mub, mua, mubb = f32t["e1"], f32t["e2"], f32t["e3"]
                    stg = [f32t["sg"], f32t["a"]]
                    bcast_load(mub, 0, mu_d[col - NCV:col - NCV + ncol], ncol)
                    S.op("dve", lambda: nc.vector.tensor_scalar(out=mua[:, 0:ncol], in0=mub[:, 0:ncol], scalar1=-1.0,
                                                                scalar2=1.0, op0=ALU.mult, op1=ALU.add), [mub.b], [mua.b])
                    S.op("dve", lambda: nc.vector.tensor_scalar(out=mubb[:, 0:ncol], in0=mub[:, 0:ncol], scalar1=0.5,
                                                                scalar2=None, op0=ALU.mult), [mub.b], [mubb.b])
                    for bi in range(ncol // 64):
                        w_ = stg[bi % 2]
                        wv = w_[:].rearrange("p (c n) -> p c n", n=64)
                        o = bi * 64
                        S.dma(wv, w_in_d[:, col + o:col + o + 64].rearrange("(c p) n -> p c n", p=128), [], [w_.b])
                        S.op("dve", lambda: nc.vector.tensor_tensor(
                            out=dst[:, 0:8, dcol + o:dcol + o + 64], in0=wv,
                            in1=bc(mua[:, o:o + 64].unsqueeze(1), [128, 8, 64]), op=ALU.mult), [w_.b, mua.b], [dst.b])
                        S.op("pool", lambda: nc.gpsimd.tensor_tensor(
                            out=dst[:, 8:16, dcol + o:dcol + o + 64], in0=wv,
                            in1=bc(mubb[:, o:o + 64].unsqueeze(1), [128, 8, 64]), op=ALU.mult), [w_.b, mubb.b], [dst.b])
                prep_w(wl, 0, NCV + 4096, 256)

                for dr in range(2):
                    bwd = dr == 1
                    esd = contextlib.ExitStack()
                    qlist = [3] if bwd else [0, 1, 2]
                    qmap = {q: j for j, q in enumerate(qlist)}
                    wq = sb(esd, "wq", [128, 16, len(qlist) * GC], BF16)
                    for q in qlist:
                        prep_w(wq, qmap[q] * GC, NCV + q * 1024 + c0, GC)
                    RKN = ("rS", "kS", "kkr", "vS", "kd0")
                    if bwd:
                        for n_ in RKN:
                            f32p[1][n_] = sb(esd, "w1_" + n_, [128, GC], F32)

                    def load_rkv(par_, ti):
                        for j_, n_ in enumerate(RKN):
                            dst_ = f32p[par_][n_]
                            S.dma(dst_[:], RKV[ti, :, j_, :], [RKV_bs[ti]], [dst_.b])
                    cb = 128 + dr * 896
                    m1 = cst[:, cb:cb + 256]; m2 = cst[:, cb + 256:cb + 512]; m3 = cst[:, cb + 512:cb + 640]
                    tinc = cst[:, cb + 640:cb + 768]; texc = cst[:, cb + 768:cb + 896]
                    ccol = cst[:, CST_COLS - 1:CST_COLS]
                    S.op("dve", lambda: nc.vector.memset(Hf[:], 0.0), [], [Hf.b])
                    S.op("dve", lambda: nc.vector.memset(Hb[:], 0.0), [], [Hb.b])
                    order = list(range(NT - 1, -1, -1)) if bwd else list(range(NT))
                    load_xth(xth[0], order[0])
                    if bwd:
                        load_rkv(0, order[0])
                    v3 = lambda ap: ap.rearrange("p (h c) -> p h c", c=HD)

                    def front(oi, i):
                        par = oi % 2
                        t = f32p[par]; bft = bftp[par]
                        kT = kT2[par]; bT = bT2[par]; krT = krT2[par]; gam = gam2[par]
                        V = bft["V"]; nU = bft["nU"]
                        xh_ = xth[oi % 2]
                        if oi + 1 < NT:
                            load_xth(xth[(oi + 1) % 2], order[oi + 1])
                            if bwd:
                                load_rkv((oi + 1) % 2, order[oi + 1])
                        if bwd:
                            yield
                            S.dma(t["yf"][:], YF[i * 128:(i + 1) * 128, :], [YF_bs[i]], [t["yf"].b])
                        yield
                        S.op("pool", lambda: nc.gpsimd.tensor_tensor(out=xst[:], in0=xh_[:, :, 0:128], in1=xh_[:, :, 2:130],
                                                                      op=ALU.add), [xh_.b], [xst.b])

                        def proj_fm(pt_ap, wcol, ncol):
                            ins = None
                            for kc in range(16):
                                rhs = xh_[:, kc, 1:129] if kc < 8 else xst[:, kc - 8, :]
                                ins = nc.tensor.matmul(pt_ap, lhsT=wl[:, kc, wcol:wcol + ncol], rhs=rhs,
                                                       start=(kc == 0), stop=(kc == 15))
                            return ins

                        def proj_tm(pt_ap, q):
                            ins = None
                            for kc in range(16):
                                lhsT = xh_[:, kc, 1:129] if kc < 8 else xst[:, kc - 8, :]
                                ins = nc.tensor.matmul(pt_ap, lhsT=lhsT, rhs=wq[:, kc, qmap[q] * GC:(qmap[q] + 1) * GC],
                                                       start=(kc == 0), stop=(kc == 15))
                            return ins

                        pc = pbank("f")
                        ncode = 2

                        def codes():
                            ins = proj_fm(pc[0:64, 0:128], dr * 64, 64)
                            ins = proj_fm(pc[0:64, 128:256], 128 + dr * 64, 64)
                            return ins
                        yield
                        S.op("pe", codes, [xh_.b, xst.b, wl.b], [pc.b])
                        yield
                        S.op("act", lambda: nc.scalar.activation(out=tl_[:, 0, :], in_=pc[0:64, 0:128], func=AF.Tanh),
                             [pc.b], [tl_.b])
                        yield
                        S.op("act", lambda: nc.scalar.copy(out=tl_[:, 1:ncode, :].rearrange("p a t -> p (a t)"),
                                                           in_=pc[0:64, 128:128 * ncode]), [pc.b], [tl_.b])

                        def lowrank(pt, ci, wi, d):
                            def f():
                                nc.tensor.matmul(pt[:], lhsT=tl_[:, ci, :], rhs=upw[:, wi, d, :], start=True, stop=False)
                                nc.tensor.matmul(pt[:], lhsT=onesr[:], rhs=b0h[:, wi, d, :], start=False, stop=False)
                                return nc.tensor.matmul(pt[:], lhsT=onesr[:], rhs=b0l[:, wi, d, :], start=False, stop=True)
                            S.op("pe", f, [tl_.b, upw.b, onesr.b, b0h.b, b0l.b], [pt.b])
                        yield
                        pd_ = pbank("f"); lowrank(pd_, 0, 0, dr)
                        yield
                        S.op("act", lambda: nc.scalar.activation(out=t["sg"][:], in_=pd_[:], func=AF.Sigmoid),
                             [pd_.b], [t["sg"].b])
                        yield
                        pa_ = pbank("f"); lowrank(pa_, 1, 1, dr)
                        yield
                        S.op("act", lambda: nc.scalar.activation(out=t["a"][:], in_=pa_[:], func=AF.Sigmoid),
                             [pa_.b], [t["a"].b])
                        sg = t["sg"]
                        pcum = pbank("f")
                        yield
                        S.op("pe", lambda: nc.tensor.matmul(pcum[:], lhsT=tinc, rhs=sg[:], start=True, stop=True),
                             [cst.b, sg.b], [pcum.b])
                        yield
                        S.op("act", lambda: nc.scalar.activation(out=t["e1"][:], in_=pcum[:], func=AF.Exp, scale=-1.0),
                             [pcum.b], [t["e1"].b])
                        yield
                        S.op("act", lambda: nc.scalar.activation(out=t["e3"][:], in_=pcum[:], func=AF.Exp),
                             [pcum.b], [t["e3"].b])
                        pcx = pbank("f")

                        yield
                        S.op("pe", lambda: nc.tensor.matmul(pcx[:], lhsT=texc, rhs=sg[:], start=True, stop=True),
                             [cst.b, sg.b], [pcx.b])
                        yield
                        S.op("act", lambda: nc.scalar.activation(out=t["e2"][:], in_=pcx[:], func=AF.Exp),
                             [pcx.b], [t["e2"].b])
                        pgm = pbank("f")

                        def gsum():
                            ins = None
                            for h in range(NH):
                                ins = nc.tensor.matmul(pgm[0:64, h:h + 1], lhsT=sg[:, h * HD:(h + 1) * HD], rhs=ccol,
                                                       start=True, stop=True)
                            return ins
                        yield
                        S.op("pe", gsum, [sg.b, cst.b], [pgm.b])
                        yield
                        S.op("act", lambda: nc.scalar.activation(out=gam[:], in_=pgm[0:64, 0:NH], func=AF.Exp), [pgm.b], [gam.b])

                        rS, kS, vS = t["rS"], t["kS"], t["vS"]
                        V = bft["V"]
                        v3 = lambda ap: ap.rearrange("p (h c) -> p h c", c=HD)
                        if not bwd:
                            pr = pbank("f"); S.op("pe", lambda: proj_tm(pr[:], 0), [xh_.b, xst.b, wq.b], [pr.b])
                            yield
                            S.op("act", lambda: nc.scalar.copy(out=rS[:], in_=pr[:]), [pr.b], [rS.b])
                            pk = pbank("f"); S.op("pe", lambda: proj_tm(pk[:], 1), [xh_.b, xst.b, wq.b], [pk.b])
                            yield
                            S.op("act", lambda: nc.scalar.copy(out=kS[:], in_=pk[:]), [pk.b], [kS.b])
                            pv = pbank("f"); S.op("pe", lambda: proj_tm(pv[:], 2), [xh_.b, xst.b, wq.b], [pv.b])
                            yield
                            S.op("act", lambda: nc.scalar.copy(out=vS[:], in_=pv[:]), [pv.b], [vS.b])
                            yield
                            S.op("act", lambda: nc.scalar.copy(out=V[:], in_=pv[:]), [pv.b], [V.b])
                            yield
                            S.op("dve", lambda: nc.vector.tensor_tensor(out=t["kkr"][:], in0=kS[:], in1=prm["kk"][:], op=ALU.mult),
                                 [kS.b, prm["kk"].b], [t["kkr"].b])
                            yield
                            S.op("pool", lambda: nc.gpsimd.tensor_tensor(out=t["t1"][:], in0=t["kkr"][:], in1=t["kkr"][:], op=ALU.mult),
                                 [t["kkr"].b], [t["t1"].b])
                            yield
                            S.op("dve", lambda: nc.vector.tensor_reduce(out=ss[:], in_=v3(t["t1"][:]), axis=AX.X, op=ALU.add),
                                 [t["t1"].b], [ss.b])
                            yield
                            S.op("act", lambda: nc.scalar.activation(out=rn[:], in_=ss[:], func=AF.Sqrt), [ss.b], [rn.b])
                            yield
                            S.op("dve", lambda: nc.vector.tensor_scalar(out=rn[:], in0=rn[:], scalar1=1e-12, scalar2=None,
                                                                        op0=ALU.max), [rn.b], [rn.b])
                            yield
                            S.op("dve", lambda: nc.vector.reciprocal(out=rn[:], in_=rn[:]), [rn.b], [rn.b])
                            yield
                            S.op("dve", lambda: nc.vector.tensor_tensor(out=v3(t["kkr"][:]), in0=v3(t["kkr"][:]),
                                                                        in1=bc(rn[:].unsqueeze(2), [128, NH, HD]), op=ALU.mult),
                                 [t["kkr"].b, rn.b], [t["kkr"].b])
                        else:
                            yield
                            S.op("act", lambda: nc.scalar.copy(out=V[:], in_=vS[:]), [vS.b], [V.b])
                        yield
                        S.op("dve", lambda: nc.vector.scalar_tensor_tensor(out=t["t1"][:], in0=t["a"][:], scalar=-1.0,
                                                                             in1=prm["ka"][:], op0=ALU.add, op1=ALU.mult),
                             [t["a"].b, prm["ka"].b], [t["t1"].b])
                        yield
                        S.op("dve", lambda: nc.vector.scalar_tensor_tensor(out=t["kd"][:], in0=t["t1"][:], scalar=1.0,
                                                                           in1=kS[:], op0=ALU.add, op1=ALU.mult),
                             [t["t1"].b, kS.b], [t["kd"].b])
                        yield
                        S.op("pool", lambda: nc.gpsimd.tensor_tensor(out=t["t1"][:], in0=t["kkr"][:], in1=t["a"][:], op=ALU.mult),
                             [t["kkr"].b, t["a"].b], [t["t1"].b])
                        yield
                        S.op("dve", lambda: nc.vector.tensor_tensor(out=bft["kh"][:], in0=t["kd"][:], in1=t["e1"][:], op=ALU.mult),
                             [t["kd"].b, t["e1"].b], [bft["kh"].b])
                        yield
                        S.op("pool", lambda: nc.gpsimd.tensor_tensor(out=bft["bh"][:], in0=t["t1"][:], in1=t["e1"][:], op=ALU.mult),
                             [t["t1"].b, t["e1"].b], [bft["bh"].b])
                        yield
                        S.op("pool", lambda: nc.gpsimd.tensor_tensor(out=bft["kkt"][:], in0=t["kkr"][:], in1=t["e2"][:], op=ALU.mult),
                             [t["kkr"].b, t["e2"].b], [bft["kkt"].b])
                        yield
                        S.op("dve", lambda: nc.vector.tensor_tensor(out=bft["rt"][:], in0=rS[:], in1=t["e3"][:], op=ALU.mult),
                             [rS.b, t["e3"].b], [bft["rt"].b])
                        if not bwd:
                            for j_, src_ in enumerate((rS, kS, t["kkr"], vS, t["kd"])):
                                yield
                                S.dma(RKV[i, :, j_, :], src_[:], [src_.b], [RKV_bs[i]])
                        else:
                            yield
                            S.op("pool", lambda: nc.gpsimd.tensor_tensor(out=t["kd0"][:], in0=t["kd0"][:], in1=t["kd"][:], op=ALU.add),
                                 [t["kd0"].b, t["kd"].b], [t["kd0"].b])
                            yield
                            S.op("pool", lambda: nc.gpsimd.tensor_tensor(out=t["kd0"][:], in0=t["kd0"][:], in1=prm["rk"][:], op=ALU.mult),
                                 [t["kd0"].b, prm["rk"].b], [t["kd0"].b])
                            yield
                            S.op("dve", lambda: nc.vector.tensor_tensor(out=t["kd0"][:], in0=rS[:], in1=t["kd0"][:], op=ALU.mult),
                                 [rS.b, t["kd0"].b], [t["kd0"].b])
                            yield
                            S.op("dve", lambda: nc.vector.tensor_reduce(out=bs[:], in_=v3(t["kd0"][:]), axis=AX.X, op=ALU.add),
                                 [t["kd0"].b], [bs.b])
                            yield
                            S.op("dve", lambda: nc.vector.scalar_tensor_tensor(
                                out=v3(t["vbon"][:]), in0=v3(vS[:]), scalar=0.5, in1=bc(bs[:].unsqueeze(2), [128, NH, HD]),
                                op0=ALU.mult, op1=ALU.mult), [vS.b, bs.b], [t["vbon"].b])
                            pz = pbank("f"); S.op("pe", lambda: proj_tm(pz[:], 3), [xh_.b, xst.b, wq.b], [pz.b])
                            yield
                            S.op("act", lambda: nc.scalar.activation(out=t["sz"][:], in_=pz[:], func=AF.Silu), [pz.b], [t["sz"].b])

                        def trans8(src_, dst_ap, eng):
                            pt = pbank("f")
                            ptv = pt[:].bitcast(BF16)

                            def f():
                                ins = None
                                for h in range(NH):
                                    ins = nc.tensor.transpose(out=ptv[0:64, h * 128:(h + 1) * 128], in_=src_[:, h * HD:(h + 1) * HD],
                                                              identity=identb[:])
                                return ins
                            S.op("pe", f, [src_.b, identb.b], [pt.b])
                            src_v = ptv[0:64, :].rearrange("p (h t) -> p h t", t=128)
                            if eng == "act":
                                S.op("act", lambda: nc.scalar.copy(out=dst_ap[0], in_=src_v), [pt.b], [dst_ap[1]])
                            else:
                                S.op("dve", lambda: nc.vector.tensor_copy(out=dst_ap[0], in_=src_v), [pt.b], [dst_ap[1]])
                        yield
                        trans8(bft["kh"], (kT[:], kT.b), "act")
                        yield
                        trans8(bft["bh"], (bT[:], bT.b), "dve")
                        yield
                        trans8(bft["kkt"], (krT[:, :, 0, :], krT.b), "act")
                        yield
                        trans8(bft["rt"], (krT[:, :, 1, :], krT.b), "dve")

                    def back(oi, i):
                        par = oi % 2
                        t = f32p[par]; bft = bftp[par]
                        kT = kT2[par]; bT = bT2[par]; krT = krT2[par]; gam = gam2[par]
                        V = bft["V"]; nU = bft["nU"]
                        for hp in range(NH // 2):
                            for which, lT, dst, msk in ((0, kT, MA1, m1), (1, bT, MA2, m2)):
                                pm = pbank("b")

                                def f():
                                    ins = None
                                    for hh in range(2):
                                        h = hp * 2 + hh
                                        ins = nc.tensor.matmul(pm[:, hh * 256:(hh + 1) * 256], lhsT=lT[:, h, :],
                                                               rhs=krT[:, h, :, :].rearrange("p a t -> p (a t)"),
                                                               start=True, stop=True)
                                    return ins
                                yield
                                S.op("pe", f, [lT.b, krT.b], [pm.b])
                                yield
                                S.op("dve", lambda: nc.vector.tensor_tensor(
                                    out=dst[:, hp * 2:hp * 2 + 2, :, :].rearrange("p h a t -> p h (a t)"),
                                    in0=pm[:].rearrange("p (h x) -> p h x", h=2),
                                    in1=bc(msk.unsqueeze(1), [128, 2, 256]), op=ALU.mult), [pm.b, cst.b], [dst.b])
                        for hq in range(NH // 4):
                            pm = pbank("b")

                            def f():
                                ins = None
                                for hh in range(4):
                                    h = hq * 4 + hh
                                    ins = nc.tensor.matmul(pm[:, hh * 128:(hh + 1) * 128], lhsT=krT[:, h, 0, :],
                                                           rhs=bT[:, h, :], start=True, stop=True)
                                return ins
                            yield
                            S.op("pe", f, [krT.b, bT.b], [pm.b])
                            yield
                            S.op("dve", lambda: nc.vector.tensor_tensor(
                                out=Xtq[0][hq][:], in0=pm[:].rearrange("p (h x) -> p h x", h=4),
                                in1=bc(m3.unsqueeze(1), [128, 4, 128]), op=ALU.mult), [pm.b, cst.b], [Xtq[0][hq].b])

                        def quad_mm(pm, lhs_of, rhs_of):
                            def f():
                                ins = None
                                for hh in range(4):
                                    ins = nc.tensor.matmul(pm[:, hh * 128:(hh + 1) * 128], lhsT=lhs_of(hh), rhs=rhs_of(hh),
                                                           start=True, stop=True)
                                return ins
                            return f

                        def quad_copy(eng, dst, pm):
                            src_ = pm[:].rearrange("p (h x) -> p h x", h=4)
                            if eng == "act":
                                S.op("act", lambda: nc.scalar.copy(out=dst[:], in_=src_), [pm.b], [dst.b])
                            else:
                                S.op("dve", lambda: nc.vector.tensor_copy(out=dst[:], in_=src_), [pm.b], [dst.b])
                        for hq in range(2):
                            hs = slice(hq * 4, hq * 4 + 4)
                            x0 = lambda hh: MA2[:, hq * 4 + hh, 0, :]
                            xt0 = lambda hh: Xtq[0][hq][:, hh, :]
                            yield
                            S.op("pool", lambda: nc.gpsimd.tensor_tensor(
                                out=Qs[0][hq][:], in0=MA2[:, hs, 0, :], in1=bc(identb[:].unsqueeze(1), [128, 4, 128]),
                                op=ALU.add), [MA2.b, identb.b], [Qs[0][hq].b])
                            pm = pbank("b")
                            yield
                            S.op("pe", quad_mm(pm, xt0, x0), [Xtq[0][hq].b, MA2.b], [pm.b])
                            yield
                            quad_copy("act", Xs[1][hq], pm)
                            pm2 = pbank("b")
                            yield
                            S.op("pe", quad_mm(pm2, x0, xt0), [Xtq[0][hq].b, MA2.b], [pm2.b])
                            yield
                            quad_copy("act" if hq == 0 else "dve", Xtq[1][hq], pm2)
                        for lev in range(1, NLEV):
                            for hq in range(2):
                                Xk = Xs[lev % 2][hq]; Xtk = Xtq[lev % 2][hq]; Qp = Qs[(lev - 1) % 2][hq]
                                xk = lambda hh: Xk[:, hh, :]
                                xtk = lambda hh: Xtk[:, hh, :]
                                qp = lambda hh: Qp[:, hh, :]
                                if lev <= NLEV - 3:
                                    pmA = pbank("b")
                                    yield
                                    S.op("pe", quad_mm(pmA, xtk, xk), [Xk.b, Xtk.b], [pmA.b])
                                    yield
                                    quad_copy("act", Xs[(lev + 1) % 2][hq], pmA)
                                pmB = pbank("b")
                                qdst = Qs[lev % 2][hq] if lev < NLEV - 1 else TTq[hq]
                                yield
                                S.op("pe", quad_mm(pmB, xtk, qp), [Xtk.b, Qp.b], [pmB.b])
                                yield
                                S.op("dve", lambda: nc.vector.tensor_tensor(out=qdst[:], in0=pmB[:].rearrange("p (h x) -> p h x", h=4),
                                                                            in1=Qp[:], op=ALU.add), [pmB.b, Qp.b], [qdst.b])
                                if lev <= NLEV - 2:
                                    pmC = pbank("b")
                                    yield
                                    S.op("pe", quad_mm(pmC, xk, xtk), [Xk.b, Xtk.b], [pmC.b])
                                    yield
                                    quad_copy("act" if hq == 0 else "dve", Xtq[(lev + 1) % 2][hq], pmC)

                        Vh = lambda h: V[:, h * HD:(h + 1) * HD]
                        pzz = pbank("b")

                        def fz():
                            ins = None
                            for h in range(NH):
                                nc.tensor.matmul(pzz[:, h * HD:(h + 1) * HD], lhsT=krT[:, h, 0, :],
                                                 rhs=Hb[:, h, :], start=True, stop=False)
                                ins = nc.tensor.matmul(pzz[:, h * HD:(h + 1) * HD], lhsT=MA1[:, h, 0, :], rhs=Vh(h),
                                                       start=False, stop=True)
                            return ins
                        yield
                        S.op("pe", fz, [krT.b, Hb.b, MA1.b, V.b], [pzz.b])
                        yield
                        S.op("act", lambda: nc.scalar.copy(out=bft["Zs"][:], in_=pzz[:]), [pzz.b], [bft["Zs"].b])
                        pu = pbank("b")

                        def fu():
                            ins = None
                            for h in range(NH):
                                ins = nc.tensor.matmul(pu[:, h * HD:(h + 1) * HD], lhsT=TTq[h // 4][:, h % 4, :],
                                                       rhs=bft["Zs"][:, h * HD:(h + 1) * HD], start=True, stop=True)
                            return ins
                        yield
                        S.op("pe", fu, [TTq[0].b, TTq[1].b, bft["Zs"].b], [pu.b])
                        nU = bft["nU"]
                        yield
                        S.op("act", lambda: nc.scalar.activation(out=nU[:], in_=pu[:], func=AF.Identity, scale=-1.0), [pu.b], [nU.b])
                        py = pbank("b")

                        def fy():
                            ins = None
                            for h in range(NH):
                                o = slice(h * HD, (h + 1) * HD)
                                nc.tensor.matmul(py[:, o], lhsT=krT[:, h, 1, :], rhs=Hb[:, h, :],
                                                 start=True, stop=False)
                                nc.tensor.matmul(py[:, o], lhsT=MA1[:, h, 1, :], rhs=Vh(h), start=False, stop=False)
                                ins = nc.tensor.matmul(py[:, o], lhsT=MA2[:, h, 1, :], rhs=nU[:, o], start=False, stop=True)
                            return ins
                        yield
                        S.op("pe", fy, [krT.b, Hb.b, MA1.b, MA2.b, V.b, nU.b], [py.b])
                        ph = pbank("b")

                        def fh():
                            ins = None
                            for h in range(NH):
                                o = slice(h * HD, (h + 1) * HD)
                                nc.tensor.matmul(ph[0:64, o], lhsT=bft["kh"][:, o], rhs=V[:, o], start=True, stop=False)
                                ins = nc.tensor.matmul(ph[0:64, o], lhsT=bft["bh"][:, o], rhs=nU[:, o], start=False, stop=True)
                            return ins
                        yield
                        S.op("pe", fh, [bft["kh"].b, bft["bh"].b, V.b, nU.b], [ph.b])
                        yield
                        S.op("dve", lambda: nc.vector.tensor_tensor(out=Ht[:], in0=ph[0:64, :].rearrange("p (h v) -> p h v", v=HD),
                                                                    in1=Hf[:], op=ALU.add), [ph.b, Hf.b], [Ht.b])
                        yield
                        S.op("dve", lambda: nc.vector.tensor_tensor(out=Hf[:], in0=Ht[:], in1=bc(gam[:].unsqueeze(2), [64, NH, HD]),
                                                                    op=ALU.mult), [Ht.b, gam.b], [Hf.b])
                        nxt_i = i - 1 if bwd else i + 1
                        bt = i if bwd else i + 1
                        if 0 <= nxt_i < NT and bt % BP == 0:
                            bidx = bt // BP - 1
                            yield
                            S.op("dve", lambda: nc.vector.tensor_scalar(out=Hf[:], in0=Hf[:], scalar1=keep[0:64, bidx:bidx + 1],
                                                                        scalar2=None, op0=ALU.mult), [Hf.b, keep.b], [Hf.b])
                        yield
                        S.op("act", lambda: nc.scalar.copy(out=Hb[:], in_=Hf[:]), [Hf.b], [Hb.b])

                        if not bwd:
                            yield
                            S.op("act", lambda: nc.scalar.copy(out=t["ysum"][:], in_=py[:]), [py.b], [t["ysum"].b])
                            yield
                            S.dma(YF[i * 128:(i + 1) * 128, :], t["ysum"][:], [t["ysum"].b], [YF_bs[i]], q="act")
                        else:
                            yield
                            S.op("dve", lambda: nc.vector.tensor_tensor(out=t["ysum"][:], in0=py[:], in1=t["yf"][:], op=ALU.add),
                                 [py.b, t["yf"].b], [t["ysum"].b])
                            yield
                            S.op("dve", lambda: nc.vector.tensor_reduce(out=gs1[:], in_=v3(t["ysum"][:]), axis=AX.X, op=ALU.add),
                                 [t["ysum"].b], [gs1.b])
                            yield
                            S.op("dve", lambda: nc.vector.tensor_scalar(out=gs1[:], in0=gs1[:], scalar1=-1.0 / HD, scalar2=None,
                                                                        op0=ALU.mult), [gs1.b], [gs1.b])
                            yield
                            S.op("dve", lambda: nc.vector.tensor_tensor(out=v3(t["ysum"][:]), in0=v3(t["ysum"][:]),
                                                                        in1=bc(gs1[:].unsqueeze(2), [128, NH, HD]), op=ALU.add),
                                 [t["ysum"].b, gs1.b], [t["ysum"].b])
                            yield
                            S.op("pool", lambda: nc.gpsimd.tensor_tensor(out=t["sq2"][:], in0=t["ysum"][:], in1=t["ysum"][:], op=ALU.mult),
                                 [t["ysum"].b], [t["sq2"].b])
                            yield
                            S.op("dve", lambda: nc.vector.tensor_reduce(out=gs2[:], in_=v3(t["sq2"][:]), axis=AX.X, op=ALU.add),
                                 [t["sq2"].b], [gs2.b])
                            yield
                            S.op("dve", lambda: nc.vector.tensor_scalar(out=gs2[:], in0=gs2[:], scalar1=1.0 / HD, scalar2=LNX_EPS,
                                                                        op0=ALU.mult, op1=ALU.add), [gs2.b], [gs2.b])
                            yield
                            S.op("act", lambda: nc.scalar.activation(out=grs[:], in_=gs2[:], func=AF.Sqrt), [gs2.b], [grs.b])
                            yield
                            S.op("dve", lambda: nc.vector.reciprocal(out=grs[:], in_=grs[:]), [grs.b], [grs.b])
                            yield
                            S.op("dve", lambda: nc.vector.tensor_tensor(out=v3(t["ysum"][:]), in0=v3(t["ysum"][:]),
                                                                        in1=bc(grs[:].unsqueeze(2), [128, NH, HD]), op=ALU.mult),
                                 [t["ysum"].b, grs.b], [t["ysum"].b])
                            yield
                            S.op("pool", lambda: nc.gpsimd.tensor_tensor(out=t["ysum"][:], in0=t["ysum"][:], in1=prm["lxg"][:], op=ALU.mult),
                                 [t["ysum"].b, prm["lxg"].b], [t["ysum"].b])
                            yield
                            S.op("pool", lambda: nc.gpsimd.tensor_tensor(out=t["ysum"][:], in0=t["ysum"][:], in1=prm["lxb"][:], op=ALU.add),
                                 [t["ysum"].b, prm["lxb"].b], [t["ysum"].b])
                            yield
                            S.op("dve", lambda: nc.vector.tensor_tensor(out=t["ysum"][:], in0=t["ysum"][:], in1=t["vbon"][:], op=ALU.add),
                                 [t["ysum"].b, t["vbon"].b], [t["ysum"].b])
                            yield
                            S.op("dve", lambda: nc.vector.tensor_tensor(out=bft["ob"][:], in0=t["ysum"][:], in1=t["sz"][:], op=ALU.mult),
                                 [t["ysum"].b, t["sz"].b], [bft["ob"].b])
                            pto = pbank("b")
                            ptvo = pto[:].bitcast(BF16)

                            def fo():
                                ins = None
                                for blk in range(4):
                                    ins = nc.tensor.transpose(out=ptvo[:, blk * 128:(blk + 1) * 128],
                                                              in_=bft["ob"][:, blk * 128:(blk + 1) * 128], identity=identb[:])
                                return ins
                            yield
                            S.op("pe", fo, [bft["ob"].b, identb.b], [pto.b])
                            yield
                            S.op("act", lambda: nc.scalar.copy(out=oT[:].rearrange("p b t -> p (b t)"), in_=ptvo[:, 0:512]),
                                 [pto.b], [oT.b])
                            yield
                            S.dma(YR[:, g * 4:(g + 1) * 4, i * 128:(i + 1) * 128], oT[:], [oT.b], [YR_bs[g][i]], q="act")

                    def interleave(ga, gb):
                        ratio = 3.0 if bwd else 1.5
                        da = db = False
                        acc = 0.0
                        while not (da and db):
                            if not da:
                                try:
                                    next(ga)
                                except StopIteration:
                                    da = True
                            acc += ratio
                            while acc >= 1.0 or (da and not db):
                                acc -= 1.0
                                if db:
                                    acc = 0.0
                                    break
                                try:
                                    next(gb)
                                except StopIteration:
                                    db = True

                    prev = None
                    for oi, i in enumerate(order):
                        interleave(front(oi, i), back(*prev) if prev is not None else iter(()))
                        prev = (oi, i)
                    interleave(iter(()), back(*prev))
                    S.barrier()
                    esd.close()
                S.barrier()

        with contextlib.ExitStack() as es:
            if not DBG.get("c", True):
                raise_skip = True
            else:
                raise_skip = False
            wc = sb(es, "wc", [128, 8, NCV], BF16)
            wo = sb(es, "wo", [128, 16, D], BF16)
            with contextlib.ExitStack() as es2:
                wst = [sb(es2, f"cwst{i}", [128, 8, 512], F32) for i in range(2)]
                for bi in range(NCV // 512 if DBG.get("cw", True) else 0):
                    w_ = wst[bi % 2]
                    S.dma(w_[:], w_in_d[:, bi * 512:(bi + 1) * 512].rearrange("(c p) n -> p c n", p=128), [], [w_.b])
                    if bi % 2 == 0:
                        S.op("act", lambda: nc.scalar.copy(out=wc[:, :, bi * 512:(bi + 1) * 512], in_=w_[:]), [w_.b], [wc.b])
                    else:
                        S.op("dve", lambda: nc.vector.tensor_copy(out=wc[:, :, bi * 512:(bi + 1) * 512], in_=w_[:]), [w_.b], [wc.b])
                for bi in range(4 if DBG.get("cw", True) else 0):
                    w_ = wst[bi % 2]
                    S.dma(w_[:, 0:4, :], wout_d[bi * 512:(bi + 1) * 512, 0:512].rearrange("(c p) n -> p c n", p=128), [], [w_.b])
                    S.dma(w_[:, 4:8, :], wout_d[bi * 512:(bi + 1) * 512, 512:1024].rearrange("(c p) n -> p c n", p=128), [], [w_.b])
                    S.op("act", lambda: nc.scalar.copy(out=wo[:, bi * 4:(bi + 1) * 4, 0:512], in_=w_[:, 0:4, :]), [w_.b], [wo.b])
                    S.op("dve", lambda: nc.vector.tensor_copy(out=wo[:, bi * 4:(bi + 1) * 4, 512:1024], in_=w_[:, 4:8, :]), [w_.b], [wo.b])
                S.barrier()
            cpar = sb(es, "cpar", [128, 8, 4], F32)
            for j in range(3 if DBG.get("cp", True) else 0):
                S.dma(cpar[:, :, j:j + 1], conv_w_d[j, :].rearrange("(c p o) -> p c o", p=128, o=1), [], [cpar.b])
            S.dma(cpar[:, :, 3:4], conv_b_d.rearrange("(c p o) -> p c o", p=128, o=1), [], [cpar.b])
            gbc = sb(es, "c_g", [128, D], F32); bbc = sb(es, "c_b", [128, D], F32)
            g2 = sb(es, "c_g2", [128, D], F32); b2 = sb(es, "c_b2", [128, D], F32)
            for tl, src in ((gbc, emb_g_d), (bbc, emb_b_d), (g2, lng_d), (b2, lnb_d)):
                bcast_load(tl, 0, src, D)
            xth = [sb(es, f"cxth{i}", [128, 8, 130], BF16) for i in range(2)]
            ymT = [sb(es, f"ymT{i}", [128, 16, 128], BF16) for i in range(2)]
            xt = [sb(es, f"c_x{i}", [128, D], F32) for i in range(2)]
            xn = sb(es, "c_xn", [128, D], F32); xg = sb(es, "c_xg", [128, D], F32); xh = sb(es, "c_xh", [128, D], F32)
            sres = sb(es, "c_s", [128, D], F32); on = sb(es, "c_on", [128, D], F32); og = sb(es, "c_og", [128, D], F32)
            yo = [sb(es, f"c_yo{i}", [128, D], F32) for i in range(2)]
            st6 = sb(es, "c_st", [128, 2, SD], F32); mv = sb(es, "c_mv", [128, AD], F32)
            rstd = sb(es, "c_rs", [128, 1], F32); nb_ = sb(es, "c_nb", [128, 1], F32)
            st6b = sb(es, "c_stb", [128, 2, SD], F32); mvb = sb(es, "c_mvb", [128, AD], F32)
            rstdb = sb(es, "c_rsb", [128, 1], F32); nbb = sb(es, "c_nbb", [128, 1], F32)
            hS = sb(es, "c_hS", [128, 130], F32); pp = sb(es, "c_pp", [128, 130], F32)
            qq = sb(es, "c_qq", [128, 128], F32); szc = sb(es, "c_sz", [128, 128], F32)
            def c_loads(ti):
                p_ = ti % 2
                load_xth(xth[p_], ti)
                S.dma(xt[p_][:], xs[ti * 128:(ti + 1) * 128, :], [], [xt[p_].b])
                S.dma(ymT[p_][:, 8:16, :], YR[:, :, ti * 128:(ti + 1) * 128], [YR_bs[0][ti], YR_bs[1][ti]], [ymT[p_].b])
            if not raise_skip:
                c_loads(0)
            for i in range(0 if raise_skip else NT):
                p = i % 2
                xh_ = xth[p]
                if i + 1 < NT:
                    c_loads(i + 1)
                for cbk in range(8):
                    pa = pbank("f"); pb = pbank("f")

                    def fa():
                        ins = None
                        for qi, q in enumerate((0, 2)):
                            for kc in range(8):
                                ins = nc.tensor.matmul(pa[:, qi * 130:(qi + 1) * 130],
                                                       lhsT=wc[:, kc, q * 1024 + cbk * 128:q * 1024 + (cbk + 1) * 128],
                                                       rhs=xh_[:, kc, :], start=(kc == 0), stop=(kc == 7))
                        return ins

                    def fb():
                        ins = None
                        for qi, q in enumerate((1, 3)):
                            for kc in range(8):
                                ins = nc.tensor.matmul(pb[:, qi * 128:(qi + 1) * 128],
                                                       lhsT=wc[:, kc, q * 1024 + cbk * 128:q * 1024 + (cbk + 1) * 128],
                                                       rhs=xh_[:, kc, 1:129], start=(kc == 0), stop=(kc == 7))
                        return ins
                    S.op("pe", fa, [wc.b, xh_.b], [pa.b])
                    S.op("pe", fb, [wc.b, xh_.b], [pb.b])
                    S.op("act", lambda: nc.scalar.copy(out=hS[:], in_=pa[:, 0:130]), [pa.b], [hS.b])
                    S.op("dve", lambda: nc.vector.tensor_tensor(out=pp[:], in0=hS[:], in1=pa[:, 130:260], op=ALU.mult),
                         [hS.b, pa.b], [pp.b])
                    S.op("dve", lambda: nc.vector.tensor_scalar(out=qq[:], in0=pp[:, 1:129], scalar1=cpar[:, cbk, 1:2],
                                                                scalar2=cpar[:, cbk, 3:4], op0=ALU.mult, op1=ALU.add),
                         [pp.b, cpar.b], [qq.b])
                    S.op("dve", lambda: nc.vector.scalar_tensor_tensor(out=qq[:], in0=pp[:, 0:128], scalar=cpar[:, cbk, 0:1],
                                                                         in1=qq[:], op0=ALU.mult, op1=ALU.add),
                         [pp.b, cpar.b, qq.b], [qq.b])
                    S.op("dve", lambda: nc.vector.scalar_tensor_tensor(out=qq[:], in0=pp[:, 2:130], scalar=cpar[:, cbk, 2:3],
                                                                         in1=qq[:], op0=ALU.mult, op1=ALU.add),
                         [pp.b, cpar.b, qq.b], [qq.b])
                    S.op("act", lambda: nc.scalar.activation(out=szc[:], in_=pb[:, 128:256], func=AF.Silu), [pb.b], [szc.b])
                    S.op("dve", lambda: nc.vector.tensor_tensor(out=qq[:], in0=qq[:], in1=pb[:, 0:128], op=ALU.mult),
                         [qq.b, pb.b], [qq.b])
                    S.op("pool", lambda: nc.gpsimd.tensor_tensor(out=ymT[p][:, cbk, :], in0=qq[:], in1=szc[:], op=ALU.mult),
                         [qq.b, szc.b], [ymT[p].b])
                po = [pbank("b"), pbank("b")]
                for hf in range(2):
                    def fo():
                        ins = None
                        for mc in range(16):
                            ins = nc.tensor.matmul(po[hf][:], lhsT=ymT[p][:, mc, :], rhs=wo[:, mc, hf * 512:(hf + 1) * 512],
                                                   start=(mc == 0), stop=(mc == 15))
                        return ins
                    S.op("pe", fo, [ymT[p].b, wo.b], [po[hf].b])
                ln_stats("c", xt[p], mv, rstd, nb_, st6, LN_EPS)
                S.op("act", lambda: nc.scalar.activation(out=xn[:], in_=xt[p][:], func=AF.Identity, bias=nb_[:, 0:1],
                                                         scale=rstd[:, 0:1]), [xt[p].b, rstd.b, nb_.b], [xn.b])
                S.op("dve", lambda: nc.vector.tensor_tensor(out=xg[:], in0=xn[:], in1=gbc[:], op=ALU.mult), [xn.b, gbc.b], [xg.b])
                S.op("pool", lambda: nc.gpsimd.tensor_tensor(out=xh[:], in0=xg[:], in1=bbc[:], op=ALU.add), [xg.b, bbc.b], [xh.b])
                for hf in range(2):
                    o = slice(hf * 512, (hf + 1) * 512)
                    S.op("dve", lambda: nc.vector.scalar_tensor_tensor(out=sres[:, o], in0=xh[:, o], scalar=DN_ALPHA,
                                                                       in1=po[hf][:], op0=ALU.mult, op1=ALU.add),
                         [xh.b, po[hf].b], [sres.b])
                ln_stats("c2", sres, mvb, rstdb, nbb, st6b, LN_EPS)
                S.op("act", lambda: nc.scalar.activation(out=on[:], in_=sres[:], func=AF.Identity, bias=nbb[:, 0:1],
                                                         scale=rstdb[:, 0:1]), [sres.b, rstdb.b, nbb.b], [on.b])
                S.op("dve", lambda: nc.vector.tensor_tensor(out=og[:], in0=on[:], in1=g2[:], op=ALU.mult), [on.b, g2.b], [og.b])
                S.op("pool", lambda: nc.gpsimd.tensor_tensor(out=yo[p][:], in0=og[:], in1=b2[:], op=ALU.add), [og.b, b2.b], [yo[p].b])
                S.dma(ys[i * 128:(i + 1) * 128, :], yo[p][:], [yo[p].b], [ys_bs[i]])
            S.barrier()

        S.finish()
    return nc


CST_COLS = 128 + 2 * 896 + 1
DBG = {}
NAMES = {}


def make_consts():
    r = np.arange(128)[:, None]
    c = np.arange(128)[None, :]
    SU = (r < c).astype(np.float32); IU = (r <= c).astype(np.float32)
    SL = (r > c).astype(np.float32); IL = (r >= c).astype(np.float32)
    parts = [np.eye(128, dtype=np.float32)]
    for (S_, I_, St) in ((SU, IU, SL), (SL, IL, SU)):
        parts += [S_, I_, -S_, I_, -St, CDEC * I_, CDEC * S_]
    parts.append(np.full((128, 1), CDEC, np.float32))
    out = np.concatenate(parts, axis=1).astype(np.float32)
    assert out.shape[1] == CST_COLS
    return np.ascontiguousarray(out)


W_NAMES = ["emb_ln_g", "emb_ln_b", "w_in", "conv_w", "conv_b", "shift_mu", "w0", "w_up", "a0", "a_up",
           "k_k", "k_a", "r_k", "lnx_g", "lnx_b", "w_out", "ln_g", "ln_b"]


def weight_map(inp):
    m = {}
    for n in W_NAMES:
        a = np.asarray(inp[n], dtype=np.float32)
        if n not in ("emb_ln_g", "emb_ln_b"):
            a = a[0]
        if n == "r_k":
            a = a.reshape(1024)
        m[n] = np.ascontiguousarray(a)
    m["cst"] = make_consts()
    return m


_NC_CACHE = {}


def run_streams(streams, keeps, inp, NT, BP):
    key = (NT, BP)
    if key not in _NC_CACHE:
        _NC_CACHE[key] = build(NT, BP)
    nc = _NC_CACHE[key]
    wm = weight_map(inp)
    in_maps = []
    for s, k in zip(streams, keeps):
        d = dict(wm)
        d["xs"] = np.ascontiguousarray(s, dtype=np.float32)
        d["keep"] = np.ascontiguousarray(k, dtype=np.float32)
        in_maps.append(d)
    res = run_bass_kernel_spmd(nc, in_maps, core_ids=list(range(len(streams))))
    return [r["ys"] for r in res.results]


def kernel(**inp):
    xp = np.asarray(inp["x_prompt"], dtype=np.float32)
    xsm = np.asarray(inp["x_sample"], dtype=np.float32)
    NT, BP = 128, 16
    NB = NT // BP - 1
    ntok = NT * 128
    streams = [xsm[0], xsm[1], xp.reshape(ntok, D)]
    keeps = [np.ones((128, NB), np.float32), np.ones((128, NB), np.float32), np.zeros((128, NB), np.float32)]
    for _ in range(5):
        streams.append(np.zeros((ntok, D), np.float32))
        keeps.append(np.zeros((128, NB), np.float32))
    outs = run_streams(streams, keeps, inp, NT, BP)
    y_sample = np.stack([outs[0], outs[1]], axis=0).reshape(2, 16384, D)
    y_prompt = outs[2].reshape(8, 2048, D)
    return (y_prompt.astype(np.float32), y_sample.astype(np.float32))
```

```python
import contextlib
import numpy as np
import concourse.bass as bass
import concourse.mybir as mybir
from concourse.bass_utils import run_bass_kernel_spmd

F32 = mybir.dt.float32
BF16 = mybir.dt.bfloat16
AF = mybir.ActivationFunctionType
ALU = mybir.AluOpType
AX = mybir.AxisListType

D = 1024
NCV = 4096
NRW = 4352
NIN = NCV + NRW
HD = 64
LN_EPS = 1e-5
LNX_EPS = 64e-5
DN_ALPHA = 2.0 ** 0.25
CDEC = -float(np.exp(-0.5))
NG = 2
GC = 512
NH = 8
NLEV = 7


class Buf:
    __slots__ = ("name", "w", "rs", "excl")

    def __init__(self, name):
        self.name = name
        self.w = None
        self.rs = {}
        self.excl = False


class Sch:
    R = 4
    ND = 8

    def __init__(self, nc, es):
        self.nc = nc
        self.eng = {"pe": nc.tensor, "act": nc.scalar, "dve": nc.vector, "pool": nc.gpsimd, "sp": nc.sync}
        self.sems = {e: [es.enter_context(nc.semaphore(f"s_{e}{i}")) for i in range(self.R)]
                     for e in ("pe", "act", "dve", "pool")}
        self.cnt = {e: 0 for e in self.sems}
        self.known = {e: {} for e in self.eng}
        self.dsems = [es.enter_context(nc.semaphore(f"s_d{i}")) for i in range(self.ND)]
        self.dval = [0] * self.ND
        self.dnext = 0
        self.all_bufs = []

    def buf(self, name):
        b = Buf(name)
        self.all_bufs.append(b)
        return b

    def _wait(self, waiter, ev):
        key = (ev[0], ev[1])
        if ev[0] == "E" and ev[1] == waiter and waiter == "pe":
            return
        if self.known[waiter].get(key, -1) >= ev[2]:
            return
        if ev[0] == "E":
            sem = self.sems[ev[1]][ev[2] % self.R]
            val = ev[2] // self.R + 1
        else:
            sem = self.dsems[ev[1]]
            val = ev[2]
        self.eng[waiter].wait_ge(sem, val)
        self.known[waiter][key] = ev[2]

    def _deps(self, waiter, reads, writes):
        for b in reads:
            if b.w is not None:
                self._wait(waiter, b.w)
            if b.excl:
                for ev in b.rs.values():
                    if not (ev[0] == "E" and ev[1] == waiter):
                        self._wait(waiter, ev)
        for b in writes:
            if b.w is not None:
                self._wait(waiter, b.w)
            for ev in b.rs.values():
                self._wait(waiter, ev)

    def _commit(self, ev, reads, writes):
        for b in reads:
            b.rs[(ev[0], ev[1])] = ev
        for b in writes:
            b.w = ev
            b.rs = {}

    muted = False

    def stage(self, k):
        self.muted = k > DBG.get("stage", 99)

    def op(self, eng, fn, reads, writes):
        if self.muted:
            return
        self._deps(eng, reads, writes)
        ins = fn()
        idx = self.cnt[eng]
        self.cnt[eng] += 1
        ins.then_inc(self.sems[eng][idx % self.R], 1)
        self._commit(("E", eng, idx), reads, writes)

    def dma(self, out, in_, reads, writes, q="sp", **kw):
        if self.muted:
            return
        k = self.dnext
        self.dnext = (self.dnext + 1) % self.ND
        if self.dval[k] > 0:
            self._wait(q, ("D", k, self.dval[k]))
        self._deps(q, reads, writes)
        ins = self.eng[q].dma_start(out=out, in_=in_, **kw)
        self.dval[k] += 16
        ins.then_inc(self.dsems[k], 16)
        self._commit(("D", k, self.dval[k]), reads, writes)

    def barrier(self):
        self.muted = False
        for w in self.eng:
            for k in range(self.ND):
                if self.dval[k] > 0:
                    self._wait(w, ("D", k, self.dval[k]))
            for e in self.cnt:
                if self.cnt[e] > 0:
                    self._wait(w, ("E", e, self.cnt[e] - 1))

    def finish(self):
        for k in range(self.ND):
            if self.dval[k] > 0:
                self._wait("sp", ("D", k, self.dval[k]))
        for e in self.cnt:
            if self.cnt[e] > 0:
                self._wait("sp", ("E", e, self.cnt[e] - 1))


class TL:
    def __init__(self, sch, t, name):
        self.t = t
        self.b = sch.buf(name)

    def __getitem__(self, k):
        return self.t[k]


def bc(ap, shape):
    return ap.broadcast_to(list(shape))


def build(NT, BP):
    NTOK = NT * 128
    NB = max(1, NT // BP - 1)
    nc = bass.Bass("TRN2", target_bir_lowering=False)
    dt_in = lambda n, s: nc.dram_tensor(n, s, F32, kind="ExternalInput").ap()
    xs = dt_in("xs", [NTOK, D])
    keep_d = dt_in("keep", [128, NB])
    cst_d = dt_in("cst", [128, CST_COLS])
    emb_g_d = dt_in("emb_ln_g", [D]); emb_b_d = dt_in("emb_ln_b", [D])
    w_in_d = dt_in("w_in", [D, NIN])
    conv_w_d = dt_in("conv_w", [3, 1024]); conv_b_d = dt_in("conv_b", [1024])
    mu_d = dt_in("shift_mu", [NRW])
    w0_d = dt_in("w0", [2, 1024]); wup_d = dt_in("w_up", [2, 64, 1024])
    a0_d = dt_in("a0", [2, 1024]); aup_d = dt_in("a_up", [2, 64, 1024])
    kk_d = dt_in("k_k", [1024]); ka_d = dt_in("k_a", [1024]); rk_d = dt_in("r_k", [1024])
    lxg_d = dt_in("lnx_g", [1024]); lxb_d = dt_in("lnx_b", [1024])
    wout_d = dt_in("w_out", [2048, D])
    lng_d = dt_in("ln_g", [D]); lnb_d = dt_in("ln_b", [D])
    ys = nc.dram_tensor("ys", [NTOK, D], F32, kind="ExternalOutput").ap()
    XT = nc.dram_tensor("XT", [128, 8, NTOK + 2], BF16).ap()
    YF = nc.dram_tensor("YF", [NTOK, GC], F32).ap()
    YR = nc.dram_tensor("YR", [128, 8, NTOK], BF16).ap()
    RKV = nc.dram_tensor("RKV", [NT, 128, 5, GC], F32).ap()

    with contextlib.ExitStack() as es0:
        es0.enter_context(nc.allow_non_contiguous_dma(reason="small strided param / halo loads"))
        S = Sch(nc, es0)
        XT_bs = [S.buf(f"XT{i}") for i in range(NT + 2)]
        YF_bs = [S.buf(f"YF{i}") for i in range(NT)]
        RKV_bs = [S.buf(f"RKV{i}") for i in range(NT)]
        YR_bs = [[S.buf(f"YR{g}_{i}") for i in range(NT)] for g in range(NG)]
        ys_bs = [S.buf(f"ys{i}") for i in range(NT)]

        uid = [0]

        def sb(es, name, shape, dtype):
            uid[0] += 1
            nm = f"sb{uid[0]}_{name}"
            NAMES[name] = nm
            return TL(S, es.enter_context(nc.sbuf_tensor(nm, list(shape), dtype)), nm)

        PS = [TL(S, es0.enter_context(nc.psum_tensor(f"ps{i}", [128, 512], F32)), f"ps{i}") for i in range(8)]
        for p_ in PS:
            p_.b.excl = True
        prot = {"f": [0, [0, 1, 2]], "b1": [0, [3, 4, 5]], "b": [0, [6, 7]]}

        def pbank(kind):
            st = prot[kind]
            p = PS[st[1][st[0] % len(st[1])]]
            st[0] += 1
            return p

        cst = sb(es0, "cst", [128, CST_COLS], F32)
        S.dma(cst[:], cst_d, [], [cst.b])
        identb = sb(es0, "identb", [128, 128], BF16)
        S.op("dve", lambda: nc.vector.tensor_copy(out=identb[:], in_=cst[:, 0:128]), [cst.b], [identb.b])
        keep = sb(es0, "keep", [128, NB], F32)
        S.dma(keep[:], keep_d, [], [keep.b])
        zpad = sb(es0, "zpad", [128, 8, 1], BF16)
        S.op("pool", lambda: nc.gpsimd.memset(zpad[:], 0.0), [], [zpad.b])
        S.dma(XT[:, :, 0:1], zpad[:], [zpad.b], [XT_bs[0]])
        S.dma(XT[:, :, NTOK + 1:NTOK + 2], zpad[:], [zpad.b], [XT_bs[NT + 1]])

        onesf = sb(es0, "onesf", [1, 128], F32)
        S.op("pool", lambda: nc.gpsimd.memset(onesf[:], 1.0), [], [onesf.b])
        rowbuf = sb(es0, "rowbuf", [1, 1024], F32)

        def bcast_load(dst, dcol, src1d, ncol):
            S.dma(rowbuf[0:1, 0:ncol], src1d.rearrange("(o n) -> o n", o=1), [], [rowbuf.b])
            for c_ in range(0, ncol, 512):
                n_ = min(512, ncol - c_)
                pb_ = pbank("f")
                S.op("pe", lambda: nc.tensor.matmul(pb_[:, 0:n_], lhsT=onesf[0:1, :], rhs=rowbuf[0:1, c_:c_ + n_],
                                                    start=True, stop=True), [onesf.b, rowbuf.b], [pb_.b])
                S.op("act", lambda: nc.scalar.copy(out=dst[:, dcol + c_:dcol + c_ + n_], in_=pb_[:, 0:n_]), [pb_.b], [dst.b])

        def ln_stats(es_tag, src, mv, rstd, nb_, st6, eps):
            S.op("dve", lambda: nc.vector.bn_stats(out=st6[:, 0, :], in_=src[:, 0:512]), [src.b], [st6.b])
            S.op("dve", lambda: nc.vector.bn_stats(out=st6[:, 1, :], in_=src[:, 512:1024]), [src.b], [st6.b])
            S.op("dve", lambda: nc.vector.bn_aggr(out=mv[:], in_=st6[:]), [st6.b], [mv.b])
            S.op("act", lambda: nc.scalar.activation(out=rstd[:], in_=mv[:, 1:2], func=AF.Sqrt, bias=eps, scale=1.0),
                 [mv.b], [rstd.b])
            S.op("dve", lambda: nc.vector.reciprocal(out=rstd[:], in_=rstd[:]), [rstd.b], [rstd.b])
            S.op("dve", lambda: nc.vector.scalar_tensor_tensor(out=nb_[:], in0=mv[:, 0:1], scalar=-1.0, in1=rstd[:],
                                                               op0=ALU.mult, op1=ALU.mult), [mv.b, rstd.b], [nb_.b])

        SD = int(nc.vector.BN_STATS_DIM)
        AD = int(nc.vector.BN_AGGR_DIM)

        with contextlib.ExitStack() as es:
            gbc = sb(es, "p1_g", [128, D], F32); bbc = sb(es, "p1_b", [128, D], F32)
            bcast_load(gbc, 0, emb_g_d, D)
            bcast_load(bbc, 0, emb_b_d, D)
            xt = [sb(es, f"p1_x{i}", [128, D], F32) for i in range(2)]
            xn = [sb(es, f"p1_xn{i}", [128, D], F32) for i in range(2)]
            xg = [sb(es, f"p1_xg{i}", [128, D], F32) for i in range(2)]
            xh = [sb(es, f"p1_xh{i}", [128, D], BF16) for i in range(2)]
            xT = [sb(es, f"p1_xT{i}", [128, 8, 128], BF16) for i in range(2)]
            st6 = [sb(es, f"p1_st{i}", [128, 2, SD], F32) for i in range(2)]
            mv = [sb(es, f"p1_mv{i}", [128, AD], F32) for i in range(2)]
            rstd = [sb(es, f"p1_rs{i}", [128, 1], F32) for i in range(2)]
            nb_ = [sb(es, f"p1_nb{i}", [128, 1], F32) for i in range(2)]
            for i in range(NT if DBG.get("p1", True) else 0):
                p = i % 2
                S.dma(xt[p][:], xs[i * 128:(i + 1) * 128, :], [], [xt[p].b])
                ln_stats("p1", xt[p], mv[p], rstd[p], nb_[p], st6[p], LN_EPS)
                S.op("act", lambda: nc.scalar.activation(out=xn[p][:], in_=xt[p][:], func=AF.Identity,
                                                         bias=nb_[p][:, 0:1], scale=rstd[p][:, 0:1]),
                     [xt[p].b, rstd[p].b, nb_[p].b], [xn[p].b])
                S.op("dve", lambda: nc.vector.tensor_tensor(out=xg[p][:], in0=xn[p][:], in1=gbc[:], op=ALU.mult),
                     [xn[p].b, gbc.b], [xg[p].b])
                S.op("pool", lambda: nc.gpsimd.tensor_tensor(out=xh[p][:], in0=xg[p][:], in1=bbc[:], op=ALU.add),
                     [xg[p].b, bbc.b], [xh[p].b])
                pt = pbank("f")
                ptv = pt[:].bitcast(BF16)

                def tr():
                    ins = None
                    for c in range(8):
                        ins = nc.tensor.transpose(out=ptv[:, c * 128:(c + 1) * 128], in_=xh[p][:, c * 128:(c + 1) * 128],
                                                  identity=identb[:])
                    return ins
                S.op("pe", tr, [xh[p].b, identb.b], [pt.b])
                S.op("act", lambda: nc.scalar.copy(out=xT[p][:].rearrange("p c t -> p (c t)"), in_=ptv), [pt.b], [xT[p].b])
                S.dma(XT[:, :, 1 + i * 128:1 + (i + 1) * 128], xT[p][:], [xT[p].b], [XT_bs[i + 1]], q="act")
            S.barrier()

        def load_xth(xth, i):
            S.dma(xth[:], XT[:, :, i * 128:i * 128 + 130], [XT_bs[i], XT_bs[i + 1], XT_bs[i + 2]], [xth.b])
            if i % BP == 0 and i > 0:
                bidx = i // BP - 1
                S.op("dve", lambda: nc.vector.tensor_scalar(out=xth[:, :, 0:1], in0=xth[:, :, 0:1],
                                                            scalar1=keep[:, bidx:bidx + 1], scalar2=None, op0=ALU.mult),
                     [xth.b, keep.b], [xth.b])
            if (i + 1) % BP == 0 and i + 1 < NT:
                bidx = (i + 1) // BP - 1
                S.op("dve", lambda: nc.vector.tensor_scalar(out=xth[:, :, 129:130], in0=xth[:, :, 129:130],
                                                            scalar1=keep[:, bidx:bidx + 1], scalar2=None, op0=ALU.mult),
                     [xth.b, keep.b], [xth.b])

        for g in range(NG if DBG.get("rwkv", True) else 0):
            with contextlib.ExitStack() as es:
                c0 = g * GC
                wl = sb(es, "wl", [128, 16, 256], BF16)
                upw = sb(es, "upw", [64, 2, 2, GC], BF16)
                b0h = sb(es, "b0h", [1, 2, 2, GC], BF16); b0l = sb(es, "b0l", [1, 2, 2, GC], BF16)
                onesr = sb(es, "onesr", [1, 128], BF16)
                S.op("pool", lambda: nc.gpsimd.memset(onesr[:], 1.0), [], [onesr.b])
                with contextlib.ExitStack() as es2:
                    upf = sb(es2, "upf", [64, 2, 2, GC], F32)
                    b0f = sb(es2, "b0f", [1, 2, 2, GC], F32); b0t = sb(es2, "b0t", [1, 2, 2, GC], F32)
                    for wi, (ud, bd) in enumerate(((wup_d, w0_d), (aup_d, a0_d))):
                        for d in range(2):
                            S.dma(upf[:, wi, d, :], ud[d, :, c0:c0 + GC], [], [upf.b])
                            S.dma(b0f[:, wi, d, :], bd[d:d + 1, c0:c0 + GC], [], [b0f.b])
                    S.op("dve", lambda: nc.vector.tensor_copy(out=upw[:], in_=upf[:]), [upf.b], [upw.b])
                    S.op("dve", lambda: nc.vector.tensor_copy(out=b0h[:], in_=b0f[:]), [b0f.b], [b0h.b])
                    S.op("dve", lambda: nc.vector.tensor_tensor(out=b0t[:], in0=b0f[:], in1=b0h[:], op=ALU.subtract),
                         [b0f.b, b0h.b], [b0t.b])
                    S.op("dve", lambda: nc.vector.tensor_copy(out=b0l[:], in_=b0t[:]), [b0t.b], [b0l.b])
                    S.barrier()
                prm = {}
                for nm, src in (("kk", kk_d), ("ka", ka_d), ("rk", rk_d), ("lxg", lxg_d), ("lxb", lxb_d)):
                    prm[nm] = sb(es, "prm_" + nm, [128, GC], F32)
                    bcast_load(prm[nm], 0, src[c0:c0 + GC], GC)

                xth = [sb(es, f"xth{i}", [128, 8, 130], BF16) for i in range(2)]
                xst = sb(es, "xst", [128, 8, 128], BF16)
                f32t = {n: sb(es, "w_" + n, [128, GC], F32) for n in
                        ("sg", "a", "e1", "e2", "e3", "kkr", "t1", "kd", "ysum", "rS", "kS", "vS")}
                bft_ = {n: sb(es, "h_" + n, [128, GC], BF16) for n in ("V", "kh", "bh", "kkt", "rt", "Zs", "nU", "ob")}
                f32p = [dict(f32t) for _ in range(3)]
                bftp = [dict(bft_) for _ in range(3)]
                for pi in (1, 2):
                    for n_ in ("V", "kh", "bh"):
                        bftp[pi][n_] = sb(es, f"h{pi}_" + n_, [128, GC], BF16)
                tl_ = sb(es, "tl_", [64, 3, 128], BF16)
                kT2 = [sb(es, f"kT{i}", [64, NH, 128], BF16) for i in range(2)]
                bT2 = [sb(es, f"bT{i}", [64, NH, 128], BF16) for i in range(2)]
                krT3 = [sb(es, f"krT{i}", [64, NH, 2, 128], BF16) for i in range(3)]
                oT = sb(es, "oT", [128, 4, 128], BF16)
                MA1p = [sb(es, f"MA1{i}", [128, NH, 2, 128], BF16) for i in range(2)]
                MA2p = [sb(es, f"MA2{i}", [128, NH, 2, 128], BF16) for i in range(2)]
                Xs = [[sb(es, f"Xs{i}{q}", [128, 4, 128], BF16) for q in range(2)] for i in range(2)]
                Qs = [[sb(es, f"Qs{i}{q}", [128, 4, 128], BF16) for q in range(2)] for i in range(2)]
                Xtq = [[sb(es, f"Xt{i}{q}", [128, 4, 128], BF16) for q in range(2)] for i in range(2)]
                TTqp = [[sb(es, f"TT{i}{q}", [128, 4, 128], BF16) for q in range(2)] for i in range(2)]

                Hf = sb(es, "Hf", [64, NH, 64], F32); Hb = sb(es, "Hb", [64, NH, 64], BF16)
                Ht = sb(es, "Ht", [64, NH, 64], F32)
                gam3 = [sb(es, f"gam{i}", [64, NH], F32) for i in range(3)]
                ss = sb(es, "ss", [128, NH], F32); rn = sb(es, "rn", [128, NH], F32)
                bs = sb(es, "bs", [128, NH], F32)
                gs1 = sb(es, "gs1", [128, NH], F32); gs2 = sb(es, "gs2", [128, NH], F32)
                grs = sb(es, "grs", [128, NH], F32)
                if DBG.get("verbose"):
                    print("sbuf bytes remaining (rwkv scope):", nc.sbuf_bytes_remaining)

                def prep_w(dst, dcol, col, ncol):
                    mub, mua, mubb = f32t["e1"], f32t["e2"], f32t["e3"]
                    stg = [f32t["sg"], f32t["a"]]
                    bcast_load(mub, 0, mu_d[col - NCV:col - NCV + ncol], ncol)
                    S.op("dve", lambda: nc.vector.tensor_scalar(out=mua[:, 0:ncol], in0=mub[:, 0:ncol], scalar1=-1.0,
                                                                scalar2=1.0, op0=ALU.mult, op1=ALU.add), [mub.b], [mua.b])
                    S.op("dve", lambda: nc.vector.tensor_scalar(out=mubb[:, 0:ncol], in0=mub[:, 0:ncol], scalar1=0.5,
                                                                scalar2=None, op0=ALU.mult), [mub.b], [mubb.b])
                    for bi in range(ncol // 64):
                        w_ = stg[bi % 2]
                        wv = w_[:].rearrange("p (c n) -> p c n", n=64)
                        o = bi * 64
                        S.dma(wv, w_in_d[:, col + o:col + o + 64].rearrange("(c p) n -> p c n", p=128), [], [w_.b])
                        S.op("dve", lambda: nc.vector.tensor_tensor(
                            out=dst[:, 0:8, dcol + o:dcol + o + 64], in0=wv,
                            in1=bc(mua[:, o:o + 64].unsqueeze(1), [128, 8, 64]), op=ALU.mult), [w_.b, mua.b], [dst.b])
                        S.op("pool", lambda: nc.gpsimd.tensor_tensor(
                            out=dst[:, 8:16, dcol + o:dcol + o + 64], in0=wv,
                            in1=bc(mubb[:, o:o + 64].unsqueeze(1), [128, 8, 64]), op=ALU.mult), [w_.b, mubb.b], [dst.b])
                prep_w(wl, 0, NCV + 4096, 256)

                for dr in range(2):
                    bwd = dr == 1
                    esd = contextlib.ExitStack()
                    qlist = [3] if bwd else [0, 1, 2]
                    qmap = {q: j for j, q in enumerate(qlist)}
                    wq = sb(esd, "wq", [128, 16, len(qlist) * GC], BF16)
                    for q in qlist:
                        prep_w(wq, qmap[q] * GC, NCV + q * 1024 + c0, GC)
                    RKN = ("rS", "kS", "kkr", "vS", "kd0")
                    yf3 = None
                    if bwd:
                        yf3 = [sb(esd, f"yf3_{i}", [128, GC], F32) for i in range(3)]
                        sq2_ = sb(esd, "w_sq2", [128, GC], F32)
                        for pi in range(3):
                            f32p[pi]["sq2"] = sq2_
                            for n_ in ("vbon", "sz", "kd0"):
                                f32p[pi][n_] = sb(esd, f"w{pi}_" + n_, [128, GC], F32)
                            if pi > 0:
                                for n_ in ("rS", "kS", "kkr", "vS"):
                                    f32p[pi][n_] = sb(esd, f"w{pi}_" + n_, [128, GC], F32)

                    def load_rkv(par_, ti):
                        for j_, n_ in enumerate(RKN):
                            dst_ = f32p[par_][n_]
                            S.dma(dst_[:], RKV[ti, :, j_, :], [RKV_bs[ti]], [dst_.b])
                    cb = 128 + dr * 896
                    m1 = cst[:, cb:cb + 256]; m2 = cst[:, cb + 256:cb + 512]; m3 = cst[:, cb + 512:cb + 640]
                    tinc = cst[:, cb + 640:cb + 768]; texc = cst[:, cb + 768:cb + 896]
                    ccol = cst[:, CST_COLS - 1:CST_COLS]
                    S.op("dve", lambda: nc.vector.memset(Hf[:], 0.0), [], [Hf.b])
                    S.op("dve", lambda: nc.vector.memset(Hb[:], 0.0), [], [Hb.b])
                    order = list(range(NT - 1, -1, -1)) if bwd else list(range(NT))
                    load_xth(xth[0], order[0])
                    if bwd:
                        load_rkv(0, order[0])
                    v3 = lambda ap: ap.rearrange("p (h c) -> p h c", c=HD)

                    def front(oi, i):
                        par = oi % 2
                        t = f32p[oi % 3]; bft = bftp[oi % 3]
                        kT = kT2[par]; bT = bT2[par]; krT = krT3[oi % 3]; gam = gam3[oi % 3]
                        V = bft["V"]; nU = bft["nU"]
                        xh_ = xth[oi % 2]
                        if oi + 1 < NT:
                            load_xth(xth[(oi + 1) % 2], order[oi + 1])
                            if bwd:
                                load_rkv((oi + 1) % 3, order[oi + 1])
                        if bwd:
                            yield
                            S.dma(yf3[oi % 3][:], YF[i * 128:(i + 1) * 128, :], [YF_bs[i]], [yf3[oi % 3].b])
                        yield
                        S.op("pool", lambda: nc.gpsimd.tensor_tensor(out=xst[:], in0=xh_[:, :, 0:128], in1=xh_[:, :, 2:130],
                                                                      op=ALU.add), [xh_.b], [xst.b])

                        def proj_fm(pt_ap, wcol, ncol):
                            ins = None
                            for kc in range(16):
                                rhs = xh_[:, kc, 1:129] if kc < 8 else xst[:, kc - 8, :]
                                ins = nc.tensor.matmul(pt_ap, lhsT=wl[:, kc, wcol:wcol + ncol], rhs=rhs,
                                                       start=(kc == 0), stop=(kc == 15))
                            return ins

                        def proj_tm(pt_ap, q):
                            ins = None
                            for kc in range(16):
                                lhsT = xh_[:, kc, 1:129] if kc < 8 else xst[:, kc - 8, :]
                                ins = nc.tensor.matmul(pt_ap, lhsT=lhsT, rhs=wq[:, kc, qmap[q] * GC:(qmap[q] + 1) * GC],
                                                       start=(kc == 0), stop=(kc == 15))
                            return ins

                        pc = pbank("f")
                        ncode = 2

                        def codes():
                            ins = proj_fm(pc[0:64, 0:128], dr * 64, 64)
                            ins = proj_fm(pc[0:64, 128:256], 128 + dr * 64, 64)
                            return ins
                        yield
                        S.op("pe", codes, [xh_.b, xst.b, wl.b], [pc.b])
                        yield
                        S.op("act", lambda: nc.scalar.activation(out=tl_[:, 0, :], in_=pc[0:64, 0:128], func=AF.Tanh),
                             [pc.b], [tl_.b])
                        yield
                        S.op("act", lambda: nc.scalar.copy(out=tl_[:, 1:ncode, :].rearrange("p a t -> p (a t)"),
                                                           in_=pc[0:64, 128:128 * ncode]), [pc.b], [tl_.b])

                        def lowrank(pt, ci, wi, d):
                            def f():
                                nc.tensor.matmul(pt[:], lhsT=tl_[:, ci, :], rhs=upw[:, wi, d, :], start=True, stop=False)
                                nc.tensor.matmul(pt[:], lhsT=onesr[:], rhs=b0h[:, wi, d, :], start=False, stop=False)
                                return nc.tensor.matmul(pt[:], lhsT=onesr[:], rhs=b0l[:, wi, d, :], start=False, stop=True)
                            S.op("pe", f, [tl_.b, upw.b, onesr.b, b0h.b, b0l.b], [pt.b])
                        yield
                        pd_ = pbank("f"); lowrank(pd_, 0, 0, dr)
                        yield
                        S.op("act", lambda: nc.scalar.activation(out=t["sg"][:], in_=pd_[:], func=AF.Sigmoid),
                             [pd_.b], [t["sg"].b])
                        yield
                        pa_ = pbank("f"); lowrank(pa_, 1, 1, dr)
                        yield
                        S.op("act", lambda: nc.scalar.activation(out=t["a"][:], in_=pa_[:], func=AF.Sigmoid),
                             [pa_.b], [t["a"].b])
                        sg = t["sg"]
                        pcum = pbank("f")
                        yield
                        S.op("pe", lambda: nc.tensor.matmul(pcum[:], lhsT=tinc, rhs=sg[:], start=True, stop=True),
                             [cst.b, sg.b], [pcum.b])
                        yield
                        S.op("act", lambda: nc.scalar.activation(out=t["e1"][:], in_=pcum[:], func=AF.Exp, scale=-1.0),
                             [pcum.b], [t["e1"].b])
                        yield
                        S.op("act", lambda: nc.scalar.activation(out=t["e3"][:], in_=pcum[:], func=AF.Exp),
                             [pcum.b], [t["e3"].b])
                        pcx = pbank("f")

                        yield
                        S.op("pe", lambda: nc.tensor.matmul(pcx[:], lhsT=texc, rhs=sg[:], start=True, stop=True),
                             [cst.b, sg.b], [pcx.b])
                        yield
                        S.op("act", lambda: nc.scalar.activation(out=t["e2"][:], in_=pcx[:], func=AF.Exp),
                             [pcx.b], [t["e2"].b])
                        pgm = pbank("f")

                        def gsum():
                            ins = None
                            for h in range(NH):
                                ins = nc.tensor.matmul(pgm[0:64, h:h + 1], lhsT=sg[:, h * HD:(h + 1) * HD], rhs=ccol,
                                                       start=True, stop=True)
                            return ins
                        yield
                        S.op("pe", gsum, [sg.b, cst.b], [pgm.b])
                        yield
                        S.op("act", lambda: nc.scalar.activation(out=gam[:], in_=pgm[0:64, 0:NH], func=AF.Exp), [pgm.b], [gam.b])

                        rS, kS, vS = t["rS"], t["kS"], t["vS"]
                        V = bft["V"]
                        v3 = lambda ap: ap.rearrange("p (h c) -> p h c", c=HD)
                        if not bwd:
                            pr = pbank("f"); S.op("pe", lambda: proj_tm(pr[:], 0), [xh_.b, xst.b, wq.b], [pr.b])
                            yield
                            S.op("act", lambda: nc.scalar.copy(out=rS[:], in_=pr[:]), [pr.b], [rS.b])
                            pk = pbank("f"); S.op("pe", lambda: proj_tm(pk[:], 1), [xh_.b, xst.b, wq.b], [pk.b])
                            yield
                            S.op("act", lambda: nc.scalar.copy(out=kS[:], in_=pk[:]), [pk.b], [kS.b])
                            pv = pbank("f"); S.op("pe", lambda: proj_tm(pv[:], 2), [xh_.b, xst.b, wq.b], [pv.b])
                            yield
                            S.op("act", lambda: nc.scalar.copy(out=vS[:], in_=pv[:]), [pv.b], [vS.b])
                            yield
                            S.op("act", lambda: nc.scalar.copy(out=V[:], in_=pv[:]), [pv.b], [V.b])
                            yield
                            S.op("dve", lambda: nc.vector.tensor_tensor(out=t["kkr"][:], in0=kS[:], in1=prm["kk"][:], op=ALU.mult),
                                 [kS.b, prm["kk"].b], [t["kkr"].b])
                            yield
                            S.op("pool", lambda: nc.gpsimd.tensor_tensor(out=t["t1"][:], in0=t["kkr"][:], in1=t["kkr"][:], op=ALU.mult),
                                 [t["kkr"].b], [t["t1"].b])
                            yield
                            S.op("dve", lambda: nc.vector.tensor_reduce(out=ss[:], in_=v3(t["t1"][:]), axis=AX.X, op=ALU.add),
                                 [t["t1"].b], [ss.b])
                            yield
                            S.op("act", lambda: nc.scalar.activation(out=rn[:], in_=ss[:], func=AF.Sqrt), [ss.b], [rn.b])
                            yield
                            S.op("dve", lambda: nc.vector.tensor_scalar(out=rn[:], in0=rn[:], scalar1=1e-12, scalar2=None,
                                                                        op0=ALU.max), [rn.b], [rn.b])
                            yield
                            S.op("dve", lambda: nc.vector.reciprocal(out=rn[:], in_=rn[:]), [rn.b], [rn.b])
                            yield
                            S.op("dve", lambda: nc.vector.tensor_tensor(out=v3(t["kkr"][:]), in0=v3(t["kkr"][:]),
                                                                        in1=bc(rn[:].unsqueeze(2), [128, NH, HD]), op=ALU.mult),
                                 [t["kkr"].b, rn.b], [t["kkr"].b])
                        else:
                            yield
                            S.op("act", lambda: nc.scalar.copy(out=V[:], in_=vS[:]), [vS.b], [V.b])
                        yield
                        S.op("dve", lambda: nc.vector.scalar_tensor_tensor(out=t["t1"][:], in0=t["a"][:], scalar=-1.0,
                                                                             in1=prm["ka"][:], op0=ALU.add, op1=ALU.mult),
                             [t["a"].b, prm["ka"].b], [t["t1"].b])
                        yield
                        S.op("dve", lambda: nc.vector.scalar_tensor_tensor(out=t["kd"][:], in0=t["t1"][:], scalar=1.0,
                                                                           in1=kS[:], op0=ALU.add, op1=ALU.mult),
                             [t["t1"].b, kS.b], [t["kd"].b])
                        yield
                        S.op("pool", lambda: nc.gpsimd.tensor_tensor(out=t["t1"][:], in0=t["kkr"][:], in1=t["a"][:], op=ALU.mult),
                             [t["kkr"].b, t["a"].b], [t["t1"].b])
                        yield
                        S.op("dve", lambda: nc.vector.tensor_tensor(out=bft["kh"][:], in0=t["kd"][:], in1=t["e1"][:], op=ALU.mult),
                             [t["kd"].b, t["e1"].b], [bft["kh"].b])
                        yield
                        S.op("pool", lambda: nc.gpsimd.tensor_tensor(out=bft["bh"][:], in0=t["t1"][:], in1=t["e1"][:], op=ALU.mult),
                             [t["t1"].b, t["e1"].b], [bft["bh"].b])
                        yield
                        S.op("pool", lambda: nc.gpsimd.tensor_tensor(out=bft["kkt"][:], in0=t["kkr"][:], in1=t["e2"][:], op=ALU.mult),
                             [t["kkr"].b, t["e2"].b], [bft["kkt"].b])
                        yield
                        S.op("dve", lambda: nc.vector.tensor_tensor(out=bft["rt"][:], in0=rS[:], in1=t["e3"][:], op=ALU.mult),
                             [rS.b, t["e3"].b], [bft["rt"].b])
                        if not bwd:
                            for j_, src_ in enumerate((rS, kS, t["kkr"], vS, t["kd"])):
                                yield
                                S.dma(RKV[i, :, j_, :], src_[:], [src_.b], [RKV_bs[i]])
                        else:
                            yield
                            S.op("pool", lambda: nc.gpsimd.tensor_tensor(out=t["kd0"][:], in0=t["kd0"][:], in1=t["kd"][:], op=ALU.add),
                                 [t["kd0"].b, t["kd"].b], [t["kd0"].b])
                            yield
                            S.op("pool", lambda: nc.gpsimd.tensor_tensor(out=t["kd0"][:], in0=t["kd0"][:], in1=prm["rk"][:], op=ALU.mult),
                                 [t["kd0"].b, prm["rk"].b], [t["kd0"].b])
                            yield
                            S.op("dve", lambda: nc.vector.tensor_tensor(out=t["kd0"][:], in0=rS[:], in1=t["kd0"][:], op=ALU.mult),
                                 [rS.b, t["kd0"].b], [t["kd0"].b])
                            yield
                            S.op("dve", lambda: nc.vector.tensor_reduce(out=bs[:], in_=v3(t["kd0"][:]), axis=AX.X, op=ALU.add),
                                 [t["kd0"].b], [bs.b])
                            yield
                            S.op("dve", lambda: nc.vector.scalar_tensor_tensor(
                                out=v3(t["vbon"][:]), in0=v3(vS[:]), scalar=0.5, in1=bc(bs[:].unsqueeze(2), [128, NH, HD]),
                                op0=ALU.mult, op1=ALU.mult), [vS.b, bs.b], [t["vbon"].b])
                            pz = pbank("f"); S.op("pe", lambda: proj_tm(pz[:], 3), [xh_.b, xst.b, wq.b], [pz.b])
                            yield
                            S.op("act", lambda: nc.scalar.activation(out=t["sz"][:], in_=pz[:], func=AF.Silu), [pz.b], [t["sz"].b])

                        def trans8(src_, dst_ap, eng):
                            pt = pbank("f")
                            ptv = pt[:].bitcast(BF16)

                            def f():
                                ins = None
                                for h in range(NH):
                                    ins = nc.tensor.transpose(out=ptv[0:64, h * 128:(h + 1) * 128], in_=src_[:, h * HD:(h + 1) * HD],
                                                              identity=identb[:])
                                return ins
                            S.op("pe", f, [src_.b, identb.b], [pt.b])
                            src_v = ptv[0:64, :].rearrange("p (h t) -> p h t", t=128)
                            if eng == "act":
                                S.op("act", lambda: nc.scalar.copy(out=dst_ap[0], in_=src_v), [pt.b], [dst_ap[1]])
                            else:
                                S.op("dve", lambda: nc.vector.tensor_copy(out=dst_ap[0], in_=src_v), [pt.b], [dst_ap[1]])
                        yield
                        trans8(bft["kh"], (kT[:], kT.b), "act")
                        yield
                        trans8(bft["bh"], (bT[:], bT.b), "dve")
                        yield
                        trans8(bft["kkt"], (krT[:, :, 0, :], krT.b), "act")
                        yield
                        trans8(bft["rt"], (krT[:, :, 1, :], krT.b), "dve")

                    def back1(oi, i):
                        par = oi % 2
                        t = f32p[oi % 3]; bft = bftp[oi % 3]
                        kT = kT2[par]; bT = bT2[par]; krT = krT3[oi % 3]; gam = gam3[oi % 3]
                        V = bft["V"]; nU = bft["nU"]
                        MA1 = MA1p[par]; MA2 = MA2p[par]; TTq = TTqp[par]
                        for hp in range(NH // 2):
                            for which, lT, dst, msk in ((0, kT, MA1, m1), (1, bT, MA2, m2)):
                                pm = pbank("b1")

                                def f():
                                    ins = None
                                    for hh in range(2):
                                        h = hp * 2 + hh
                                        ins = nc.tensor.matmul(pm[:, hh * 256:(hh + 1) * 256], lhsT=lT[:, h, :],
                                                               rhs=krT[:, h, :, :].rearrange("p a t -> p (a t)"),
                                                               start=True, stop=True)
                                    return ins
                                yield
                                S.op("pe", f, [lT.b, krT.b], [pm.b])
                                yield
                                S.op("dve", lambda: nc.vector.tensor_tensor(
                                    out=dst[:, hp * 2:hp * 2 + 2, :, :].rearrange("p h a t -> p h (a t)"),
                                    in0=pm[:].rearrange("p (h x) -> p h x", h=2),
                                    in1=bc(msk.unsqueeze(1), [128, 2, 256]), op=ALU.mult), [pm.b, cst.b], [dst.b])
                        for hq in range(NH // 4):
                            pm = pbank("b1")

                            def f():
                                ins = None
                                for hh in range(4):
                                    h = hq * 4 + hh
                                    ins = nc.tensor.matmul(pm[:, hh * 128:(hh + 1) * 128], lhsT=krT[:, h, 0, :],
                                                           rhs=bT[:, h, :], start=True, stop=True)
                                return ins
                            yield
                            S.op("pe", f, [krT.b, bT.b], [pm.b])
                            yield
                            S.op("dve", lambda: nc.vector.tensor_tensor(
                                out=Xtq[0][hq][:], in0=pm[:].rearrange("p (h x) -> p h x", h=4),
                                in1=bc(m3.unsqueeze(1), [128, 4, 128]), op=ALU.mult), [pm.b, cst.b], [Xtq[0][hq].b])

                        def quad_mm(pm, lhs_of, rhs_of):
                            def f():
                                ins = None
                                for hh in range(4):
                                    ins = nc.tensor.matmul(pm[:, hh * 128:(hh + 1) * 128], lhsT=lhs_of(hh), rhs=rhs_of(hh),
                                                           start=True, stop=True)
                                return ins
                            return f

                        def quad_copy(eng, dst, pm):
                            src_ = pm[:].rearrange("p (h x) -> p h x", h=4)
                            if eng == "act":
                                S.op("act", lambda: nc.scalar.copy(out=dst[:], in_=src_), [pm.b], [dst.b])
                            else:
                                S.op("dve", lambda: nc.vector.tensor_copy(out=dst[:], in_=src_), [pm.b], [dst.b])
                        for hq in range(2):
                            hs = slice(hq * 4, hq * 4 + 4)
                            x0 = lambda hh: MA2[:, hq * 4 + hh, 0, :]
                            xt0 = lambda hh: Xtq[0][hq][:, hh, :]
                            yield
                            S.op("pool", lambda: nc.gpsimd.tensor_tensor(
                                out=Qs[0][hq][:], in0=MA2[:, hs, 0, :], in1=bc(identb[:].unsqueeze(1), [128, 4, 128]),
                                op=ALU.add), [MA2.b, identb.b], [Qs[0][hq].b])
                            pm = pbank("b1")
                            yield
                            S.op("pe", quad_mm(pm, xt0, x0), [Xtq[0][hq].b, MA2.b], [pm.b])
                            yield
                            quad_copy("act", Xs[1][hq], pm)
                            pm2 = pbank("b1")
                            yield
                            S.op("pe", quad_mm(pm2, x0, xt0), [Xtq[0][hq].b, MA2.b], [pm2.b])
                            yield
                            quad_copy("act" if hq == 0 else "dve", Xtq[1][hq], pm2)
                        for lev in range(1, NLEV):
                            for hq in range(2):
                                Xk = Xs[lev % 2][hq]; Xtk = Xtq[lev % 2][hq]; Qp = Qs[(lev - 1) % 2][hq]
                                xk = lambda hh: Xk[:, hh, :]
                                xtk = lambda hh: Xtk[:, hh, :]
                                qp = lambda hh: Qp[:, hh, :]
                                if lev <= NLEV - 3:
                                    pmA = pbank("b1")
                                    yield
                                    S.op("pe", quad_mm(pmA, xtk, xk), [Xk.b, Xtk.b], [pmA.b])
                                    yield
                                    quad_copy("act", Xs[(lev + 1) % 2][hq], pmA)
                                pmB = pbank("b1")
                                qdst = Qs[lev % 2][hq] if lev < NLEV - 1 else TTq[hq]
                                yield
                                S.op("pe", quad_mm(pmB, xtk, qp), [Xtk.b, Qp.b], [pmB.b])
                                yield
                                S.op("dve", lambda: nc.vector.tensor_tensor(out=qdst[:], in0=pmB[:].rearrange("p (h x) -> p h x", h=4),
                                                                            in1=Qp[:], op=ALU.add), [pmB.b, Qp.b], [qdst.b])
                                if lev <= NLEV - 2:
                                    pmC = pbank("b1")
                                    yield
                                    S.op("pe", quad_mm(pmC, xk, xtk), [Xk.b, Xtk.b], [pmC.b])
                                    yield
                                    quad_copy("act" if hq == 0 else "dve", Xtq[(lev + 1) % 2][hq], pmC)

                    def back2(oi, i):
                        par = oi % 2
                        t = f32p[oi % 3]; bft = bftp[oi % 3]
                        kT = kT2[par]; bT = bT2[par]; krT = krT3[oi % 3]; gam = gam3[oi % 3]
                        V = bft["V"]; nU = bft["nU"]
                        MA1 = MA1p[par]; MA2 = MA2p[par]; TTq = TTqp[par]
                        yf = yf3[oi % 3] if bwd else None
                        Vh = lambda h: V[:, h * HD:(h + 1) * HD]
                        pzz = pbank("b")

                        def fz():
                            ins = None
                            for h in range(NH):
                                nc.tensor.matmul(pzz[:, h * HD:(h + 1) * HD], lhsT=krT[:, h, 0, :],
                                                 rhs=Hb[:, h, :], start=True, stop=False)
                                ins = nc.tensor.matmul(pzz[:, h * HD:(h + 1) * HD], lhsT=MA1[:, h, 0, :], rhs=Vh(h),
                                                       start=False, stop=True)
                            return ins
                        yield
                        S.op("pe", fz, [krT.b, Hb.b, MA1.b, V.b], [pzz.b])
                        yield
                        S.op("act", lambda: nc.scalar.copy(out=bft["Zs"][:], in_=pzz[:]), [pzz.b], [bft["Zs"].b])
                        pu = pbank("b")

                        def fu():
                            ins = None
                            for h in range(NH):
                                ins = nc.tensor.matmul(pu[:, h * HD:(h + 1) * HD], lhsT=TTq[h // 4][:, h % 4, :],
                                                       rhs=bft["Zs"][:, h * HD:(h + 1) * HD], start=True, stop=True)
                            return ins
                        yield
                        S.op("pe", fu, [TTq[0].b, TTq[1].b, bft["Zs"].b], [pu.b])
                        nU = bft["nU"]
                        yield
                        S.op("act", lambda: nc.scalar.activation(out=nU[:], in_=pu[:], func=AF.Identity, scale=-1.0), [pu.b], [nU.b])
                        py = pbank("b")

                        def fy():
                            ins = None
                            for h in range(NH):
                                o = slice(h * HD, (h + 1) * HD)
                                nc.tensor.matmul(py[:, o], lhsT=krT[:, h, 1, :], rhs=Hb[:, h, :],
                                                 start=True, stop=False)
                                nc.tensor.matmul(py[:, o], lhsT=MA1[:, h, 1, :], rhs=Vh(h), start=False, stop=False)
                                ins = nc.tensor.matmul(py[:, o], lhsT=MA2[:, h, 1, :], rhs=nU[:, o], start=False, stop=True)
                            return ins
                        yield
                        S.op("pe", fy, [krT.b, Hb.b, MA1.b, MA2.b, V.b, nU.b], [py.b])
                        ph = pbank("b")

                        def fh():
                            ins = None
                            for h in range(NH):
                                o = slice(h * HD, (h + 1) * HD)
                                nc.tensor.matmul(ph[0:64, o], lhsT=bft["kh"][:, o], rhs=V[:, o], start=True, stop=False)
                                ins = nc.tensor.matmul(ph[0:64, o], lhsT=bft["bh"][:, o], rhs=nU[:, o], start=False, stop=True)
                            return ins
                        yield
                        S.op("pe", fh, [bft["kh"].b, bft["bh"].b, V.b, nU.b], [ph.b])
                        yield
                        S.op("dve", lambda: nc.vector.tensor_tensor(out=Ht[:], in0=ph[0:64, :].rearrange("p (h v) -> p h v", v=HD),
                                                                    in1=Hf[:], op=ALU.add), [ph.b, Hf.b], [Ht.b])
                        yield
                        S.op("dve", lambda: nc.vector.tensor_tensor(out=Hf[:], in0=Ht[:], in1=bc(gam[:].unsqueeze(2), [64, NH, HD]),
                                                                    op=ALU.mult), [Ht.b, gam.b], [Hf.b])
                        nxt_i = i - 1 if bwd else i + 1
                        bt = i if bwd else i + 1
                        if 0 <= nxt_i < NT and bt % BP == 0:
                            bidx = bt // BP - 1
                            yield
                            S.op("dve", lambda: nc.vector.tensor_scalar(out=Hf[:], in0=Hf[:], scalar1=keep[0:64, bidx:bidx + 1],
                                                                        scalar2=None, op0=ALU.mult), [Hf.b, keep.b], [Hf.b])
                        yield
                        S.op("act", lambda: nc.scalar.copy(out=Hb[:], in_=Hf[:]), [Hf.b], [Hb.b])

                        if not bwd:
                            yield
                            S.op("act", lambda: nc.scalar.copy(out=t["ysum"][:], in_=py[:]), [py.b], [t["ysum"].b])
                            yield
                            S.dma(YF[i * 128:(i + 1) * 128, :], t["ysum"][:], [t["ysum"].b], [YF_bs[i]], q="act")
                        else:
                            yield
                            S.op("dve", lambda: nc.vector.tensor_tensor(out=t["ysum"][:], in0=py[:], in1=yf[:], op=ALU.add),
                                 [py.b, yf.b], [t["ysum"].b])
                            yield
                            S.op("dve", lambda: nc.vector.tensor_reduce(out=gs1[:], in_=v3(t["ysum"][:]), axis=AX.X, op=ALU.add),
                                 [t["ysum"].b], [gs1.b])
                            yield
                            S.op("dve", lambda: nc.vector.tensor_scalar(out=gs1[:], in0=gs1[:], scalar1=-1.0 / HD, scalar2=None,
                                                                        op0=ALU.mult), [gs1.b], [gs1.b])
                            yield
                            S.op("dve", lambda: nc.vector.tensor_tensor(out=v3(t["ysum"][:]), in0=v3(t["ysum"][:]),
                                                                        in1=bc(gs1[:].unsqueeze(2), [128, NH, HD]), op=ALU.add),
                                 [t["ysum"].b, gs1.b], [t["ysum"].b])
                            yield
                            S.op("pool", lambda: nc.gpsimd.tensor_tensor(out=t["sq2"][:], in0=t["ysum"][:], in1=t["ysum"][:], op=ALU.mult),
                                 [t["ysum"].b], [t["sq2"].b])
                            yield
                            S.op("dve", lambda: nc.vector.tensor_reduce(out=gs2[:], in_=v3(t["sq2"][:]), axis=AX.X, op=ALU.add),
                                 [t["sq2"].b], [gs2.b])
                            yield
                            S.op("dve", lambda: nc.vector.tensor_scalar(out=gs2[:], in0=gs2[:], scalar1=1.0 / HD, scalar2=LNX_EPS,
                                                                        op0=ALU.mult, op1=ALU.add), [gs2.b], [gs2.b])
                            yield
                            S.op("act", lambda: nc.scalar.activation(out=grs[:], in_=gs2[:], func=AF.Sqrt), [gs2.b], [grs.b])
                            yield
                            S.op("dve", lambda: nc.vector.reciprocal(out=grs[:], in_=grs[:]), [grs.b], [grs.b])
                            yield
                            S.op("dve", lambda: nc.vector.tensor_tensor(out=v3(t["ysum"][:]), in0=v3(t["ysum"][:]),
                                                                        in1=bc(grs[:].unsqueeze(2), [128, NH, HD]), op=ALU.mult),
                                 [t["ysum"].b, grs.b], [t["ysum"].b])
                            yield
                            S.op("pool", lambda: nc.gpsimd.tensor_tensor(out=t["ysum"][:], in0=t["ysum"][:], in1=prm["lxg"][:], op=ALU.mult),
                                 [t["ysum"].b, prm["lxg"].b], [t["ysum"].b])
                            yield
                            S.op("pool", lambda: nc.gpsimd.tensor_tensor(out=t["ysum"][:], in0=t["ysum"][:], in1=prm["lxb"][:], op=ALU.add),
                                 [t["ysum"].b, prm["lxb"].b], [t["ysum"].b])
                            yield
                            S.op("dve", lambda: nc.vector.tensor_tensor(out=t["ysum"][:], in0=t["ysum"][:], in1=t["vbon"][:], op=ALU.add),
                                 [t["ysum"].b, t["vbon"].b], [t["ysum"].b])
                            yield
                            S.op("dve", lambda: nc.vector.tensor_tensor(out=bft["ob"][:], in0=t["ysum"][:], in1=t["sz"][:], op=ALU.mult),
                                 [t["ysum"].b, t["sz"].b], [bft["ob"].b])
                            pto = pbank("b")
                            ptvo = pto[:].bitcast(BF16)

                            def fo():
                                ins = None
                                for blk in range(4):
                                    ins = nc.tensor.transpose(out=ptvo[:, blk * 128:(blk + 1) * 128],
                                                              in_=bft["ob"][:, blk * 128:(blk + 1) * 128], identity=identb[:])
                                return ins
                            yield
                            S.op("pe", fo, [bft["ob"].b, identb.b], [pto.b])
                            yield
                            S.op("act", lambda: nc.scalar.copy(out=oT[:].rearrange("p b t -> p (b t)"), in_=ptvo[:, 0:512]),
                                 [pto.b], [oT.b])
                            yield
                            S.dma(YR[:, g * 4:(g + 1) * 4, i * 128:(i + 1) * 128], oT[:], [oT.b], [YR_bs[g][i]], q="act")

                    def interleave3(gens, weights):
                        done = [False] * len(gens)
                        acc = [0.0] * len(gens)
                        while not all(done):
                            for gi, g_ in enumerate(gens):
                                if done[gi]:
                                    continue
                                acc[gi] += weights[gi]
                                while acc[gi] >= 1.0 and not done[gi]:
                                    acc[gi] -= 1.0
                                    try:
                                        next(g_)
                                    except StopIteration:
                                        done[gi] = True

                    wts = (1.0, 2.0, 1.5) if bwd else (1.0, 1.4, 0.5)
                    for rnd in range(NT + 2):
                        gens = []
                        ws = []
                        if rnd < NT:
                            gens.append(front(rnd, order[rnd])); ws.append(wts[0])
                        if 0 <= rnd - 1 < NT:
                            gens.append(back1(rnd - 1, order[rnd - 1])); ws.append(wts[1])
                        if 0 <= rnd - 2 < NT:
                            gens.append(back2(rnd - 2, order[rnd - 2])); ws.append(wts[2])
                        interleave3(gens, ws)
                    S.barrier()
                    esd.close()
                S.barrier()

        with contextlib.ExitStack() as es:
            if not DBG.get("c", True):
                raise_skip = True
            else:
                raise_skip = False
            wc = sb(es, "wc", [128, 8, NCV], BF16)
            wo = sb(es, "wo", [128, 16, D], BF16)
            with contextlib.ExitStack() as es2:
                wst = [sb(es2, f"cwst{i}", [128, 8, 512], F32) for i in range(2)]
                for bi in range(NCV // 512 if DBG.get("cw", True) else 0):
                    w_ = wst[bi % 2]
                    S.dma(w_[:], w_in_d[:, bi * 512:(bi + 1) * 512].rearrange("(c p) n -> p c n", p=128), [], [w_.b])
                    if bi % 2 == 0:
                        S.op("act", lambda: nc.scalar.copy(out=wc[:, :, bi * 512:(bi + 1) * 512], in_=w_[:]), [w_.b], [wc.b])
                    else:
                        S.op("dve", lambda: nc.vector.tensor_copy(out=wc[:, :, bi * 512:(bi + 1) * 512], in_=w_[:]), [w_.b], [wc.b])
                for bi in range(4 if DBG.get("cw", True) else 0):
                    w_ = wst[bi % 2]
                    S.dma(w_[:, 0:4, :], wout_d[bi * 512:(bi + 1) * 512, 0:512].rearrange("(c p) n -> p c n", p=128), [], [w_.b])
                    S.dma(w_[:, 4:8, :], wout_d[bi * 512:(bi + 1) * 512, 512:1024].rearrange("(c p) n -> p c n", p=128), [], [w_.b])
                    S.op("act", lambda: nc.scalar.copy(out=wo[:, bi * 4:(bi + 1) * 4, 0:512], in_=w_[:, 0:4, :]), [w_.b], [wo.b])
                    S.op("dve", lambda: nc.vector.tensor_copy(out=wo[:, bi * 4:(bi + 1) * 4, 512:1024], in_=w_[:, 4:8, :]), [w_.b], [wo.b])
                S.barrier()
            cpar = sb(es, "cpar", [128, 8, 4], F32)
            for j in range(3 if DBG.get("cp", True) else 0):
                S.dma(cpar[:, :, j:j + 1], conv_w_d[j, :].rearrange("(c p o) -> p c o", p=128, o=1), [], [cpar.b])
            S.dma(cpar[:, :, 3:4], conv_b_d.rearrange("(c p o) -> p c o", p=128, o=1), [], [cpar.b])
            gbc = sb(es, "c_g", [128, D], F32); bbc = sb(es, "c_b", [128, D], F32)
            g2 = sb(es, "c_g2", [128, D], F32); b2 = sb(es, "c_b2", [128, D], F32)
            for tl, src in ((gbc, emb_g_d), (bbc, emb_b_d), (g2, lng_d), (b2, lnb_d)):
                bcast_load(tl, 0, src, D)
            xth = [sb(es, f"cxth{i}", [128, 8, 130], BF16) for i in range(2)]
            ymT = [sb(es, f"ymT{i}", [128, 16, 128], BF16) for i in range(2)]
            xt = [sb(es, f"c_x{i}", [128, D], F32) for i in range(2)]
            xn = sb(es, "c_xn", [128, D], F32); xg = sb(es, "c_xg", [128, D], F32); xh = sb(es, "c_xh", [128, D], F32)
            sres = sb(es, "c_s", [128, D], F32); on = sb(es, "c_on", [128, D], F32); og = sb(es, "c_og", [128, D], F32)
            yo = [sb(es, f"c_yo{i}", [128, D], F32) for i in range(2)]
            st6 = sb(es, "c_st", [128, 2, SD], F32); mv = sb(es, "c_mv", [128, AD], F32)
            rstd = sb(es, "c_rs", [128, 1], F32); nb_ = sb(es, "c_nb", [128, 1], F32)
            st6b = sb(es, "c_stb", [128, 2, SD], F32); mvb = sb(es, "c_mvb", [128, AD], F32)
            rstdb = sb(es, "c_rsb", [128, 1], F32); nbb = sb(es, "c_nbb", [128, 1], F32)
            hS = sb(es, "c_hS", [128, 130], F32); pp = sb(es, "c_pp", [128, 130], F32)
            qq = sb(es, "c_qq", [128, 128], F32); szc = sb(es, "c_sz", [128, 128], F32)
            def c_loads(ti):
                p_ = ti % 2
                load_xth(xth[p_], ti)
                S.dma(xt[p_][:], xs[ti * 128:(ti + 1) * 128, :], [], [xt[p_].b])
                S.dma(ymT[p_][:, 8:16, :], YR[:, :, ti * 128:(ti + 1) * 128], [YR_bs[0][ti], YR_bs[1][ti]], [ymT[p_].b])
            if not raise_skip:
                c_loads(0)
            for i in range(0 if raise_skip else NT):
                p = i % 2
                xh_ = xth[p]
                if i + 1 < NT:
                    c_loads(i + 1)
                for cbk in range(8):
                    pa = pbank("f"); pb = pbank("f")

                    def fa():
                        ins = None
                        for qi, q in enumerate((0, 2)):
                            for kc in range(8):
                                ins = nc.tensor.matmul(pa[:, qi * 130:(qi + 1) * 130],
                                                       lhsT=wc[:, kc, q * 1024 + cbk * 128:q * 1024 + (cbk + 1) * 128],
                                                       rhs=xh_[:, kc, :], start=(kc == 0), stop=(kc == 7))
                        return ins

                    def fb():
                        ins = None
                        for qi, q in enumerate((1, 3)):
                            for kc in range(8):
                                ins = nc.tensor.matmul(pb[:, qi * 128:(qi + 1) * 128],
                                                       lhsT=wc[:, kc, q * 1024 + cbk * 128:q * 1024 + (cbk + 1) * 128],
                                                       rhs=xh_[:, kc, 1:129], start=(kc == 0), stop=(kc == 7))
                        return ins
                    S.op("pe", fa, [wc.b, xh_.b], [pa.b])
                    S.op("pe", fb, [wc.b, xh_.b], [pb.b])
                    S.op("act", lambda: nc.scalar.copy(out=hS[:], in_=pa[:, 0:130]), [pa.b], [hS.b])
                    S.op("dve", lambda: nc.vector.tensor_tensor(out=pp[:], in0=hS[:], in1=pa[:, 130:260], op=ALU.mult),
                         [hS.b, pa.b], [pp.b])
                    S.op("dve", lambda: nc.vector.tensor_scalar(out=qq[:], in0=pp[:, 1:129], scalar1=cpar[:, cbk, 1:2],
                                                                scalar2=cpar[:, cbk, 3:4], op0=ALU.mult, op1=ALU.add),
                         [pp.b, cpar.b], [qq.b])
                    S.op("dve", lambda: nc.vector.scalar_tensor_tensor(out=qq[:], in0=pp[:, 0:128], scalar=cpar[:, cbk, 0:1],
                                                                         in1=qq[:], op0=ALU.mult, op1=ALU.add),
                         [pp.b, cpar.b, qq.b], [qq.b])
                    S.op("dve", lambda: nc.vector.scalar_tensor_tensor(out=qq[:], in0=pp[:, 2:130], scalar=cpar[:, cbk, 2:3],
                                                                         in1=qq[:], op0=ALU.mult, op1=ALU.add),
                         [pp.b, cpar.b, qq.b], [qq.b])
                    S.op("act", lambda: nc.scalar.activation(out=szc[:], in_=pb[:, 128:256], func=AF.Silu), [pb.b], [szc.b])
                    S.op("dve", lambda: nc.vector.tensor_tensor(out=qq[:], in0=qq[:], in1=pb[:, 0:128], op=ALU.mult),
                         [qq.b, pb.b], [qq.b])
                    S.op("pool", lambda: nc.gpsimd.tensor_tensor(out=ymT[p][:, cbk, :], in0=qq[:], in1=szc[:], op=ALU.mult),
                         [qq.b, szc.b], [ymT[p].b])
                po = [pbank("b"), pbank("b")]
                for hf in range(2):
                    def fo():
                        ins = None
                        for mc in range(16):
                            ins = nc.tensor.matmul(po[hf][:], lhsT=ymT[p][:, mc, :], rhs=wo[:, mc, hf * 512:(hf + 1) * 512],
                                                   start=(mc == 0), stop=(mc == 15))
                        return ins
                    S.op("pe", fo, [ymT[p].b, wo.b], [po[hf].b])
                ln_stats("c", xt[p], mv, rstd, nb_, st6, LN_EPS)
                S.op("act", lambda: nc.scalar.activation(out=xn[:], in_=xt[p][:], func=AF.Identity, bias=nb_[:, 0:1],
                                                         scale=rstd[:, 0:1]), [xt[p].b, rstd.b, nb_.b], [xn.b])
                S.op("dve", lambda: nc.vector.tensor_tensor(out=xg[:], in0=xn[:], in1=gbc[:], op=ALU.mult), [xn.b, gbc.b], [xg.b])
                S.op("pool", lambda: nc.gpsimd.tensor_tensor(out=xh[:], in0=xg[:], in1=bbc[:], op=ALU.add), [xg.b, bbc.b], [xh.b])
                for hf in range(2):
                    o = slice(hf * 512, (hf + 1) * 512)
                    S.op("dve", lambda: nc.vector.scalar_tensor_tensor(out=sres[:, o], in0=xh[:, o], scalar=DN_ALPHA,
                                                                       in1=po[hf][:], op0=ALU.mult, op1=ALU.add),
                         [xh.b, po[hf].b], [sres.b])
                ln_stats("c2", sres, mvb, rstdb, nbb, st6b, LN_EPS)
                S.op("act", lambda: nc.scalar.activation(out=on[:], in_=sres[:], func=AF.Identity, bias=nbb[:, 0:1],
                                                         scale=rstdb[:, 0:1]), [sres.b, rstdb.b, nbb.b], [on.b])
                S.op("dve", lambda: nc.vector.tensor_tensor(out=og[:], in0=on[:], in1=g2[:], op=ALU.mult), [on.b, g2.b], [og.b])
                S.op("pool", lambda: nc.gpsimd.tensor_tensor(out=yo[p][:], in0=og[:], in1=b2[:], op=ALU.add), [og.b, b2.b], [yo[p].b])
                S.dma(ys[i * 128:(i + 1) * 128, :], yo[p][:], [yo[p].b], [ys_bs[i]])
            S.barrier()

        S.finish()
    return nc


CST_COLS = 128 + 2 * 896 + 1
DBG = {}
NAMES = {}


def make_consts():
    r = np.arange(128)[:, None]
    c = np.arange(128)[None, :]
    SU = (r < c).astype(np.float32); IU = (r <= c).astype(np.float32)
    SL = (r > c).astype(np.float32); IL = (r >= c).astype(np.float32)
    parts = [np.eye(128, dtype=np.float32)]
    for (S_, I_, St) in ((SU, IU, SL), (SL, IL, SU)):
        parts += [S_, I_, -S_, I_, -St, CDEC * I_, CDEC * S_]
    parts.append(np.full((128, 1), CDEC, np.float32))
    out = np.concatenate(parts, axis=1).astype(np.float32)
    assert out.shape[1] == CST_COLS
    return np.ascontiguousarray(out)


W_NAMES = ["emb_ln_g", "emb_ln_b", "w_in", "conv_w", "conv_b", "shift_mu", "w0", "w_up", "a0", "a_up",
           "k_k", "k_a", "r_k", "lnx_g", "lnx_b", "w_out", "ln_g", "ln_b"]


def weight_map(inp):
    m = {}
    for n in W_NAMES:
        a = np.asarray(inp[n], dtype=np.float32)
        if n not in ("emb_ln_g", "emb_ln_b"):
            a = a[0]
        if n == "r_k":
            a = a.reshape(1024)
        m[n] = np.ascontiguousarray(a)
    m["cst"] = make_consts()
    return m


_NC_CACHE = {}


def run_streams(streams, keeps, inp, NT, BP):
    key = (NT, BP)
    if key not in _NC_CACHE:
        _NC_CACHE[key] = build(NT, BP)
    nc = _NC_CACHE[key]
    wm = weight_map(inp)
    in_maps = []
    for s, k in zip(streams, keeps):
        d = dict(wm)
        d["xs"] = np.ascontiguousarray(s, dtype=np.float32)
        d["keep"] = np.ascontiguousarray(k, dtype=np.float32)
        in_maps.append(d)
    res = run_bass_kernel_spmd(nc, in_maps, core_ids=list(range(len(streams))))
    return [r["ys"] for r in res.results]


def kernel(**inp):
    xp = np.asarray(inp["x_prompt"], dtype=np.float32)
    xsm = np.asarray(inp["x_sample"], dtype=np.float32)
    NT, BP = 128, 16
    NB = NT // BP - 1
    ntok = NT * 128
    streams = [xsm[0], xsm[1], xp.reshape(ntok, D)]
    keeps = [np.ones((128, NB), np.float32), np.ones((128, NB), np.float32), np.zeros((128, NB), np.float32)]
    for _ in range(5):
        streams.append(np.zeros((ntok, D), np.float32))
        keeps.append(np.zeros((128, NB), np.float32))
    outs = run_streams(streams, keeps, inp, NT, BP)
    y_sample = np.stack([outs[0], outs[1]], axis=0).reshape(2, 16384, D)
    y_prompt = outs[2].reshape(8, 2048, D)
    return (y_prompt.astype(np.float32), y_sample.astype(np.float32))
```
